# Optimizing a Trainium2 kernel written in Bass

```python
import math
import jax, jax.numpy as jnp
from jax import lax
import numpy as np

D_MODEL = 2048
BATCH = 16
SEQ = 2048
DEPTH = 4

GRID_W = 64
CTX_LEN = 256
N_EVEN = (DEPTH + 1) // 2
N_ODD = DEPTH // 2

DA_HEADS = 8
DA_D = 64
DA_WIDTH = DA_HEADS * 2 * DA_D
HG_HEADS = 8
HG_K = 128
HG_V = 128
HG_WIDTH = HG_HEADS * HG_V
HG_CHUNK = 32
POOL_WIDTH = D_MODEL
POOL_WINDOWS = (2, 4, 8, 16)
POOL_GROUP = POOL_WIDTH // len(POOL_WINDOWS)

ROPE_BASE = 10000.0
Q_BLOCK = 128
EPS = 1e-6

EVEN_SPLITS = (DA_WIDTH, DA_WIDTH, DA_WIDTH, DA_WIDTH,
               HG_HEADS * HG_K, HG_HEADS * HG_K, HG_HEADS * HG_K, HG_WIDTH, HG_WIDTH)
EVEN_IN = sum(EVEN_SPLITS)
EVEN_OUT = DA_WIDTH + HG_WIDTH

kernel_name = "hybrid_diffattn_hgrn2_pool_dit"


def rms_norm(x, g):
    xf = x.astype(jnp.float32)
    y = xf * lax.rsqrt(jnp.mean(xf * xf, axis=-1, keepdims=True) + EPS)
    return (y * g.astype(jnp.float32)).astype(x.dtype)


def axial_rope(x, row, col):
    n_freq = DA_D // 4
    inv = ROPE_BASE ** (-jnp.arange(n_freq, dtype=jnp.float32) / n_freq)

    def rot(xa, pos):
        ang = pos.astype(jnp.float32)[:, None] * inv[None, :]
        cos = jnp.cos(ang)[None, :, None, None, :]
        sin = jnp.sin(ang)[None, :, None, None, :]
        x1, x2 = jnp.split(xa.astype(jnp.float32), 2, axis=-1)
        return jnp.concatenate([x1 * cos - x2 * sin, x2 * cos + x1 * sin], axis=-1)

    half = DA_D // 2
    out = jnp.concatenate([rot(x[..., :half], row), rot(x[..., half:], col)], axis=-1)
    return out.astype(x.dtype)


def diff_attention(q, k, v, lam):
    B, Tq = q.shape[0], q.shape[1]
    nb = Tq // Q_BLOCK
    qb = jnp.moveaxis(q.reshape(B, nb, Q_BLOCK, DA_HEADS, 2, DA_D), 1, 0)
    scale = DA_D ** -0.5

    def block(qi):
        s = jnp.einsum('bqhmd,bkhmd->bhmqk', qi, k, preferred_element_type=jnp.float32) * scale
        p = jax.nn.softmax(s, axis=-1)
        a = p[:, :, 0] - lam * p[:, :, 1]
        return jnp.einsum('bhqk,bkhe->bqhe', a.astype(v.dtype), v)

    o = lax.map(block, qb)
    return jnp.moveaxis(o, 0, 1).reshape(B, Tq, DA_HEADS, 2 * DA_D)


def hgrn_scan(q, k, v, logf, s0):
    B, T = q.shape[0], q.shape[1]
    n = T // HG_CHUNK

    def chunks(a):
        return jnp.moveaxis(a.astype(jnp.float32).reshape(B, n, HG_CHUNK, HG_HEADS, a.shape[-1]), 1, 0)

    tri = jnp.tril(jnp.ones((HG_CHUNK, HG_CHUNK), dtype=bool))[None, :, :, None, None]

    def step(s, inp):
        qc, kc, vc, gc = inp
        b = jnp.cumsum(gc, axis=1)
        inter = jnp.einsum('bchk,bhkv->bchv', qc * jnp.exp(b), s)
        diff = b[:, :, None] - b[:, None, :]
        decay = jnp.exp(jnp.where(tri, diff, -jnp.inf))
        att = jnp.einsum('bthk,btshk,bshk->bhts', qc, decay, kc)
        intra = jnp.einsum('bhts,bshv->bthv', att, vc)
        b_last = b[:, -1]
        s_new = jnp.exp(b_last)[..., None] * s + jnp.einsum(
            'bshk,bshv->bhkv', kc * jnp.exp(b_last[:, None] - b), vc)
        return s_new, inter + intra

    s_fin, o = lax.scan(step, s0, (chunks(q), chunks(k), chunks(v), chunks(logf)))
    return jnp.moveaxis(o, 0, 1).reshape(B, T, HG_HEADS, HG_V), s_fin


def directional_scan(q_c, k_c, v_c, g_c, q_l, k_l, v_l, g_l, reverse):
    fl = (lambda a: jnp.flip(a, axis=1)) if reverse else (lambda a: a)
    s0 = jnp.zeros((q_c.shape[0], HG_HEADS, HG_K, HG_V), jnp.float32)
    o_c, s_c = hgrn_scan(fl(q_c), fl(k_c), fl(v_c), fl(g_c), s0)
    o_l, _ = hgrn_scan(fl(q_l), fl(k_l), fl(v_l), fl(g_l), s_c)
    return fl(o_c), fl(o_l)


def hg_inputs(p, lb):
    B_, T_ = p[4].shape[0], p[4].shape[1]
    hk = lambda a: a.reshape(B_, T_, HG_HEADS, HG_K)
    q = jax.nn.silu(hk(p[4]))
    v = p[7].reshape(B_, T_, HG_HEADS, HG_V)
    dirs = []
    for z, lbd in ((p[5], lb[0]), (p[6], lb[1])):
        z = hk(z).astype(jnp.float32)
        lbd = lbd.reshape(HG_HEADS, HG_K)
        logf = jnp.log(lbd + (1.0 - lbd) * jax.nn.sigmoid(z))
        k = (1.0 - lbd) * jax.nn.sigmoid(-z)
        dirs.append((k, logf))
    return q, v, dirs


def even_mixer(h_lat, h_ctx, w_in, w_out, lam_vecs, subln_g, lb, hg_norm_g, layer_idx, row, col, need_ctx):
    idx = tuple(int(i) for i in np.cumsum(EVEN_SPLITS)[:-1])
    pl = jnp.split(h_lat @ w_in, idx, axis=-1)
    pc = jnp.split(h_ctx @ w_in, idx, axis=-1)

    lam_init = 0.8 - 0.6 * math.exp(-0.3 * layer_idx)
    lv = lam_vecs.astype(jnp.float32)
    lam = jnp.exp(jnp.sum(lv[0] * lv[1])) - jnp.exp(jnp.sum(lv[2] * lv[3])) + lam_init
    qk_shape = lambda a: a.reshape(a.shape[0], a.shape[1], DA_HEADS, 2, DA_D)
    v_shape = lambda a: a.reshape(a.shape[0], a.shape[1], DA_HEADS, 2 * DA_D)
    q_l = axial_rope(qk_shape(pl[0]), row, col)
    k_l = axial_rope(qk_shape(pl[1]), row, col)
    k_c = qk_shape(pc[1])
    v_c = v_shape(pc[2])
    keys = jnp.concatenate([k_c, k_l], axis=1)
    vals = jnp.concatenate([v_c, v_shape(pl[2])], axis=1)

    def da_finish(o, gate):
        o = rms_norm(o.astype(gate.dtype), subln_g) * (1.0 - lam_init)
        return o.reshape(o.shape[0], o.shape[1], DA_WIDTH) * jax.nn.silu(gate)

    a_lat = da_finish(diff_attention(q_l, keys, vals, lam), pl[3])

    q_hl, v_hl, dirs_l = hg_inputs(pl, lb)
    q_hc, v_hc, dirs_c = hg_inputs(pc, lb)
    oc_f, ol_f = directional_scan(q_hc, dirs_c[0][0], v_hc, dirs_c[0][1],
                                  q_hl, dirs_l[0][0], v_hl, dirs_l[0][1], reverse=False)
    oc_b, ol_b = directional_scan(q_hc, dirs_c[1][0], v_hc, dirs_c[1][1],
                                  q_hl, dirs_l[1][0], v_hl, dirs_l[1][1], reverse=True)

    def hg_finish(o, gate):
        o = rms_norm(o.astype(gate.dtype), hg_norm_g)
        return o.reshape(o.shape[0], o.shape[1], HG_WIDTH) * jax.nn.silu(gate)

    b_lat = hg_finish(ol_f + ol_b, pl[8])
    y_lat = jnp.concatenate([a_lat, b_lat], axis=-1) @ w_out

    y_ctx = None
    if need_ctx:
        a_ctx = da_finish(diff_attention(qk_shape(pc[0]), k_c, v_c, lam), pc[3])
        b_ctx = hg_finish(oc_f + oc_b, pc[8])
        y_ctx = jnp.concatenate([a_ctx, b_ctx], axis=-1) @ w_out
    return y_lat, y_ctx


def pool_minus_identity(u):
    B, T, E = u.shape
    ug = u.reshape(B, T, len(POOL_WINDOWS), POOL_GROUP)
    t = np.arange(T)
    outs = []
    for j, w in enumerate(POOL_WINDOWS):
        xg = ug[:, :, j].astype(jnp.float32)
        cs = jnp.concatenate([jnp.zeros((B, 1, POOL_GROUP), jnp.float32), jnp.cumsum(xg, axis=1)], axis=1)
        lo = np.clip(t - w // 2, 0, T)
        hi = np.clip(t + (w - w // 2), 0, T)
        cnt = (hi - lo).astype(np.float32)[None, :, None]
        mean = (cs[:, hi] - cs[:, lo]) / cnt
        outs.append(mean - xg)
    return jnp.stack(outs, axis=2).reshape(B, T, E).astype(u.dtype)


def odd_mixer(h, w_in, w_pool, ls, w_out):
    u, z = jnp.split(h @ w_in, 2, axis=-1)
    r = pool_minus_identity(u)
    B, T = r.shape[0], r.shape[1]
    y = jnp.einsum('btgc,gcd->btgd', r.reshape(B, T, len(POOL_WINDOWS), POOL_GROUP), w_pool)
    y = y.reshape(B, T, POOL_WIDTH) * ls * jax.nn.silu(z)
    return y @ w_out


def setup_inputs(seed: int = 0) -> dict:
    key = jax.random.key(seed)
    ks = jax.random.split(key, 20)
    D = D_MODEL
    nrm = lambda k, shape, s: jax.random.normal(k, shape, jnp.float32) * s
    return {
        "x": nrm(ks[0], (BATCH, SEQ, D), 1.0),
        "c": nrm(ks[1], (BATCH, D), 1.0),
        "ctx": nrm(ks[2], (BATCH, CTX_LEN, D), 1.0),
        "c_ctx": nrm(ks[3], (D,), 1.0),
        "w_mod": nrm(ks[4], (DEPTH, D, 3 * D), 0.5 * D ** -0.5),
        "b_mod": nrm(ks[5], (DEPTH, 3 * D), 0.02),
        "g_pre": 1.0 + nrm(ks[6], (DEPTH, D), 0.02),
        "g_post": 1.0 + nrm(ks[7], (DEPTH, D), 0.02),
        "ev_w_in": nrm(ks[8], (N_EVEN, D, EVEN_IN), D ** -0.5),
        "ev_w_out": nrm(ks[9], (N_EVEN, EVEN_OUT, D), EVEN_OUT ** -0.5),
        "ev_lambda": nrm(ks[10], (N_EVEN, 4, DA_D), 0.1),
        "ev_subln_g": 1.0 + nrm(ks[11], (N_EVEN, 2 * DA_D), 0.02),
        "ev_hg_lb_logits": nrm(ks[12], (N_EVEN, 2, HG_HEADS * HG_K), 1.0),
        "ev_hg_norm_g": 1.0 + nrm(ks[13], (N_EVEN, HG_V), 0.02),
        "od_w_in": nrm(ks[14], (N_ODD, D, 2 * POOL_WIDTH), D ** -0.5),
        "od_w_pool": nrm(ks[15], (N_ODD, len(POOL_WINDOWS), POOL_GROUP, POOL_GROUP), POOL_GROUP ** -0.5),
        "od_scale": 1.0 + nrm(ks[16], (N_ODD, POOL_WIDTH), 0.02),
        "od_w_out": nrm(ks[17], (N_ODD, POOL_WIDTH, D), POOL_WIDTH ** -0.5),
    }


def reference(x, c, ctx, c_ctx, w_mod, b_mod, g_pre, g_post, ev_w_in, ev_w_out, ev_lambda, ev_subln_g,
              ev_hg_lb_logits, ev_hg_norm_g, od_w_in, od_w_pool, od_scale, od_w_out):
    S = x.shape[1]
    ROWS = S // GRID_W
    row = jnp.repeat(jnp.arange(ROWS), GRID_W)
    col = jnp.tile(jnp.arange(GRID_W), ROWS)
    lb_cum = jnp.cumsum(jax.nn.softmax(ev_hg_lb_logits.astype(jnp.float32), axis=0), axis=0)
    lb_all = lb_cum - lb_cum[0]
    silu_c = jax.nn.silu(c)
    silu_cc = jax.nn.silu(c_ctx)
    for l in range(DEPTH):
        even = (l % 2 == 0)
        need_ctx = l < DEPTH - 1
        ctx_active = even or need_ctx
        mod_l = silu_c @ w_mod[l] + b_mod[l]
        sh_l, sc_l, gt_l = jnp.split(mod_l[:, None, :], 3, axis=-1)
        h_lat = rms_norm(x, g_pre[l]) * (1.0 + sc_l) + sh_l
        if ctx_active:
            mod_c = silu_cc @ w_mod[l] + b_mod[l]
            sh_c, sc_c, gt_c = jnp.split(mod_c, 3, axis=-1)
            h_ctx = rms_norm(ctx, g_pre[l]) * (1.0 + sc_c) + sh_c
        if even:
            e = l // 2
            y_lat, y_ctx = even_mixer(h_lat, h_ctx, ev_w_in[e], ev_w_out[e], ev_lambda[e], ev_subln_g[e],
                                      lb_all[e], ev_hg_norm_g[e], l, row, col, need_ctx)
        else:
            o = l // 2
            y_lat = odd_mixer(h_lat, od_w_in[o], od_w_pool[o], od_scale[o], od_w_out[o])
            y_ctx = odd_mixer(h_ctx, od_w_in[o], od_w_pool[o], od_scale[o], od_w_out[o]) if need_ctx else None
        x = x + gt_l * rms_norm(y_lat, g_post[l])
        if need_ctx:
            ctx = ctx + gt_c * rms_norm(y_ctx, g_post[l])
    return x
```

```python
import math
from contextlib import ExitStack, contextmanager
import numpy as np
import concourse.bass as bass
import concourse.mybir as mybir
from concourse.bass_utils import run_bass_kernel_spmd

F32 = mybir.dt.float32
BF16 = mybir.dt.bfloat16
AF = mybir.ActivationFunctionType
ALU = mybir.AluOpType
AX = mybir.AxisListType
EPS = 1e-6
ROPE_BASE = 10000.0
POOL_WINDOWS = (2, 4, 8, 16)
HC = 32


class Cfg:
    def __init__(s, D=2048, T_LAT=2048, T_CTX=256, GRID_W=64, DA_HEADS=8, HG_HEADS=8, NB=2, DEPTH=4):
        s.D, s.T_LAT, s.T_CTX, s.GRID_W = D, T_LAT, T_CTX, GRID_W
        s.DA_HEADS, s.HG_HEADS, s.NB, s.DEPTH = DA_HEADS, HG_HEADS, NB, DEPTH
        s.KC = D // 128
        s.NT_C = T_CTX // 128
        s.NT_L = T_LAT // 128
        s.NT = s.NT_C + s.NT_L
        s.T = T_CTX + T_LAT
        s.DA_W = DA_HEADS * 128
        s.HG_W = HG_HEADS * 128
        s.EVEN_IN = 4 * s.DA_W + 5 * s.HG_W
        s.EVEN_OUT = s.DA_W + s.HG_W
        assert s.EVEN_OUT == D
        s.PG = D // 4
        s.PGC = s.PG // 128
        s.R = NB + 1
        s.NBK = max(1, D // 512)
        s.BW = min(512, D)
        s.N_EVEN = (DEPTH + 1) // 2
        s.N_ODD = DEPTH // 2


FULL = Cfg()


class Ev:
    __slots__ = ("sem", "val", "ek")

    def __init__(s, sem, val, ek):
        s.sem, s.val, s.ek = sem, val, ek


class Buf:
    __slots__ = ("name", "w", "rs")

    def __init__(s, name):
        s.name, s.w, s.rs = name, None, {}


ENG = ("pe", "act", "dve", "pool", "sp")
SEM_LIMIT = 15000


class Prog:
    def __init__(s, nc, stack, n_dma=32):
        s.nc = nc
        s.stack = stack
        s.e = {"pe": nc.tensor, "act": nc.scalar, "dve": nc.vector, "pool": nc.gpsimd, "sp": nc.sync}
        nsem = {"pe": 12, "act": 5, "dve": 6, "pool": 5, "sp": 1}
        s.sems = {k: [stack.enter_context(nc.semaphore(f"s_{k}_{i}")) for i in range(n)] for k, n in nsem.items()}
        s.si = {k: 0 for k in ENG}
        s.cnt = {k: 0 for k in ENG}
        s.seen = {k: {} for k in ENG}
        s.dsems = [stack.enter_context(nc.semaphore(f"s_dma_{i}")) for i in range(n_dma)]
        s.dval = [0] * n_dma
        s.di = 0
        s.uid = 0
        s.nwait = 0
        s.nops = 0

    def _wait(s, ek, ev):
        seen = s.seen[ek]
        if seen.get(ev.sem, 0) >= ev.val:
            return
        s.e[ek].wait_ge(ev.sem, ev.val)
        seen[ev.sem] = ev.val
        s.nwait += 1

    def _deps(s, ek, r, w):
        for b in r:
            if b.w is not None and not (b.w.ek == ek and ek == "pe"):
                s._wait(ek, b.w)
        for b in w:
            if b.w is not None and not (b.w.ek == ek and ek == "pe"):
                s._wait(ek, b.w)
            for ev in b.rs.values():
                if ev.ek != ek:
                    s._wait(ek, ev)

    def _mark(s, ev, r, w):
        for b in r:
            b.rs[ev.sem] = ev
        for b in w:
            b.w = ev
            b.rs = {}

    def op(s, ek, fn, r=(), w=()):
        s._deps(ek, r, w)
        ins = fn(s.e[ek])
        if s.cnt[ek] >= SEM_LIMIT:
            s.si[ek] += 1
            s.cnt[ek] = 0
        s.cnt[ek] += 1
        sem = s.sems[ek][s.si[ek]]
        ins.then_inc(sem, 1)
        ev = Ev(sem, s.cnt[ek], ek)
        s._mark(ev, r, w)
        s.nops += 1
        return ev

    def dma(s, qk, out, in_, r=(), w=()):
        s._deps(qk, r, w)
        i = s.di
        s.di = (s.di + 1) % len(s.dsems)
        sem = s.dsems[i]
        if s.dval[i] > 0:
            s._wait(qk, Ev(sem, s.dval[i], None))
        s.dval[i] += 16
        s.e[qk].dma_start(out=out, in_=in_).then_inc(sem, 16)
        ev = Ev(sem, s.dval[i], None)
        s._mark(ev, r, w)
        s.nops += 1
        return ev

    def barrier(s, engines=ENG):
        evs = [Ev(s.sems[k][s.si[k]], s.cnt[k], k) for k in ENG if s.cnt[k] > 0]
        evs += [Ev(s.dsems[i], s.dval[i], None) for i in range(len(s.dsems)) if s.dval[i] > 0]
        for ek in engines:
            for ev in evs:
                if ev.ek != ek:
                    s._wait(ek, ev)

    def final_wait(s, ek="sp"):
        s.barrier(engines=(ek,))

    @contextmanager
    def scope(s):
        st = ExitStack()
        sc = Scope(s, st)
        try:
            yield sc
        finally:
            s.barrier()
            st.close()


class Scope:
    def __init__(s, P, st):
        s.P, s.st = P, st

    def sb(s, name, shape, dt):
        s.P.uid += 1
        return s.st.enter_context(s.P.nc.sbuf_tensor(f"{name}_{s.P.uid}", list(shape), dt))

    def ps(s, name, shape, dt):
        s.P.uid += 1
        return s.st.enter_context(s.P.nc.psum_tensor(f"{name}_{s.P.uid}", list(shape), dt))


def rope_tables(cfg):
    T = cfg.T_LAT
    t = np.arange(T)
    row, col = t // cfg.GRID_W, t % cfg.GRID_W
    nf = 16
    inv = ROPE_BASE ** (-np.arange(nf, dtype=np.float32) / nf)
    cosT = np.zeros((T, 64), np.float32)
    sinT = np.zeros((T, 64), np.float32)
    for blk, pos in ((0, row), (1, col)):
        ang = pos.astype(np.float32)[:, None] * inv[None, :]
        c, sn = np.cos(ang), np.sin(ang)
        cosT[:, blk * 32:blk * 32 + 16] = c
        cosT[:, blk * 32 + 16:blk * 32 + 32] = c
        sinT[:, blk * 32:blk * 32 + 16] = -sn
        sinT[:, blk * 32 + 16:blk * 32 + 32] = sn
    cos2 = np.concatenate([cosT, cosT], 1)
    sin2 = np.concatenate([sinT, sinT], 1)
    cos2 = cos2.reshape(cfg.NT_L, 128, 128).transpose(1, 0, 2).copy()
    sin2 = sin2.reshape(cfg.NT_L, 128, 128).transpose(1, 0, 2).copy()
    return cos2, sin2


def hgrn_consts():
    n = 128
    s = np.arange(n)[:, None]
    t = np.arange(n)[None, :]
    same = (s // HC) == (t // HC)
    mid_f = (t // HC) * HC + HC // 2 - 1
    mid_b = (t // HC) * HC + HC // 2
    M1f = (same & (s <= t)).astype(np.float32) - (same & (s <= mid_f)).astype(np.float32)
    M1b = (same & (s >= t)).astype(np.float32) - (same & (s >= mid_b)).astype(np.float32)
    maskf = (same & (s <= t)).astype(np.float32)
    maskb = (same & (s >= t)).astype(np.float32)
    nchunk = n // HC
    wf = np.zeros((n, nchunk * 3), np.float32)
    wb = np.zeros((n, nchunk * 3), np.float32)
    for c in range(nchunk):
        lo, hi = c * HC, (c + 1) * HC
        mf = lo + HC // 2 - 1
        mb = lo + HC // 2
        idx = np.arange(n)
        inck = (idx >= lo) & (idx < hi)
        wf[:, c * 3 + 0] = inck
        wf[:, c * 3 + 1] = inck & (idx > mf)
        wf[:, c * 3 + 2] = inck & (idx <= mf)
        wb[:, c * 3 + 0] = inck
        wb[:, c * 3 + 1] = inck & (idx < mb)
        wb[:, c * 3 + 2] = inck & (idx >= mb)
    return np.stack([M1f, M1b]), np.stack([maskf, maskb]), np.stack([wf, wb])


def pool_blocks(T):
    out = {}
    nt = T // 128
    for wi, w in enumerate(POOL_WINDOWS):
        t = np.arange(T)
        lo = np.clip(t - w // 2, 0, T)
        hi = np.clip(t + (w - w // 2), 0, T)
        M = np.zeros((T, T), np.float32)
        for tt in range(T):
            M[tt, lo[tt]:hi[tt]] = 1.0 / (hi[tt] - lo[tt])
            M[tt, tt] -= 1.0
        MT = M.T
        for ti in range(nt):
            for di in (-1, 0, 1):
                si = ti + di
                if 0 <= si < nt:
                    out[(wi, ti, di)] = MT[si * 128:(si + 1) * 128, ti * 128:(ti + 1) * 128].copy()
    return out


def pool_consts(cfg):
    uniq = []
    keys = {}
    maps = {}
    for seg, T in (("c", cfg.T_CTX), ("l", cfg.T_LAT)):
        blocks = pool_blocks(T)
        for k, blk in blocks.items():
            kb = blk.tobytes()
            if kb not in keys:
                keys[kb] = len(uniq)
                uniq.append(blk)
            maps[(seg,) + k] = keys[kb]
    return np.stack(uniq), maps


def build(cfg, layers=None, pool_maps=None, n_pool_blocks=0):
    layers = list(range(cfg.DEPTH)) if layers is None else layers
    D, KC, T, NT, NT_C, NT_L, R, NB = cfg.D, cfg.KC, cfg.T, cfg.NT, cfg.NT_C, cfg.NT_L, cfg.R, cfg.NB
    BW, NBK = cfg.BW, cfg.NBK
    nc = bass.Bass("TRN2", target_bir_lowering=False)

    def din(name, shape, dt=F32):
        return nc.dram_tensor(name, list(shape), dt, kind="ExternalInput").ap()

    x_in = din("x", [NB, cfg.T_LAT, D])
    ctx_in = din("ctx", [NB, cfg.T_CTX, D])
    cT_in = din("cT", [128, KC, R])
    wmod_in = din("w_mod", [cfg.DEPTH, D, 3 * D])
    bmodc_in = din("bmodc", [128, cfg.DEPTH, 2 * KC, R])
    gprec_in = din("gprec", [128, cfg.DEPTH, KC, R])
    bmodg_in = din("bmodg", [R, cfg.DEPTH, D])
    gpostr_in = din("gpostr", [R, cfg.DEPTH, D])
    sel_in = din("sel", [R, R * 128])
    ident_in = din("ident", [128, 128])
    wA_in = din("wA", [cfg.N_EVEN, cfg.DA_HEADS, 128, KC, 512])
    wB_in = din("wB", [cfg.N_EVEN, cfg.HG_HEADS, 128, KC, 640])
    wEO_in = din("wEO", [cfg.N_EVEN, 128, KC, D])
    lam_in = din("lamb", [128, cfg.N_EVEN, 4, 64])
    subln_in = din("sublnb", [128, cfg.N_EVEN, 128])
    hgn_in = din("hgnb", [128, cfg.N_EVEN, 128])
    lbl_in = din("lblb", [128, cfg.N_EVEN, 2, cfg.HG_W])
    cos_in = din("ropec", [128, NT_L, 128])
    sin_in = din("ropes", [128, NT_L, 128])
    hgM_in = din("hgM", [2, 128, 128])
    hgmask_in = din("hgmask", [2, 128, 128])
    hgw_in = din("hgw", [2, 128, 12])
    hgcm_in = din("hgcm", [128, 4])
    wOD_in = din("wOD", [cfg.N_ODD, 4, 128, KC, 2 * cfg.PG])
    wPL_in = din("wPL", [cfg.N_ODD, 4, 128, cfg.PGC, cfg.PG])
    wOO_in = din("wOO", [cfg.N_ODD, 128, KC, D])
    odsc_in = din("odsc", [128, cfg.N_ODD, KC])
    poolc_in = din("poolc", [max(1, n_pool_blocks), 128, 128])

    y_out = nc.dram_tensor("y", [NB, cfg.T_LAT, D], F32, kind="ExternalOutput").ap()
    cscr = nc.dram_tensor("cscr", [NB, cfg.T_CTX, D], F32, kind="Internal").ap()
    gscr = nc.dram_tensor("gscr", [KC, 128, T], BF16, kind="Internal").ap()

    with ExitStack() as stack:
        P = Prog(nc, stack)
        top = Scope(P, stack)

        hT = top.sb("hT", [128, KC, T], BF16)
        HT = [Buf(f"HT{i}") for i in range(NT)]
        ident_f = top.sb("identf", [128, 128], F32)
        ident_b = top.sb("identb", [128, 128], BF16)
        scT = top.sb("scT", [128, KC, R], F32)
        selT = top.sb("selT", [R, R * 128], F32)
        modc = top.sb("modc", [128, 2 * KC, R], F32)
        Acol = top.sb("Acol", [128, KC, R], F32)
        GTrow = top.sb("GTrow", [R, D], F32)
        bmodc = top.sb("bmodcs", [128, cfg.DEPTH, 2 * KC, R], F32)
        gprec = top.sb("gprecs", [128, cfg.DEPTH, KC, R], F32)
        B_const = Buf("const")
        B_scT = Buf("scT")
        B_modc = Buf("modc")
        B_Acol = Buf("Acol")
        B_GTrow = Buf("GTrow")
        DXL = [[Buf(f"dxl{s}_{i}") for i in range(NT_L)] for s in range(NB)]
        DXC = [[Buf(f"dxc{s}_{i}") for i in range(NT_C)] for s in range(NB)]
        GS = [Buf(f"gs{k}") for k in range(KC)]

        P.dma("sp", ident_f[:], ident_in[:, :], w=[B_const])
        P.dma("sp", scT[:], cT_in[:, :, :], w=[B_scT])
        P.dma("sp", selT[:], sel_in[:, :], w=[B_const])
        P.dma("sp", bmodc[:], bmodc_in[:, :, :, :], w=[B_const])
        P.dma("sp", gprec[:], gprec_in[:, :, :, :], w=[B_const])
        P.op("dve", lambda e: e.tensor_copy(ident_b[:], ident_f[:]), r=[B_const], w=[B_const])
        P.op("act", lambda e: e.activation(out=scT[:], in_=scT[:], func=AF.Silu), r=[B_scT], w=[B_scT])

        st_state = {"i": 0}

        def load_w(sc_stage, SB, dst, src, nk, ncols, wbuf, kq=4):
            for k0 in range(0, nk, kq):
                k1 = min(nk, k0 + kq)
                i = st_state["i"] % 2
                st_state["i"] += 1
                stg = sc_stage[i]
                v = stg[:, 0:(k1 - k0) * ncols].rearrange("p (k n) -> p k n", n=ncols)
                P.dma("sp", v, src[:, k0:k1, :], w=[SB[i]])
                P.op("pool", lambda e, v=v, k0=k0, k1=k1: e.tensor_copy(dst[:, k0:k1, :], v), r=[SB[i]], w=[wbuf])

        def mod_phase(l):
            with P.scope() as sc:
                wm = [sc.sb("wm", [128, KC, 512], F32) for _ in range(2)]
                WM = [Buf("wm0"), Buf("wm1")]
                bg = sc.sb("bg", [R, D], F32)
                gp = sc.sb("gp", [R, D], F32)
                B_bg = Buf("bg")
                psA = sc.ps("psA", [128, 512], F32)
                psB = sc.ps("psB", [128, 512], F32)
                PSA, PSB = Buf("psA"), Buf("psB")
                P.dma("sp", bg[:], bmodg_in[:, l, :], w=[B_bg])
                P.dma("sp", gp[:], gpostr_in[:, l, :], w=[B_bg])
                wsrc = wmod_in[l].rearrange("(kc p) n -> p kc n", p=128)
                ncolp = (2 * D) // 512
                for piece in range((3 * D) // 512):
                    sl = piece % 2
                    n0 = piece * 512
                    P.dma("sp", wm[sl][:], wsrc[:, :, n0:n0 + 512], w=[WM[sl]])
                    if piece < ncolp:
                        for jj in range(4):
                            for kc in range(KC):
                                P.op("pe", lambda e, jj=jj, kc=kc, sl=sl: e.matmul(
                                    psA[:, jj * R:(jj + 1) * R], wm[sl][:, kc, jj * 128:(jj + 1) * 128], scT[:, kc, :],
                                    start=(jj == 0 and kc == 0), stop=(jj == 3 and kc == KC - 1)),
                                    r=[WM[sl], B_scT], w=[PSA])
                        j0 = piece * 4
                        P.op("dve", lambda e, j0=j0: e.tensor_tensor(
                            out=modc[:, j0:j0 + 4, :], in0=psA[:, 0:4 * R].rearrange("p (j r) -> p j r", r=R),
                            in1=bmodc[:, l, j0:j0 + 4, :], op=ALU.add), r=[PSA, B_const], w=[B_modc])
                    else:
                        nb = piece - ncolp
                        for kc in range(KC):
                            P.op("pe", lambda e, kc=kc, sl=sl: e.matmul(
                                psB[0:R, :], scT[:, kc, :], wm[sl][:, kc, :], start=(kc == 0), stop=(kc == KC - 1)),
                                r=[WM[sl], B_scT], w=[PSB])
                        P.op("dve", lambda e, nb=nb: e.tensor_tensor(
                            out=GTrow[0:R, nb * 512:(nb + 1) * 512], in0=psB[0:R, :],
                            in1=bg[0:R, nb * 512:(nb + 1) * 512], op=ALU.add), r=[PSB, B_bg], w=[B_GTrow])
                        P.op("dve", lambda e, nb=nb: e.tensor_tensor(
                            out=GTrow[0:R, nb * 512:(nb + 1) * 512], in0=GTrow[0:R, nb * 512:(nb + 1) * 512],
                            in1=gp[0:R, nb * 512:(nb + 1) * 512], op=ALU.mult), r=[B_GTrow, B_bg], w=[B_GTrow])
                P.op("dve", lambda e: e.scalar_tensor_tensor(
                    out=Acol[:], in0=modc[:, KC:2 * KC, :], scalar=1.0, in1=gprec[:, l, :, :],
                    op0=ALU.add, op1=ALU.mult), r=[B_modc, B_const], w=[B_Acol])

        def x_src(l, s, ti):
            if ti < NT_C:
                src = ctx_in if l == layers[0] else cscr
                return src[s, ti * 128:(ti + 1) * 128, :], DXC[s][ti]
            tl = ti - NT_C
            src = x_in if l == layers[0] else y_out
            return src[s, tl * 128:(tl + 1) * 128, :], DXL[s][tl]

        def phase_n(l, s, ctx_active):
            with P.scope() as sc:
                xt = [sc.sb("xt", [128, D], F32) for _ in range(2)]
                xh = [sc.sb("xh", [128, D], BF16) for _ in range(2)]
                junk = sc.sb("junk", [128, D], BF16)
                stt = [sc.sb("stt", [128, 4], F32) for _ in range(2)]
                pst = [sc.ps("pst", [128, KC * 128], BF16) for _ in range(2)]
                XT = [Buf("xt0"), Buf("xt1")]
                XH = [Buf("xh0"), Buf("xh1")]
                ST = [Buf("st0"), Buf("st1")]
                PST = [Buf("pst0"), Buf("pst1")]
                JK = Buf("junk")
                tiles = list(range(NT)) if ctx_active else list(range(NT_C, NT))
                for n, ti in enumerate(tiles):
                    p = n % 2
                    r = R - 1 if ti < NT_C else s
                    src, dbuf = x_src(l, s, ti)
                    P.dma("sp", xt[p][:], src, r=[dbuf], w=[XT[p]])
                    P.op("act", lambda e, p=p: e.activation(out=junk[:], in_=xt[p][:], func=AF.Square,
                                                            accum_out=stt[p][:, 0:1]), r=[XT[p]], w=[JK, ST[p]])
                    P.op("dve", lambda e, p=p: e.tensor_scalar(out=stt[p][:, 1:2], in0=stt[p][:, 0:1], scalar1=1.0 / D,
                                                               scalar2=EPS, op0=ALU.mult, op1=ALU.add), r=[ST[p]], w=[ST[p]])
                    P.op("act", lambda e, p=p: e.activation(out=stt[p][:, 2:3], in_=stt[p][:, 1:2], func=AF.Ln),
                         r=[ST[p]], w=[ST[p]])
                    P.op("act", lambda e, p=p: e.activation(out=stt[p][:, 3:4], in_=stt[p][:, 2:3], func=AF.Exp, scale=-0.5),
                         r=[ST[p]], w=[ST[p]])
                    P.op("dve", lambda e, p=p: e.tensor_scalar(out=xh[p][:], in0=xt[p][:], scalar1=stt[p][:, 3:4],
                                                               scalar2=None, op0=ALU.mult), r=[XT[p], ST[p]], w=[XH[p]])
                    for kc in range(KC):
                        P.op("pe", lambda e, p=p, kc=kc: e.transpose(
                            out=pst[p][:, kc * 128:(kc + 1) * 128], in_=xh[p][:, kc * 128:(kc + 1) * 128],
                            identity=ident_b[:]), r=[XH[p], B_const], w=[PST[p]])
                    for kc in range(KC):
                        ek = "act" if kc % 2 == 0 else "dve"
                        if ek == "act":
                            P.op("act", lambda e, p=p, kc=kc, ti=ti, r=r: e.activation(
                                out=hT[:, kc, ti * 128:(ti + 1) * 128], in_=pst[p][:, kc * 128:(kc + 1) * 128],
                                func=AF.Identity, scale=Acol[:, kc, r:r + 1], bias=modc[:, kc, r:r + 1]),
                                r=[PST[p], B_Acol, B_modc], w=[HT[ti]])
                        else:
                            P.op("dve", lambda e, p=p, kc=kc, ti=ti, r=r: e.tensor_scalar(
                                out=hT[:, kc, ti * 128:(ti + 1) * 128], in0=pst[p][:, kc * 128:(kc + 1) * 128],
                                scalar1=Acol[:, kc, r:r + 1], scalar2=modc[:, kc, r:r + 1], op0=ALU.mult, op1=ALU.add),
                                r=[PST[p], B_Acol, B_modc], w=[HT[ti]])

        def phase_o(l, s, wo_src, need_ctx):
            with P.scope() as sc:
                wO = sc.sb("wO", [128, KC, D], BF16)
                WO = Buf("wO")
                stg = [sc.sb("stg", [128, 2 * 512], F32) for _ in range(2)]
                SB = [Buf("stg0"), Buf("stg1")]
                GTb = [sc.sb("GTb", [128, D], F32) for _ in range(2)]
                B_GTb = Buf("GTb")
                xt = [sc.sb("xo", [128, D], F32) for _ in range(2)]
                tt = sc.sb("tt", [128, D], F32)
                junk = sc.sb("junko", [128, NBK, BW], BF16)
                stt = [sc.sb("stto", [128, 8], F32) for _ in range(2)]
                psy = [[sc.ps("psy", [128, 512], F32) for _ in range(NBK)] for _ in range(2 if NBK <= 4 else 1)]
                PSY = [[Buf("psy") for _ in range(NBK)] for _ in range(len(psy))]
                XT = [Buf("xo0"), Buf("xo1")]
                TT = Buf("tt")
                JKS = [Buf(f"jko{i}") for i in range(NBK)]
                ST = [Buf("sto0"), Buf("sto1")]
                tlo = 0 if need_ctx else NT_C * 128
                for kc in range(KC):
                    P.dma("sp", hT[:, kc, tlo:T], gscr[kc, :, tlo:T], r=[GS[kc]], w=HT)
                for nb in range(NBK):
                    load_w(stg, SB, wO[:, :, nb * BW:(nb + 1) * BW], wo_src[:, :, nb * BW:(nb + 1) * BW], KC, BW, WO, kq=2)
                rows = [s, R - 1] if need_ctx else [s]
                for gi, r in enumerate(rows):
                    for nb in range(NBK):
                        P.op("pe", lambda e, r=r, nb=nb: e.matmul(
                            psy[0][nb][:, 0:BW], selT[0:R, r * 128:(r + 1) * 128], GTrow[0:R, nb * BW:(nb + 1) * BW],
                            start=True, stop=True), r=[B_const, B_GTrow], w=[PSY[0][nb]])
                        P.op("act", lambda e, gi=gi, nb=nb: e.activation(
                            out=GTb[gi][:, nb * BW:(nb + 1) * BW], in_=psy[0][nb][:, 0:BW], func=AF.Copy),
                            r=[PSY[0][nb]], w=[B_GTb])
                tiles = list(range(NT)) if need_ctx else list(range(NT_C, NT))
                for n, ti in enumerate(tiles):
                    p = n % 2
                    pp = n % len(psy)
                    gi = 1 if ti < NT_C else 0
                    src, dbuf = x_src(l, s, ti)
                    P.dma("sp", xt[p][:], src, r=[dbuf], w=[XT[p]])
                    for nb in range(NBK):
                        for kc in range(KC):
                            P.op("pe", lambda e, pp=pp, nb=nb, kc=kc, ti=ti: e.matmul(
                                psy[pp][nb][:, 0:BW], hT[:, kc, ti * 128:(ti + 1) * 128],
                                wO[:, kc, nb * BW:(nb + 1) * BW], start=(kc == 0), stop=(kc == KC - 1)),
                                r=[HT[ti], WO], w=[PSY[pp][nb]])
                    for nb in range(NBK):
                        P.op("act", lambda e, p=p, pp=pp, nb=nb: e.activation(
                            out=junk[:, nb, :], in_=psy[pp][nb][:, 0:BW], func=AF.Square, accum_out=stt[p][:, nb:nb + 1]),
                            r=[PSY[pp][nb]], w=[JKS[nb], ST[p]])
                    P.op("dve", lambda e, p=p: e.reduce_sum(out=stt[p][:, 4:5], in_=stt[p][:, 0:NBK], axis=AX.X),
                         r=[ST[p]], w=[ST[p]])
                    P.op("dve", lambda e, p=p: e.tensor_scalar(out=stt[p][:, 5:6], in0=stt[p][:, 4:5], scalar1=1.0 / D,
                                                               scalar2=EPS, op0=ALU.mult, op1=ALU.add), r=[ST[p]], w=[ST[p]])
                    P.op("act", lambda e, p=p: e.activation(out=stt[p][:, 6:7], in_=stt[p][:, 5:6], func=AF.Ln),
                         r=[ST[p]], w=[ST[p]])
                    P.op("act", lambda e, p=p: e.activation(out=stt[p][:, 7:8], in_=stt[p][:, 6:7], func=AF.Exp, scale=-0.5),
                         r=[ST[p]], w=[ST[p]])
                    for nb in range(NBK):
                        P.op("dve", lambda e, p=p, pp=pp, nb=nb, gi=gi: e.scalar_tensor_tensor(
                            out=tt[:, nb * BW:(nb + 1) * BW], in0=psy[pp][nb][:, 0:BW], scalar=stt[p][:, 7:8],
                            in1=GTb[gi][:, nb * BW:(nb + 1) * BW], op0=ALU.mult, op1=ALU.mult),
                            r=[PSY[pp][nb], ST[p], B_GTb], w=[TT])
                    P.op("pool", lambda e, p=p: e.tensor_tensor(out=xt[p][:], in0=tt[:], in1=xt[p][:], op=ALU.add),
                         r=[TT, XT[p]], w=[XT[p]])
                    if ti < NT_C:
                        dst, dbuf2 = cscr[s, ti * 128:(ti + 1) * 128, :], DXC[s][ti]
                    else:
                        tl = ti - NT_C
                        dst, dbuf2 = y_out[s, tl * 128:(tl + 1) * 128, :], DXL[s][tl]
                    P.dma("sp", dst, xt[p][:], r=[XT[p]], w=[dbuf2])

        def odd_mixer(o, s, ctx_active):
            PG, PGC = cfg.PG, cfg.PGC
            segs = ([("c", 0, NT_C)] if ctx_active else []) + [("l", NT_C, NT_L)]
            with P.scope() as sc:
                stg = [sc.sb("stg", [128, 4 * 640], F32) for _ in range(2)]
                SB = [Buf("stg0"), Buf("stg1")]
                wU = sc.sb("wU", [128, KC, PG], BF16)
                wZ = sc.sb("wZ", [128, KC, PG], BF16)
                wP = sc.sb("wP", [128, PGC, PG], BF16)
                WU, WZ, WP = Buf("wU"), Buf("wZ"), Buf("wP")
                pcf = sc.sb("pcf", [128, n_pool_blocks, 128], F32)
                pcb = sc.sb("pcb", [128, n_pool_blocks, 128], BF16)
                B_pc = Buf("pc")
                lsc = sc.sb("lsc", [128, KC], F32)
                utm = sc.sb("utm", [128, NT, PG], BF16)
                UT = [Buf(f"ut{i}") for i in range(NT)]
                rT = sc.sb("rT", [128, PGC, T], BF16)
                RT = [Buf(f"rt{i}") for i in range(NT)]
                sz = [sc.sb("sz", [128, 512], BF16) for _ in range(2)]
                SZ = [Buf("sz0"), Buf("sz1")]
                gch = [sc.sb("gch", [128, T], BF16) for _ in range(2)]
                GCH = [Buf("gch0"), Buf("gch1")]
                psu = [sc.ps("psu", [128, 512], F32) for _ in range(2)]
                PSU = [Buf("psu0"), Buf("psu1")]
                psr = [sc.ps("psr", [128, 512], F32) for _ in range(2)]
                PSR = [Buf("psr0"), Buf("psr1")]
                psq = [sc.ps("psq", [128, 512], F32) for _ in range(2)]
                PSQ = [Buf("psq0"), Buf("psq1")]
                psz = [sc.ps("psz", [128, 512], F32) for _ in range(2)]
                PSZ = [Buf("psz0"), Buf("psz1")]
                P.dma("sp", pcf[:], poolc_in.rearrange("n p c -> p n c"), w=[B_pc])
                P.op("dve", lambda e: e.tensor_copy(pcb[:], pcf[:]), r=[B_pc], w=[B_pc])
                P.dma("sp", lsc[:], odsc_in[:, o, :], w=[B_pc])
                cu = cr = cq = 0
                gcount = 0
                for j in range(4):
                    load_w(stg, SB, wU[:], wOD_in[o, j][:, :, 0:PG], KC, PG, WU)
                    load_w(stg, SB, wZ[:], wOD_in[o, j][:, :, PG:2 * PG], KC, PG, WZ)
                    load_w(stg, SB, wP[:], wPL_in[o, j], PGC, PG, WP)
                    tiles = list(range(NT)) if ctx_active else list(range(NT_C, NT))
                    for ti in tiles:
                        for n0 in range(0, PG, 512):
                            nw = min(512, PG - n0)
                            p = cu % 2
                            cu += 1
                            for kc in range(KC):
                                P.op("pe", lambda e, p=p, kc=kc, ti=ti, n0=n0, nw=nw: e.matmul(
                                    psu[p][:, 0:nw], hT[:, kc, ti * 128:(ti + 1) * 128], wU[:, kc, n0:n0 + nw],
                                    start=(kc == 0), stop=(kc == KC - 1)), r=[HT[ti], WU], w=[PSU[p]])
                            P.op("act", lambda e, p=p, ti=ti, n0=n0, nw=nw: e.activation(
                                out=utm[:, ti, n0:n0 + nw], in_=psu[p][:, 0:nw], func=AF.Copy), r=[PSU[p]], w=[UT[ti]])
                    for (seg, t0, nts) in segs:
                        for fc in range(PGC):
                            for tb in range(0, nts, 4):
                                ntb = min(4, nts - tb)
                                p = cr % 2
                                cr += 1
                                for tq in range(ntb):
                                    tl = tb + tq
                                    dis = [di for di in (-1, 0, 1) if 0 <= tl + di < nts]
                                    for ii, di in enumerate(dis):
                                        bi = pool_maps[(seg, j, tl, di)]
                                        first = (tq == 0 and ii == 0)
                                        last = (tq == ntb - 1 and ii == len(dis) - 1)
                                        P.op("pe", lambda e, p=p, tq=tq, fc=fc, bi=bi, sti=t0 + tl + di, first=first, last=last:
                                             e.matmul(psr[p][:, tq * 128:(tq + 1) * 128],
                                                      utm[:, sti, fc * 128:(fc + 1) * 128], pcb[:, bi, :],
                                                      start=first, stop=last),
                                             r=[UT[t0 + tl + di], B_pc], w=[PSR[p]])
                                tg0 = (t0 + tb) * 128
                                P.op("dve", lambda e, p=p, fc=fc, tg0=tg0, ntb=ntb: e.tensor_copy(
                                    rT[:, fc, tg0:tg0 + ntb * 128], psr[p][:, 0:ntb * 128]),
                                    r=[PSR[p]], w=[RT[t0 + tb + q] for q in range(ntb)])
                    for fc in range(PGC):
                        gch_i = gcount % 2
                        gcount += 1
                        chunk = j * PGC + fc
                        for (seg, t0, nts) in segs:
                            for tb in range(0, nts, 4):
                                ntb = min(4, nts - tb)
                                nw = ntb * 128
                                tg0 = (t0 + tb) * 128
                                p = cq % 2
                                cq += 1
                                tbufs = [t0 + tb + q for q in range(ntb)]
                                for kc2 in range(PGC):
                                    P.op("pe", lambda e, p=p, kc2=kc2, fc=fc, tg0=tg0, nw=nw: e.matmul(
                                        psq[p][:, 0:nw], wP[:, kc2, fc * 128:(fc + 1) * 128], rT[:, kc2, tg0:tg0 + nw],
                                        start=(kc2 == 0), stop=(kc2 == PGC - 1)),
                                        r=[WP] + [RT[q] for q in tbufs], w=[PSQ[p]])
                                for kc in range(KC):
                                    P.op("pe", lambda e, p=p, kc=kc, fc=fc, tg0=tg0, nw=nw: e.matmul(
                                        psz[p][:, 0:nw], wZ[:, kc, fc * 128:(fc + 1) * 128], hT[:, kc, tg0:tg0 + nw],
                                        start=(kc == 0), stop=(kc == KC - 1)),
                                        r=[WZ] + [HT[q] for q in tbufs], w=[PSZ[p]])
                                P.op("act", lambda e, p=p, nw=nw: e.activation(out=sz[p][:, 0:nw], in_=psz[p][:, 0:nw],
                                                                                func=AF.Silu), r=[PSZ[p]], w=[SZ[p]])
                                P.op("dve", lambda e, p=p, nw=nw, tg0=tg0, gch_i=gch_i, chunk=chunk: e.scalar_tensor_tensor(
                                    out=gch[gch_i][:, tg0:tg0 + nw], in0=psq[p][:, 0:nw], scalar=lsc[:, chunk:chunk + 1],
                                    in1=sz[p][:, 0:nw], op0=ALU.mult, op1=ALU.mult),
                                    r=[PSQ[p], SZ[p], B_pc], w=[GCH[gch_i]])
                        tlo = 0 if ctx_active else NT_C * 128
                        P.dma("sp", gscr[chunk, :, tlo:T], gch[gch_i][:, tlo:T], r=[GCH[gch_i]], w=[GS[chunk]])

        even_mixer = make_even_mixer(cfg, nc, P, dict(
            hT=hT, HT=HT, ident_b=ident_b, B_const=B_const, gscr=gscr, GS=GS, load_w=load_w,
            wA_in=wA_in, wB_in=wB_in, lam_in=lam_in, subln_in=subln_in, hgn_in=hgn_in, lbl_in=lbl_in,
            cos_in=cos_in, sin_in=sin_in, hgM_in=hgM_in, hgmask_in=hgmask_in, hgw_in=hgw_in, hgcm_in=hgcm_in))

        for l in layers:
            even = (l % 2 == 0)
            need_ctx = l < cfg.DEPTH - 1
            ctx_active = even or need_ctx
            mod_phase(l)
            for s in range(NB):
                phase_n(l, s, ctx_active)
                if even:
                    even_mixer(l // 2, l, s, need_ctx)
                    phase_o(l, s, wEO_in[l // 2], need_ctx)
                else:
                    odd_mixer(l // 2, s, ctx_active and need_ctx)
                    phase_o(l, s, wOO_in[l // 2], need_ctx)
        P.final_wait("sp")
        build.stats = (P.nops, P.nwait, dict(P.cnt), dict(P.si))
    return nc


def make_even_mixer(cfg, nc, P, G):
    D, KC, T, NT, NT_C, NT_L, R, NB = cfg.D, cfg.KC, cfg.T, cfg.NT, cfg.NT_C, cfg.NT_L, cfg.R, cfg.NB
    hT, HT, ident_b, B_const, gscr, GS, load_w = (G[k] for k in ("hT", "HT", "ident_b", "B_const", "gscr", "GS", "load_w"))
    TC = cfg.T_CTX

    def rstd_ops(stt, ST, c_ss, c_out, n):
        P.op("dve", lambda e: e.tensor_scalar(out=stt[:, c_out:c_out + 1], in0=stt[:, c_ss:c_ss + 1], scalar1=1.0 / n,
                                              scalar2=EPS, op0=ALU.mult, op1=ALU.add), r=[ST], w=[ST])
        P.op("act", lambda e: e.activation(out=stt[:, c_out:c_out + 1], in_=stt[:, c_out:c_out + 1], func=AF.Ln),
             r=[ST], w=[ST])
        P.op("act", lambda e: e.activation(out=stt[:, c_out:c_out + 1], in_=stt[:, c_out:c_out + 1], func=AF.Exp, scale=-0.5),
             r=[ST], w=[ST])

    def attention_part(e_idx, l, s, need_ctx):
        lam_init = 0.8 - 0.6 * math.exp(-0.3 * l)
        with P.scope() as sc:
            stg = [sc.sb("stg", [128, 4 * 512], F32) for _ in range(2)]
            SB = [Buf("stg0"), Buf("stg1")]
            wbf = [sc.sb("wbfA", [128, KC, 512], BF16) for _ in range(2)]
            WB = [Buf("wbA0"), Buf("wbA1")]
            cosT = sc.sb("cosT", [128, NT_L, 128], F32)
            sinT = sc.sb("sinT", [128, NT_L, 128], F32)
            B_rope = Buf("rope")
            lamt = sc.sb("lamt", [128, 4, 64], F32)
            lamw = sc.sb("lamw", [128, 2, 64], F32)
            lams = sc.sb("lams", [128, 8], F32)
            B_lam = Buf("lam")
            Gt = sc.sb("Gt", [128, 128], F32)
            B_G = Buf("G")
            QKT = sc.sb("QKT", [128, 2, T], BF16)
            QK = [Buf(f"qk{i}") for i in range(NT)]
            Vaug = sc.sb("Vaug", [128, NT, 130], BF16)
            VA = [Buf(f"va{i}") for i in range(NT)]
            GGt = sc.sb("GGt", [128, NT, 128], F32)
            GGB = [Buf(f"gg{i}") for i in range(NT)]
            qktm = [sc.sb("qktm", [128, 256], BF16) for _ in range(2)]
            QKTM = [Buf("qktm0"), Buf("qktm1")]
            t1 = [sc.sb("t1", [128, 256], F32) for _ in range(2)]
            t2 = [sc.sb("t2", [128, 256], F32) for _ in range(2)]
            T1 = [Buf("t10"), Buf("t11")]
            T2 = [Buf("t20"), Buf("t21")]
            eg = [sc.sb("eg", [128, 128], F32) for _ in range(2)]
            EG = [Buf("eg0"), Buf("eg1")]
            Et = [sc.sb("Et", [128, 512], BF16) for _ in range(3)]
            ET = [Buf(f"et{i}") for i in range(3)]
            o1 = [sc.sb("o1", [128, 128], F32) for _ in range(2)]
            o2 = [sc.sb("o2", [128, 128], F32) for _ in range(2)]
            O1 = [Buf("o10"), Buf("o11")]
            O2 = [Buf("o20"), Buf("o21")]
            stt = [sc.sb("sta", [128, 8], F32) for _ in range(2)]
            ST = [Buf("sta0"), Buf("sta1")]
            junk = sc.sb("junka", [128, 128], BF16)
            JK = Buf("junka")
            atm = [sc.sb("atm", [128, 128], BF16) for _ in range(2)]
            ATM = [Buf("atm0"), Buf("atm1")]
            aT = [sc.sb("aT", [128, T], BF16) for _ in range(2)]
            AT = [Buf("aT0"), Buf("aT1")]
            psp = [sc.ps("psp", [128, 512], F32) for _ in range(2)]
            PSP = [Buf("psp0"), Buf("psp1")]
            pss = [sc.ps("pss", [128, 512], F32) for _ in range(2)]
            PSS = [Buf("pss0"), Buf("pss1")]
            pso = [sc.ps("pso", [128, 512], F32) for _ in range(3)]
            PSO = [Buf("pso0"), Buf("pso1"), Buf("pso2")]
            pst = sc.ps("pstA", [128, 1024], BF16)
            PSTq = Buf("pstq")
            PSTa = PSTq

            P.dma("sp", cosT[:], G["cos_in"][:, :, :], w=[B_rope])
            P.dma("sp", sinT[:], G["sin_in"][:, :, :], w=[B_rope])
            P.dma("sp", lamt[:], G["lam_in"][:, e_idx, :, :], w=[B_lam])
            P.dma("sp", Gt[:], G["subln_in"][:, e_idx, :], w=[B_G])
            P.op("dve", lambda e: e.tensor_scalar(out=Gt[:], in0=Gt[:], scalar1=(1.0 - lam_init), scalar2=None,
                                                  op0=ALU.mult), r=[B_G], w=[B_G])
            P.op("dve", lambda e: e.tensor_tensor(out=lamw[:, 0, :], in0=lamt[:, 0, :], in1=lamt[:, 1, :], op=ALU.mult),
                 r=[B_lam], w=[B_lam])
            P.op("dve", lambda e: e.tensor_tensor(out=lamw[:, 1, :], in0=lamt[:, 2, :], in1=lamt[:, 3, :], op=ALU.mult),
                 r=[B_lam], w=[B_lam])
            P.op("dve", lambda e: e.reduce_sum(out=lams[:, 0:2], in_=lamw[:, :, :], axis=AX.X), r=[B_lam], w=[B_lam])
            P.op("act", lambda e: e.activation(out=lams[:, 2:4], in_=lams[:, 0:2], func=AF.Exp), r=[B_lam], w=[B_lam])
            P.op("dve", lambda e: e.tensor_tensor(out=lams[:, 4:5], in0=lams[:, 2:3], in1=lams[:, 3:4], op=ALU.subtract),
                 r=[B_lam], w=[B_lam])
            P.op("dve", lambda e: e.tensor_scalar(out=lams[:, 5:6], in0=lams[:, 4:5], scalar1=lam_init, scalar2=-1.0,
                                                  op0=ALU.add, op1=ALU.mult), r=[B_lam], w=[B_lam])
            P.op("pool", lambda e: e.memset(Vaug[:, :, 128:130], 1.0), w=VA)

            cnt = {"p": 0, "s": 0, "e": 0, "ep": 0}
            for h in range(cfg.DA_HEADS):
                hp = h % 2
                load_w(stg, SB, wbf[hp][:], G["wA_in"][e_idx, h], KC, 512, WB[hp])
                for ti in range(NT):
                    p = cnt["p"] % 2
                    cnt["p"] += 1
                    for kc in range(KC):
                        P.op("pe", lambda e, p=p, kc=kc, ti=ti, hp=hp: e.matmul(
                            psp[p][:, :], hT[:, kc, ti * 128:(ti + 1) * 128], wbf[hp][:, kc, :],
                            start=(kc == 0), stop=(kc == KC - 1)), r=[HT[ti], WB[hp]], w=[PSP[p]])
                    if ti >= NT_C:
                        tl = ti - NT_C
                        for c0 in (0, 128):
                            X = psp[p][:, c0:c0 + 128]
                            Xv = X.rearrange("p (a h i) -> p a h i", h=2, i=16)
                            Sv = sinT[:, tl, :].rearrange("p (a h i) -> p a h i", h=2, i=16)
                            t2v = t2[p][:, c0:c0 + 128].rearrange("p (a h i) -> p a h i", h=2, i=16)
                            P.op("dve", lambda e, p=p, c0=c0, X=X, tl=tl: e.tensor_tensor(
                                out=t1[p][:, c0:c0 + 128], in0=X, in1=cosT[:, tl, :], op=ALU.mult),
                                r=[PSP[p], B_rope], w=[T1[p]])
                            P.op("dve", lambda e, Xv=Xv, Sv=Sv, t2v=t2v: e.tensor_tensor(
                                out=t2v[:, :, 0, :], in0=Xv[:, :, 1, :], in1=Sv[:, :, 0, :], op=ALU.mult),
                                r=[PSP[p], B_rope], w=[T2[p]])
                            P.op("dve", lambda e, Xv=Xv, Sv=Sv, t2v=t2v: e.tensor_tensor(
                                out=t2v[:, :, 1, :], in0=Xv[:, :, 0, :], in1=Sv[:, :, 1, :], op=ALU.mult),
                                r=[PSP[p], B_rope], w=[T2[p]])
                        P.op("pool", lambda e, p=p: e.tensor_tensor(out=qktm[p][:], in0=t1[p][:], in1=t2[p][:], op=ALU.add),
                             r=[T1[p], T2[p]], w=[QKTM[p]])
                    else:
                        P.op("act", lambda e, p=p: e.activation(out=qktm[p][:], in_=psp[p][:, 0:256], func=AF.Copy),
                             r=[PSP[p]], w=[QKTM[p]])
                    for c in range(2):
                        P.op("pe", lambda e, p=p, c=c: e.transpose(out=pst[:, c * 128:(c + 1) * 128],
                                                                    in_=qktm[p][:, c * 128:(c + 1) * 128], identity=ident_b[:]),
                             r=[QKTM[p], B_const], w=[PSTq])
                    P.op("act", lambda e, ti=ti: e.activation(
                        out=QKT[:, :, ti * 128:(ti + 1) * 128], in_=pst[:, 0:256].rearrange("p (c t) -> p c t", c=2),
                        func=AF.Copy), r=[PSTq], w=[QK[ti]])
                    P.op("act", lambda e, p=p, ti=ti: e.activation(out=Vaug[:, ti, 0:128], in_=psp[p][:, 256:384], func=AF.Copy),
                         r=[PSP[p]], w=[VA[ti]])
                    P.op("act", lambda e, p=p: e.activation(out=eg[p][:], in_=psp[p][:, 384:512], func=AF.Exp, scale=-1.0),
                         r=[PSP[p]], w=[EG[p]])
                    P.op("act", lambda e, p=p: e.activation(out=eg[p][:], in_=eg[p][:], func=AF.Ln, bias=1.0), r=[EG[p]], w=[EG[p]])
                    P.op("act", lambda e, p=p: e.activation(out=eg[p][:], in_=eg[p][:], func=AF.Exp, scale=-1.0), r=[EG[p]], w=[EG[p]])
                    P.op("pool", lambda e, p=p: e.tensor_tensor(out=eg[p][:], in0=eg[p][:], in1=Gt[:], op=ALU.mult),
                         r=[EG[p], B_G], w=[EG[p]])
                    P.op("dve", lambda e, p=p, ti=ti: e.tensor_tensor(out=GGt[:, ti, :], in0=psp[p][:, 384:512], in1=eg[p][:],
                                                                      op=ALU.mult), r=[PSP[p], EG[p]], w=[GGB[ti]])

                blocks = []
                if need_ctx:
                    blocks.append((0, TC, list(range(NT_C))))
                for qb in range(0, cfg.T_LAT, 512):
                    blocks.append((TC + qb, min(512, cfg.T_LAT - qb), list(range(NT))))
                for (q0, nq, kts) in blocks:
                    nqs = nq // 128
                    nacc = 2 * nqs
                    nbank = (nacc + 2) // 3
                    firsts = {b: True for b in range(nbank)}
                    qtiles = [q0 // 128 + i for i in range(nqs)]
                    for ki, kt in enumerate(kts):
                        for m in range(2):
                            p = cnt["s"] % 2
                            cnt["s"] += 1
                            ei = cnt["e"] % 3
                            cnt["e"] += 1
                            P.op("pe", lambda e, p=p, m=m, kt=kt, q0=q0, nq=nq: e.matmul(
                                pss[p][:, 0:nq], QKT[m * 64:(m + 1) * 64, 1, kt * 128:(kt + 1) * 128],
                                QKT[m * 64:(m + 1) * 64, 0, q0:q0 + nq], start=True, stop=True),
                                r=[QK[kt]] + [QK[q] for q in qtiles], w=[PSS[p]])
                            P.op("act", lambda e, p=p, ei=ei, nq=nq: e.activation(
                                out=Et[ei][:, 0:nq], in_=pss[p][:, 0:nq], func=AF.Exp, scale=0.125),
                                r=[PSS[p]], w=[ET[ei]])
                            for qs in range(nqs):
                                idx = m * nqs + qs
                                b, slot = idx // 3, idx % 3
                                is_first = firsts[b]
                                firsts[b] = False
                                last_idx_in_bank = min(nacc - 1, b * 3 + 2)
                                is_last = (ki == len(kts) - 1) and (idx == last_idx_in_bank)
                                P.op("pe", lambda e, ei=ei, qs=qs, kt=kt, b=b, slot=slot, is_first=is_first, is_last=is_last:
                                     e.matmul(pso[b][:, slot * 129:slot * 129 + 129], Et[ei][:, qs * 128:(qs + 1) * 128],
                                              Vaug[:, kt, 0:129], start=is_first, stop=is_last),
                                     r=[ET[ei], VA[kt]], w=[PSO[b]])
                    for qs in range(nqs):
                        ti = q0 // 128 + qs
                        ep = cnt["ep"] % 2
                        cnt["ep"] += 1
                        i0, i1 = qs, nqs + qs
                        b0, s0 = i0 // 3, (i0 % 3) * 129
                        b1, s1 = i1 // 3, (i1 % 3) * 129
                        P.op("dve", lambda e, ep=ep, b0=b0, s0=s0: e.reciprocal(out=stt[ep][:, 0:1], in_=pso[b0][:, s0 + 128:s0 + 129]),
                             r=[PSO[b0]], w=[ST[ep]])
                        P.op("dve", lambda e, ep=ep, b1=b1, s1=s1: e.reciprocal(out=stt[ep][:, 1:2], in_=pso[b1][:, s1 + 128:s1 + 129]),
                             r=[PSO[b1]], w=[ST[ep]])
                        P.op("dve", lambda e, ep=ep: e.tensor_tensor(out=stt[ep][:, 2:3], in0=stt[ep][:, 1:2], in1=lams[:, 5:6],
                                                                     op=ALU.mult), r=[ST[ep], B_lam], w=[ST[ep]])
                        P.op("dve", lambda e, ep=ep, b0=b0, s0=s0: e.tensor_scalar(
                            out=o1[ep][:], in0=pso[b0][:, s0:s0 + 128], scalar1=stt[ep][:, 0:1], scalar2=None, op0=ALU.mult),
                            r=[PSO[b0], ST[ep]], w=[O1[ep]])
                        P.op("dve", lambda e, ep=ep, b1=b1, s1=s1: e.scalar_tensor_tensor(
                            out=o2[ep][:], in0=pso[b1][:, s1:s1 + 128], scalar=stt[ep][:, 2:3], in1=o1[ep][:],
                            op0=ALU.mult, op1=ALU.add), r=[PSO[b1], ST[ep], O1[ep]], w=[O2[ep]])
                        P.op("act", lambda e, ep=ep: e.activation(out=junk[:], in_=o2[ep][:], func=AF.Square,
                                                                  accum_out=stt[ep][:, 3:4]), r=[O2[ep]], w=[JK, ST[ep]])
                        rstd_ops(stt[ep], ST[ep], 3, 4, 128)
                        P.op("dve", lambda e, ep=ep, ti=ti: e.scalar_tensor_tensor(
                            out=atm[ep][:], in0=o2[ep][:], scalar=stt[ep][:, 4:5], in1=GGt[:, ti, :],
                            op0=ALU.mult, op1=ALU.mult), r=[O2[ep], ST[ep], GGB[ti]], w=[ATM[ep]])
                        P.op("pe", lambda e, ep=ep: e.transpose(out=pst[:, 256:384], in_=atm[ep][:], identity=ident_b[:]),
                             r=[ATM[ep], B_const], w=[PSTa])
                        P.op("act", lambda e, hp=hp, ti=ti: e.activation(out=aT[hp][:, ti * 128:(ti + 1) * 128],
                                                                         in_=pst[:, 256:384], func=AF.Copy),
                             r=[PSTa], w=[AT[hp]])
                tlo = 0 if need_ctx else TC
                P.dma("sp", gscr[h, :, tlo:T], aT[hp][:, tlo:T], r=[AT[hp]], w=[GS[h]])

    def hgrn_part(e_idx, l, s, need_ctx):
        HW = cfg.HG_W
        with P.scope() as sc:
            stg = [sc.sb("stg", [128, 640], F32) for _ in range(2)]
            SB = [Buf("stg0"), Buf("stg1")]
            wbf = [sc.sb("wbfB", [128, KC, 640], BF16) for _ in range(2)]
            WB = [Buf("wbB0"), Buf("wbB1")]
            LBt = sc.sb("LBt", [128, 2, 128], F32)
            OML = sc.sb("OML", [128, 2, 128], F32)
            B_lb = Buf("lb")
            hgG = sc.sb("hgG", [128, 128], F32)
            M1 = sc.sb("M1", [128, 2, 128], F32)
            mask = sc.sb("mask", [128, 2, 128], F32)
            wcol = sc.sb("wcol", [128, 2, 12], F32)
            B_hc = Buf("hgc")
            qs_t = [sc.sb("qs", [128, 128], F32) for _ in range(2)]
            QS = [Buf("qs0"), Buf("qs1")]
            vv = sc.sb("vv", [128, NT, 128], BF16)
            VV = [Buf(f"vv{i}") for i in range(NT)]
            GGt = sc.sb("GGb", [128, NT, 128], BF16)
            GGB = [Buf(f"ggb{i}") for i in range(NT)]
            of = sc.sb("of", [128, NT, 128], F32)
            OF = [Buf(f"of{i}") for i in range(NT)]
            nsl = [2, NT]
            qkT = [sc.sb("qkTh", [128, nsl[d], 2, 128], BF16) for d in range(2)]
            ktm = [sc.sb("ktm", [128, nsl[d], 128], BF16) for d in range(2)]
            ecs = [sc.sb("ecs", [128, nsl[d], 12], F32) for d in range(2)]
            PREP = [[Buf(f"prep{d}_{i}") for i in range(nsl[d])] for d in range(2)]
            sg = [sc.sb("sg", [128, 128], F32) for _ in range(2)]
            SG = [Buf("sg0"), Buf("sg1")]
            ff = [sc.sb("ff", [128, 128], F32) for _ in range(2)]
            FF = [Buf("ff0"), Buf("ff1")]
            lf = [sc.sb("lf", [128, 128], F32) for _ in range(2)]
            LF = [Buf("lf0"), Buf("lf1")]
            kk = [sc.sb("kk", [128, 128], F32) for _ in range(2)]
            KK = [Buf("kk0"), Buf("kk1")]
            ed = [sc.sb("ed", [128, 2, 128], F32) for _ in range(2)]
            ED = [Buf("ed0"), Buf("ed1")]
            qtl = [sc.sb("qtl", [128, 128], BF16) for _ in range(2)]
            ee3 = [sc.sb("ee3", [128, 512], F32) for _ in range(2)]
            L3 = [sc.sb("L3", [128, 512], F32) for _ in range(2)]
            s3 = [sc.sb("s3", [128, 512], F32) for _ in range(2)]
            EE3 = [Buf("ee30"), Buf("ee31")]
            LL3 = [Buf("L30"), Buf("L31")]
            SS3 = [Buf("s30"), Buf("s31")]
            QTL = [Buf("qtl0"), Buf("qtl1")]
            eg = [sc.sb("egb", [128, 128], F32) for _ in range(2)]
            EG = [Buf("egb0"), Buf("egb1")]
            S = [sc.sb("S", [128, 128], F32) for _ in range(2)]
            SBF = [[sc.sb("Sbf", [128, 128], BF16) for _ in range(2)] for _ in range(2)]
            SS = [Buf("S0"), Buf("S1")]
            SSB = [[Buf("Sb00"), Buf("Sb01")], [Buf("Sb10"), Buf("Sb11")]]
            qz = [sc.sb("qz", [128, 640], BF16) for _ in range(2)]
            QZ = [Buf("qz0"), Buf("qz1")]
            cm = sc.sb("cm", [128, 4], F32)
            cmb = sc.sb("cmb", [128, 4, 128], BF16)
            ones_t = sc.sb("ones_t", [128, 128], F32)
            vblk = [sc.sb("vblk", [128, 4, 128], BF16) for _ in range(2)]
            VB = [Buf("vb0"), Buf("vb1")]
            attm = [sc.sb("attm", [128, 128], BF16) for _ in range(2)]
            ATT = [Buf("att0"), Buf("att1")]
            tmpu = [sc.sb("tmpu", [128, 128], F32) for _ in range(2)]
            TU = [Buf("tu0"), Buf("tu1")]
            osum = [sc.sb("osum", [128, 128], F32) for _ in range(2)]
            OS = [Buf("os0"), Buf("os1")]
            stt = [sc.sb("stb", [128, 8], F32) for _ in range(2)]
            ST = [Buf("stb0"), Buf("stb1")]
            junk = sc.sb("junkb", [128, 128], BF16)
            JK = Buf("junkb")
            btm = [sc.sb("btm", [128, 128], BF16) for _ in range(2)]
            BTM = [Buf("btm0"), Buf("btm1")]
            bT = [sc.sb("bT", [128, T], BF16)] * 2
            BT = [Buf("bT0")] * 2
            psp = [[sc.ps("pspb", [128, 512], F32) for _ in range(2)] for _ in range(2)]
            PSP = [[Buf("pspb00"), Buf("pspb01")], [Buf("pspb10"), Buf("pspb11")]]
            psd = sc.ps("psd", [128, 512], F32)
            PSD = Buf("psd")
            psa = sc.ps("psa", [128, 512], F32)
            PSA = Buf("psa")
            PSOt = PSA
            psu = sc.ps("psu", [128, 512], F32)
            PSU = Buf("psu")
            pst = sc.ps("pstB", [128, 1024], BF16)
            PSTq = Buf("pstqb")
            PSTb = PSTq

            P.op("pool", lambda e: e.memset(ones_t[:], 1.0), w=[B_hc])
            P.dma("sp", hgG[:], G["hgn_in"][:, e_idx, :], w=[B_hc])
            P.dma("sp", M1[:], G["hgM_in"].rearrange("d s t -> s d t"), w=[B_hc])
            P.dma("sp", mask[:], G["hgmask_in"].rearrange("d s t -> s d t"), w=[B_hc])
            P.dma("sp", wcol[:], G["hgw_in"].rearrange("d s c -> s d c"), w=[B_hc])
            P.dma("sp", cm[:], G["hgcm_in"][:, :], w=[B_hc])
            for c in range(4):
                P.op("dve", lambda e, c=c: e.tensor_scalar(out=cmb[:, c, :], in0=ones_t[:], scalar1=cm[:, c:c + 1], scalar2=None,
                                                           op0=ALU.mult), r=[B_hc], w=[B_hc])
            for i in range(2):
                P.op("pool", lambda e, i=i: e.memset(qz[i][:], 0.0), w=[QZ[i]])
            def load_lb(h):
                if e_idx == 0:
                    if h == 0:
                        P.op("pool", lambda e: e.memset(LBt[:], 0.0), w=[B_lb])
                        P.op("pool", lambda e: e.memset(OML[:], 1.0), w=[B_lb])
                    return
                P.dma("sp", LBt[:], G["lbl_in"][:, 0, :, h * 128:(h + 1) * 128], w=[B_lb])
                P.dma("sp", OML[:], G["lbl_in"][:, 1, :, h * 128:(h + 1) * 128], w=[B_lb])
                P.op("dve", lambda e: e.tensor_tensor(out=LBt[:], in0=LBt[:], in1=OML[:], op=ALU.subtract), r=[B_lb], w=[B_lb])
                P.op("act", lambda e: e.activation(out=LBt[:], in_=LBt[:], func=AF.Exp), r=[B_lb], w=[B_lb])
                P.op("dve", lambda e: e.tensor_scalar(out=LBt[:], in0=LBt[:], scalar1=1.0, scalar2=None, op0=ALU.add),
                     r=[B_lb], w=[B_lb])
                P.op("dve", lambda e: e.reciprocal(out=LBt[:], in_=LBt[:]), r=[B_lb], w=[B_lb])
                P.op("dve", lambda e: e.tensor_scalar(out=OML[:], in0=LBt[:], scalar1=-1.0, scalar2=1.0, op0=ALU.mult,
                                                      op1=ALU.add), r=[B_lb], w=[B_lb])

            cnt = {"p": 0, "t": 0, "st": 0, "fin": 0}

            sgn = -1.0 if e_idx == 0 else 1.0

            def tile_common(h, hp, ti, p):
                P.op("act", lambda e: e.activation(out=ee3[p][:, 0:384], in_=psp[p][0][:, 0:384], func=AF.Exp, scale=-1.0),
                     r=[PSP[p][0]], w=[EE3[p]])
                P.op("act", lambda e: e.activation(out=ee3[p][:, 384:512], in_=psp[p][1][:, 0:128], func=AF.Exp, scale=-1.0),
                     r=[PSP[p][1]], w=[EE3[p]])
                P.op("act", lambda e: e.activation(out=L3[p][:], in_=ee3[p][:], func=AF.Ln, bias=1.0), r=[EE3[p]], w=[LL3[p]])
                P.op("act", lambda e: e.activation(out=s3[p][:], in_=L3[p][:], func=AF.Exp, scale=-1.0), r=[LL3[p]], w=[SS3[p]])
                P.op("dve", lambda e: e.tensor_tensor(out=qs_t[p][:], in0=psp[p][0][:, 0:128], in1=s3[p][:, 0:128], op=ALU.mult),
                     r=[PSP[p][0], SS3[p]], w=[QS[p]])
                P.op("act", lambda e: e.activation(out=vv[:, ti, :], in_=psp[p][0][:, 384:512], func=AF.Copy),
                     r=[PSP[p][0]], w=[VV[ti]])
                P.op("pool", lambda e: e.tensor_tensor(out=eg[p][:], in0=s3[p][:, 384:512], in1=hgG[:], op=ALU.mult),
                     r=[SS3[p], B_hc], w=[EG[p]])
                P.op("dve", lambda e: e.tensor_tensor(out=GGt[:, ti, :], in0=psp[p][1][:, 0:128], in1=eg[p][:], op=ALU.mult),
                     r=[PSP[p][1], EG[p]], w=[GGB[ti]])

            def prep(h, hp, ti, p, d, slot):
                k = cnt["t"] % 2
                cnt["t"] += 1
                c0 = 128 + d * 128
                if e_idx == 0:
                    lfa = L3[p][:, c0:c0 + 128]
                    LFB = LL3[p]
                    P.op("pool", lambda e: e.tensor_tensor(out=kk[k][:], in0=ee3[p][:, c0:c0 + 128], in1=s3[p][:, c0:c0 + 128],
                                                           op=ALU.mult), r=[EE3[p], SS3[p]], w=[KK[k]])
                else:
                    P.op("pool", lambda e: e.tensor_tensor(out=sg[k][:], in0=s3[p][:, c0:c0 + 128], in1=OML[:, d, :], op=ALU.mult),
                         r=[SS3[p], B_lb], w=[SG[k]])
                    P.op("pool", lambda e: e.tensor_tensor(out=ff[k][:], in0=sg[k][:], in1=LBt[:, d, :], op=ALU.add),
                         r=[SG[k], B_lb], w=[FF[k]])
                    P.op("pool", lambda e: e.tensor_tensor(out=kk[k][:], in0=OML[:, d, :], in1=sg[k][:], op=ALU.subtract),
                         r=[SG[k], B_lb], w=[KK[k]])
                    P.op("act", lambda e: e.activation(out=lf[k][:], in_=ff[k][:], func=AF.Ln), r=[FF[k]], w=[LF[k]])
                    lfa = lf[k][:]
                    LFB = LF[k]
                P.op("pe", lambda e: e.matmul(psd[:, 0:128], M1[:, d, :], lfa, start=True, stop=True),
                     r=[B_hc, LFB], w=[PSD])
                P.op("pe", lambda e: e.matmul(psd[:, 128:140], lfa, wcol[:, d, :], start=True, stop=True),
                     r=[B_hc, LFB], w=[PSD])
                P.op("act", lambda e: e.activation(out=ed[k][:, 0, :], in_=psd[:, 0:128], func=AF.Exp, scale=sgn), r=[PSD], w=[ED[k]])
                P.op("act", lambda e: e.activation(out=ed[k][:, 1, :], in_=psd[:, 0:128], func=AF.Exp, scale=-sgn),
                     r=[PSD], w=[ED[k]])
                P.op("act", lambda e: e.activation(out=ecs[d][:, slot, :], in_=psd[:, 128:140], func=AF.Exp, scale=sgn),
                     r=[PSD], w=[PREP[d][slot]])
                P.op("pool", lambda e: e.tensor_tensor(out=qtl[k][:], in0=qs_t[p][:], in1=ed[k][:, 0, :], op=ALU.mult),
                     r=[QS[p], ED[k]], w=[QTL[k]])
                P.op("pool", lambda e: e.tensor_tensor(out=ktm[d][:, slot, :], in0=kk[k][:], in1=ed[k][:, 1, :], op=ALU.mult),
                     r=[KK[k], ED[k]], w=[PREP[d][slot]])
                P.op("pe", lambda e: e.transpose(out=pst[:, 0:128], in_=qtl[k][:], identity=ident_b[:]),
                     r=[QTL[k], B_const], w=[PSTq])
                P.op("pe", lambda e: e.transpose(out=pst[:, 128:256], in_=ktm[d][:, slot, :], identity=ident_b[:]),
                     r=[PREP[d][slot], B_const], w=[PSTq])
                P.op("act", lambda e: e.activation(out=qkT[d][:, slot, :, :],
                                                   in_=pst[:, 0:256].rearrange("p (c t) -> p c t", c=2), func=AF.Copy),
                     r=[PSTq], w=[PREP[d][slot]])

            def step(h, hp, ti, d, slot):
                a = cnt["st"] % 2
                cnt["st"] += 1
                P.op("pe", lambda e: e.matmul(psa[:, 0:128], qkT[d][:, slot, 1, :], qkT[d][:, slot, 0, :], start=True, stop=True),
                     r=[PREP[d][slot]], w=[PSA])
                P.op("dve", lambda e: e.tensor_tensor(out=attm[a][:], in0=psa[:, 0:128], in1=mask[:, d, :], op=ALU.mult),
                     r=[PSA, B_hc], w=[ATT[a]])
                P.op("dve", lambda e: e.tensor_tensor(out=vblk[a][:], in0=vv[:, ti, :].unsqueeze(1).broadcast_to([128, 4, 128]),
                                                      in1=cmb[:], op=ALU.mult), r=[VV[ti], B_hc], w=[VB[a]])
                P.op("pe", lambda e: e.matmul(psu[:, 0:512], ktm[d][:, slot, :], vblk[a][:, :, :].rearrange("p c v -> p (c v)"),
                                              start=True, stop=True), r=[PREP[d][slot], VB[a]], w=[PSU])
                P.op("pe", lambda e: e.matmul(psa[:, 128:256], attm[a][:], vv[:, ti, :], start=True, stop=False),
                     r=[ATT[a], VV[ti]], w=[PSOt])
                P.op("pool", lambda e: e.tensor_copy(
                    qz[a][:, 0:640].rearrange("p (c x) -> p c x", x=160)[:, :, 0:32],
                    qkT[d][:, slot, 0, :].rearrange("p (c i) -> p c i", i=32)), r=[PREP[d][slot]], w=[QZ[a]])
                order = range(4) if d == 0 else range(3, -1, -1)
                for n, c in enumerate(order):
                    sb = n % 2
                    P.op("dve", lambda e, c=c, sb=sb: e.tensor_scalar(out=SBF[d][sb][:], in0=S[d][:],
                                                                       scalar1=ecs[d][:, slot, 3 * c + 2:3 * c + 3], scalar2=None,
                                                                       op0=ALU.mult), r=[SS[d], PREP[d][slot]], w=[SSB[d][sb]])
                    P.op("pe", lambda e, c=c, sb=sb, n=n: e.matmul(psa[:, 128:256], qz[a][:, c * 128:(c + 1) * 128],
                                                                    SBF[d][sb][:], start=False, stop=(n == 3)),
                         r=[QZ[a], SSB[d][sb]], w=[PSOt])
                    P.op("dve", lambda e, c=c, a=a: e.tensor_scalar(out=tmpu[a][:], in0=psu[:, c * 128:(c + 1) * 128],
                                                                     scalar1=ecs[d][:, slot, 3 * c + 1:3 * c + 2], scalar2=None,
                                                                     op0=ALU.mult), r=[PSU, PREP[d][slot]], w=[TU[a]])
                    P.op("dve", lambda e, c=c, a=a: e.scalar_tensor_tensor(out=S[d][:], in0=S[d][:],
                                                                            scalar=ecs[d][:, slot, 3 * c:3 * c + 1], in1=tmpu[a][:],
                                                                            op0=ALU.mult, op1=ALU.add),
                         r=[SS[d], TU[a], PREP[d][slot]], w=[SS[d]])
                if d == 0:
                    P.op("act", lambda e: e.activation(out=of[:, ti, :], in_=psa[:, 128:256], func=AF.Copy), r=[PSOt], w=[OF[ti]])
                else:
                    f = cnt["fin"] % 2
                    cnt["fin"] += 1
                    P.op("dve", lambda e: e.tensor_tensor(out=osum[f][:], in0=psa[:, 128:256], in1=of[:, ti, :], op=ALU.add),
                         r=[PSOt, OF[ti]], w=[OS[f]])
                    P.op("act", lambda e: e.activation(out=junk[:], in_=osum[f][:], func=AF.Square, accum_out=stt[f][:, 0:1]),
                         r=[OS[f]], w=[JK, ST[f]])
                    rstd_ops(stt[f], ST[f], 0, 1, 128)
                    P.op("dve", lambda e: e.scalar_tensor_tensor(out=btm[f][:], in0=osum[f][:], scalar=stt[f][:, 1:2],
                                                                 in1=GGt[:, ti, :], op0=ALU.mult, op1=ALU.mult),
                         r=[OS[f], ST[f], GGB[ti]], w=[BTM[f]])
                    P.op("pe", lambda e: e.transpose(out=pst[:, 256:384], in_=btm[f][:], identity=ident_b[:]),
                         r=[BTM[f], B_const], w=[PSTb])
                    P.op("act", lambda e: e.activation(out=bT[hp][:, ti * 128:(ti + 1) * 128], in_=pst[:, 256:384], func=AF.Copy),
                         r=[PSTb], w=[BT[hp]])

            for h in range(cfg.HG_HEADS):
                hp = h % 2
                load_w(stg, SB, wbf[hp][:], G["wB_in"][e_idx, h], KC, 640, WB[hp], kq=1)
                load_lb(h)
                P.op("pool", lambda e: e.memset(S[0][:], 0.0), w=[SS[0]])
                P.op("pool", lambda e: e.memset(S[1][:], 0.0), w=[SS[1]])
                for ti in range(NT):
                    p = cnt["p"] % 2
                    cnt["p"] += 1
                    for kc in range(KC):
                        P.op("pe", lambda e, p=p, kc=kc, ti=ti: e.matmul(
                            psp[p][0][:, :], hT[:, kc, ti * 128:(ti + 1) * 128], wbf[hp][:, kc, 0:512],
                            start=(kc == 0), stop=(kc == KC - 1)), r=[HT[ti], WB[hp]], w=[PSP[p][0]])
                    for kc in range(KC):
                        P.op("pe", lambda e, p=p, kc=kc, ti=ti: e.matmul(
                            psp[p][1][:, 0:128], hT[:, kc, ti * 128:(ti + 1) * 128], wbf[hp][:, kc, 512:640],
                            start=(kc == 0), stop=(kc == KC - 1)), r=[HT[ti], WB[hp]], w=[PSP[p][1]])
                    tile_common(h, hp, ti, p)
                    prep(h, hp, ti, p, 0, ti % 2)
                    prep(h, hp, ti, p, 1, ti)
                    step(h, hp, ti, 0, ti % 2)
                order_b = list(range(NT_C - 1, -1, -1)) + list(range(NT - 1, NT_C - 1, -1))
                for ti in order_b:
                    step(h, hp, ti, 1, ti)
                tlo = 0 if need_ctx else TC
                ch = cfg.DA_HEADS + h
                P.dma("sp", gscr[ch, :, tlo:T], bT[hp][:, tlo:T], r=[BT[hp]], w=[GS[ch]])

    def even_mixer(e_idx, l, s, need_ctx):
        attention_part(e_idx, l, s, need_ctx)
        hgrn_part(e_idx, l, s, need_ctx)

    return even_mixer


def prep_shared(cfg, inp):
    D, KC, R = cfg.D, cfg.KC, cfg.R
    f = lambda a: np.ascontiguousarray(a, dtype=np.float32)
    sh = {}
    sh["w_mod"] = f(inp["w_mod"])
    b_mod = np.asarray(inp["b_mod"], np.float32)
    bc = b_mod[:, :2 * D].reshape(cfg.DEPTH, 2 * KC, 128).transpose(2, 0, 1)
    sh["bmodc"] = f(np.repeat(bc[:, :, :, None], R, axis=3))
    gp = np.asarray(inp["g_pre"], np.float32).reshape(cfg.DEPTH, KC, 128).transpose(2, 0, 1)
    sh["gprec"] = f(np.repeat(gp[:, :, :, None], R, axis=3))
    sh["bmodg"] = f(np.repeat(b_mod[None, :, 2 * D:], R, axis=0))
    sh["gpostr"] = f(np.repeat(np.asarray(inp["g_post"], np.float32)[None], R, axis=0))
    sel = np.zeros((R, R * 128), np.float32)
    for r in range(R):
        sel[r, r * 128:(r + 1) * 128] = 1.0
    sh["sel"] = sel
    sh["ident"] = np.eye(128, dtype=np.float32)
    W = np.asarray(inp["ev_w_in"], np.float32)
    NE = cfg.N_EVEN
    A = W[:, :, :4 * cfg.DA_W].reshape(NE, KC, 128, 4, cfg.DA_HEADS, 128)
    sh["wA"] = f(A.transpose(0, 4, 2, 1, 3, 5).reshape(NE, cfg.DA_HEADS, 128, KC, 512))
    B = W[:, :, 4 * cfg.DA_W:].reshape(NE, KC, 128, 5, cfg.HG_HEADS, 128)
    sh["wB"] = f(B.transpose(0, 4, 2, 1, 3, 5).reshape(NE, cfg.HG_HEADS, 128, KC, 640))
    sh["wEO"] = f(np.asarray(inp["ev_w_out"], np.float32).reshape(NE, KC, 128, D).transpose(0, 2, 1, 3))
    bcast = lambda a: f(np.broadcast_to(np.asarray(a, np.float32)[None], (128,) + tuple(np.shape(a))))
    sh["lamb"] = bcast(inp["ev_lambda"])
    sh["sublnb"] = bcast(inp["ev_subln_g"])
    sh["hgnb"] = bcast(inp["ev_hg_norm_g"])
    sh["lblb"] = bcast(inp["ev_hg_lb_logits"])
    sh["ropec"], sh["ropes"] = rope_tables(cfg)
    sh["hgM"], sh["hgmask"], sh["hgw"] = hgrn_consts()
    cm = np.zeros((128, 4), np.float32)
    for c in range(4):
        cm[c * 32:(c + 1) * 32, c] = 1.0
    sh["hgcm"] = cm
    NO = cfg.N_ODD
    PG, PW = cfg.PG, cfg.D
    Wo = np.asarray(inp["od_w_in"], np.float32)
    U = Wo[:, :, :PW].reshape(NO, KC, 128, 4, PG)
    Z = Wo[:, :, PW:].reshape(NO, KC, 128, 4, PG)
    UZ = np.concatenate([U, Z], axis=4)
    sh["wOD"] = f(UZ.transpose(0, 3, 2, 1, 4))
    WP = np.asarray(inp["od_w_pool"], np.float32).reshape(NO, 4, cfg.PGC, 128, PG)
    sh["wPL"] = f(WP.transpose(0, 1, 3, 2, 4))
    sh["wOO"] = f(np.asarray(inp["od_w_out"], np.float32).reshape(NO, KC, 128, D).transpose(0, 2, 1, 3))
    sh["odsc"] = f(np.asarray(inp["od_scale"], np.float32).reshape(NO, KC, 128).transpose(2, 0, 1))
    pc, maps = pool_consts(cfg)
    sh["poolc"] = f(pc)
    return sh, maps, pc.shape[0]


def prep_core(cfg, inp, sh, b0):
    NB, KC, R = cfg.NB, cfg.KC, cfg.R
    m = dict(sh)
    m["x"] = np.ascontiguousarray(inp["x"][b0:b0 + NB], dtype=np.float32)
    m["ctx"] = np.ascontiguousarray(inp["ctx"][b0:b0 + NB], dtype=np.float32)
    rows = np.concatenate([np.asarray(inp["c"], np.float32)[b0:b0 + NB], np.asarray(inp["c_ctx"], np.float32)[None]], 0)
    m["cT"] = np.ascontiguousarray(rows.reshape(R, KC, 128).transpose(2, 1, 0))
    return m


def kernel(**inputs):
    cfg = FULL
    ncores = 8
    sh, maps, nblk = prep_shared(cfg, inputs)
    nc = build(cfg, pool_maps=maps, n_pool_blocks=nblk)
    in_maps = [prep_core(cfg, inputs, sh, i * cfg.NB) for i in range(ncores)]
    res = run_bass_kernel_spmd(nc, in_maps, core_ids=list(range(ncores)))
    out = np.concatenate([np.asarray(r["y"], dtype=np.float32) for r in res.results], axis=0)
    return out
```

```python
import math
from contextlib import ExitStack, contextmanager
import numpy as np
import concourse.bass as bass
import concourse.mybir as mybir
from concourse.bass_utils import run_bass_kernel_spmd

F32 = mybir.dt.float32
BF16 = mybir.dt.bfloat16
AF = mybir.ActivationFunctionType
ALU = mybir.AluOpType
AX = mybir.AxisListType
EPS = 1e-6
ROPE_BASE = 10000.0
POOL_WINDOWS = (2, 4, 8, 16)
HC = 32


class Cfg:
    def __init__(s, D=2048, T_LAT=2048, T_CTX=256, GRID_W=64, DA_HEADS=8, HG_HEADS=8, NB=2, DEPTH=4):
        s.D, s.T_LAT, s.T_CTX, s.GRID_W = D, T_LAT, T_CTX, GRID_W
        s.DA_HEADS, s.HG_HEADS, s.NB, s.DEPTH = DA_HEADS, HG_HEADS, NB, DEPTH
        s.KC = D // 128
        s.NT_C = T_CTX // 128
        s.NT_L = T_LAT // 128
        s.NT = s.NT_C + s.NT_L
        s.T = T_CTX + T_LAT
        s.DA_W = DA_HEADS * 128
        s.HG_W = HG_HEADS * 128
        s.EVEN_IN = 4 * s.DA_W + 5 * s.HG_W
        s.EVEN_OUT = s.DA_W + s.HG_W
        assert s.EVEN_OUT == D
        s.PG = D // 4
        s.PGC = s.PG // 128
        s.R = NB + 1
        s.NBK = max(1, D // 512)
        s.BW = min(512, D)
        s.N_EVEN = (DEPTH + 1) // 2
        s.N_ODD = DEPTH // 2


FULL = Cfg()


class Ev:
    __slots__ = ("sem", "val", "ek")

    def __init__(s, sem, val, ek):
        s.sem, s.val, s.ek = sem, val, ek


class Buf:
    __slots__ = ("name", "w", "rs")

    def __init__(s, name):
        s.name, s.w, s.rs = name, None, {}


ENG = ("pe", "act", "dve", "pool", "sp")
SEM_LIMIT = 15000


class Prog:
    def __init__(s, nc, stack, n_dma=32):
        s.nc = nc
        s.stack = stack
        s.e = {"pe": nc.tensor, "act": nc.scalar, "dve": nc.vector, "pool": nc.gpsimd, "sp": nc.sync}
        nsem = {"pe": 12, "act": 5, "dve": 6, "pool": 5, "sp": 1}
        s.sems = {k: [stack.enter_context(nc.semaphore(f"s_{k}_{i}")) for i in range(n)] for k, n in nsem.items()}
        s.si = {k: 0 for k in ENG}
        s.cnt = {k: 0 for k in ENG}
        s.seen = {k: {} for k in ENG}
        s.dsems = [stack.enter_context(nc.semaphore(f"s_dma_{i}")) for i in range(n_dma)]
        s.dval = [0] * n_dma
        s.di = 0
        s.uid = 0
        s.nwait = 0
        s.nops = 0

    def _wait(s, ek, ev):
        seen = s.seen[ek]
        if seen.get(ev.sem, 0) >= ev.val:
            return
        s.e[ek].wait_ge(ev.sem, ev.val)
        seen[ev.sem] = ev.val
        s.nwait += 1

    def _deps(s, ek, r, w):
        for b in r:
            if b.w is not None and not (b.w.ek == ek and ek == "pe"):
                s._wait(ek, b.w)
        for b in w:
            if b.w is not None and not (b.w.ek == ek and ek == "pe"):
                s._wait(ek, b.w)
            for ev in b.rs.values():
                if ev.ek != ek:
                    s._wait(ek, ev)

    def _mark(s, ev, r, w):
        for b in r:
            b.rs[ev.sem] = ev
        for b in w:
            b.w = ev
            b.rs = {}

    def op(s, ek, fn, r=(), w=()):
        s._deps(ek, r, w)
        ins = fn(s.e[ek])
        if s.cnt[ek] >= SEM_LIMIT:
            s.si[ek] += 1
            s.cnt[ek] = 0
        s.cnt[ek] += 1
        sem = s.sems[ek][s.si[ek]]
        ins.then_inc(sem, 1)
        ev = Ev(sem, s.cnt[ek], ek)
        s._mark(ev, r, w)
        s.nops += 1
        return ev

    def dma(s, qk, out, in_, r=(), w=()):
        s._deps(qk, r, w)
        i = s.di
        s.di = (s.di + 1) % len(s.dsems)
        sem = s.dsems[i]
        if s.dval[i] > 0:
            s._wait(qk, Ev(sem, s.dval[i], None))
        s.dval[i] += 16
        s.e[qk].dma_start(out=out, in_=in_).then_inc(sem, 16)
        ev = Ev(sem, s.dval[i], None)
        s._mark(ev, r, w)
        s.nops += 1
        return ev

    def barrier(s, engines=ENG):
        evs = [Ev(s.sems[k][s.si[k]], s.cnt[k], k) for k in ENG if s.cnt[k] > 0]
        evs += [Ev(s.dsems[i], s.dval[i], None) for i in range(len(s.dsems)) if s.dval[i] > 0]
        for ek in engines:
            for ev in evs:
                if ev.ek != ek:
                    s._wait(ek, ev)

    def final_wait(s, ek="sp"):
        s.barrier(engines=(ek,))

    @contextmanager
    def scope(s):
        st = ExitStack()
        sc = Scope(s, st)
        try:
            yield sc
        finally:
            s.barrier()
            st.close()


class Scope:
    def __init__(s, P, st):
        s.P, s.st = P, st

    def sb(s, name, shape, dt):
        s.P.uid += 1
        return s.st.enter_context(s.P.nc.sbuf_tensor(f"{name}_{s.P.uid}", list(shape), dt))

    def ps(s, name, shape, dt):
        s.P.uid += 1
        return s.st.enter_context(s.P.nc.psum_tensor(f"{name}_{s.P.uid}", list(shape), dt))


def rope_tables(cfg):
    T = cfg.T_LAT
    t = np.arange(T)
    row, col = t // cfg.GRID_W, t % cfg.GRID_W
    nf = 16
    inv = ROPE_BASE ** (-np.arange(nf, dtype=np.float32) / nf)
    cosT = np.zeros((T, 64), np.float32)
    sinT = np.zeros((T, 64), np.float32)
    for blk, pos in ((0, row), (1, col)):
        ang = pos.astype(np.float32)[:, None] * inv[None, :]
        c, sn = np.cos(ang), np.sin(ang)
        cosT[:, blk * 32:blk * 32 + 16] = c
        cosT[:, blk * 32 + 16:blk * 32 + 32] = c
        sinT[:, blk * 32:blk * 32 + 16] = -sn
        sinT[:, blk * 32 + 16:blk * 32 + 32] = sn
    cos2 = np.concatenate([cosT, cosT], 1)
    sin2 = np.concatenate([sinT, sinT], 1)
    cos2 = cos2.reshape(cfg.NT_L, 128, 128).transpose(1, 0, 2).copy()
    sin2 = sin2.reshape(cfg.NT_L, 128, 128).transpose(1, 0, 2).copy()
    return cos2, sin2


def hgrn_consts():
    n = 128
    s = np.arange(n)[:, None]
    t = np.arange(n)[None, :]
    same = (s // HC) == (t // HC)
    mid_f = (t // HC) * HC + HC // 2 - 1
    mid_b = (t // HC) * HC + HC // 2
    M1f = (same & (s <= t)).astype(np.float32) - (same & (s <= mid_f)).astype(np.float32)
    M1b = (same & (s >= t)).astype(np.float32) - (same & (s >= mid_b)).astype(np.float32)
    maskf = (same & (s <= t)).astype(np.float32)
    maskb = (same & (s >= t)).astype(np.float32)
    nchunk = n // HC
    wf = np.zeros((n, nchunk * 3), np.float32)
    wb = np.zeros((n, nchunk * 3), np.float32)
    for c in range(nchunk):
        lo, hi = c * HC, (c + 1) * HC
        mf = lo + HC // 2 - 1
        mb = lo + HC // 2
        idx = np.arange(n)
        inck = (idx >= lo) & (idx < hi)
        wf[:, c * 3 + 0] = inck
        wf[:, c * 3 + 1] = inck & (idx > mf)
        wf[:, c * 3 + 2] = inck & (idx <= mf)
        wb[:, c * 3 + 0] = inck
        wb[:, c * 3 + 1] = inck & (idx < mb)
        wb[:, c * 3 + 2] = inck & (idx >= mb)
    return np.stack([M1f, M1b]), np.stack([maskf, maskb]), np.stack([wf, wb])


def pool_blocks(T):
    out = {}
    nt = T // 128
    for wi, w in enumerate(POOL_WINDOWS):
        t = np.arange(T)
        lo = np.clip(t - w // 2, 0, T)
        hi = np.clip(t + (w - w // 2), 0, T)
        M = np.zeros((T, T), np.float32)
        for tt in range(T):
            M[tt, lo[tt]:hi[tt]] = 1.0 / (hi[tt] - lo[tt])
            M[tt, tt] -= 1.0
        MT = M.T
        for ti in range(nt):
            for di in (-1, 0, 1):
                si = ti + di
                if 0 <= si < nt:
                    out[(wi, ti, di)] = MT[si * 128:(si + 1) * 128, ti * 128:(ti + 1) * 128].copy()
    return out


def pool_consts(cfg):
    uniq = []
    keys = {}
    maps = {}
    for seg, T in (("c", cfg.T_CTX), ("l", cfg.T_LAT)):
        blocks = pool_blocks(T)
        for k, blk in blocks.items():
            kb = blk.tobytes()
            if kb not in keys:
                keys[kb] = len(uniq)
                uniq.append(blk)
            maps[(seg,) + k] = keys[kb]
    return np.stack(uniq), maps


def build(cfg, layers=None, pool_maps=None, n_pool_blocks=0):
    layers = list(range(cfg.DEPTH)) if layers is None else layers
    D, KC, T, NT, NT_C, NT_L, R, NB = cfg.D, cfg.KC, cfg.T, cfg.NT, cfg.NT_C, cfg.NT_L, cfg.R, cfg.NB
    BW, NBK = cfg.BW, cfg.NBK
    nc = bass.Bass("TRN2", target_bir_lowering=False)

    def din(name, shape, dt=F32):
        return nc.dram_tensor(name, list(shape), dt, kind="ExternalInput").ap()

    x_in = din("x", [NB, cfg.T_LAT, D])
    ctx_in = din("ctx", [NB, cfg.T_CTX, D])
    cT_in = din("cT", [128, KC, R])
    wmod_in = din("w_mod", [cfg.DEPTH, D, 3 * D])
    bmodc_in = din("bmodc", [128, cfg.DEPTH, 2 * KC, R])
    gprec_in = din("gprec", [128, cfg.DEPTH, KC, R])
    bmodg_in = din("bmodg", [R, cfg.DEPTH, D])
    gpostr_in = din("gpostr", [R, cfg.DEPTH, D])
    sel_in = din("sel", [R, R * 128])
    ident_in = din("ident", [128, 128])
    wA_in = din("wA", [cfg.N_EVEN, cfg.DA_HEADS, 128, KC, 512])
    wB_in = din("wB", [cfg.N_EVEN, cfg.HG_HEADS, 128, KC, 640])
    wEO_in = din("wEO", [cfg.N_EVEN, 128, KC, D])
    lam_in = din("lamb", [128, cfg.N_EVEN, 4, 64])
    subln_in = din("sublnb", [128, cfg.N_EVEN, 128])
    hgn_in = din("hgnb", [128, cfg.N_EVEN, 128])
    lbl_in = din("lblb", [128, cfg.N_EVEN, 2, cfg.HG_W])
    cos_in = din("ropec", [128, NT_L, 128])
    sin_in = din("ropes", [128, NT_L, 128])
    hgM_in = din("hgM", [2, 128, 128])
    hgmask_in = din("hgmask", [2, 128, 128])
    hgw_in = din("hgw", [2, 128, 12])
    hgcm_in = din("hgcm", [128, 4])
    wOD_in = din("wOD", [cfg.N_ODD, 4, 128, KC, 2 * cfg.PG])
    wPL_in = din("wPL", [cfg.N_ODD, 4, 128, cfg.PGC, cfg.PG])
    wOO_in = din("wOO", [cfg.N_ODD, 128, KC, D])
    odsc_in = din("odsc", [128, cfg.N_ODD, KC])
    poolc_in = din("poolc", [max(1, n_pool_blocks), 128, 128])

    y_out = nc.dram_tensor("y", [NB, cfg.T_LAT, D], F32, kind="ExternalOutput").ap()
    cscr = nc.dram_tensor("cscr", [NB, cfg.T_CTX, D], F32, kind="Internal").ap()
    gscr = nc.dram_tensor("gscr", [KC, 128, T], BF16, kind="Internal").ap()

    with ExitStack() as stack:
        P = Prog(nc, stack)
        top = Scope(P, stack)

        hT = top.sb("hT", [128, KC, T], BF16)
        HT = [Buf(f"HT{i}") for i in range(NT)]
        ident_f = top.sb("identf", [128, 128], F32)
        ident_b = top.sb("identb", [128, 128], BF16)
        scT = top.sb("scT", [128, KC, R], F32)
        selT = top.sb("selT", [R, R * 128], F32)
        modc = top.sb("modc", [128, 2 * KC, R], F32)
        Acol = top.sb("Acol", [128, KC, R], F32)
        GTrow = top.sb("GTrow", [R, D], F32)
        bmodc = top.sb("bmodcs", [128, cfg.DEPTH, 2 * KC, R], F32)
        gprec = top.sb("gprecs", [128, cfg.DEPTH, KC, R], F32)
        B_const = Buf("const")
        B_scT = Buf("scT")
        B_modc = Buf("modc")
        B_Acol = Buf("Acol")
        B_GTrow = Buf("GTrow")
        DXL = [[Buf(f"dxl{s}_{i}") for i in range(NT_L)] for s in range(NB)]
        DXC = [[Buf(f"dxc{s}_{i}") for i in range(NT_C)] for s in range(NB)]
        GS = [Buf(f"gs{k}") for k in range(KC)]

        P.dma("sp", ident_f[:], ident_in[:, :], w=[B_const])
        P.dma("sp", scT[:], cT_in[:, :, :], w=[B_scT])
        P.dma("sp", selT[:], sel_in[:, :], w=[B_const])
        P.dma("sp", bmodc[:], bmodc_in[:, :, :, :], w=[B_const])
        P.dma("sp", gprec[:], gprec_in[:, :, :, :], w=[B_const])
        P.op("dve", lambda e: e.tensor_copy(ident_b[:], ident_f[:]), r=[B_const], w=[B_const])
        P.op("act", lambda e: e.activation(out=scT[:], in_=scT[:], func=AF.Silu), r=[B_scT], w=[B_scT])

        st_state = {"i": 0}

        def load_w(sc_stage, SB, dst, src, nk, ncols, wbuf, kq=4):
            for k0 in range(0, nk, kq):
                k1 = min(nk, k0 + kq)
                i = st_state["i"] % 2
                st_state["i"] += 1
                stg = sc_stage[i]
                v = stg[:, 0:(k1 - k0) * ncols].rearrange("p (k n) -> p k n", n=ncols)
                P.dma("sp", v, src[:, k0:k1, :], w=[SB[i]])
                P.op("pool", lambda e, v=v, k0=k0, k1=k1: e.tensor_copy(dst[:, k0:k1, :], v), r=[SB[i]], w=[wbuf])

        def mod_phase(l):
            with P.scope() as sc:
                wm = [sc.sb("wm", [128, KC, 512], F32) for _ in range(2)]
                WM = [Buf("wm0"), Buf("wm1")]
                bg = sc.sb("bg", [R, D], F32)
                gp = sc.sb("gp", [R, D], F32)
                B_bg = Buf("bg")
                psA = sc.ps("psA", [128, 512], F32)
                psB = sc.ps("psB", [128, 512], F32)
                PSA, PSB = Buf("psA"), Buf("psB")
                P.dma("sp", bg[:], bmodg_in[:, l, :], w=[B_bg])
                P.dma("sp", gp[:], gpostr_in[:, l, :], w=[B_bg])
                wsrc = wmod_in[l].rearrange("(kc p) n -> p kc n", p=128)
                ncolp = (2 * D) // 512
                for piece in range((3 * D) // 512):
                    sl = piece % 2
                    n0 = piece * 512
                    P.dma("sp", wm[sl][:], wsrc[:, :, n0:n0 + 512], w=[WM[sl]])
                    if piece < ncolp:
                        for jj in range(4):
                            for kc in range(KC):
                                P.op("pe", lambda e, jj=jj, kc=kc, sl=sl: e.matmul(
                                    psA[:, jj * R:(jj + 1) * R], wm[sl][:, kc, jj * 128:(jj + 1) * 128], scT[:, kc, :],
                                    start=(jj == 0 and kc == 0), stop=(jj == 3 and kc == KC - 1)),
                                    r=[WM[sl], B_scT], w=[PSA])
                        j0 = piece * 4
                        P.op("dve", lambda e, j0=j0: e.tensor_tensor(
                            out=modc[:, j0:j0 + 4, :], in0=psA[:, 0:4 * R].rearrange("p (j r) -> p j r", r=R),
                            in1=bmodc[:, l, j0:j0 + 4, :], op=ALU.add), r=[PSA, B_const], w=[B_modc])
                    else:
                        nb = piece - ncolp
                        for kc in range(KC):
                            P.op("pe", lambda e, kc=kc, sl=sl: e.matmul(
                                psB[0:R, :], scT[:, kc, :], wm[sl][:, kc, :], start=(kc == 0), stop=(kc == KC - 1)),
                                r=[WM[sl], B_scT], w=[PSB])
                        P.op("dve", lambda e, nb=nb: e.tensor_tensor(
                            out=GTrow[0:R, nb * 512:(nb + 1) * 512], in0=psB[0:R, :],
                            in1=bg[0:R, nb * 512:(nb + 1) * 512], op=ALU.add), r=[PSB, B_bg], w=[B_GTrow])
                        P.op("dve", lambda e, nb=nb: e.tensor_tensor(
                            out=GTrow[0:R, nb * 512:(nb + 1) * 512], in0=GTrow[0:R, nb * 512:(nb + 1) * 512],
                            in1=gp[0:R, nb * 512:(nb + 1) * 512], op=ALU.mult), r=[B_GTrow, B_bg], w=[B_GTrow])
                P.op("dve", lambda e: e.scalar_tensor_tensor(
                    out=Acol[:], in0=modc[:, KC:2 * KC, :], scalar=1.0, in1=gprec[:, l, :, :],
                    op0=ALU.add, op1=ALU.mult), r=[B_modc, B_const], w=[B_Acol])

        def x_src(l, s, ti):
            if ti < NT_C:
                src = ctx_in if l == layers[0] else cscr
                return src[s, ti * 128:(ti + 1) * 128, :], DXC[s][ti]
            tl = ti - NT_C
            src = x_in if l == layers[0] else y_out
            return src[s, tl * 128:(tl + 1) * 128, :], DXL[s][tl]

        def phase_n(l, s, ctx_active):
            with P.scope() as sc:
                xt = [sc.sb("xt", [128, D], F32) for _ in range(2)]
                xh = [sc.sb("xh", [128, D], BF16) for _ in range(2)]
                junk = sc.sb("junk", [128, D], BF16)
                stt = [sc.sb("stt", [128, 4], F32) for _ in range(2)]
                pst = [sc.ps("pst", [128, KC * 128], BF16) for _ in range(2)]
                XT = [Buf("xt0"), Buf("xt1")]
                XH = [Buf("xh0"), Buf("xh1")]
                ST = [Buf("st0"), Buf("st1")]
                PST = [Buf("pst0"), Buf("pst1")]
                JK = Buf("junk")
                tiles = list(range(NT)) if ctx_active else list(range(NT_C, NT))
                for n, ti in enumerate(tiles):
                    p = n % 2
                    r = R - 1 if ti < NT_C else s
                    src, dbuf = x_src(l, s, ti)
                    P.dma("sp", xt[p][:], src, r=[dbuf], w=[XT[p]])
                    P.op("act", lambda e, p=p: e.activation(out=junk[:], in_=xt[p][:], func=AF.Square,
                                                            accum_out=stt[p][:, 0:1]), r=[XT[p]], w=[JK, ST[p]])
                    P.op("dve", lambda e, p=p: e.tensor_scalar(out=stt[p][:, 1:2], in0=stt[p][:, 0:1], scalar1=1.0 / D,
                                                               scalar2=EPS, op0=ALU.mult, op1=ALU.add), r=[ST[p]], w=[ST[p]])
                    P.op("act", lambda e, p=p: e.activation(out=stt[p][:, 2:3], in_=stt[p][:, 1:2], func=AF.Ln),
                         r=[ST[p]], w=[ST[p]])
                    P.op("act", lambda e, p=p: e.activation(out=stt[p][:, 3:4], in_=stt[p][:, 2:3], func=AF.Exp, scale=-0.5),
                         r=[ST[p]], w=[ST[p]])
                    P.op("dve", lambda e, p=p: e.tensor_scalar(out=xh[p][:], in0=xt[p][:], scalar1=stt[p][:, 3:4],
                                                               scalar2=None, op0=ALU.mult), r=[XT[p], ST[p]], w=[XH[p]])
                    for kc in range(KC):
                        P.op("pe", lambda e, p=p, kc=kc: e.transpose(
                            out=pst[p][:, kc * 128:(kc + 1) * 128], in_=xh[p][:, kc * 128:(kc + 1) * 128],
                            identity=ident_b[:]), r=[XH[p], B_const], w=[PST[p]])
                    for kc in range(KC):
                        ek = "act" if kc % 2 == 0 else "dve"
                        if ek == "act":
                            P.op("act", lambda e, p=p, kc=kc, ti=ti, r=r: e.activation(
                                out=hT[:, kc, ti * 128:(ti + 1) * 128], in_=pst[p][:, kc * 128:(kc + 1) * 128],
                                func=AF.Identity, scale=Acol[:, kc, r:r + 1], bias=modc[:, kc, r:r + 1]),
                                r=[PST[p], B_Acol, B_modc], w=[HT[ti]])
                        else:
                            P.op("dve", lambda e, p=p, kc=kc, ti=ti, r=r: e.tensor_scalar(
                                out=hT[:, kc, ti * 128:(ti + 1) * 128], in0=pst[p][:, kc * 128:(kc + 1) * 128],
                                scalar1=Acol[:, kc, r:r + 1], scalar2=modc[:, kc, r:r + 1], op0=ALU.mult, op1=ALU.add),
                                r=[PST[p], B_Acol, B_modc], w=[HT[ti]])

        def phase_o(l, s, wo_src, need_ctx):
            with P.scope() as sc:
                wO = sc.sb("wO", [128, KC, D], BF16)
                WO = Buf("wO")
                stg = [sc.sb("stg", [128, 2 * 512], F32) for _ in range(2)]
                SB = [Buf("stg0"), Buf("stg1")]
                GTb = [sc.sb("GTb", [128, D], F32) for _ in range(2)]
                B_GTb = Buf("GTb")
                xt = [sc.sb("xo", [128, D], F32) for _ in range(2)]
                tt = sc.sb("tt", [128, D], F32)
                junk = sc.sb("junko", [128, NBK, BW], BF16)
                stt = [sc.sb("stto", [128, 8], F32) for _ in range(2)]
                psy = [[sc.ps("psy", [128, 512], F32) for _ in range(NBK)] for _ in range(2 if NBK <= 4 else 1)]
                PSY = [[Buf("psy") for _ in range(NBK)] for _ in range(len(psy))]
                XT = [Buf("xo0"), Buf("xo1")]
                TT = Buf("tt")
                JKS = [Buf(f"jko{i}") for i in range(NBK)]
                ST = [Buf("sto0"), Buf("sto1")]
                tlo = 0 if need_ctx else NT_C * 128
                for kc in range(KC):
                    P.dma("sp", hT[:, kc, tlo:T], gscr[kc, :, tlo:T], r=[GS[kc]], w=HT)
                for nb in range(NBK):
                    load_w(stg, SB, wO[:, :, nb * BW:(nb + 1) * BW], wo_src[:, :, nb * BW:(nb + 1) * BW], KC, BW, WO, kq=2)
                rows = [s, R - 1] if need_ctx else [s]
                for gi, r in enumerate(rows):
                    for nb in range(NBK):
                        P.op("pe", lambda e, r=r, nb=nb: e.matmul(
                            psy[0][nb][:, 0:BW], selT[0:R, r * 128:(r + 1) * 128], GTrow[0:R, nb * BW:(nb + 1) * BW],
                            start=True, stop=True), r=[B_const, B_GTrow], w=[PSY[0][nb]])
                        P.op("act", lambda e, gi=gi, nb=nb: e.activation(
                            out=GTb[gi][:, nb * BW:(nb + 1) * BW], in_=psy[0][nb][:, 0:BW], func=AF.Copy),
                            r=[PSY[0][nb]], w=[B_GTb])
                tiles = list(range(NT)) if need_ctx else list(range(NT_C, NT))
                for n, ti in enumerate(tiles):
                    p = n % 2
                    pp = n % len(psy)
                    gi = 1 if ti < NT_C else 0
                    src, dbuf = x_src(l, s, ti)
                    P.dma("sp", xt[p][:], src, r=[dbuf], w=[XT[p]])
                    for nb in range(NBK):
                        for kc in range(KC):
                            P.op("pe", lambda e, pp=pp, nb=nb, kc=kc, ti=ti: e.matmul(
                                psy[pp][nb][:, 0:BW], hT[:, kc, ti * 128:(ti + 1) * 128],
                                wO[:, kc, nb * BW:(nb + 1) * BW], start=(kc == 0), stop=(kc == KC - 1)),
                                r=[HT[ti], WO], w=[PSY[pp][nb]])
                    for nb in range(NBK):
                        P.op("act", lambda e, p=p, pp=pp, nb=nb: e.activation(
                            out=junk[:, nb, :], in_=psy[pp][nb][:, 0:BW], func=AF.Square, accum_out=stt[p][:, nb:nb + 1]),
                            r=[PSY[pp][nb]], w=[JKS[nb], ST[p]])
                    P.op("dve", lambda e, p=p: e.reduce_sum(out=stt[p][:, 4:5], in_=stt[p][:, 0:NBK], axis=AX.X),
                         r=[ST[p]], w=[ST[p]])
                    P.op("dve", lambda e, p=p: e.tensor_scalar(out=stt[p][:, 5:6], in0=stt[p][:, 4:5], scalar1=1.0 / D,
                                                               scalar2=EPS, op0=ALU.mult, op1=ALU.add), r=[ST[p]], w=[ST[p]])
                    P.op("act", lambda e, p=p: e.activation(out=stt[p][:, 6:7], in_=stt[p][:, 5:6], func=AF.Ln),
                         r=[ST[p]], w=[ST[p]])
                    P.op("act", lambda e, p=p: e.activation(out=stt[p][:, 7:8], in_=stt[p][:, 6:7], func=AF.Exp, scale=-0.5),
                         r=[ST[p]], w=[ST[p]])
                    for nb in range(NBK):
                        P.op("dve", lambda e, p=p, pp=pp, nb=nb, gi=gi: e.scalar_tensor_tensor(
                            out=tt[:, nb * BW:(nb + 1) * BW], in0=psy[pp][nb][:, 0:BW], scalar=stt[p][:, 7:8],
                            in1=GTb[gi][:, nb * BW:(nb + 1) * BW], op0=ALU.mult, op1=ALU.mult),
                            r=[PSY[pp][nb], ST[p], B_GTb], w=[TT])
                    P.op("pool", lambda e, p=p: e.tensor_tensor(out=xt[p][:], in0=tt[:], in1=xt[p][:], op=ALU.add),
                         r=[TT, XT[p]], w=[XT[p]])
                    if ti < NT_C:
                        dst, dbuf2 = cscr[s, ti * 128:(ti + 1) * 128, :], DXC[s][ti]
                    else:
                        tl = ti - NT_C
                        dst, dbuf2 = y_out[s, tl * 128:(tl + 1) * 128, :], DXL[s][tl]
                    P.dma("sp", dst, xt[p][:], r=[XT[p]], w=[dbuf2])

        def odd_mixer(o, s, ctx_active):
            PG, PGC = cfg.PG, cfg.PGC
            segs = ([("c", 0, NT_C)] if ctx_active else []) + [("l", NT_C, NT_L)]
            with P.scope() as sc:
                stg = [sc.sb("stg", [128, 4 * 640], F32) for _ in range(2)]
                SB = [Buf("stg0"), Buf("stg1")]
                wU = sc.sb("wU", [128, KC, PG], BF16)
                wZ = sc.sb("wZ", [128, KC, PG], BF16)
                wP = sc.sb("wP", [128, PGC, PG], BF16)
                WU, WZ, WP = Buf("wU"), Buf("wZ"), Buf("wP")
                pcf = sc.sb("pcf", [128, n_pool_blocks, 128], F32)
                pcb = sc.sb("pcb", [128, n_pool_blocks, 128], BF16)
                B_pc = Buf("pc")
                lsc = sc.sb("lsc", [128, KC], F32)
                utm = sc.sb("utm", [128, NT, PG], BF16)
                UT = [Buf(f"ut{i}") for i in range(NT)]
                rT = sc.sb("rT", [128, PGC, T], BF16)
                RT = [Buf(f"rt{i}") for i in range(NT)]
                sz = [sc.sb("sz", [128, 512], BF16) for _ in range(2)]
                SZ = [Buf("sz0"), Buf("sz1")]
                gch = [sc.sb("gch", [128, T], BF16) for _ in range(2)]
                GCH = [Buf("gch0"), Buf("gch1")]
                psu = [sc.ps("psu", [128, 512], F32) for _ in range(2)]
                PSU = [Buf("psu0"), Buf("psu1")]
                psr = [sc.ps("psr", [128, 512], F32) for _ in range(2)]
                PSR = [Buf("psr0"), Buf("psr1")]
                psq = [sc.ps("psq", [128, 512], F32) for _ in range(2)]
                PSQ = [Buf("psq0"), Buf("psq1")]
                psz = [sc.ps("psz", [128, 512], F32) for _ in range(2)]
                PSZ = [Buf("psz0"), Buf("psz1")]
                P.dma("sp", pcf[:], poolc_in.rearrange("n p c -> p n c"), w=[B_pc])
                P.op("dve", lambda e: e.tensor_copy(pcb[:], pcf[:]), r=[B_pc], w=[B_pc])
                P.dma("sp", lsc[:], odsc_in[:, o, :], w=[B_pc])
                cu = cr = cq = 0
                gcount = 0
                for j in range(4):
                    load_w(stg, SB, wU[:], wOD_in[o, j][:, :, 0:PG], KC, PG, WU)
                    load_w(stg, SB, wZ[:], wOD_in[o, j][:, :, PG:2 * PG], KC, PG, WZ)
                    load_w(stg, SB, wP[:], wPL_in[o, j], PGC, PG, WP)
                    tiles = list(range(NT)) if ctx_active else list(range(NT_C, NT))
                    for ti in tiles:
                        for n0 in range(0, PG, 512):
                            nw = min(512, PG - n0)
                            p = cu % 2
                            cu += 1
                            for kc in range(KC):
                                P.op("pe", lambda e, p=p, kc=kc, ti=ti, n0=n0, nw=nw: e.matmul(
                                    psu[p][:, 0:nw], hT[:, kc, ti * 128:(ti + 1) * 128], wU[:, kc, n0:n0 + nw],
                                    start=(kc == 0), stop=(kc == KC - 1)), r=[HT[ti], WU], w=[PSU[p]])
                            P.op("act", lambda e, p=p, ti=ti, n0=n0, nw=nw: e.activation(
                                out=utm[:, ti, n0:n0 + nw], in_=psu[p][:, 0:nw], func=AF.Copy), r=[PSU[p]], w=[UT[ti]])
                    for (seg, t0, nts) in segs:
                        for fc in range(PGC):
                            for tb in range(0, nts, 4):
                                ntb = min(4, nts - tb)
                                p = cr % 2
                                cr += 1
                                for tq in range(ntb):
                                    tl = tb + tq
                                    dis = [di for di in (-1, 0, 1) if 0 <= tl + di < nts]
                                    for ii, di in enumerate(dis):
                                        bi = pool_maps[(seg, j, tl, di)]
                                        first = (tq == 0 and ii == 0)
                                        last = (tq == ntb - 1 and ii == len(dis) - 1)
                                        P.op("pe", lambda e, p=p, tq=tq, fc=fc, bi=bi, sti=t0 + tl + di, first=first, last=last:
                                             e.matmul(psr[p][:, tq * 128:(tq + 1) * 128],
                                                      utm[:, sti, fc * 128:(fc + 1) * 128], pcb[:, bi, :],
                                                      start=first, stop=last),
                                             r=[UT[t0 + tl + di], B_pc], w=[PSR[p]])
                                tg0 = (t0 + tb) * 128
                                P.op("dve", lambda e, p=p, fc=fc, tg0=tg0, ntb=ntb: e.tensor_copy(
                                    rT[:, fc, tg0:tg0 + ntb * 128], psr[p][:, 0:ntb * 128]),
                                    r=[PSR[p]], w=[RT[t0 + tb + q] for q in range(ntb)])
                    for fc in range(PGC):
                        gch_i = gcount % 2
                        gcount += 1
                        chunk = j * PGC + fc
                        for (seg, t0, nts) in segs:
                            for tb in range(0, nts, 4):
                                ntb = min(4, nts - tb)
                                nw = ntb * 128
                                tg0 = (t0 + tb) * 128
                                p = cq % 2
                                cq += 1
                                tbufs = [t0 + tb + q for q in range(ntb)]
                                for kc2 in range(PGC):
                                    P.op("pe", lambda e, p=p, kc2=kc2, fc=fc, tg0=tg0, nw=nw: e.matmul(
                                        psq[p][:, 0:nw], wP[:, kc2, fc * 128:(fc + 1) * 128], rT[:, kc2, tg0:tg0 + nw],
                                        start=(kc2 == 0), stop=(kc2 == PGC - 1)),
                                        r=[WP] + [RT[q] for q in tbufs], w=[PSQ[p]])
                                for kc in range(KC):
                                    P.op("pe", lambda e, p=p, kc=kc, fc=fc, tg0=tg0, nw=nw: e.matmul(
                                        psz[p][:, 0:nw], wZ[:, kc, fc * 128:(fc + 1) * 128], hT[:, kc, tg0:tg0 + nw],
                                        start=(kc == 0), stop=(kc == KC - 1)),
                                        r=[WZ] + [HT[q] for q in tbufs], w=[PSZ[p]])
                                P.op("act", lambda e, p=p, nw=nw: e.activation(out=sz[p][:, 0:nw], in_=psz[p][:, 0:nw],
                                                                                func=AF.Silu), r=[PSZ[p]], w=[SZ[p]])
                                P.op("dve", lambda e, p=p, nw=nw, tg0=tg0, gch_i=gch_i, chunk=chunk: e.scalar_tensor_tensor(
                                    out=gch[gch_i][:, tg0:tg0 + nw], in0=psq[p][:, 0:nw], scalar=lsc[:, chunk:chunk + 1],
                                    in1=sz[p][:, 0:nw], op0=ALU.mult, op1=ALU.mult),
                                    r=[PSQ[p], SZ[p], B_pc], w=[GCH[gch_i]])
                        tlo = 0 if ctx_active else NT_C * 128
                        P.dma("sp", gscr[chunk, :, tlo:T], gch[gch_i][:, tlo:T], r=[GCH[gch_i]], w=[GS[chunk]])

        even_mixer = make_even_mixer(cfg, nc, P, dict(
            hT=hT, HT=HT, ident_b=ident_b, B_const=B_const, gscr=gscr, GS=GS, load_w=load_w,
            wA_in=wA_in, wB_in=wB_in, lam_in=lam_in, subln_in=subln_in, hgn_in=hgn_in, lbl_in=lbl_in,
            cos_in=cos_in, sin_in=sin_in, hgM_in=hgM_in, hgmask_in=hgmask_in, hgw_in=hgw_in, hgcm_in=hgcm_in))

        for l in layers:
            even = (l % 2 == 0)
            need_ctx = l < cfg.DEPTH - 1
            ctx_active = even or need_ctx
            mod_phase(l)
            for s in range(NB):
                phase_n(l, s, ctx_active)
                if even:
                    even_mixer(l // 2, l, s, need_ctx)
                    phase_o(l, s, wEO_in[l // 2], need_ctx)
                else:
                    odd_mixer(l // 2, s, ctx_active and need_ctx)
                    phase_o(l, s, wOO_in[l // 2], need_ctx)
        P.final_wait("sp")
        build.stats = (P.nops, P.nwait, dict(P.cnt), dict(P.si))
    return nc


def make_even_mixer(cfg, nc, P, G):
    D, KC, T, NT, NT_C, NT_L, R, NB = cfg.D, cfg.KC, cfg.T, cfg.NT, cfg.NT_C, cfg.NT_L, cfg.R, cfg.NB
    hT, HT, ident_b, B_const, gscr, GS, load_w = (G[k] for k in ("hT", "HT", "ident_b", "B_const", "gscr", "GS", "load_w"))
    TC = cfg.T_CTX

    def rstd_ops(stt, ST, c_ss, c_out, n):
        P.op("dve", lambda e: e.tensor_scalar(out=stt[:, c_out:c_out + 1], in0=stt[:, c_ss:c_ss + 1], scalar1=1.0 / n,
                                              scalar2=EPS, op0=ALU.mult, op1=ALU.add), r=[ST], w=[ST])
        P.op("act", lambda e: e.activation(out=stt[:, c_out:c_out + 1], in_=stt[:, c_out:c_out + 1], func=AF.Ln),
             r=[ST], w=[ST])
        P.op("act", lambda e: e.activation(out=stt[:, c_out:c_out + 1], in_=stt[:, c_out:c_out + 1], func=AF.Exp, scale=-0.5),
             r=[ST], w=[ST])

    def attention_part(e_idx, l, s, need_ctx):
        lam_init = 0.8 - 0.6 * math.exp(-0.3 * l)
        with P.scope() as sc:
            stg = [sc.sb("stg", [128, 4 * 512], F32) for _ in range(2)]
            SB = [Buf("stg0"), Buf("stg1")]
            wbf = [sc.sb("wbfA", [128, KC, 512], BF16) for _ in range(2)]
            WB = [Buf("wbA0"), Buf("wbA1")]
            cosT = sc.sb("cosT", [128, NT_L, 128], F32)
            sinT = sc.sb("sinT", [128, NT_L, 128], F32)
            B_rope = Buf("rope")
            lamt = sc.sb("lamt", [128, 4, 64], F32)
            lamw = sc.sb("lamw", [128, 2, 64], F32)
            lams = sc.sb("lams", [128, 8], F32)
            B_lam = Buf("lam")
            Gt = sc.sb("Gt", [128, 128], F32)
            B_G = Buf("G")
            QKT = sc.sb("QKT", [128, 2, T], BF16)
            QK = [Buf(f"qk{i}") for i in range(NT)]
            Vaug = sc.sb("Vaug", [128, NT, 130], BF16)
            VA = [Buf(f"va{i}") for i in range(NT)]
            GGt = sc.sb("GGt", [128, NT, 128], F32)
            GGB = [Buf(f"gg{i}") for i in range(NT)]
            qktm = [sc.sb("qktm", [128, 256], BF16) for _ in range(2)]
            QKTM = [Buf("qktm0"), Buf("qktm1")]
            t1 = [sc.sb("t1", [128, 256], F32) for _ in range(2)]
            t2 = [sc.sb("t2", [128, 256], F32) for _ in range(2)]
            T1 = [Buf("t10"), Buf("t11")]
            T2 = [Buf("t20"), Buf("t21")]
            eg = [sc.sb("eg", [128, 128], F32) for _ in range(2)]
            EG = [Buf("eg0"), Buf("eg1")]
            Et = [sc.sb("Et", [128, 512], BF16) for _ in range(3)]
            ET = [Buf(f"et{i}") for i in range(3)]
            stt4 = [sc.sb("stt4", [128, 8], F32) for _ in range(4)]
            ST4 = [Buf(f"st4{i}") for i in range(4)]
            o14 = [sc.sb("o14", [128, 128], F32) for _ in range(4)]
            O14 = [Buf(f"o14{i}") for i in range(4)]
            o24 = [sc.sb("o24", [128, 128], F32) for _ in range(4)]
            O24 = [Buf(f"o24{i}") for i in range(4)]
            junk4 = sc.sb("junk4", [128, 4, 128], BF16)
            JK4 = [Buf(f"jk4{i}") for i in range(4)]
            atm4 = [sc.sb("atm4", [128, 128], BF16) for _ in range(4)]
            ATM4 = [Buf(f"atm4{i}") for i in range(4)]
            aT = [sc.sb("aT", [128, T], BF16) for _ in range(2)]
            AT = [Buf("aT0"), Buf("aT1")]
            psp = [sc.ps("psp", [128, 512], F32) for _ in range(2)]
            PSP = [Buf("psp0"), Buf("psp1")]
            pss = [sc.ps("pss", [128, 512], F32) for _ in range(2)]
            PSS = [Buf("pss0"), Buf("pss1")]
            pso = [sc.ps("pso", [128, 512], F32) for _ in range(3)]
            PSO = [Buf("pso0"), Buf("pso1"), Buf("pso2")]
            pst = sc.ps("pstA", [128, 1024], BF16)
            PSTq = Buf("pstq")
            PSTa = PSTq

            P.dma("sp", cosT[:], G["cos_in"][:, :, :], w=[B_rope])
            P.dma("sp", sinT[:], G["sin_in"][:, :, :], w=[B_rope])
            P.dma("sp", lamt[:], G["lam_in"][:, e_idx, :, :], w=[B_lam])
            P.dma("sp", Gt[:], G["subln_in"][:, e_idx, :], w=[B_G])
            P.op("dve", lambda e: e.tensor_scalar(out=Gt[:], in0=Gt[:], scalar1=(1.0 - lam_init), scalar2=None,
                                                  op0=ALU.mult), r=[B_G], w=[B_G])
            P.op("dve", lambda e: e.tensor_tensor(out=lamw[:, 0, :], in0=lamt[:, 0, :], in1=lamt[:, 1, :], op=ALU.mult),
                 r=[B_lam], w=[B_lam])
            P.op("dve", lambda e: e.tensor_tensor(out=lamw[:, 1, :], in0=lamt[:, 2, :], in1=lamt[:, 3, :], op=ALU.mult),
                 r=[B_lam], w=[B_lam])
            P.op("dve", lambda e: e.reduce_sum(out=lams[:, 0:2], in_=lamw[:, :, :], axis=AX.X), r=[B_lam], w=[B_lam])
            P.op("act", lambda e: e.activation(out=lams[:, 2:4], in_=lams[:, 0:2], func=AF.Exp), r=[B_lam], w=[B_lam])
            P.op("dve", lambda e: e.tensor_tensor(out=lams[:, 4:5], in0=lams[:, 2:3], in1=lams[:, 3:4], op=ALU.subtract),
                 r=[B_lam], w=[B_lam])
            P.op("dve", lambda e: e.tensor_scalar(out=lams[:, 5:6], in0=lams[:, 4:5], scalar1=lam_init, scalar2=-1.0,
                                                  op0=ALU.add, op1=ALU.mult), r=[B_lam], w=[B_lam])
            P.op("pool", lambda e: e.memset(Vaug[:, :, 128:130], 1.0), w=VA)

            cnt = {"p": 0, "s": 0, "e": 0, "ep": 0}
            for h in range(cfg.DA_HEADS):
                hp = h % 2
                load_w(stg, SB, wbf[hp][:], G["wA_in"][e_idx, h], KC, 512, WB[hp])
                for ti in range(NT):
                    p = cnt["p"] % 2
                    cnt["p"] += 1
                    for kc in range(KC):
                        P.op("pe", lambda e, p=p, kc=kc, ti=ti, hp=hp: e.matmul(
                            psp[p][:, :], hT[:, kc, ti * 128:(ti + 1) * 128], wbf[hp][:, kc, :],
                            start=(kc == 0), stop=(kc == KC - 1)), r=[HT[ti], WB[hp]], w=[PSP[p]])
                    if ti >= NT_C:
                        tl = ti - NT_C
                        for c0 in (0, 128):
                            X = psp[p][:, c0:c0 + 128]
                            Xv = X.rearrange("p (a h i) -> p a h i", h=2, i=16)
                            Sv = sinT[:, tl, :].rearrange("p (a h i) -> p a h i", h=2, i=16)
                            t2v = t2[p][:, c0:c0 + 128].rearrange("p (a h i) -> p a h i", h=2, i=16)
                            P.op("dve", lambda e, p=p, c0=c0, X=X, tl=tl: e.tensor_tensor(
                                out=t1[p][:, c0:c0 + 128], in0=X, in1=cosT[:, tl, :], op=ALU.mult),
                                r=[PSP[p], B_rope], w=[T1[p]])
                            P.op("dve", lambda e, Xv=Xv, Sv=Sv, t2v=t2v: e.tensor_tensor(
                                out=t2v[:, :, 0, :], in0=Xv[:, :, 1, :], in1=Sv[:, :, 0, :], op=ALU.mult),
                                r=[PSP[p], B_rope], w=[T2[p]])
                            P.op("dve", lambda e, Xv=Xv, Sv=Sv, t2v=t2v: e.tensor_tensor(
                                out=t2v[:, :, 1, :], in0=Xv[:, :, 0, :], in1=Sv[:, :, 1, :], op=ALU.mult),
                                r=[PSP[p], B_rope], w=[T2[p]])
                        P.op("pool", lambda e, p=p: e.tensor_tensor(out=qktm[p][:], in0=t1[p][:], in1=t2[p][:], op=ALU.add),
                             r=[T1[p], T2[p]], w=[QKTM[p]])
                    else:
                        P.op("act", lambda e, p=p: e.activation(out=qktm[p][:], in_=psp[p][:, 0:256], func=AF.Copy),
                             r=[PSP[p]], w=[QKTM[p]])
                    for c in range(2):
                        P.op("pe", lambda e, p=p, c=c: e.transpose(out=pst[:, c * 128:(c + 1) * 128],
                                                                    in_=qktm[p][:, c * 128:(c + 1) * 128], identity=ident_b[:]),
                             r=[QKTM[p], B_const], w=[PSTq])
                    P.op("act", lambda e, ti=ti: e.activation(
                        out=QKT[:, :, ti * 128:(ti + 1) * 128], in_=pst[:, 0:256].rearrange("p (c t) -> p c t", c=2),
                        func=AF.Copy), r=[PSTq], w=[QK[ti]])
                    P.op("act", lambda e, p=p, ti=ti: e.activation(out=Vaug[:, ti, 0:128], in_=psp[p][:, 256:384], func=AF.Copy),
                         r=[PSP[p]], w=[VA[ti]])
                    P.op("act", lambda e, p=p: e.activation(out=eg[p][:], in_=psp[p][:, 384:512], func=AF.Exp, scale=-1.0),
                         r=[PSP[p]], w=[EG[p]])
                    P.op("act", lambda e, p=p: e.activation(out=eg[p][:], in_=eg[p][:], func=AF.Ln, bias=1.0), r=[EG[p]], w=[EG[p]])
                    P.op("act", lambda e, p=p: e.activation(out=eg[p][:], in_=eg[p][:], func=AF.Exp, scale=-1.0), r=[EG[p]], w=[EG[p]])
                    P.op("pool", lambda e, p=p: e.tensor_tensor(out=eg[p][:], in0=eg[p][:], in1=Gt[:], op=ALU.mult),
                         r=[EG[p], B_G], w=[EG[p]])
                    P.op("dve", lambda e, p=p, ti=ti: e.tensor_tensor(out=GGt[:, ti, :], in0=psp[p][:, 384:512], in1=eg[p][:],
                                                                      op=ALU.mult), r=[PSP[p], EG[p]], w=[GGB[ti]])

                blocks = []
                if need_ctx:
                    blocks.append((0, TC, list(range(NT_C))))
                for qb in range(0, cfg.T_LAT, 512):
                    blocks.append((TC + qb, min(512, cfg.T_LAT - qb), list(range(NT))))
                for (q0, nq, kts) in blocks:
                    nqs = nq // 128
                    nacc = 2 * nqs
                    nbank = (nacc + 2) // 3
                    firsts = {b: True for b in range(nbank)}
                    qtiles = [q0 // 128 + i for i in range(nqs)]
                    for ki, kt in enumerate(kts):
                        for m in range(2):
                            p = cnt["s"] % 2
                            cnt["s"] += 1
                            ei = cnt["e"] % 3
                            cnt["e"] += 1
                            P.op("pe", lambda e, p=p, m=m, kt=kt, q0=q0, nq=nq: e.matmul(
                                pss[p][:, 0:nq], QKT[m * 64:(m + 1) * 64, 1, kt * 128:(kt + 1) * 128],
                                QKT[m * 64:(m + 1) * 64, 0, q0:q0 + nq], start=True, stop=True),
                                r=[QK[kt]] + [QK[q] for q in qtiles], w=[PSS[p]])
                            P.op("act", lambda e, p=p, ei=ei, nq=nq: e.activation(
                                out=Et[ei][:, 0:nq], in_=pss[p][:, 0:nq], func=AF.Exp, scale=0.125),
                                r=[PSS[p]], w=[ET[ei]])
                            for qs in range(nqs):
                                idx = m * nqs + qs
                                b, slot = idx // 3, idx % 3
                                is_first = firsts[b]
                                firsts[b] = False
                                last_idx_in_bank = min(nacc - 1, b * 3 + 2)
                                is_last = (ki == len(kts) - 1) and (idx == last_idx_in_bank)
                                P.op("pe", lambda e, ei=ei, qs=qs, kt=kt, b=b, slot=slot, is_first=is_first, is_last=is_last:
                                     e.matmul(pso[b][:, slot * 129:slot * 129 + 129], Et[ei][:, qs * 128:(qs + 1) * 128],
                                              Vaug[:, kt, 0:129], start=is_first, stop=is_last),
                                     r=[ET[ei], VA[kt]], w=[PSO[b]])
                    QS_ = list(range(nqs))
                    tis = [q0 // 128 + qs for qs in QS_]
                    loc = []
                    for qs in QS_:
                        i0_, i1_ = qs, nqs + qs
                        loc.append((i0_ // 3, (i0_ % 3) * 129, i1_ // 3, (i1_ % 3) * 129))
                    for qs in QS_:
                        b0, s0, b1, s1 = loc[qs]
                        P.op("dve", lambda e, qs=qs, b0=b0, s0=s0: e.reciprocal(out=stt4[qs][:, 0:1], in_=pso[b0][:, s0 + 128:s0 + 129]),
                             r=[PSO[b0]], w=[ST4[qs]])
                        P.op("dve", lambda e, qs=qs, b1=b1, s1=s1: e.reciprocal(out=stt4[qs][:, 1:2], in_=pso[b1][:, s1 + 128:s1 + 129]),
                             r=[PSO[b1]], w=[ST4[qs]])
                    for qs in QS_:
                        P.op("dve", lambda e, qs=qs: e.tensor_tensor(out=stt4[qs][:, 2:3], in0=stt4[qs][:, 1:2], in1=lams[:, 5:6],
                                                                     op=ALU.mult), r=[ST4[qs], B_lam], w=[ST4[qs]])
                    for qs in QS_:
                        b0, s0, b1, s1 = loc[qs]
                        P.op("dve", lambda e, qs=qs, b0=b0, s0=s0: e.tensor_scalar(
                            out=o14[qs][:], in0=pso[b0][:, s0:s0 + 128], scalar1=stt4[qs][:, 0:1], scalar2=None, op0=ALU.mult),
                            r=[PSO[b0], ST4[qs]], w=[O14[qs]])
                    for qs in QS_:
                        b0, s0, b1, s1 = loc[qs]
                        P.op("dve", lambda e, qs=qs, b1=b1, s1=s1: e.scalar_tensor_tensor(
                            out=o24[qs][:], in0=pso[b1][:, s1:s1 + 128], scalar=stt4[qs][:, 2:3], in1=o14[qs][:],
                            op0=ALU.mult, op1=ALU.add), r=[PSO[b1], ST4[qs], O14[qs]], w=[O24[qs]])
                    for qs in QS_:
                        P.op("act", lambda e, qs=qs: e.activation(out=junk4[:, qs, :], in_=o24[qs][:], func=AF.Square,
                                                                  accum_out=stt4[qs][:, 3:4]), r=[O24[qs]], w=[JK4[qs], ST4[qs]])
                    for qs in QS_:
                        P.op("dve", lambda e, qs=qs: e.tensor_scalar(out=stt4[qs][:, 4:5], in0=stt4[qs][:, 3:4], scalar1=1.0 / 128,
                                                                     scalar2=EPS, op0=ALU.mult, op1=ALU.add), r=[ST4[qs]], w=[ST4[qs]])
                    for qs in QS_:
                        P.op("act", lambda e, qs=qs: e.activation(out=stt4[qs][:, 4:5], in_=stt4[qs][:, 4:5], func=AF.Ln),
                             r=[ST4[qs]], w=[ST4[qs]])
                    for qs in QS_:
                        P.op("act", lambda e, qs=qs: e.activation(out=stt4[qs][:, 4:5], in_=stt4[qs][:, 4:5], func=AF.Exp, scale=-0.5),
                             r=[ST4[qs]], w=[ST4[qs]])
                    for qs in QS_:
                        P.op("dve", lambda e, qs=qs: e.scalar_tensor_tensor(
                            out=atm4[qs][:], in0=o24[qs][:], scalar=stt4[qs][:, 4:5], in1=GGt[:, tis[qs], :],
                            op0=ALU.mult, op1=ALU.mult), r=[O24[qs], ST4[qs], GGB[tis[qs]]], w=[ATM4[qs]])
                    for qs in QS_:
                        P.op("pe", lambda e, qs=qs: e.transpose(out=pst[:, 256 + qs * 128:384 + qs * 128], in_=atm4[qs][:],
                                                                identity=ident_b[:]), r=[ATM4[qs], B_const], w=[PSTa])
                    P.op("act", lambda e: e.activation(out=aT[hp][:, q0:q0 + nq], in_=pst[:, 256:256 + nq], func=AF.Copy),
                         r=[PSTa], w=[AT[hp]])
                tlo = 0 if need_ctx else TC
                P.dma("sp", gscr[h, :, tlo:T], aT[hp][:, tlo:T], r=[AT[hp]], w=[GS[h]])

    def hgrn_part(e_idx, l, s, need_ctx):
        HW = cfg.HG_W
        with P.scope() as sc:
            stg = [sc.sb("stg", [128, 640], F32) for _ in range(2)]
            SB = [Buf("stg0"), Buf("stg1")]
            wbf = [sc.sb("wbfB", [128, KC, 640], BF16) for _ in range(2)]
            WB = [Buf("wbB0"), Buf("wbB1")]
            LBt = sc.sb("LBt", [128, 2, 128], F32)
            OML = sc.sb("OML", [128, 2, 128], F32)
            B_lb = Buf("lb")
            hgG = sc.sb("hgG", [128, 128], F32)
            M1 = sc.sb("M1", [128, 2, 128], F32)
            mask = sc.sb("mask", [128, 2, 128], F32)
            wcol = sc.sb("wcol", [128, 2, 12], F32)
            B_hc = Buf("hgc")
            qs_t = [sc.sb("qs", [128, 128], F32) for _ in range(2)]
            QS = [Buf("qs0"), Buf("qs1")]
            vv = sc.sb("vv", [128, NT, 128], BF16)
            VV = [Buf(f"vv{i}") for i in range(NT)]
            GGt = sc.sb("GGb", [128, NT, 128], BF16)
            GGB = [Buf(f"ggb{i}") for i in range(NT)]
            of = sc.sb("of", [128, NT, 128], F32)
            OF = [Buf(f"of{i}") for i in range(NT)]
            nsl = [2, NT]
            qkT = [sc.sb("qkTh", [128, nsl[d], 2, 128], BF16) for d in range(2)]
            ktm = [sc.sb("ktm", [128, nsl[d], 128], BF16) for d in range(2)]
            ecs = [sc.sb("ecs", [128, nsl[d], 12], F32) for d in range(2)]
            PREP = [[Buf(f"prep{d}_{i}") for i in range(nsl[d])] for d in range(2)]
            sg = [sc.sb("sg", [128, 128], F32) for _ in range(2)]
            SG = [Buf("sg0"), Buf("sg1")]
            ff = [sc.sb("ff", [128, 128], F32) for _ in range(2)]
            FF = [Buf("ff0"), Buf("ff1")]
            lf = [sc.sb("lf", [128, 128], F32) for _ in range(2)]
            LF = [Buf("lf0"), Buf("lf1")]
            kk = [sc.sb("kk", [128, 128], F32) for _ in range(2)]
            KK = [Buf("kk0"), Buf("kk1")]
            ed = [sc.sb("ed", [128, 2, 128], F32) for _ in range(2)]
            ED = [Buf("ed0"), Buf("ed1")]
            qtl = [sc.sb("qtl", [128, 128], BF16) for _ in range(2)]
            ee3 = [sc.sb("ee3", [128, 512], F32) for _ in range(2)]
            L3 = [sc.sb("L3", [128, 512], F32) for _ in range(2)]
            s3 = [sc.sb("s3", [128, 512], F32) for _ in range(2)]
            EE3 = [Buf("ee30"), Buf("ee31")]
            LL3 = [Buf("L30"), Buf("L31")]
            SS3 = [Buf("s30"), Buf("s31")]
            QTL = [Buf("qtl0"), Buf("qtl1")]
            eg = [sc.sb("egb", [128, 128], F32) for _ in range(2)]
            EG = [Buf("egb0"), Buf("egb1")]
            SBF = [[sc.sb("Sbf", [128, 128], BF16) for _ in range(2)] for _ in range(2)]
            SS = [Buf("S0"), Buf("S1")]
            SSB = [[Buf("Sb00"), Buf("Sb01")], [Buf("Sb10"), Buf("Sb11")]]
            qz = [sc.sb("qz", [128, 640], BF16) for _ in range(2)]
            QZ = [Buf("qz0"), Buf("qz1")]
            cm = sc.sb("cm", [128, 4], F32)
            cmb = sc.sb("cmb", [128, 4, 128], BF16)
            ones_t = sc.sb("ones_t", [128, 128], F32)
            vblk = [sc.sb("vblk", [128, 4, 128], BF16) for _ in range(2)]
            VB = [Buf("vb0"), Buf("vb1")]
            attm = [sc.sb("attm", [128, 128], BF16) for _ in range(2)]
            ATT = [Buf("att0"), Buf("att1")]
            tmpu4 = sc.sb("tmpu4", [128, 4, 128], F32)
            TU4 = Buf("tu4")
            S2 = [[sc.sb("S2", [128, 128], F32) for _ in range(2)] for _ in range(2)]
            SS2 = [[Buf("S200"), Buf("S201")], [Buf("S210"), Buf("S211")]]
            spar = [0, 0]
            osum = [sc.sb("osum", [128, 128], F32) for _ in range(2)]
            OS = [Buf("os0"), Buf("os1")]
            stt = [sc.sb("stb", [128, 8], F32) for _ in range(2)]
            ST = [Buf("stb0"), Buf("stb1")]
            junk = sc.sb("junkb", [128, 128], BF16)
            JK = Buf("junkb")
            btm = [sc.sb("btm", [128, 128], BF16) for _ in range(2)]
            BTM = [Buf("btm0"), Buf("btm1")]
            bT = [sc.sb("bT", [128, T], BF16)] * 2
            BT = [Buf("bT0")] * 2
            psp = [[sc.ps("pspb", [128, 512], F32) for _ in range(2)] for _ in range(2)]
            PSP = [[Buf("pspb00"), Buf("pspb01")], [Buf("pspb10"), Buf("pspb11")]]
            psd = sc.ps("psd", [128, 512], F32)
            PSD = Buf("psd")
            psa = sc.ps("psa", [128, 512], F32)
            PSA = Buf("psa")
            PSOt = PSA
            psu = sc.ps("psu", [128, 512], F32)
            PSU = Buf("psu")
            pst = sc.ps("pstB", [128, 1024], BF16)
            PSTq = Buf("pstqb")
            PSTb = PSTq

            P.op("pool", lambda e: e.memset(ones_t[:], 1.0), w=[B_hc])
            P.dma("sp", hgG[:], G["hgn_in"][:, e_idx, :], w=[B_hc])
            P.dma("sp", M1[:], G["hgM_in"].rearrange("d s t -> s d t"), w=[B_hc])
            P.dma("sp", mask[:], G["hgmask_in"].rearrange("d s t -> s d t"), w=[B_hc])
            P.dma("sp", wcol[:], G["hgw_in"].rearrange("d s c -> s d c"), w=[B_hc])
            P.dma("sp", cm[:], G["hgcm_in"][:, :], w=[B_hc])
            for c in range(4):
                P.op("dve", lambda e, c=c: e.tensor_scalar(out=cmb[:, c, :], in0=ones_t[:], scalar1=cm[:, c:c + 1], scalar2=None,
                                                           op0=ALU.mult), r=[B_hc], w=[B_hc])
            for i in range(2):
                P.op("pool", lambda e, i=i: e.memset(qz[i][:], 0.0), w=[QZ[i]])
            def load_lb(h):
                if e_idx == 0:
                    if h == 0:
                        P.op("pool", lambda e: e.memset(LBt[:], 0.0), w=[B_lb])
                        P.op("pool", lambda e: e.memset(OML[:], 1.0), w=[B_lb])
                    return
                P.dma("sp", LBt[:], G["lbl_in"][:, 0, :, h * 128:(h + 1) * 128], w=[B_lb])
                P.dma("sp", OML[:], G["lbl_in"][:, 1, :, h * 128:(h + 1) * 128], w=[B_lb])
                P.op("dve", lambda e: e.tensor_tensor(out=LBt[:], in0=LBt[:], in1=OML[:], op=ALU.subtract), r=[B_lb], w=[B_lb])
                P.op("act", lambda e: e.activation(out=LBt[:], in_=LBt[:], func=AF.Exp), r=[B_lb], w=[B_lb])
                P.op("dve", lambda e: e.tensor_scalar(out=LBt[:], in0=LBt[:], scalar1=1.0, scalar2=None, op0=ALU.add),
                     r=[B_lb], w=[B_lb])
                P.op("dve", lambda e: e.reciprocal(out=LBt[:], in_=LBt[:]), r=[B_lb], w=[B_lb])
                P.op("dve", lambda e: e.tensor_scalar(out=OML[:], in0=LBt[:], scalar1=-1.0, scalar2=1.0, op0=ALU.mult,
                                                      op1=ALU.add), r=[B_lb], w=[B_lb])

            cnt = {"p": 0, "t": 0, "st": 0, "fin": 0}

            sgn = -1.0 if e_idx == 0 else 1.0

            def tile_common(h, hp, ti, p):
                P.op("act", lambda e: e.activation(out=ee3[p][:, 0:384], in_=psp[p][0][:, 0:384], func=AF.Exp, scale=-1.0),
                     r=[PSP[p][0]], w=[EE3[p]])
                P.op("act", lambda e: e.activation(out=ee3[p][:, 384:512], in_=psp[p][1][:, 0:128], func=AF.Exp, scale=-1.0),
                     r=[PSP[p][1]], w=[EE3[p]])
                P.op("act", lambda e: e.activation(out=L3[p][:], in_=ee3[p][:], func=AF.Ln, bias=1.0), r=[EE3[p]], w=[LL3[p]])
                P.op("act", lambda e: e.activation(out=s3[p][:], in_=L3[p][:], func=AF.Exp, scale=-1.0), r=[LL3[p]], w=[SS3[p]])
                P.op("dve", lambda e: e.tensor_tensor(out=qs_t[p][:], in0=psp[p][0][:, 0:128], in1=s3[p][:, 0:128], op=ALU.mult),
                     r=[PSP[p][0], SS3[p]], w=[QS[p]])
                P.op("act", lambda e: e.activation(out=vv[:, ti, :], in_=psp[p][0][:, 384:512], func=AF.Copy),
                     r=[PSP[p][0]], w=[VV[ti]])
                P.op("pool", lambda e: e.tensor_tensor(out=eg[p][:], in0=s3[p][:, 384:512], in1=hgG[:], op=ALU.mult),
                     r=[SS3[p], B_hc], w=[EG[p]])
                P.op("dve", lambda e: e.tensor_tensor(out=GGt[:, ti, :], in0=psp[p][1][:, 0:128], in1=eg[p][:], op=ALU.mult),
                     r=[PSP[p][1], EG[p]], w=[GGB[ti]])

            def prep(h, hp, ti, p, d, slot):
                k = cnt["t"] % 2
                cnt["t"] += 1
                c0 = 128 + d * 128
                if e_idx == 0:
                    lfa = L3[p][:, c0:c0 + 128]
                    LFB = LL3[p]
                    P.op("pool", lambda e: e.tensor_tensor(out=kk[k][:], in0=ee3[p][:, c0:c0 + 128], in1=s3[p][:, c0:c0 + 128],
                                                           op=ALU.mult), r=[EE3[p], SS3[p]], w=[KK[k]])
                else:
                    P.op("pool", lambda e: e.tensor_tensor(out=sg[k][:], in0=s3[p][:, c0:c0 + 128], in1=OML[:, d, :], op=ALU.mult),
                         r=[SS3[p], B_lb], w=[SG[k]])
                    P.op("pool", lambda e: e.tensor_tensor(out=ff[k][:], in0=sg[k][:], in1=LBt[:, d, :], op=ALU.add),
                         r=[SG[k], B_lb], w=[FF[k]])
                    P.op("pool", lambda e: e.tensor_tensor(out=kk[k][:], in0=OML[:, d, :], in1=sg[k][:], op=ALU.subtract),
                         r=[SG[k], B_lb], w=[KK[k]])
                    P.op("act", lambda e: e.activation(out=lf[k][:], in_=ff[k][:], func=AF.Ln), r=[FF[k]], w=[LF[k]])
                    lfa = lf[k][:]
                    LFB = LF[k]
                P.op("pe", lambda e: e.matmul(psd[:, 0:128], M1[:, d, :], lfa, start=True, stop=True),
                     r=[B_hc, LFB], w=[PSD])
                P.op("pe", lambda e: e.matmul(psd[:, 128:140], lfa, wcol[:, d, :], start=True, stop=True),
                     r=[B_hc, LFB], w=[PSD])
                P.op("act", lambda e: e.activation(out=ed[k][:, 0, :], in_=psd[:, 0:128], func=AF.Exp, scale=sgn), r=[PSD], w=[ED[k]])
                P.op("act", lambda e: e.activation(out=ed[k][:, 1, :], in_=psd[:, 0:128], func=AF.Exp, scale=-sgn),
                     r=[PSD], w=[ED[k]])
                P.op("act", lambda e: e.activation(out=ecs[d][:, slot, :], in_=psd[:, 128:140], func=AF.Exp, scale=sgn),
                     r=[PSD], w=[PREP[d][slot]])
                P.op("pool", lambda e: e.tensor_tensor(out=qtl[k][:], in0=qs_t[p][:], in1=ed[k][:, 0, :], op=ALU.mult),
                     r=[QS[p], ED[k]], w=[QTL[k]])
                P.op("pool", lambda e: e.tensor_tensor(out=ktm[d][:, slot, :], in0=kk[k][:], in1=ed[k][:, 1, :], op=ALU.mult),
                     r=[KK[k], ED[k]], w=[PREP[d][slot]])
                P.op("pe", lambda e: e.transpose(out=pst[:, 0:128], in_=qtl[k][:], identity=ident_b[:]),
                     r=[QTL[k], B_const], w=[PSTq])
                P.op("pe", lambda e: e.transpose(out=pst[:, 128:256], in_=ktm[d][:, slot, :], identity=ident_b[:]),
                     r=[PREP[d][slot], B_const], w=[PSTq])
                P.op("act", lambda e: e.activation(out=qkT[d][:, slot, :, :],
                                                   in_=pst[:, 0:256].rearrange("p (c t) -> p c t", c=2), func=AF.Copy),
                     r=[PSTq], w=[PREP[d][slot]])

            def step(h, hp, ti, d, slot):
                a = cnt["st"] % 2
                cnt["st"] += 1
                P.op("pe", lambda e: e.matmul(psa[:, 0:128], qkT[d][:, slot, 1, :], qkT[d][:, slot, 0, :], start=True, stop=True),
                     r=[PREP[d][slot]], w=[PSA])
                P.op("dve", lambda e: e.tensor_tensor(out=attm[a][:], in0=psa[:, 0:128], in1=mask[:, d, :], op=ALU.mult),
                     r=[PSA, B_hc], w=[ATT[a]])
                P.op("dve", lambda e: e.tensor_tensor(out=vblk[a][:], in0=vv[:, ti, :].unsqueeze(1).broadcast_to([128, 4, 128]),
                                                      in1=cmb[:], op=ALU.mult), r=[VV[ti], B_hc], w=[VB[a]])
                P.op("pe", lambda e: e.matmul(psu[:, 0:512], ktm[d][:, slot, :], vblk[a][:, :, :].rearrange("p c v -> p (c v)"),
                                              start=True, stop=True), r=[PREP[d][slot], VB[a]], w=[PSU])
                P.op("pe", lambda e: e.matmul(psa[:, 128:256], attm[a][:], vv[:, ti, :], start=True, stop=False),
                     r=[ATT[a], VV[ti]], w=[PSOt])
                P.op("pool", lambda e: e.tensor_copy(
                    qz[a][:, 0:640].rearrange("p (c x) -> p c x", x=160)[:, :, 0:32],
                    qkT[d][:, slot, 0, :].rearrange("p (c i) -> p c i", i=32)), r=[PREP[d][slot]], w=[QZ[a]])
                order = range(4) if d == 0 else range(3, -1, -1)
                for c in range(4):
                    P.op("dve", lambda e, c=c: e.tensor_scalar(out=tmpu4[:, c, :], in0=psu[:, c * 128:(c + 1) * 128],
                                                               scalar1=ecs[d][:, slot, 3 * c + 1:3 * c + 2], scalar2=None,
                                                               op0=ALU.mult), r=[PSU, PREP[d][slot]], w=[TU4])
                for n, c in enumerate(order):
                    sb = n % 2
                    cur = spar[d]
                    nxt = 1 - cur
                    P.op("act", lambda e, c=c, sb=sb, cur=cur: e.activation(out=SBF[d][sb][:], in_=S2[d][cur][:], func=AF.Identity,
                                                                             scale=ecs[d][:, slot, 3 * c + 2:3 * c + 3]),
                         r=[SS2[d][cur], PREP[d][slot]], w=[SSB[d][sb]])
                    P.op("pe", lambda e, c=c, sb=sb, n=n: e.matmul(psa[:, 128:256], qz[a][:, c * 128:(c + 1) * 128],
                                                                    SBF[d][sb][:], start=False, stop=(n == 3)),
                         r=[QZ[a], SSB[d][sb]], w=[PSOt])
                    P.op("dve", lambda e, c=c, cur=cur, nxt=nxt: e.scalar_tensor_tensor(
                        out=S2[d][nxt][:], in0=S2[d][cur][:], scalar=ecs[d][:, slot, 3 * c:3 * c + 1], in1=tmpu4[:, c, :],
                        op0=ALU.mult, op1=ALU.add), r=[SS2[d][cur], TU4, PREP[d][slot]], w=[SS2[d][nxt]])
                    spar[d] = nxt
                if d == 0:
                    P.op("act", lambda e: e.activation(out=of[:, ti, :], in_=psa[:, 128:256], func=AF.Copy), r=[PSOt], w=[OF[ti]])
                else:
                    f = cnt["fin"] % 2
                    cnt["fin"] += 1
                    P.op("dve", lambda e: e.tensor_tensor(out=osum[f][:], in0=psa[:, 128:256], in1=of[:, ti, :], op=ALU.add),
                         r=[PSOt, OF[ti]], w=[OS[f]])
                    P.op("act", lambda e: e.activation(out=junk[:], in_=osum[f][:], func=AF.Square, accum_out=stt[f][:, 0:1]),
                         r=[OS[f]], w=[JK, ST[f]])
                    rstd_ops(stt[f], ST[f], 0, 1, 128)
                    P.op("dve", lambda e: e.scalar_tensor_tensor(out=btm[f][:], in0=osum[f][:], scalar=stt[f][:, 1:2],
                                                                 in1=GGt[:, ti, :], op0=ALU.mult, op1=ALU.mult),
                         r=[OS[f], ST[f], GGB[ti]], w=[BTM[f]])
                    P.op("pe", lambda e: e.transpose(out=pst[:, 256:384], in_=btm[f][:], identity=ident_b[:]),
                         r=[BTM[f], B_const], w=[PSTb])
                    P.op("act", lambda e: e.activation(out=bT[hp][:, ti * 128:(ti + 1) * 128], in_=pst[:, 256:384], func=AF.Copy),
                         r=[PSTb], w=[BT[hp]])

            for h in range(cfg.HG_HEADS):
                hp = h % 2
                load_w(stg, SB, wbf[hp][:], G["wB_in"][e_idx, h], KC, 640, WB[hp], kq=1)
                load_lb(h)
                P.op("pool", lambda e: e.memset(S2[0][spar[0]][:], 0.0), w=[SS2[0][spar[0]]])
                P.op("pool", lambda e: e.memset(S2[1][spar[1]][:], 0.0), w=[SS2[1][spar[1]]])
                for ti in range(NT):
                    p = cnt["p"] % 2
                    cnt["p"] += 1
                    for kc in range(KC):
                        P.op("pe", lambda e, p=p, kc=kc, ti=ti: e.matmul(
                            psp[p][0][:, :], hT[:, kc, ti * 128:(ti + 1) * 128], wbf[hp][:, kc, 0:512],
                            start=(kc == 0), stop=(kc == KC - 1)), r=[HT[ti], WB[hp]], w=[PSP[p][0]])
                    for kc in range(KC):
                        P.op("pe", lambda e, p=p, kc=kc, ti=ti: e.matmul(
                            psp[p][1][:, 0:128], hT[:, kc, ti * 128:(ti + 1) * 128], wbf[hp][:, kc, 512:640],
                            start=(kc == 0), stop=(kc == KC - 1)), r=[HT[ti], WB[hp]], w=[PSP[p][1]])
                    tile_common(h, hp, ti, p)
                    prep(h, hp, ti, p, 0, ti % 2)
                    prep(h, hp, ti, p, 1, ti)
                    step(h, hp, ti, 0, ti % 2)
                order_b = list(range(NT_C - 1, -1, -1)) + list(range(NT - 1, NT_C - 1, -1))
                for ti in order_b:
                    step(h, hp, ti, 1, ti)
                tlo = 0 if need_ctx else TC
                ch = cfg.DA_HEADS + h
                P.dma("sp", gscr[ch, :, tlo:T], bT[hp][:, tlo:T], r=[BT[hp]], w=[GS[ch]])

    def even_mixer(e_idx, l, s, need_ctx):
        attention_part(e_idx, l, s, need_ctx)
        hgrn_part(e_idx, l, s, need_ctx)

    return even_mixer


def prep_shared(cfg, inp):
    D, KC, R = cfg.D, cfg.KC, cfg.R
    f = lambda a: np.ascontiguousarray(a, dtype=np.float32)
    sh = {}
    sh["w_mod"] = f(inp["w_mod"])
    b_mod = np.asarray(inp["b_mod"], np.float32)
    bc = b_mod[:, :2 * D].reshape(cfg.DEPTH, 2 * KC, 128).transpose(2, 0, 1)
    sh["bmodc"] = f(np.repeat(bc[:, :, :, None], R, axis=3))
    gp = np.asarray(inp["g_pre"], np.float32).reshape(cfg.DEPTH, KC, 128).transpose(2, 0, 1)
    sh["gprec"] = f(np.repeat(gp[:, :, :, None], R, axis=3))
    sh["bmodg"] = f(np.repeat(b_mod[None, :, 2 * D:], R, axis=0))
    sh["gpostr"] = f(np.repeat(np.asarray(inp["g_post"], np.float32)[None], R, axis=0))
    sel = np.zeros((R, R * 128), np.float32)
    for r in range(R):
        sel[r, r * 128:(r + 1) * 128] = 1.0
    sh["sel"] = sel
    sh["ident"] = np.eye(128, dtype=np.float32)
    W = np.asarray(inp["ev_w_in"], np.float32)
    NE = cfg.N_EVEN
    A = W[:, :, :4 * cfg.DA_W].reshape(NE, KC, 128, 4, cfg.DA_HEADS, 128)
    sh["wA"] = f(A.transpose(0, 4, 2, 1, 3, 5).reshape(NE, cfg.DA_HEADS, 128, KC, 512))
    B = W[:, :, 4 * cfg.DA_W:].reshape(NE, KC, 128, 5, cfg.HG_HEADS, 128)
    sh["wB"] = f(B.transpose(0, 4, 2, 1, 3, 5).reshape(NE, cfg.HG_HEADS, 128, KC, 640))
    sh["wEO"] = f(np.asarray(inp["ev_w_out"], np.float32).reshape(NE, KC, 128, D).transpose(0, 2, 1, 3))
    bcast = lambda a: f(np.broadcast_to(np.asarray(a, np.float32)[None], (128,) + tuple(np.shape(a))))
    sh["lamb"] = bcast(inp["ev_lambda"])
    sh["sublnb"] = bcast(inp["ev_subln_g"])
    sh["hgnb"] = bcast(inp["ev_hg_norm_g"])
    sh["lblb"] = bcast(inp["ev_hg_lb_logits"])
    sh["ropec"], sh["ropes"] = rope_tables(cfg)
    sh["hgM"], sh["hgmask"], sh["hgw"] = hgrn_consts()
    cm = np.zeros((128, 4), np.float32)
    for c in range(4):
        cm[c * 32:(c + 1) * 32, c] = 1.0
    sh["hgcm"] = cm
    NO = cfg.N_ODD
    PG, PW = cfg.PG, cfg.D
    Wo = np.asarray(inp["od_w_in"], np.float32)
    U = Wo[:, :, :PW].reshape(NO, KC, 128, 4, PG)
    Z = Wo[:, :, PW:].reshape(NO, KC, 128, 4, PG)
    UZ = np.concatenate([U, Z], axis=4)
    sh["wOD"] = f(UZ.transpose(0, 3, 2, 1, 4))
    WP = np.asarray(inp["od_w_pool"], np.float32).reshape(NO, 4, cfg.PGC, 128, PG)
    sh["wPL"] = f(WP.transpose(0, 1, 3, 2, 4))
    sh["wOO"] = f(np.asarray(inp["od_w_out"], np.float32).reshape(NO, KC, 128, D).transpose(0, 2, 1, 3))
    sh["odsc"] = f(np.asarray(inp["od_scale"], np.float32).reshape(NO, KC, 128).transpose(2, 0, 1))
    pc, maps = pool_consts(cfg)
    sh["poolc"] = f(pc)
    return sh, maps, pc.shape[0]


def prep_core(cfg, inp, sh, b0):
    NB, KC, R = cfg.NB, cfg.KC, cfg.R
    m = dict(sh)
    m["x"] = np.ascontiguousarray(inp["x"][b0:b0 + NB], dtype=np.float32)
    m["ctx"] = np.ascontiguousarray(inp["ctx"][b0:b0 + NB], dtype=np.float32)
    rows = np.concatenate([np.asarray(inp["c"], np.float32)[b0:b0 + NB], np.asarray(inp["c_ctx"], np.float32)[None]], 0)
    m["cT"] = np.ascontiguousarray(rows.reshape(R, KC, 128).transpose(2, 1, 0))
    return m


def kernel(**inputs):
    cfg = FULL
    ncores = 8
    sh, maps, nblk = prep_shared(cfg, inputs)
    nc = build(cfg, pool_maps=maps, n_pool_blocks=nblk)
    in_maps = [prep_core(cfg, inputs, sh, i * cfg.NB) for i in range(ncores)]
    res = run_bass_kernel_spmd(nc, in_maps, core_ids=list(range(ncores)))
    out = np.concatenate([np.asarray(r["y"], dtype=np.float32) for r in res.results], axis=0)
    return out
```

```python
import math
from contextlib import ExitStack, contextmanager
import numpy as np
import concourse.bass as bass
import concourse.mybir as mybir
from concourse.bass_utils import run_bass_kernel_spmd

F32 = mybir.dt.float32
BF16 = mybir.dt.bfloat16
AF = mybir.ActivationFunctionType
ALU = mybir.AluOpType
AX = mybir.AxisListType
EPS = 1e-6
ROPE_BASE = 10000.0
POOL_WINDOWS = (2, 4, 8, 16)
HC = 32


class Cfg:
    def __init__(s, D=2048, T_LAT=2048, T_CTX=256, GRID_W=64, DA_HEADS=8, HG_HEADS=8, NB=2, DEPTH=4):
        s.D, s.T_LAT, s.T_CTX, s.GRID_W = D, T_LAT, T_CTX, GRID_W
        s.DA_HEADS, s.HG_HEADS, s.NB, s.DEPTH = DA_HEADS, HG_HEADS, NB, DEPTH
        s.KC = D // 128
        s.NT_C = T_CTX // 128
        s.NT_L = T_LAT // 128
        s.NT = s.NT_C + s.NT_L
        s.T = T_CTX + T_LAT
        s.DA_W = DA_HEADS * 128
        s.HG_W = HG_HEADS * 128
        s.EVEN_IN = 4 * s.DA_W + 5 * s.HG_W
        s.EVEN_OUT = s.DA_W + s.HG_W
        assert s.EVEN_OUT == D
        s.PG = D // 4
        s.PGC = s.PG // 128
        s.R = NB + 1
        s.NBK = max(1, D // 512)
        s.BW = min(512, D)
        s.N_EVEN = (DEPTH + 1) // 2
        s.N_ODD = DEPTH // 2


FULL = Cfg()


class Ev:
    __slots__ = ("sem", "val", "ek")

    def __init__(s, sem, val, ek):
        s.sem, s.val, s.ek = sem, val, ek


class Buf:
    __slots__ = ("name", "w", "rs")

    def __init__(s, name):
        s.name, s.w, s.rs = name, None, {}


ENG = ("pe", "act", "dve", "pool", "sp")
SEM_LIMIT = 15000


class Prog:
    def __init__(s, nc, stack, n_dma=32):
        s.nc = nc
        s.stack = stack
        s.e = {"pe": nc.tensor, "act": nc.scalar, "dve": nc.vector, "pool": nc.gpsimd, "sp": nc.sync}
        nsem = {"pe": 12, "act": 5, "dve": 6, "pool": 5, "sp": 1}
        s.sems = {k: [stack.enter_context(nc.semaphore(f"s_{k}_{i}")) for i in range(n)] for k, n in nsem.items()}
        s.si = {k: 0 for k in ENG}
        s.cnt = {k: 0 for k in ENG}
        s.seen = {k: {} for k in ENG}
        s.dsems = [stack.enter_context(nc.semaphore(f"s_dma_{i}")) for i in range(n_dma)]
        s.dval = [0] * n_dma
        s.di = 0
        s.uid = 0
        s.nwait = 0
        s.nops = 0

    def _wait(s, ek, ev):
        seen = s.seen[ek]
        if seen.get(ev.sem, 0) >= ev.val:
            return
        s.e[ek].wait_ge(ev.sem, ev.val)
        seen[ev.sem] = ev.val
        s.nwait += 1

    def _deps(s, ek, r, w):
        for b in r:
            if b.w is not None and not (b.w.ek == ek and ek == "pe"):
                s._wait(ek, b.w)
        for b in w:
            if b.w is not None and not (b.w.ek == ek and ek == "pe"):
                s._wait(ek, b.w)
            for ev in b.rs.values():
                if ev.ek != ek:
                    s._wait(ek, ev)

    def _mark(s, ev, r, w):
        for b in r:
            b.rs[ev.sem] = ev
        for b in w:
            b.w = ev
            b.rs = {}

    def op(s, ek, fn, r=(), w=()):
        s._deps(ek, r, w)
        ins = fn(s.e[ek])
        if s.cnt[ek] >= SEM_LIMIT:
            s.si[ek] += 1
            s.cnt[ek] = 0
        s.cnt[ek] += 1
        sem = s.sems[ek][s.si[ek]]
        ins.then_inc(sem, 1)
        ev = Ev(sem, s.cnt[ek], ek)
        s._mark(ev, r, w)
        s.nops += 1
        return ev

    def dma(s, qk, out, in_, r=(), w=()):
        s._deps(qk, r, w)
        i = s.di
        s.di = (s.di + 1) % len(s.dsems)
        sem = s.dsems[i]
        if s.dval[i] > 0:
            s._wait(qk, Ev(sem, s.dval[i], None))
        s.dval[i] += 16
        s.e[qk].dma_start(out=out, in_=in_).then_inc(sem, 16)
        ev = Ev(sem, s.dval[i], None)
        s._mark(ev, r, w)
        s.nops += 1
        return ev

    def barrier(s, engines=ENG):
        evs = [Ev(s.sems[k][s.si[k]], s.cnt[k], k) for k in ENG if s.cnt[k] > 0]
        evs += [Ev(s.dsems[i], s.dval[i], None) for i in range(len(s.dsems)) if s.dval[i] > 0]
        for ek in engines:
            for ev in evs:
                if ev.ek != ek:
                    s._wait(ek, ev)

    def final_wait(s, ek="sp"):
        s.barrier(engines=(ek,))

    @contextmanager
    def scope(s):
        st = ExitStack()
        sc = Scope(s, st)
        try:
            yield sc
        finally:
            s.barrier()
            st.close()


class Scope:
    def __init__(s, P, st):
        s.P, s.st = P, st

    def sb(s, name, shape, dt):
        s.P.uid += 1
        return s.st.enter_context(s.P.nc.sbuf_tensor(f"{name}_{s.P.uid}", list(shape), dt))

    def ps(s, name, shape, dt):
        s.P.uid += 1
        return s.st.enter_context(s.P.nc.psum_tensor(f"{name}_{s.P.uid}", list(shape), dt))


def rope_tables(cfg):
    T = cfg.T_LAT
    t = np.arange(T)
    row, col = t // cfg.GRID_W, t % cfg.GRID_W
    nf = 16
    inv = ROPE_BASE ** (-np.arange(nf, dtype=np.float32) / nf)
    cosT = np.zeros((T, 64), np.float32)
    sinT = np.zeros((T, 64), np.float32)
    for blk, pos in ((0, row), (1, col)):
        ang = pos.astype(np.float32)[:, None] * inv[None, :]
        c, sn = np.cos(ang), np.sin(ang)
        cosT[:, blk * 32:blk * 32 + 16] = c
        cosT[:, blk * 32 + 16:blk * 32 + 32] = c
        sinT[:, blk * 32:blk * 32 + 16] = -sn
        sinT[:, blk * 32 + 16:blk * 32 + 32] = sn
    cos2 = np.concatenate([cosT, cosT], 1)
    sin2 = np.concatenate([sinT, sinT], 1)
    cos2 = cos2.reshape(cfg.NT_L, 128, 128).transpose(1, 0, 2).copy()
    sin2 = sin2.reshape(cfg.NT_L, 128, 128).transpose(1, 0, 2).copy()
    return cos2, sin2


def hgrn_consts():
    n = 128
    s = np.arange(n)[:, None]
    t = np.arange(n)[None, :]
    same = (s // HC) == (t // HC)
    mid_f = (t // HC) * HC + HC // 2 - 1
    mid_b = (t // HC) * HC + HC // 2
    M1f = (same & (s <= t)).astype(np.float32) - (same & (s <= mid_f)).astype(np.float32)
    M1b = (same & (s >= t)).astype(np.float32) - (same & (s >= mid_b)).astype(np.float32)
    maskf = (same & (s <= t)).astype(np.float32)
    maskb = (same & (s >= t)).astype(np.float32)
    nchunk = n // HC
    wf = np.zeros((n, nchunk * 3), np.float32)
    wb = np.zeros((n, nchunk * 3), np.float32)
    for c in range(nchunk):
        lo, hi = c * HC, (c + 1) * HC
        mf = lo + HC // 2 - 1
        mb = lo + HC // 2
        idx = np.arange(n)
        inck = (idx >= lo) & (idx < hi)
        wf[:, c * 3 + 0] = inck
        wf[:, c * 3 + 1] = inck & (idx > mf)
        wf[:, c * 3 + 2] = inck & (idx <= mf)
        wb[:, c * 3 + 0] = inck
        wb[:, c * 3 + 1] = inck & (idx < mb)
        wb[:, c * 3 + 2] = inck & (idx >= mb)
    return np.stack([M1f, M1b]), np.stack([maskf, maskb]), np.stack([wf, wb])


def pool_blocks(T):
    out = {}
    nt = T // 128
    for wi, w in enumerate(POOL_WINDOWS):
        t = np.arange(T)
        lo = np.clip(t - w // 2, 0, T)
        hi = np.clip(t + (w - w // 2), 0, T)
        M = np.zeros((T, T), np.float32)
        for tt in range(T):
            M[tt, lo[tt]:hi[tt]] = 1.0 / (hi[tt] - lo[tt])
            M[tt, tt] -= 1.0
        MT = M.T
        for ti in range(nt):
            for di in (-1, 0, 1):
                si = ti + di
                if 0 <= si < nt:
                    out[(wi, ti, di)] = MT[si * 128:(si + 1) * 128, ti * 128:(ti + 1) * 128].copy()
    return out


def pool_consts(cfg):
    uniq = []
    keys = {}
    maps = {}
    for seg, T in (("c", cfg.T_CTX), ("l", cfg.T_LAT)):
        blocks = pool_blocks(T)
        for k, blk in blocks.items():
            kb = blk.tobytes()
            if kb not in keys:
                keys[kb] = len(uniq)
                uniq.append(blk)
            maps[(seg,) + k] = keys[kb]
    return np.stack(uniq), maps


def build(cfg, layers=None, pool_maps=None, n_pool_blocks=0):
    layers = list(range(cfg.DEPTH)) if layers is None else layers
    D, KC, T, NT, NT_C, NT_L, R, NB = cfg.D, cfg.KC, cfg.T, cfg.NT, cfg.NT_C, cfg.NT_L, cfg.R, cfg.NB
    BW, NBK = cfg.BW, cfg.NBK
    nc = bass.Bass("TRN2", target_bir_lowering=False)

    def din(name, shape, dt=F32):
        return nc.dram_tensor(name, list(shape), dt, kind="ExternalInput").ap()

    x_in = din("x", [NB, cfg.T_LAT, D])
    ctx_in = din("ctx", [NB, cfg.T_CTX, D])
    cT_in = din("cT", [128, KC, R])
    wmod_in = din("w_mod", [cfg.DEPTH, D, 3 * D])
    bmodc_in = din("bmodc", [128, cfg.DEPTH, 2 * KC, R])
    gprec_in = din("gprec", [128, cfg.DEPTH, KC, R])
    bmodg_in = din("bmodg", [R, cfg.DEPTH, D])
    gpostr_in = din("gpostr", [R, cfg.DEPTH, D])
    sel_in = din("sel", [R, R * 128])
    ident_in = din("ident", [128, 128])
    wA_in = din("wA", [cfg.N_EVEN, cfg.DA_HEADS, 128, KC, 512])
    wB_in = din("wB", [cfg.N_EVEN, cfg.HG_HEADS, 128, KC, 640])
    wEO_in = din("wEO", [cfg.N_EVEN, 128, KC, D])
    lam_in = din("lamb", [128, cfg.N_EVEN, 4, 64])
    subln_in = din("sublnb", [128, cfg.N_EVEN, 128])
    hgn_in = din("hgnb", [128, cfg.N_EVEN, 128])
    lbl_in = din("lblb", [128, cfg.N_EVEN, 2, cfg.HG_W])
    cos_in = din("ropec", [128, NT_L, 128])
    sin_in = din("ropes", [128, NT_L, 128])
    hgM_in = din("hgM", [2, 128, 128])
    hgmask_in = din("hgmask", [2, 128, 128])
    hgw_in = din("hgw", [2, 128, 12])
    hgcm_in = din("hgcm", [128, 4])
    wOD_in = din("wOD", [cfg.N_ODD, 4, 128, KC, 2 * cfg.PG])
    wPL_in = din("wPL", [cfg.N_ODD, 4, 128, cfg.PGC, cfg.PG])
    wOO_in = din("wOO", [cfg.N_ODD, 128, KC, D])
    odsc_in = din("odsc", [128, cfg.N_ODD, KC])
    poolc_in = din("poolc", [max(1, n_pool_blocks), 128, 128])

    y_out = nc.dram_tensor("y", [NB, cfg.T_LAT, D], F32, kind="ExternalOutput").ap()
    cscr = nc.dram_tensor("cscr", [NB, cfg.T_CTX, D], F32, kind="Internal").ap()
    gscr = nc.dram_tensor("gscr", [KC, 128, T], BF16, kind="Internal").ap()

    with ExitStack() as stack:
        P = Prog(nc, stack)
        top = Scope(P, stack)

        hT = top.sb("hT", [128, KC, T], BF16)
        HT = [Buf(f"HT{i}") for i in range(NT)]
        ident_f = top.sb("identf", [128, 128], F32)
        ident_b = top.sb("identb", [128, 128], BF16)
        scT = top.sb("scT", [128, KC, R], F32)
        selT = top.sb("selT", [R, R * 128], F32)
        modc = top.sb("modc", [128, 2 * KC, R], F32)
        Acol = top.sb("Acol", [128, KC, R], F32)
        GTrow = top.sb("GTrow", [R, D], F32)
        bmodc = top.sb("bmodcs", [128, cfg.DEPTH, 2 * KC, R], F32)
        gprec = top.sb("gprecs", [128, cfg.DEPTH, KC, R], F32)
        B_const = Buf("const")
        B_scT = Buf("scT")
        B_modc = Buf("modc")
        B_Acol = Buf("Acol")
        B_GTrow = Buf("GTrow")
        DXL = [[Buf(f"dxl{s}_{i}") for i in range(NT_L)] for s in range(NB)]
        DXC = [[Buf(f"dxc{s}_{i}") for i in range(NT_C)] for s in range(NB)]
        GS = [Buf(f"gs{k}") for k in range(KC)]

        P.dma("sp", ident_f[:], ident_in[:, :], w=[B_const])
        P.dma("sp", scT[:], cT_in[:, :, :], w=[B_scT])
        P.dma("sp", selT[:], sel_in[:, :], w=[B_const])
        P.dma("sp", bmodc[:], bmodc_in[:, :, :, :], w=[B_const])
        P.dma("sp", gprec[:], gprec_in[:, :, :, :], w=[B_const])
        P.op("dve", lambda e: e.tensor_copy(ident_b[:], ident_f[:]), r=[B_const], w=[B_const])
        P.op("act", lambda e: e.activation(out=scT[:], in_=scT[:], func=AF.Silu), r=[B_scT], w=[B_scT])

        st_state = {"i": 0}

        def load_w(sc_stage, SB, dst, src, nk, ncols, wbuf, kq=4):
            for k0 in range(0, nk, kq):
                k1 = min(nk, k0 + kq)
                i = st_state["i"] % 2
                st_state["i"] += 1
                stg = sc_stage[i]
                v = stg[:, 0:(k1 - k0) * ncols].rearrange("p (k n) -> p k n", n=ncols)
                P.dma("sp", v, src[:, k0:k1, :], w=[SB[i]])
                P.op("pool", lambda e, v=v, k0=k0, k1=k1: e.tensor_copy(dst[:, k0:k1, :], v), r=[SB[i]], w=[wbuf])

        def mod_phase(l):
            with P.scope() as sc:
                wm = [sc.sb("wm", [128, KC, 512], F32) for _ in range(2)]
                WM = [Buf("wm0"), Buf("wm1")]
                bg = sc.sb("bg", [R, D], F32)
                gp = sc.sb("gp", [R, D], F32)
                B_bg = Buf("bg")
                psA = sc.ps("psA", [128, 512], F32)
                psB = sc.ps("psB", [128, 512], F32)
                PSA, PSB = Buf("psA"), Buf("psB")
                P.dma("sp", bg[:], bmodg_in[:, l, :], w=[B_bg])
                P.dma("sp", gp[:], gpostr_in[:, l, :], w=[B_bg])
                wsrc = wmod_in[l].rearrange("(kc p) n -> p kc n", p=128)
                ncolp = (2 * D) // 512
                for piece in range((3 * D) // 512):
                    sl = piece % 2
                    n0 = piece * 512
                    P.dma("sp", wm[sl][:], wsrc[:, :, n0:n0 + 512], w=[WM[sl]])
                    if piece < ncolp:
                        for jj in range(4):
                            for kc in range(KC):
                                P.op("pe", lambda e, jj=jj, kc=kc, sl=sl: e.matmul(
                                    psA[:, jj * R:(jj + 1) * R], wm[sl][:, kc, jj * 128:(jj + 1) * 128], scT[:, kc, :],
                                    start=(jj == 0 and kc == 0), stop=(jj == 3 and kc == KC - 1)),
                                    r=[WM[sl], B_scT], w=[PSA])
                        j0 = piece * 4
                        P.op("dve", lambda e, j0=j0: e.tensor_tensor(
                            out=modc[:, j0:j0 + 4, :], in0=psA[:, 0:4 * R].rearrange("p (j r) -> p j r", r=R),
                            in1=bmodc[:, l, j0:j0 + 4, :], op=ALU.add), r=[PSA, B_const], w=[B_modc])
                    else:
                        nb = piece - ncolp
                        for kc in range(KC):
                            P.op("pe", lambda e, kc=kc, sl=sl: e.matmul(
                                psB[0:R, :], scT[:, kc, :], wm[sl][:, kc, :], start=(kc == 0), stop=(kc == KC - 1)),
                                r=[WM[sl], B_scT], w=[PSB])
                        P.op("dve", lambda e, nb=nb: e.tensor_tensor(
                            out=GTrow[0:R, nb * 512:(nb + 1) * 512], in0=psB[0:R, :],
                            in1=bg[0:R, nb * 512:(nb + 1) * 512], op=ALU.add), r=[PSB, B_bg], w=[B_GTrow])
                        P.op("dve", lambda e, nb=nb: e.tensor_tensor(
                            out=GTrow[0:R, nb * 512:(nb + 1) * 512], in0=GTrow[0:R, nb * 512:(nb + 1) * 512],
                            in1=gp[0:R, nb * 512:(nb + 1) * 512], op=ALU.mult), r=[B_GTrow, B_bg], w=[B_GTrow])
                P.op("dve", lambda e: e.scalar_tensor_tensor(
                    out=Acol[:], in0=modc[:, KC:2 * KC, :], scalar=1.0, in1=gprec[:, l, :, :],
                    op0=ALU.add, op1=ALU.mult), r=[B_modc, B_const], w=[B_Acol])

        def x_src(l, s, ti):
            if ti < NT_C:
                src = ctx_in if l == layers[0] else cscr
                return src[s, ti * 128:(ti + 1) * 128, :], DXC[s][ti]
            tl = ti - NT_C
            src = x_in if l == layers[0] else y_out
            return src[s, tl * 128:(tl + 1) * 128, :], DXL[s][tl]

        def phase_n(l, s, ctx_active):
            with P.scope() as sc:
                xt = [sc.sb("xt", [128, D], F32) for _ in range(2)]
                xh = [sc.sb("xh", [128, D], BF16) for _ in range(2)]
                junk = sc.sb("junk", [128, D], BF16)
                stt = [sc.sb("stt", [128, 4], F32) for _ in range(2)]
                pst = [sc.ps("pst", [128, KC * 128], BF16) for _ in range(2)]
                XT = [Buf("xt0"), Buf("xt1")]
                XH = [Buf("xh0"), Buf("xh1")]
                ST = [Buf("st0"), Buf("st1")]
                PST = [Buf("pst0"), Buf("pst1")]
                JK = Buf("junk")
                tiles = list(range(NT)) if ctx_active else list(range(NT_C, NT))
                for n, ti in enumerate(tiles):
                    p = n % 2
                    r = R - 1 if ti < NT_C else s
                    src, dbuf = x_src(l, s, ti)
                    P.dma("sp", xt[p][:], src, r=[dbuf], w=[XT[p]])
                    P.op("act", lambda e, p=p: e.activation(out=junk[:], in_=xt[p][:], func=AF.Square,
                                                            accum_out=stt[p][:, 0:1]), r=[XT[p]], w=[JK, ST[p]])
                    P.op("dve", lambda e, p=p: e.tensor_scalar(out=stt[p][:, 1:2], in0=stt[p][:, 0:1], scalar1=1.0 / D,
                                                               scalar2=EPS, op0=ALU.mult, op1=ALU.add), r=[ST[p]], w=[ST[p]])
                    P.op("act", lambda e, p=p: e.activation(out=stt[p][:, 2:3], in_=stt[p][:, 1:2], func=AF.Ln),
                         r=[ST[p]], w=[ST[p]])
                    P.op("act", lambda e, p=p: e.activation(out=stt[p][:, 3:4], in_=stt[p][:, 2:3], func=AF.Exp, scale=-0.5),
                         r=[ST[p]], w=[ST[p]])
                    P.op("dve", lambda e, p=p: e.tensor_scalar(out=xh[p][:], in0=xt[p][:], scalar1=stt[p][:, 3:4],
                                                               scalar2=None, op0=ALU.mult), r=[XT[p], ST[p]], w=[XH[p]])
                    for kc in range(KC):
                        P.op("pe", lambda e, p=p, kc=kc: e.transpose(
                            out=pst[p][:, kc * 128:(kc + 1) * 128], in_=xh[p][:, kc * 128:(kc + 1) * 128],
                            identity=ident_b[:]), r=[XH[p], B_const], w=[PST[p]])
                    for kc in range(KC):
                        ek = "act" if kc % 2 == 0 else "dve"
                        if ek == "act":
                            P.op("act", lambda e, p=p, kc=kc, ti=ti, r=r: e.activation(
                                out=hT[:, kc, ti * 128:(ti + 1) * 128], in_=pst[p][:, kc * 128:(kc + 1) * 128],
                                func=AF.Identity, scale=Acol[:, kc, r:r + 1], bias=modc[:, kc, r:r + 1]),
                                r=[PST[p], B_Acol, B_modc], w=[HT[ti]])
                        else:
                            P.op("dve", lambda e, p=p, kc=kc, ti=ti, r=r: e.tensor_scalar(
                                out=hT[:, kc, ti * 128:(ti + 1) * 128], in0=pst[p][:, kc * 128:(kc + 1) * 128],
                                scalar1=Acol[:, kc, r:r + 1], scalar2=modc[:, kc, r:r + 1], op0=ALU.mult, op1=ALU.add),
                                r=[PST[p], B_Acol, B_modc], w=[HT[ti]])

        def phase_o(l, s, wo_src, need_ctx):
            with P.scope() as sc:
                wO = sc.sb("wO", [128, KC, D], BF16)
                WO = Buf("wO")
                stg = [sc.sb("stg", [128, 2 * 512], F32) for _ in range(2)]
                SB = [Buf("stg0"), Buf("stg1")]
                GTb = [sc.sb("GTb", [128, D], F32) for _ in range(2)]
                B_GTb = Buf("GTb")
                xt = [sc.sb("xo", [128, D], F32) for _ in range(2)]
                tt = sc.sb("tt", [128, D], F32)
                junk = sc.sb("junko", [128, NBK, BW], BF16)
                stt = [sc.sb("stto", [128, 8], F32) for _ in range(2)]
                psy = [[sc.ps("psy", [128, 512], F32) for _ in range(NBK)] for _ in range(2 if NBK <= 4 else 1)]
                PSY = [[Buf("psy") for _ in range(NBK)] for _ in range(len(psy))]
                XT = [Buf("xo0"), Buf("xo1")]
                TT = Buf("tt")
                JKS = [Buf(f"jko{i}") for i in range(NBK)]
                ST = [Buf("sto0"), Buf("sto1")]
                tlo = 0 if need_ctx else NT_C * 128
                for kc in range(KC):
                    P.dma("sp", hT[:, kc, tlo:T], gscr[kc, :, tlo:T], r=[GS[kc]], w=HT)
                for nb in range(NBK):
                    load_w(stg, SB, wO[:, :, nb * BW:(nb + 1) * BW], wo_src[:, :, nb * BW:(nb + 1) * BW], KC, BW, WO, kq=2)
                rows = [s, R - 1] if need_ctx else [s]
                for gi, r in enumerate(rows):
                    for nb in range(NBK):
                        P.op("pe", lambda e, r=r, nb=nb: e.matmul(
                            psy[0][nb][:, 0:BW], selT[0:R, r * 128:(r + 1) * 128], GTrow[0:R, nb * BW:(nb + 1) * BW],
                            start=True, stop=True), r=[B_const, B_GTrow], w=[PSY[0][nb]])
                        P.op("act", lambda e, gi=gi, nb=nb: e.activation(
                            out=GTb[gi][:, nb * BW:(nb + 1) * BW], in_=psy[0][nb][:, 0:BW], func=AF.Copy),
                            r=[PSY[0][nb]], w=[B_GTb])
                tiles = list(range(NT)) if need_ctx else list(range(NT_C, NT))
                for n, ti in enumerate(tiles):
                    p = n % 2
                    pp = n % len(psy)
                    gi = 1 if ti < NT_C else 0
                    src, dbuf = x_src(l, s, ti)
                    P.dma("sp", xt[p][:], src, r=[dbuf], w=[XT[p]])
                    for nb in range(NBK):
                        for kc in range(KC):
                            P.op("pe", lambda e, pp=pp, nb=nb, kc=kc, ti=ti: e.matmul(
                                psy[pp][nb][:, 0:BW], hT[:, kc, ti * 128:(ti + 1) * 128],
                                wO[:, kc, nb * BW:(nb + 1) * BW], start=(kc == 0), stop=(kc == KC - 1)),
                                r=[HT[ti], WO], w=[PSY[pp][nb]])
                    for nb in range(NBK):
                        P.op("act", lambda e, p=p, pp=pp, nb=nb: e.activation(
                            out=junk[:, nb, :], in_=psy[pp][nb][:, 0:BW], func=AF.Square, accum_out=stt[p][:, nb:nb + 1]),
                            r=[PSY[pp][nb]], w=[JKS[nb], ST[p]])
                    P.op("dve", lambda e, p=p: e.reduce_sum(out=stt[p][:, 4:5], in_=stt[p][:, 0:NBK], axis=AX.X),
                         r=[ST[p]], w=[ST[p]])
                    P.op("dve", lambda e, p=p: e.tensor_scalar(out=stt[p][:, 5:6], in0=stt[p][:, 4:5], scalar1=1.0 / D,
                                                               scalar2=EPS, op0=ALU.mult, op1=ALU.add), r=[ST[p]], w=[ST[p]])
                    P.op("act", lambda e, p=p: e.activation(out=stt[p][:, 6:7], in_=stt[p][:, 5:6], func=AF.Ln),
                         r=[ST[p]], w=[ST[p]])
                    P.op("act", lambda e, p=p: e.activation(out=stt[p][:, 7:8], in_=stt[p][:, 6:7], func=AF.Exp, scale=-0.5),
                         r=[ST[p]], w=[ST[p]])
                    for nb in range(NBK):
                        P.op("dve", lambda e, p=p, pp=pp, nb=nb, gi=gi: e.scalar_tensor_tensor(
                            out=tt[:, nb * BW:(nb + 1) * BW], in0=psy[pp][nb][:, 0:BW], scalar=stt[p][:, 7:8],
                            in1=GTb[gi][:, nb * BW:(nb + 1) * BW], op0=ALU.mult, op1=ALU.mult),
                            r=[PSY[pp][nb], ST[p], B_GTb], w=[TT])
                    P.op("pool", lambda e, p=p: e.tensor_tensor(out=xt[p][:], in0=tt[:], in1=xt[p][:], op=ALU.add),
                         r=[TT, XT[p]], w=[XT[p]])
                    if ti < NT_C:
                        dst, dbuf2 = cscr[s, ti * 128:(ti + 1) * 128, :], DXC[s][ti]
                    else:
                        tl = ti - NT_C
                        dst, dbuf2 = y_out[s, tl * 128:(tl + 1) * 128, :], DXL[s][tl]
                    P.dma("sp", dst, xt[p][:], r=[XT[p]], w=[dbuf2])

        def odd_mixer(o, s, ctx_active):
            PG, PGC = cfg.PG, cfg.PGC
            segs = ([("c", 0, NT_C)] if ctx_active else []) + [("l", NT_C, NT_L)]
            with P.scope() as sc:
                stg = [sc.sb("stg", [128, 4 * 640], F32) for _ in range(2)]
                SB = [Buf("stg0"), Buf("stg1")]
                wU = sc.sb("wU", [128, KC, PG], BF16)
                wZ = sc.sb("wZ", [128, KC, PG], BF16)
                wP = sc.sb("wP", [128, PGC, PG], BF16)
                WU, WZ, WP = Buf("wU"), Buf("wZ"), Buf("wP")
                pcf = sc.sb("pcf", [128, n_pool_blocks, 128], F32)
                pcb = sc.sb("pcb", [128, n_pool_blocks, 128], BF16)
                B_pc = Buf("pc")
                lsc = sc.sb("lsc", [128, KC], F32)
                utm = sc.sb("utm", [128, NT, PG], BF16)
                UT = [Buf(f"ut{i}") for i in range(NT)]
                rT = sc.sb("rT", [128, PGC, T], BF16)
                RT = [Buf(f"rt{i}") for i in range(NT)]
                sz = [sc.sb("sz", [128, 512], BF16) for _ in range(2)]
                SZ = [Buf("sz0"), Buf("sz1")]
                gch = [sc.sb("gch", [128, T], BF16) for _ in range(2)]
                GCH = [Buf("gch0"), Buf("gch1")]
                psu = [sc.ps("psu", [128, 512], F32) for _ in range(2)]
                PSU = [Buf("psu0"), Buf("psu1")]
                psr = [sc.ps("psr", [128, 512], F32) for _ in range(2)]
                PSR = [Buf("psr0"), Buf("psr1")]
                psq = [sc.ps("psq", [128, 512], F32) for _ in range(2)]
                PSQ = [Buf("psq0"), Buf("psq1")]
                psz = [sc.ps("psz", [128, 512], F32) for _ in range(2)]
                PSZ = [Buf("psz0"), Buf("psz1")]
                P.dma("sp", pcf[:], poolc_in.rearrange("n p c -> p n c"), w=[B_pc])
                P.op("dve", lambda e: e.tensor_copy(pcb[:], pcf[:]), r=[B_pc], w=[B_pc])
                P.dma("sp", lsc[:], odsc_in[:, o, :], w=[B_pc])
                cu = cr = cq = 0
                gcount = 0
                for j in range(4):
                    load_w(stg, SB, wU[:], wOD_in[o, j][:, :, 0:PG], KC, PG, WU)
                    load_w(stg, SB, wZ[:], wOD_in[o, j][:, :, PG:2 * PG], KC, PG, WZ)
                    load_w(stg, SB, wP[:], wPL_in[o, j], PGC, PG, WP)
                    tiles = list(range(NT)) if ctx_active else list(range(NT_C, NT))
                    for ti in tiles:
                        for n0 in range(0, PG, 512):
                            nw = min(512, PG - n0)
                            p = cu % 2
                            cu += 1
                            for kc in range(KC):
                                P.op("pe", lambda e, p=p, kc=kc, ti=ti, n0=n0, nw=nw: e.matmul(
                                    psu[p][:, 0:nw], hT[:, kc, ti * 128:(ti + 1) * 128], wU[:, kc, n0:n0 + nw],
                                    start=(kc == 0), stop=(kc == KC - 1)), r=[HT[ti], WU], w=[PSU[p]])
                            P.op("act", lambda e, p=p, ti=ti, n0=n0, nw=nw: e.activation(
                                out=utm[:, ti, n0:n0 + nw], in_=psu[p][:, 0:nw], func=AF.Copy), r=[PSU[p]], w=[UT[ti]])
                    for (seg, t0, nts) in segs:
                        for fc in range(PGC):
                            for tb in range(0, nts, 4):
                                ntb = min(4, nts - tb)
                                p = cr % 2
                                cr += 1
                                for tq in range(ntb):
                                    tl = tb + tq
                                    dis = [di for di in (-1, 0, 1) if 0 <= tl + di < nts]
                                    for ii, di in enumerate(dis):
                                        bi = pool_maps[(seg, j, tl, di)]
                                        first = (tq == 0 and ii == 0)
                                        last = (tq == ntb - 1 and ii == len(dis) - 1)
                                        P.op("pe", lambda e, p=p, tq=tq, fc=fc, bi=bi, sti=t0 + tl + di, first=first, last=last:
                                             e.matmul(psr[p][:, tq * 128:(tq + 1) * 128],
                                                      utm[:, sti, fc * 128:(fc + 1) * 128], pcb[:, bi, :],
                                                      start=first, stop=last),
                                             r=[UT[t0 + tl + di], B_pc], w=[PSR[p]])
                                tg0 = (t0 + tb) * 128
                                P.op("dve", lambda e, p=p, fc=fc, tg0=tg0, ntb=ntb: e.tensor_copy(
                                    rT[:, fc, tg0:tg0 + ntb * 128], psr[p][:, 0:ntb * 128]),
                                    r=[PSR[p]], w=[RT[t0 + tb + q] for q in range(ntb)])
                    for fc in range(PGC):
                        gch_i = gcount % 2
                        gcount += 1
                        chunk = j * PGC + fc
                        for (seg, t0, nts) in segs:
                            for tb in range(0, nts, 4):
                                ntb = min(4, nts - tb)
                                nw = ntb * 128
                                tg0 = (t0 + tb) * 128
                                p = cq % 2
                                cq += 1
                                tbufs = [t0 + tb + q for q in range(ntb)]
                                for kc2 in range(PGC):
                                    P.op("pe", lambda e, p=p, kc2=kc2, fc=fc, tg0=tg0, nw=nw: e.matmul(
                                        psq[p][:, 0:nw], wP[:, kc2, fc * 128:(fc + 1) * 128], rT[:, kc2, tg0:tg0 + nw],
                                        start=(kc2 == 0), stop=(kc2 == PGC - 1)),
                                        r=[WP] + [RT[q] for q in tbufs], w=[PSQ[p]])
                                for kc in range(KC):
                                    P.op("pe", lambda e, p=p, kc=kc, fc=fc, tg0=tg0, nw=nw: e.matmul(
                                        psz[p][:, 0:nw], wZ[:, kc, fc * 128:(fc + 1) * 128], hT[:, kc, tg0:tg0 + nw],
                                        start=(kc == 0), stop=(kc == KC - 1)),
                                        r=[WZ] + [HT[q] for q in tbufs], w=[PSZ[p]])
                                P.op("act", lambda e, p=p, nw=nw: e.activation(out=sz[p][:, 0:nw], in_=psz[p][:, 0:nw],
                                                                                func=AF.Silu), r=[PSZ[p]], w=[SZ[p]])
                                P.op("dve", lambda e, p=p, nw=nw, tg0=tg0, gch_i=gch_i, chunk=chunk: e.scalar_tensor_tensor(
                                    out=gch[gch_i][:, tg0:tg0 + nw], in0=psq[p][:, 0:nw], scalar=lsc[:, chunk:chunk + 1],
                                    in1=sz[p][:, 0:nw], op0=ALU.mult, op1=ALU.mult),
                                    r=[PSQ[p], SZ[p], B_pc], w=[GCH[gch_i]])
                        tlo = 0 if ctx_active else NT_C * 128
                        P.dma("sp", gscr[chunk, :, tlo:T], gch[gch_i][:, tlo:T], r=[GCH[gch_i]], w=[GS[chunk]])

        even_mixer = make_even_mixer(cfg, nc, P, dict(
            hT=hT, HT=HT, ident_b=ident_b, B_const=B_const, gscr=gscr, GS=GS, load_w=load_w,
            wA_in=wA_in, wB_in=wB_in, lam_in=lam_in, subln_in=subln_in, hgn_in=hgn_in, lbl_in=lbl_in,
            cos_in=cos_in, sin_in=sin_in, hgM_in=hgM_in, hgmask_in=hgmask_in, hgw_in=hgw_in, hgcm_in=hgcm_in))

        for l in layers:
            even = (l % 2 == 0)
            need_ctx = l < cfg.DEPTH - 1
            ctx_active = even or need_ctx
            mod_phase(l)
            for s in range(NB):
                phase_n(l, s, ctx_active)
                if even:
                    even_mixer(l // 2, l, s, need_ctx)
                    phase_o(l, s, wEO_in[l // 2], need_ctx)
                else:
                    odd_mixer(l // 2, s, ctx_active and need_ctx)
                    phase_o(l, s, wOO_in[l // 2], need_ctx)
        P.final_wait("sp")
        build.stats = (P.nops, P.nwait, dict(P.cnt), dict(P.si))
    return nc


def make_even_mixer(cfg, nc, P, G):
    D, KC, T, NT, NT_C, NT_L, R, NB = cfg.D, cfg.KC, cfg.T, cfg.NT, cfg.NT_C, cfg.NT_L, cfg.R, cfg.NB
    hT, HT, ident_b, B_const, gscr, GS, load_w = (G[k] for k in ("hT", "HT", "ident_b", "B_const", "gscr", "GS", "load_w"))
    TC = cfg.T_CTX

    def rstd_ops(stt, ST, c_ss, c_out, n):
        P.op("dve", lambda e: e.tensor_scalar(out=stt[:, c_out:c_out + 1], in0=stt[:, c_ss:c_ss + 1], scalar1=1.0 / n,
                                              scalar2=EPS, op0=ALU.mult, op1=ALU.add), r=[ST], w=[ST])
        P.op("act", lambda e: e.activation(out=stt[:, c_out:c_out + 1], in_=stt[:, c_out:c_out + 1], func=AF.Ln),
             r=[ST], w=[ST])
        P.op("act", lambda e: e.activation(out=stt[:, c_out:c_out + 1], in_=stt[:, c_out:c_out + 1], func=AF.Exp, scale=-0.5),
             r=[ST], w=[ST])

    def attention_part(e_idx, l, s, need_ctx):
        lam_init = 0.8 - 0.6 * math.exp(-0.3 * l)
        with P.scope() as sc:
            stg = [sc.sb("stg", [128, 4 * 512], F32) for _ in range(2)]
            SB = [Buf("stg0"), Buf("stg1")]
            wbf = [sc.sb("wbfA", [128, KC, 512], BF16) for _ in range(2)]
            WB = [Buf("wbA0"), Buf("wbA1")]
            cosT = sc.sb("cosT", [128, NT_L, 128], F32)
            sinT = sc.sb("sinT", [128, NT_L, 128], F32)
            B_rope = Buf("rope")
            lamt = sc.sb("lamt", [128, 4, 64], F32)
            lamw = sc.sb("lamw", [128, 2, 64], F32)
            lams = sc.sb("lams", [128, 8], F32)
            B_lam = Buf("lam")
            Gt = sc.sb("Gt", [128, 128], F32)
            B_G = Buf("G")
            QKT = sc.sb("QKT", [128, 2, T], BF16)
            QK = [Buf(f"qk{i}") for i in range(NT)]
            Vaug = sc.sb("Vaug", [128, NT, 130], BF16)
            VA = [Buf(f"va{i}") for i in range(NT)]
            GGt = sc.sb("GGt", [128, NT, 128], F32)
            GGB = [Buf(f"gg{i}") for i in range(NT)]
            qktm = [sc.sb("qktm", [128, 256], BF16) for _ in range(2)]
            QKTM = [Buf("qktm0"), Buf("qktm1")]
            t1 = [sc.sb("t1", [128, 256], F32) for _ in range(2)]
            t2 = [sc.sb("t2", [128, 256], F32) for _ in range(2)]
            T1 = [Buf("t10"), Buf("t11")]
            T2 = [Buf("t20"), Buf("t21")]
            eg = [sc.sb("eg", [128, 128], F32) for _ in range(2)]
            EG = [Buf("eg0"), Buf("eg1")]
            Et = [sc.sb("Et", [128, 512], BF16) for _ in range(3)]
            ET = [Buf(f"et{i}") for i in range(3)]
            stt4 = [sc.sb("stt4", [128, 8], F32) for _ in range(4)]
            ST4 = [Buf(f"st4{i}") for i in range(4)]
            o14 = [sc.sb("o14", [128, 128], F32) for _ in range(4)]
            O14 = [Buf(f"o14{i}") for i in range(4)]
            o24 = [sc.sb("o24", [128, 128], F32) for _ in range(4)]
            O24 = [Buf(f"o24{i}") for i in range(4)]
            junk4 = sc.sb("junk4", [128, 4, 128], BF16)
            JK4 = [Buf(f"jk4{i}") for i in range(4)]
            atm4 = [sc.sb("atm4", [128, 128], BF16) for _ in range(4)]
            ATM4 = [Buf(f"atm4{i}") for i in range(4)]
            aT = [sc.sb("aT", [128, T], BF16) for _ in range(2)]
            AT = [Buf("aT0"), Buf("aT1")]
            psp = [sc.ps("psp", [128, 512], F32) for _ in range(2)]
            PSP = [Buf("psp0"), Buf("psp1")]
            pss = [sc.ps("pss", [128, 512], F32) for _ in range(2)]
            PSS = [Buf("pss0"), Buf("pss1")]
            pso = [sc.ps("pso", [128, 512], F32) for _ in range(3)]
            PSO = [Buf("pso0"), Buf("pso1"), Buf("pso2")]
            pst = sc.ps("pstA", [128, 1024], BF16)
            PSTq = Buf("pstq")
            PSTa = PSTq

            P.dma("sp", cosT[:], G["cos_in"][:, :, :], w=[B_rope])
            P.dma("sp", sinT[:], G["sin_in"][:, :, :], w=[B_rope])
            P.dma("sp", lamt[:], G["lam_in"][:, e_idx, :, :], w=[B_lam])
            P.dma("sp", Gt[:], G["subln_in"][:, e_idx, :], w=[B_G])
            P.op("dve", lambda e: e.tensor_scalar(out=Gt[:], in0=Gt[:], scalar1=(1.0 - lam_init), scalar2=None,
                                                  op0=ALU.mult), r=[B_G], w=[B_G])
            P.op("dve", lambda e: e.tensor_tensor(out=lamw[:, 0, :], in0=lamt[:, 0, :], in1=lamt[:, 1, :], op=ALU.mult),
                 r=[B_lam], w=[B_lam])
            P.op("dve", lambda e: e.tensor_tensor(out=lamw[:, 1, :], in0=lamt[:, 2, :], in1=lamt[:, 3, :], op=ALU.mult),
                 r=[B_lam], w=[B_lam])
            P.op("dve", lambda e: e.reduce_sum(out=lams[:, 0:2], in_=lamw[:, :, :], axis=AX.X), r=[B_lam], w=[B_lam])
            P.op("act", lambda e: e.activation(out=lams[:, 2:4], in_=lams[:, 0:2], func=AF.Exp), r=[B_lam], w=[B_lam])
            P.op("dve", lambda e: e.tensor_tensor(out=lams[:, 4:5], in0=lams[:, 2:3], in1=lams[:, 3:4], op=ALU.subtract),
                 r=[B_lam], w=[B_lam])
            P.op("dve", lambda e: e.tensor_scalar(out=lams[:, 5:6], in0=lams[:, 4:5], scalar1=lam_init, scalar2=-1.0,
                                                  op0=ALU.add, op1=ALU.mult), r=[B_lam], w=[B_lam])
            P.op("pool", lambda e: e.memset(Vaug[:, :, 128:130], 1.0), w=VA)

            cnt = {"p": 0, "s": 0, "e": 0, "ep": 0}
            for h in range(cfg.DA_HEADS):
                hp = h % 2
                load_w(stg, SB, wbf[hp][:], G["wA_in"][e_idx, h], KC, 512, WB[hp])
                for ti in range(NT):
                    p = cnt["p"] % 2
                    cnt["p"] += 1
                    for kc in range(KC):
                        P.op("pe", lambda e, p=p, kc=kc, ti=ti, hp=hp: e.matmul(
                            psp[p][:, :], hT[:, kc, ti * 128:(ti + 1) * 128], wbf[hp][:, kc, :],
                            start=(kc == 0), stop=(kc == KC - 1)), r=[HT[ti], WB[hp]], w=[PSP[p]])
                    if ti >= NT_C:
                        tl = ti - NT_C
                        for c0 in (0, 128):
                            X = psp[p][:, c0:c0 + 128]
                            Xv = X.rearrange("p (a h i) -> p a h i", h=2, i=16)
                            Sv = sinT[:, tl, :].rearrange("p (a h i) -> p a h i", h=2, i=16)
                            t2v = t2[p][:, c0:c0 + 128].rearrange("p (a h i) -> p a h i", h=2, i=16)
                            P.op("dve", lambda e, p=p, c0=c0, X=X, tl=tl: e.tensor_tensor(
                                out=t1[p][:, c0:c0 + 128], in0=X, in1=cosT[:, tl, :], op=ALU.mult),
                                r=[PSP[p], B_rope], w=[T1[p]])
                            P.op("dve", lambda e, Xv=Xv, Sv=Sv, t2v=t2v: e.tensor_tensor(
                                out=t2v[:, :, 0, :], in0=Xv[:, :, 1, :], in1=Sv[:, :, 0, :], op=ALU.mult),
                                r=[PSP[p], B_rope], w=[T2[p]])
                            P.op("dve", lambda e, Xv=Xv, Sv=Sv, t2v=t2v: e.tensor_tensor(
                                out=t2v[:, :, 1, :], in0=Xv[:, :, 0, :], in1=Sv[:, :, 1, :], op=ALU.mult),
                                r=[PSP[p], B_rope], w=[T2[p]])
                        P.op("pool", lambda e, p=p: e.tensor_tensor(out=qktm[p][:], in0=t1[p][:], in1=t2[p][:], op=ALU.add),
                             r=[T1[p], T2[p]], w=[QKTM[p]])
                    else:
                        P.op("act", lambda e, p=p: e.activation(out=qktm[p][:], in_=psp[p][:, 0:256], func=AF.Copy),
                             r=[PSP[p]], w=[QKTM[p]])
                    for c in range(2):
                        P.op("pe", lambda e, p=p, c=c: e.transpose(out=pst[:, c * 128:(c + 1) * 128],
                                                                    in_=qktm[p][:, c * 128:(c + 1) * 128], identity=ident_b[:]),
                             r=[QKTM[p], B_const], w=[PSTq])
                    P.op("act", lambda e, ti=ti: e.activation(
                        out=QKT[:, :, ti * 128:(ti + 1) * 128], in_=pst[:, 0:256].rearrange("p (c t) -> p c t", c=2),
                        func=AF.Copy), r=[PSTq], w=[QK[ti]])
                    P.op("act", lambda e, p=p, ti=ti: e.activation(out=Vaug[:, ti, 0:128], in_=psp[p][:, 256:384], func=AF.Copy),
                         r=[PSP[p]], w=[VA[ti]])
                    P.op("act", lambda e, p=p: e.activation(out=eg[p][:], in_=psp[p][:, 384:512], func=AF.Exp, scale=-1.0),
                         r=[PSP[p]], w=[EG[p]])
                    P.op("act", lambda e, p=p: e.activation(out=eg[p][:], in_=eg[p][:], func=AF.Ln, bias=1.0), r=[EG[p]], w=[EG[p]])
                    P.op("act", lambda e, p=p: e.activation(out=eg[p][:], in_=eg[p][:], func=AF.Exp, scale=-1.0), r=[EG[p]], w=[EG[p]])
                    P.op("pool", lambda e, p=p: e.tensor_tensor(out=eg[p][:], in0=eg[p][:], in1=Gt[:], op=ALU.mult),
                         r=[EG[p], B_G], w=[EG[p]])
                    P.op("dve", lambda e, p=p, ti=ti: e.tensor_tensor(out=GGt[:, ti, :], in0=psp[p][:, 384:512], in1=eg[p][:],
                                                                      op=ALU.mult), r=[PSP[p], EG[p]], w=[GGB[ti]])

                blocks = []
                if need_ctx:
                    blocks.append((0, TC, list(range(NT_C))))
                for qb in range(0, cfg.T_LAT, 512):
                    blocks.append((TC + qb, min(512, cfg.T_LAT - qb), list(range(NT))))
                for (q0, nq, kts) in blocks:
                    nqs = nq // 128
                    nacc = 2 * nqs
                    nbank = (nacc + 2) // 3
                    firsts = {b: True for b in range(nbank)}
                    qtiles = [q0 // 128 + i for i in range(nqs)]
                    for ki, kt in enumerate(kts):
                        for m in range(2):
                            p = cnt["s"] % 2
                            cnt["s"] += 1
                            ei = cnt["e"] % 3
                            cnt["e"] += 1
                            P.op("pe", lambda e, p=p, m=m, kt=kt, q0=q0, nq=nq: e.matmul(
                                pss[p][:, 0:nq], QKT[m * 64:(m + 1) * 64, 1, kt * 128:(kt + 1) * 128],
                                QKT[m * 64:(m + 1) * 64, 0, q0:q0 + nq], start=True, stop=True),
                                r=[QK[kt]] + [QK[q] for q in qtiles], w=[PSS[p]])
                            P.op("act", lambda e, p=p, ei=ei, nq=nq: e.activation(
                                out=Et[ei][:, 0:nq], in_=pss[p][:, 0:nq], func=AF.Exp, scale=0.125),
                                r=[PSS[p]], w=[ET[ei]])
                            for qs in range(nqs):
                                idx = m * nqs + qs
                                b, slot = idx // 3, idx % 3
                                is_first = firsts[b]
                                firsts[b] = False
                                last_idx_in_bank = min(nacc - 1, b * 3 + 2)
                                is_last = (ki == len(kts) - 1) and (idx == last_idx_in_bank)
                                P.op("pe", lambda e, ei=ei, qs=qs, kt=kt, b=b, slot=slot, is_first=is_first, is_last=is_last:
                                     e.matmul(pso[b][:, slot * 129:slot * 129 + 129], Et[ei][:, qs * 128:(qs + 1) * 128],
                                              Vaug[:, kt, 0:129], start=is_first, stop=is_last),
                                     r=[ET[ei], VA[kt]], w=[PSO[b]])
                    QS_ = list(range(nqs))
                    tis = [q0 // 128 + qs for qs in QS_]
                    loc = []
                    for qs in QS_:
                        i0_, i1_ = qs, nqs + qs
                        loc.append((i0_ // 3, (i0_ % 3) * 129, i1_ // 3, (i1_ % 3) * 129))
                    for qs in QS_:
                        b0, s0, b1, s1 = loc[qs]
                        P.op("dve", lambda e, qs=qs, b0=b0, s0=s0: e.reciprocal(out=stt4[qs][:, 0:1], in_=pso[b0][:, s0 + 128:s0 + 129]),
                             r=[PSO[b0]], w=[ST4[qs]])
                        P.op("dve", lambda e, qs=qs, b1=b1, s1=s1: e.reciprocal(out=stt4[qs][:, 1:2], in_=pso[b1][:, s1 + 128:s1 + 129]),
                             r=[PSO[b1]], w=[ST4[qs]])
                    for qs in QS_:
                        P.op("dve", lambda e, qs=qs: e.tensor_tensor(out=stt4[qs][:, 2:3], in0=stt4[qs][:, 1:2], in1=lams[:, 5:6],
                                                                     op=ALU.mult), r=[ST4[qs], B_lam], w=[ST4[qs]])
                    for qs in QS_:
                        b0, s0, b1, s1 = loc[qs]
                        P.op("dve", lambda e, qs=qs, b0=b0, s0=s0: e.tensor_scalar(
                            out=o14[qs][:], in0=pso[b0][:, s0:s0 + 128], scalar1=stt4[qs][:, 0:1], scalar2=None, op0=ALU.mult),
                            r=[PSO[b0], ST4[qs]], w=[O14[qs]])
                    for qs in QS_:
                        b0, s0, b1, s1 = loc[qs]
                        P.op("dve", lambda e, qs=qs, b1=b1, s1=s1: e.scalar_tensor_tensor(
                            out=o24[qs][:], in0=pso[b1][:, s1:s1 + 128], scalar=stt4[qs][:, 2:3], in1=o14[qs][:],
                            op0=ALU.mult, op1=ALU.add), r=[PSO[b1], ST4[qs], O14[qs]], w=[O24[qs]])
                    for qs in QS_:
                        P.op("act", lambda e, qs=qs: e.activation(out=junk4[:, qs, :], in_=o24[qs][:], func=AF.Square,
                                                                  accum_out=stt4[qs][:, 3:4]), r=[O24[qs]], w=[JK4[qs], ST4[qs]])
                    for qs in QS_:
                        P.op("dve", lambda e, qs=qs: e.tensor_scalar(out=stt4[qs][:, 4:5], in0=stt4[qs][:, 3:4], scalar1=1.0 / 128,
                                                                     scalar2=EPS, op0=ALU.mult, op1=ALU.add), r=[ST4[qs]], w=[ST4[qs]])
                    for qs in QS_:
                        P.op("act", lambda e, qs=qs: e.activation(out=stt4[qs][:, 4:5], in_=stt4[qs][:, 4:5], func=AF.Ln),
                             r=[ST4[qs]], w=[ST4[qs]])
                    for qs in QS_:
                        P.op("act", lambda e, qs=qs: e.activation(out=stt4[qs][:, 4:5], in_=stt4[qs][:, 4:5], func=AF.Exp, scale=-0.5),
                             r=[ST4[qs]], w=[ST4[qs]])
                    for qs in QS_:
                        P.op("dve", lambda e, qs=qs: e.scalar_tensor_tensor(
                            out=atm4[qs][:], in0=o24[qs][:], scalar=stt4[qs][:, 4:5], in1=GGt[:, tis[qs], :],
                            op0=ALU.mult, op1=ALU.mult), r=[O24[qs], ST4[qs], GGB[tis[qs]]], w=[ATM4[qs]])
                    for qs in QS_:
                        P.op("pe", lambda e, qs=qs: e.transpose(out=pst[:, 256 + qs * 128:384 + qs * 128], in_=atm4[qs][:],
                                                                identity=ident_b[:]), r=[ATM4[qs], B_const], w=[PSTa])
                    P.op("act", lambda e: e.activation(out=aT[hp][:, q0:q0 + nq], in_=pst[:, 256:256 + nq], func=AF.Copy),
                         r=[PSTa], w=[AT[hp]])
                tlo = 0 if need_ctx else TC
                P.dma("sp", gscr[h, :, tlo:T], aT[hp][:, tlo:T], r=[AT[hp]], w=[GS[h]])

    def hgrn_part(e_idx, l, s, need_ctx):
        HW = cfg.HG_W
        with P.scope() as sc:
            stg = [sc.sb("stg", [128, 640], F32) for _ in range(2)]
            SB = [Buf("stg0"), Buf("stg1")]
            wbf = [sc.sb("wbfB", [128, KC, 640], BF16) for _ in range(2)]
            WB = [Buf("wbB0"), Buf("wbB1")]
            LBt = sc.sb("LBt", [128, 2, 128], F32)
            OML = sc.sb("OML", [128, 2, 128], F32)
            B_lb = Buf("lb")
            hgG = sc.sb("hgG", [128, 128], F32)
            M1 = sc.sb("M1", [128, 2, 128], F32)
            mask = sc.sb("mask", [128, 2, 128], F32)
            wcol = sc.sb("wcol", [128, 2, 12], F32)
            B_hc = Buf("hgc")
            qs_t = [sc.sb("qs", [128, 128], F32) for _ in range(2)]
            QS = [Buf("qs0"), Buf("qs1")]
            vv = sc.sb("vv", [128, NT, 128], BF16)
            VV = [Buf(f"vv{i}") for i in range(NT)]
            GGt = sc.sb("GGb", [128, NT, 128], BF16)
            GGB = [Buf(f"ggb{i}") for i in range(NT)]
            of = sc.sb("of", [128, NT, 128], F32)
            OF = [Buf(f"of{i}") for i in range(NT)]
            nsl = [2, NT]
            qkT = [sc.sb("qkTh", [128, nsl[d], 2, 128], BF16) for d in range(2)]
            ktm = [sc.sb("ktm", [128, nsl[d], 128], BF16) for d in range(2)]
            ecs = [sc.sb("ecs", [128, nsl[d], 12], F32) for d in range(2)]
            PREP = [[Buf(f"prep{d}_{i}") for i in range(nsl[d])] for d in range(2)]
            sg = [sc.sb("sg", [128, 128], F32) for _ in range(2)]
            SG = [Buf("sg0"), Buf("sg1")]
            ff = [sc.sb("ff", [128, 128], F32) for _ in range(2)]
            FF = [Buf("ff0"), Buf("ff1")]
            lf = [sc.sb("lf", [128, 128], F32) for _ in range(2)]
            LF = [Buf("lf0"), Buf("lf1")]
            kk = [sc.sb("kk", [128, 128], F32) for _ in range(2)]
            KK = [Buf("kk0"), Buf("kk1")]
            ed = [sc.sb("ed", [128, 2, 128], F32) for _ in range(2)]
            ED = [Buf("ed0"), Buf("ed1")]
            qtl = [sc.sb("qtl", [128, 128], BF16) for _ in range(2)]
            ee3 = [sc.sb("ee3", [128, 512], F32) for _ in range(2)]
            L3 = [sc.sb("L3", [128, 512], F32) for _ in range(2)]
            s3 = [sc.sb("s3", [128, 512], F32) for _ in range(2)]
            EE3 = [Buf("ee30"), Buf("ee31")]
            LL3 = [Buf("L30"), Buf("L31")]
            SS3 = [Buf("s30"), Buf("s31")]
            QTL = [Buf("qtl0"), Buf("qtl1")]
            eg = [sc.sb("egb", [128, 128], F32) for _ in range(2)]
            EG = [Buf("egb0"), Buf("egb1")]
            SBF = [[sc.sb("Sbf", [128, 128], BF16) for _ in range(2)] for _ in range(2)]
            SS = [Buf("S0"), Buf("S1")]
            SSB = [[Buf("Sb00"), Buf("Sb01")], [Buf("Sb10"), Buf("Sb11")]]
            qz = [sc.sb("qz", [128, 640], BF16) for _ in range(2)]
            QZ = [Buf("qz0"), Buf("qz1")]
            cm = sc.sb("cm", [128, 4], F32)
            cmb = sc.sb("cmb", [128, 4, 128], BF16)
            ones_t = sc.sb("ones_t", [128, 128], F32)
            vblk = [sc.sb("vblk", [128, 4, 128], BF16) for _ in range(2)]
            VB = [Buf("vb0"), Buf("vb1")]
            attm = [sc.sb("attm", [128, 128], BF16) for _ in range(2)]
            ATT = [Buf("att0"), Buf("att1")]
            tmpu4 = sc.sb("tmpu4", [128, 4, 128], F32)
            TU4 = Buf("tu4")
            S2 = [[sc.sb("S2", [128, 128], F32) for _ in range(2)] for _ in range(2)]
            SS2 = [[Buf("S200"), Buf("S201")], [Buf("S210"), Buf("S211")]]
            spar = [0, 0]
            osum = [sc.sb("osum", [128, 128], F32) for _ in range(2)]
            OS = [Buf("os0"), Buf("os1")]
            stt = [sc.sb("stb", [128, 8], F32) for _ in range(2)]
            ST = [Buf("stb0"), Buf("stb1")]
            junk = sc.sb("junkb", [128, 128], BF16)
            JK = Buf("junkb")
            btm = [sc.sb("btm", [128, 128], BF16) for _ in range(2)]
            BTM = [Buf("btm0"), Buf("btm1")]
            bT = [sc.sb("bT", [128, T], BF16)] * 2
            BT = [Buf("bT0")] * 2
            psp = [[sc.ps("pspb", [128, 512], F32) for _ in range(2)] for _ in range(2)]
            PSP = [[Buf("pspb00"), Buf("pspb01")], [Buf("pspb10"), Buf("pspb11")]]
            psd = sc.ps("psd", [128, 512], F32)
            PSD = Buf("psd")
            psa = sc.ps("psa", [128, 512], F32)
            PSA = Buf("psa")
            PSOt = PSA
            psu = sc.ps("psu", [128, 512], F32)
            PSU = Buf("psu")
            pst = sc.ps("pstB", [128, 1024], BF16)
            PSTq = Buf("pstqb")
            PSTb = PSTq

            P.op("pool", lambda e: e.memset(ones_t[:], 1.0), w=[B_hc])
            P.dma("sp", hgG[:], G["hgn_in"][:, e_idx, :], w=[B_hc])
            P.dma("sp", M1[:], G["hgM_in"].rearrange("d s t -> s d t"), w=[B_hc])
            P.dma("sp", mask[:], G["hgmask_in"].rearrange("d s t -> s d t"), w=[B_hc])
            P.dma("sp", wcol[:], G["hgw_in"].rearrange("d s c -> s d c"), w=[B_hc])
            P.dma("sp", cm[:], G["hgcm_in"][:, :], w=[B_hc])
            for c in range(4):
                P.op("dve", lambda e, c=c: e.tensor_scalar(out=cmb[:, c, :], in0=ones_t[:], scalar1=cm[:, c:c + 1], scalar2=None,
                                                           op0=ALU.mult), r=[B_hc], w=[B_hc])
            for i in range(2):
                P.op("pool", lambda e, i=i: e.memset(qz[i][:], 0.0), w=[QZ[i]])
            def load_lb(h):
                if e_idx == 0:
                    if h == 0:
                        P.op("pool", lambda e: e.memset(LBt[:], 0.0), w=[B_lb])
                        P.op("pool", lambda e: e.memset(OML[:], 1.0), w=[B_lb])
                    return
                P.dma("sp", LBt[:], G["lbl_in"][:, 0, :, h * 128:(h + 1) * 128], w=[B_lb])
                P.dma("sp", OML[:], G["lbl_in"][:, 1, :, h * 128:(h + 1) * 128], w=[B_lb])
                P.op("dve", lambda e: e.tensor_tensor(out=LBt[:], in0=LBt[:], in1=OML[:], op=ALU.subtract), r=[B_lb], w=[B_lb])
                P.op("act", lambda e: e.activation(out=LBt[:], in_=LBt[:], func=AF.Exp), r=[B_lb], w=[B_lb])
                P.op("dve", lambda e: e.tensor_scalar(out=LBt[:], in0=LBt[:], scalar1=1.0, scalar2=None, op0=ALU.add),
                     r=[B_lb], w=[B_lb])
                P.op("dve", lambda e: e.reciprocal(out=LBt[:], in_=LBt[:]), r=[B_lb], w=[B_lb])
                P.op("dve", lambda e: e.tensor_scalar(out=OML[:], in0=LBt[:], scalar1=-1.0, scalar2=1.0, op0=ALU.mult,
                                                      op1=ALU.add), r=[B_lb], w=[B_lb])

            cnt = {"p": 0, "t": 0, "st": 0, "fin": 0}

            sgn = -1.0 if e_idx == 0 else 1.0

            def tile_common(h, hp, ti, p):
                P.op("act", lambda e: e.activation(out=ee3[p][:, 0:384], in_=psp[p][0][:, 0:384], func=AF.Exp, scale=-1.0),
                     r=[PSP[p][0]], w=[EE3[p]])
                P.op("act", lambda e: e.activation(out=ee3[p][:, 384:512], in_=psp[p][1][:, 0:128], func=AF.Exp, scale=-1.0),
                     r=[PSP[p][1]], w=[EE3[p]])
                P.op("act", lambda e: e.activation(out=L3[p][:], in_=ee3[p][:], func=AF.Ln, bias=1.0), r=[EE3[p]], w=[LL3[p]])
                P.op("act", lambda e: e.activation(out=s3[p][:], in_=L3[p][:], func=AF.Exp, scale=-1.0), r=[LL3[p]], w=[SS3[p]])
                P.op("dve", lambda e: e.tensor_tensor(out=qs_t[p][:], in0=psp[p][0][:, 0:128], in1=s3[p][:, 0:128], op=ALU.mult),
                     r=[PSP[p][0], SS3[p]], w=[QS[p]])
                P.op("act", lambda e: e.activation(out=vv[:, ti, :], in_=psp[p][0][:, 384:512], func=AF.Copy),
                     r=[PSP[p][0]], w=[VV[ti]])
                P.op("pool", lambda e: e.tensor_tensor(out=eg[p][:], in0=s3[p][:, 384:512], in1=hgG[:], op=ALU.mult),
                     r=[SS3[p], B_hc], w=[EG[p]])
                P.op("dve", lambda e: e.tensor_tensor(out=GGt[:, ti, :], in0=psp[p][1][:, 0:128], in1=eg[p][:], op=ALU.mult),
                     r=[PSP[p][1], EG[p]], w=[GGB[ti]])

            def prep(h, hp, ti, p, d, slot):
                k = cnt["t"] % 2
                cnt["t"] += 1
                c0 = 128 + d * 128
                if e_idx == 0:
                    lfa = L3[p][:, c0:c0 + 128]
                    LFB = LL3[p]
                    P.op("pool", lambda e: e.tensor_tensor(out=kk[k][:], in0=ee3[p][:, c0:c0 + 128], in1=s3[p][:, c0:c0 + 128],
                                                           op=ALU.mult), r=[EE3[p], SS3[p]], w=[KK[k]])
                else:
                    P.op("pool", lambda e: e.tensor_tensor(out=sg[k][:], in0=s3[p][:, c0:c0 + 128], in1=OML[:, d, :], op=ALU.mult),
                         r=[SS3[p], B_lb], w=[SG[k]])
                    P.op("pool", lambda e: e.tensor_tensor(out=ff[k][:], in0=sg[k][:], in1=LBt[:, d, :], op=ALU.add),
                         r=[SG[k], B_lb], w=[FF[k]])
                    P.op("pool", lambda e: e.tensor_tensor(out=kk[k][:], in0=OML[:, d, :], in1=sg[k][:], op=ALU.subtract),
                         r=[SG[k], B_lb], w=[KK[k]])
                    P.op("act", lambda e: e.activation(out=lf[k][:], in_=ff[k][:], func=AF.Ln), r=[FF[k]], w=[LF[k]])
                    lfa = lf[k][:]
                    LFB = LF[k]
                P.op("pe", lambda e: e.matmul(psd[:, 0:128], M1[:, d, :], lfa, start=True, stop=True),
                     r=[B_hc, LFB], w=[PSD])
                P.op("pe", lambda e: e.matmul(psd[:, 128:140], lfa, wcol[:, d, :], start=True, stop=True),
                     r=[B_hc, LFB], w=[PSD])
                P.op("act", lambda e: e.activation(out=ed[k][:, 0, :], in_=psd[:, 0:128], func=AF.Exp, scale=sgn), r=[PSD], w=[ED[k]])
                P.op("act", lambda e: e.activation(out=ed[k][:, 1, :], in_=psd[:, 0:128], func=AF.Exp, scale=-sgn),
                     r=[PSD], w=[ED[k]])
                P.op("act", lambda e: e.activation(out=ecs[d][:, slot, :], in_=psd[:, 128:140], func=AF.Exp, scale=sgn),
                     r=[PSD], w=[PREP[d][slot]])
                P.op("pool", lambda e: e.tensor_tensor(out=qtl[k][:], in0=qs_t[p][:], in1=ed[k][:, 0, :], op=ALU.mult),
                     r=[QS[p], ED[k]], w=[QTL[k]])
                P.op("pool", lambda e: e.tensor_tensor(out=ktm[d][:, slot, :], in0=kk[k][:], in1=ed[k][:, 1, :], op=ALU.mult),
                     r=[KK[k], ED[k]], w=[PREP[d][slot]])
                P.op("pe", lambda e: e.transpose(out=pst[:, 0:128], in_=qtl[k][:], identity=ident_b[:]),
                     r=[QTL[k], B_const], w=[PSTq])
                P.op("pe", lambda e: e.transpose(out=pst[:, 128:256], in_=ktm[d][:, slot, :], identity=ident_b[:]),
                     r=[PREP[d][slot], B_const], w=[PSTq])
                P.op("act", lambda e: e.activation(out=qkT[d][:, slot, :, :],
                                                   in_=pst[:, 0:256].rearrange("p (c t) -> p c t", c=2), func=AF.Copy),
                     r=[PSTq], w=[PREP[d][slot]])

            def step(h, hp, ti, d, slot):
                a = cnt["st"] % 2
                cnt["st"] += 1
                P.op("pe", lambda e: e.matmul(psa[:, 0:128], qkT[d][:, slot, 1, :], qkT[d][:, slot, 0, :], start=True, stop=True),
                     r=[PREP[d][slot]], w=[PSA])
                P.op("dve", lambda e: e.tensor_tensor(out=attm[a][:], in0=psa[:, 0:128], in1=mask[:, d, :], op=ALU.mult),
                     r=[PSA, B_hc], w=[ATT[a]])
                P.op("dve", lambda e: e.tensor_tensor(out=vblk[a][:], in0=vv[:, ti, :].unsqueeze(1).broadcast_to([128, 4, 128]),
                                                      in1=cmb[:], op=ALU.mult), r=[VV[ti], B_hc], w=[VB[a]])
                P.op("pe", lambda e: e.matmul(psu[:, 0:512], ktm[d][:, slot, :], vblk[a][:, :, :].rearrange("p c v -> p (c v)"),
                                              start=True, stop=True), r=[PREP[d][slot], VB[a]], w=[PSU])
                P.op("pe", lambda e: e.matmul(psa[:, 128:256], attm[a][:], vv[:, ti, :], start=True, stop=False),
                     r=[ATT[a], VV[ti]], w=[PSOt])
                P.op("pool", lambda e: e.tensor_copy(
                    qz[a][:, 0:640].rearrange("p (c x) -> p c x", x=160)[:, :, 0:32],
                    qkT[d][:, slot, 0, :].rearrange("p (c i) -> p c i", i=32)), r=[PREP[d][slot]], w=[QZ[a]])
                order = range(4) if d == 0 else range(3, -1, -1)
                for c in range(4):
                    P.op("dve", lambda e, c=c: e.tensor_scalar(out=tmpu4[:, c, :], in0=psu[:, c * 128:(c + 1) * 128],
                                                               scalar1=ecs[d][:, slot, 3 * c + 1:3 * c + 2], scalar2=None,
                                                               op0=ALU.mult), r=[PSU, PREP[d][slot]], w=[TU4])
                for n, c in enumerate(order):
                    sb = n % 2
                    cur = spar[d]
                    nxt = 1 - cur
                    P.op("act", lambda e, c=c, sb=sb, cur=cur: e.activation(out=SBF[d][sb][:], in_=S2[d][cur][:], func=AF.Identity,
                                                                             scale=ecs[d][:, slot, 3 * c + 2:3 * c + 3]),
                         r=[SS2[d][cur], PREP[d][slot]], w=[SSB[d][sb]])
                    P.op("pe", lambda e, c=c, sb=sb, n=n: e.matmul(psa[:, 128:256], qz[a][:, c * 128:(c + 1) * 128],
                                                                    SBF[d][sb][:], start=False, stop=(n == 3)),
                         r=[QZ[a], SSB[d][sb]], w=[PSOt])
                    P.op("dve", lambda e, c=c, cur=cur, nxt=nxt: e.scalar_tensor_tensor(
                        out=S2[d][nxt][:], in0=S2[d][cur][:], scalar=ecs[d][:, slot, 3 * c:3 * c + 1], in1=tmpu4[:, c, :],
                        op0=ALU.mult, op1=ALU.add), r=[SS2[d][cur], TU4, PREP[d][slot]], w=[SS2[d][nxt]])
                    spar[d] = nxt
                if d == 0:
                    P.op("act", lambda e: e.activation(out=of[:, ti, :], in_=psa[:, 128:256], func=AF.Copy), r=[PSOt], w=[OF[ti]])
                else:
                    f = cnt["fin"] % 2
                    cnt["fin"] += 1
                    P.op("dve", lambda e: e.tensor_tensor(out=osum[f][:], in0=psa[:, 128:256], in1=of[:, ti, :], op=ALU.add),
                         r=[PSOt, OF[ti]], w=[OS[f]])
                    P.op("act", lambda e: e.activation(out=junk[:], in_=osum[f][:], func=AF.Square, accum_out=stt[f][:, 0:1]),
                         r=[OS[f]], w=[JK, ST[f]])
                    rstd_ops(stt[f], ST[f], 0, 1, 128)
                    P.op("dve", lambda e: e.scalar_tensor_tensor(out=btm[f][:], in0=osum[f][:], scalar=stt[f][:, 1:2],
                                                                 in1=GGt[:, ti, :], op0=ALU.mult, op1=ALU.mult),
                         r=[OS[f], ST[f], GGB[ti]], w=[BTM[f]])
                    P.op("pe", lambda e: e.transpose(out=pst[:, 256:384], in_=btm[f][:], identity=ident_b[:]),
                         r=[BTM[f], B_const], w=[PSTb])
                    P.op("act", lambda e: e.activation(out=bT[hp][:, ti * 128:(ti + 1) * 128], in_=pst[:, 256:384], func=AF.Copy),
                         r=[PSTb], w=[BT[hp]])

            for h in range(cfg.HG_HEADS):
                hp = h % 2
                load_w(stg, SB, wbf[hp][:], G["wB_in"][e_idx, h], KC, 640, WB[hp], kq=1)
                load_lb(h)
                P.op("pool", lambda e: e.memset(S2[0][spar[0]][:], 0.0), w=[SS2[0][spar[0]]])
                P.op("pool", lambda e: e.memset(S2[1][spar[1]][:], 0.0), w=[SS2[1][spar[1]]])
                pbase = cnt["p"]
                cnt["p"] += NT

                def inproj(ti):
                    p = (pbase + ti) % 2
                    for kc in range(KC):
                        P.op("pe", lambda e, kc=kc: e.matmul(
                            psp[p][0][:, :], hT[:, kc, ti * 128:(ti + 1) * 128], wbf[hp][:, kc, 0:512],
                            start=(kc == 0), stop=(kc == KC - 1)), r=[HT[ti], WB[hp]], w=[PSP[p][0]])
                    for kc in range(KC):
                        P.op("pe", lambda e, kc=kc: e.matmul(
                            psp[p][1][:, 0:128], hT[:, kc, ti * 128:(ti + 1) * 128], wbf[hp][:, kc, 512:640],
                            start=(kc == 0), stop=(kc == KC - 1)), r=[HT[ti], WB[hp]], w=[PSP[p][1]])

                inproj(0)
                for ti in range(NT):
                    p = (pbase + ti) % 2
                    if ti + 1 < NT:
                        inproj(ti + 1)
                    tile_common(h, hp, ti, p)
                    prep(h, hp, ti, p, 0, ti % 2)
                    prep(h, hp, ti, p, 1, ti)
                    step(h, hp, ti, 0, ti % 2)
                order_b = list(range(NT_C - 1, -1, -1)) + list(range(NT - 1, NT_C - 1, -1))
                for ti in order_b:
                    step(h, hp, ti, 1, ti)
                tlo = 0 if need_ctx else TC
                ch = cfg.DA_HEADS + h
                P.dma("sp", gscr[ch, :, tlo:T], bT[hp][:, tlo:T], r=[BT[hp]], w=[GS[ch]])

    def even_mixer(e_idx, l, s, need_ctx):
        attention_part(e_idx, l, s, need_ctx)
        hgrn_part(e_idx, l, s, need_ctx)

    return even_mixer


def prep_shared(cfg, inp):
    D, KC, R = cfg.D, cfg.KC, cfg.R
    f = lambda a: np.ascontiguousarray(a, dtype=np.float32)
    sh = {}
    sh["w_mod"] = f(inp["w_mod"])
    b_mod = np.asarray(inp["b_mod"], np.float32)
    bc = b_mod[:, :2 * D].reshape(cfg.DEPTH, 2 * KC, 128).transpose(2, 0, 1)
    sh["bmodc"] = f(np.repeat(bc[:, :, :, None], R, axis=3))
    gp = np.asarray(inp["g_pre"], np.float32).reshape(cfg.DEPTH, KC, 128).transpose(2, 0, 1)
    sh["gprec"] = f(np.repeat(gp[:, :, :, None], R, axis=3))
    sh["bmodg"] = f(np.repeat(b_mod[None, :, 2 * D:], R, axis=0))
    sh["gpostr"] = f(np.repeat(np.asarray(inp["g_post"], np.float32)[None], R, axis=0))
    sel = np.zeros((R, R * 128), np.float32)
    for r in range(R):
        sel[r, r * 128:(r + 1) * 128] = 1.0
    sh["sel"] = sel
    sh["ident"] = np.eye(128, dtype=np.float32)
    W = np.asarray(inp["ev_w_in"], np.float32)
    NE = cfg.N_EVEN
    A = W[:, :, :4 * cfg.DA_W].reshape(NE, KC, 128, 4, cfg.DA_HEADS, 128)
    sh["wA"] = f(A.transpose(0, 4, 2, 1, 3, 5).reshape(NE, cfg.DA_HEADS, 128, KC, 512))
    B = W[:, :, 4 * cfg.DA_W:].reshape(NE, KC, 128, 5, cfg.HG_HEADS, 128)
    sh["wB"] = f(B.transpose(0, 4, 2, 1, 3, 5).reshape(NE, cfg.HG_HEADS, 128, KC, 640))
    sh["wEO"] = f(np.asarray(inp["ev_w_out"], np.float32).reshape(NE, KC, 128, D).transpose(0, 2, 1, 3))
    bcast = lambda a: f(np.broadcast_to(np.asarray(a, np.float32)[None], (128,) + tuple(np.shape(a))))
    sh["lamb"] = bcast(inp["ev_lambda"])
    sh["sublnb"] = bcast(inp["ev_subln_g"])
    sh["hgnb"] = bcast(inp["ev_hg_norm_g"])
    sh["lblb"] = bcast(inp["ev_hg_lb_logits"])
    sh["ropec"], sh["ropes"] = rope_tables(cfg)
    sh["hgM"], sh["hgmask"], sh["hgw"] = hgrn_consts()
    cm = np.zeros((128, 4), np.float32)
    for c in range(4):
        cm[c * 32:(c + 1) * 32, c] = 1.0
    sh["hgcm"] = cm
    NO = cfg.N_ODD
    PG, PW = cfg.PG, cfg.D
    Wo = np.asarray(inp["od_w_in"], np.float32)
    U = Wo[:, :, :PW].reshape(NO, KC, 128, 4, PG)
    Z = Wo[:, :, PW:].reshape(NO, KC, 128, 4, PG)
    UZ = np.concatenate([U, Z], axis=4)
    sh["wOD"] = f(UZ.transpose(0, 3, 2, 1, 4))
    WP = np.asarray(inp["od_w_pool"], np.float32).reshape(NO, 4, cfg.PGC, 128, PG)
    sh["wPL"] = f(WP.transpose(0, 1, 3, 2, 4))
    sh["wOO"] = f(np.asarray(inp["od_w_out"], np.float32).reshape(NO, KC, 128, D).transpose(0, 2, 1, 3))
    sh["odsc"] = f(np.asarray(inp["od_scale"], np.float32).reshape(NO, KC, 128).transpose(2, 0, 1))
    pc, maps = pool_consts(cfg)
    sh["poolc"] = f(pc)
    return sh, maps, pc.shape[0]


def prep_core(cfg, inp, sh, b0):
    NB, KC, R = cfg.NB, cfg.KC, cfg.R
    m = dict(sh)
    m["x"] = np.ascontiguousarray(inp["x"][b0:b0 + NB], dtype=np.float32)
    m["ctx"] = np.ascontiguousarray(inp["ctx"][b0:b0 + NB], dtype=np.float32)
    rows = np.concatenate([np.asarray(inp["c"], np.float32)[b0:b0 + NB], np.asarray(inp["c_ctx"], np.float32)[None]], 0)
    m["cT"] = np.ascontiguousarray(rows.reshape(R, KC, 128).transpose(2, 1, 0))
    return m


def kernel(**inputs):
    cfg = FULL
    ncores = 8
    sh, maps, nblk = prep_shared(cfg, inputs)
    nc = build(cfg, pool_maps=maps, n_pool_blocks=nblk)
    in_maps = [prep_core(cfg, inputs, sh, i * cfg.NB) for i in range(ncores)]
    res = run_bass_kernel_spmd(nc, in_maps, core_ids=list(range(ncores)))
    out = np.concatenate([np.asarray(r["y"], dtype=np.float32) for r in res.results], axis=0)
    return out
```

```python
import math
from contextlib import ExitStack, contextmanager
import numpy as np
import concourse.bass as bass
import concourse.mybir as mybir
from concourse.bass_utils import run_bass_kernel_spmd

F32 = mybir.dt.float32
BF16 = mybir.dt.bfloat16
AF = mybir.ActivationFunctionType
ALU = mybir.AluOpType
AX = mybir.AxisListType
EPS = 1e-6
ROPE_BASE = 10000.0
POOL_WINDOWS = (2, 4, 8, 16)
HC = 32


class Cfg:
    def __init__(s, D=2048, T_LAT=2048, T_CTX=256, GRID_W=64, DA_HEADS=8, HG_HEADS=8, NB=2, DEPTH=4):
        s.D, s.T_LAT, s.T_CTX, s.GRID_W = D, T_LAT, T_CTX, GRID_W
        s.DA_HEADS, s.HG_HEADS, s.NB, s.DEPTH = DA_HEADS, HG_HEADS, NB, DEPTH
        s.KC = D // 128
        s.NT_C = T_CTX // 128
        s.NT_L = T_LAT // 128
        s.NT = s.NT_C + s.NT_L
        s.T = T_CTX + T_LAT
        s.DA_W = DA_HEADS * 128
        s.HG_W = HG_HEADS * 128
        s.EVEN_IN = 4 * s.DA_W + 5 * s.HG_W
        s.EVEN_OUT = s.DA_W + s.HG_W
        assert s.EVEN_OUT == D
        s.PG = D // 4
        s.PGC = s.PG // 128
        s.R = NB + 1
        s.NBK = max(1, D // 512)
        s.BW = min(512, D)
        s.N_EVEN = (DEPTH + 1) // 2
        s.N_ODD = DEPTH // 2


FULL = Cfg()


class Ev:
    __slots__ = ("sem", "val", "ek")

    def __init__(s, sem, val, ek):
        s.sem, s.val, s.ek = sem, val, ek


class Buf:
    __slots__ = ("name", "w", "rs")

    def __init__(s, name):
        s.name, s.w, s.rs = name, None, {}


ENG = ("pe", "act", "dve", "pool", "sp")
SEM_LIMIT = 15000


class Prog:
    def __init__(s, nc, stack, n_dma=32):
        s.nc = nc
        s.stack = stack
        s.e = {"pe": nc.tensor, "act": nc.scalar, "dve": nc.vector, "pool": nc.gpsimd, "sp": nc.sync}
        nsem = {"pe": 12, "act": 5, "dve": 6, "pool": 5, "sp": 1}
        s.sems = {k: [stack.enter_context(nc.semaphore(f"s_{k}_{i}")) for i in range(n)] for k, n in nsem.items()}
        s.si = {k: 0 for k in ENG}
        s.cnt = {k: 0 for k in ENG}
        s.seen = {k: {} for k in ENG}
        s.dsems = [stack.enter_context(nc.semaphore(f"s_dma_{i}")) for i in range(n_dma)]
        s.dval = [0] * n_dma
        s.di = 0
        s.uid = 0
        s.nwait = 0
        s.nops = 0

    def _wait(s, ek, ev):
        seen = s.seen[ek]
        if seen.get(ev.sem, 0) >= ev.val:
            return
        s.e[ek].wait_ge(ev.sem, ev.val)
        seen[ev.sem] = ev.val
        s.nwait += 1

    def _deps(s, ek, r, w):
        for b in r:
            if b.w is not None and not (b.w.ek == ek and ek == "pe"):
                s._wait(ek, b.w)
        for b in w:
            if b.w is not None and not (b.w.ek == ek and ek == "pe"):
                s._wait(ek, b.w)
            for ev in b.rs.values():
                if ev.ek != ek:
                    s._wait(ek, ev)

    def _mark(s, ev, r, w):
        for b in r:
            b.rs[ev.sem] = ev
        for b in w:
            b.w = ev
            b.rs = {}

    def op(s, ek, fn, r=(), w=()):
        s._deps(ek, r, w)
        ins = fn(s.e[ek])
        if s.cnt[ek] >= SEM_LIMIT:
            s.si[ek] += 1
            s.cnt[ek] = 0
        s.cnt[ek] += 1
        sem = s.sems[ek][s.si[ek]]
        ins.then_inc(sem, 1)
        ev = Ev(sem, s.cnt[ek], ek)
        s._mark(ev, r, w)
        s.nops += 1
        return ev

    def dma(s, qk, out, in_, r=(), w=()):
        s._deps(qk, r, w)
        i = s.di
        s.di = (s.di + 1) % len(s.dsems)
        sem = s.dsems[i]
        if s.dval[i] > 0:
            s._wait(qk, Ev(sem, s.dval[i], None))
        s.dval[i] += 16
        s.e[qk].dma_start(out=out, in_=in_).then_inc(sem, 16)
        ev = Ev(sem, s.dval[i], None)
        s._mark(ev, r, w)
        s.nops += 1
        return ev

    def barrier(s, engines=ENG):
        evs = [Ev(s.sems[k][s.si[k]], s.cnt[k], k) for k in ENG if s.cnt[k] > 0]
        evs += [Ev(s.dsems[i], s.dval[i], None) for i in range(len(s.dsems)) if s.dval[i] > 0]
        for ek in engines:
            for ev in evs:
                if ev.ek != ek:
                    s._wait(ek, ev)

    def final_wait(s, ek="sp"):
        s.barrier(engines=(ek,))

    @contextmanager
    def scope(s):
        st = ExitStack()
        sc = Scope(s, st)
        try:
            yield sc
        finally:
            s.barrier()
            st.close()


class Scope:
    def __init__(s, P, st):
        s.P, s.st = P, st

    def sb(s, name, shape, dt):
        s.P.uid += 1
        return s.st.enter_context(s.P.nc.sbuf_tensor(f"{name}_{s.P.uid}", list(shape), dt))

    def ps(s, name, shape, dt):
        s.P.uid += 1
        return s.st.enter_context(s.P.nc.psum_tensor(f"{name}_{s.P.uid}", list(shape), dt))


def rope_tables(cfg):
    T = cfg.T_LAT
    t = np.arange(T)
    row, col = t // cfg.GRID_W, t % cfg.GRID_W
    nf = 16
    inv = ROPE_BASE ** (-np.arange(nf, dtype=np.float32) / nf)
    cosT = np.zeros((T, 64), np.float32)
    sinT = np.zeros((T, 64), np.float32)
    for blk, pos in ((0, row), (1, col)):
        ang = pos.astype(np.float32)[:, None] * inv[None, :]
        c, sn = np.cos(ang), np.sin(ang)
        cosT[:, blk * 32:blk * 32 + 16] = c
        cosT[:, blk * 32 + 16:blk * 32 + 32] = c
        sinT[:, blk * 32:blk * 32 + 16] = -sn
        sinT[:, blk * 32 + 16:blk * 32 + 32] = sn
    cos2 = np.concatenate([cosT, cosT], 1)
    sin2 = np.concatenate([sinT, sinT], 1)
    cos2 = cos2.reshape(cfg.NT_L, 128, 128).transpose(1, 0, 2).copy()
    sin2 = sin2.reshape(cfg.NT_L, 128, 128).transpose(1, 0, 2).copy()
    return cos2, sin2


def hgrn_consts():
    n = 128
    s = np.arange(n)[:, None]
    t = np.arange(n)[None, :]
    same = (s // HC) == (t // HC)
    mid_f = (t // HC) * HC + HC // 2 - 1
    mid_b = (t // HC) * HC + HC // 2
    M1f = (same & (s <= t)).astype(np.float32) - (same & (s <= mid_f)).astype(np.float32)
    M1b = (same & (s >= t)).astype(np.float32) - (same & (s >= mid_b)).astype(np.float32)
    maskf = (same & (s <= t)).astype(np.float32)
    maskb = (same & (s >= t)).astype(np.float32)
    nchunk = n // HC
    wf = np.zeros((n, nchunk * 3), np.float32)
    wb = np.zeros((n, nchunk * 3), np.float32)
    for c in range(nchunk):
        lo, hi = c * HC, (c + 1) * HC
        mf = lo + HC // 2 - 1
        mb = lo + HC // 2
        idx = np.arange(n)
        inck = (idx >= lo) & (idx < hi)
        wf[:, c * 3 + 0] = inck
        wf[:, c * 3 + 1] = inck & (idx > mf)
        wf[:, c * 3 + 2] = inck & (idx <= mf)
        wb[:, c * 3 + 0] = inck
        wb[:, c * 3 + 1] = inck & (idx < mb)
        wb[:, c * 3 + 2] = inck & (idx >= mb)
    return np.stack([M1f, M1b]), np.stack([maskf, maskb]), np.stack([wf, wb])


def pool_blocks(T):
    out = {}
    nt = T // 128
    for wi, w in enumerate(POOL_WINDOWS):
        t = np.arange(T)
        lo = np.clip(t - w // 2, 0, T)
        hi = np.clip(t + (w - w // 2), 0, T)
        M = np.zeros((T, T), np.float32)
        for tt in range(T):
            M[tt, lo[tt]:hi[tt]] = 1.0 / (hi[tt] - lo[tt])
            M[tt, tt] -= 1.0
        MT = M.T
        for ti in range(nt):
            for di in (-1, 0, 1):
                si = ti + di
                if 0 <= si < nt:
                    out[(wi, ti, di)] = MT[si * 128:(si + 1) * 128, ti * 128:(ti + 1) * 128].copy()
    return out


def pool_consts(cfg):
    uniq = []
    keys = {}
    maps = {}
    for seg, T in (("c", cfg.T_CTX), ("l", cfg.T_LAT)):
        blocks = pool_blocks(T)
        for k, blk in blocks.items():
            kb = blk.tobytes()
            if kb not in keys:
                keys[kb] = len(uniq)
                uniq.append(blk)
            maps[(seg,) + k] = keys[kb]
    return np.stack(uniq), maps


def build(cfg, layers=None, pool_maps=None, n_pool_blocks=0):
    layers = list(range(cfg.DEPTH)) if layers is None else layers
    D, KC, T, NT, NT_C, NT_L, R, NB = cfg.D, cfg.KC, cfg.T, cfg.NT, cfg.NT_C, cfg.NT_L, cfg.R, cfg.NB
    BW, NBK = cfg.BW, cfg.NBK
    nc = bass.Bass("TRN2", target_bir_lowering=False)

    def din(name, shape, dt=F32):
        return nc.dram_tensor(name, list(shape), dt, kind="ExternalInput").ap()

    x_in = din("x", [NB, cfg.T_LAT, D])
    ctx_in = din("ctx", [NB, cfg.T_CTX, D])
    cT_in = din("cT", [128, KC, R])
    wmod_in = din("w_mod", [cfg.DEPTH, D, 3 * D])
    bmodc_in = din("bmodc", [128, cfg.DEPTH, 2 * KC, R])
    gprec_in = din("gprec", [128, cfg.DEPTH, KC, R])
    bmodg_in = din("bmodg", [R, cfg.DEPTH, D])
    gpostr_in = din("gpostr", [R, cfg.DEPTH, D])
    sel_in = din("sel", [R, R * 128])
    ident_in = din("ident", [128, 128])
    wA_in = din("wA", [cfg.N_EVEN, cfg.DA_HEADS, 128, KC, 512])
    wB_in = din("wB", [cfg.N_EVEN, cfg.HG_HEADS, 128, KC, 640])
    wEO_in = din("wEO", [cfg.N_EVEN, 128, KC, D])
    lam_in = din("lamb", [128, cfg.N_EVEN, 4, 64])
    subln_in = din("sublnb", [128, cfg.N_EVEN, 128])
    hgn_in = din("hgnb", [128, cfg.N_EVEN, 128])
    lbl_in = din("lblb", [128, cfg.N_EVEN, 2, cfg.HG_W])
    cos_in = din("ropec", [128, NT_L, 128])
    sin_in = din("ropes", [128, NT_L, 128])
    hgM_in = din("hgM", [2, 128, 128])
    hgmask_in = din("hgmask", [2, 128, 128])
    hgw_in = din("hgw", [2, 128, 12])
    hgcm_in = din("hgcm", [128, 4])
    wOD_in = din("wOD", [cfg.N_ODD, 4, 128, KC, 2 * cfg.PG])
    wPL_in = din("wPL", [cfg.N_ODD, 4, 128, cfg.PGC, cfg.PG])
    wOO_in = din("wOO", [cfg.N_ODD, 128, KC, D])
    odsc_in = din("odsc", [128, cfg.N_ODD, KC])
    poolc_in = din("poolc", [max(1, n_pool_blocks), 128, 128])

    y_out = nc.dram_tensor("y", [NB, cfg.T_LAT, D], F32, kind="ExternalOutput").ap()
    cscr = nc.dram_tensor("cscr", [NB, cfg.T_CTX, D], F32, kind="Internal").ap()
    gscr = nc.dram_tensor("gscr", [KC, 128, T], BF16, kind="Internal").ap()

    with ExitStack() as stack:
        P = Prog(nc, stack)
        top = Scope(P, stack)

        hT = top.sb("hT", [128, KC, T], BF16)
        HT = [Buf(f"HT{i}") for i in range(NT)]
        ident_f = top.sb("identf", [128, 128], F32)
        ident_b = top.sb("identb", [128, 128], BF16)
        scT = top.sb("scT", [128, KC, R], F32)
        selT = top.sb("selT", [R, R * 128], F32)
        modc = top.sb("modc", [128, 2 * KC, R], F32)
        Acol = top.sb("Acol", [128, KC, R], F32)
        GTrow = top.sb("GTrow", [R, D], F32)
        bmodc = top.sb("bmodcs", [128, cfg.DEPTH, 2 * KC, R], F32)
        gprec = top.sb("gprecs", [128, cfg.DEPTH, KC, R], F32)
        B_const = Buf("const")
        B_scT = Buf("scT")
        B_modc = Buf("modc")
        B_Acol = Buf("Acol")
        B_GTrow = Buf("GTrow")
        DXL = [[Buf(f"dxl{s}_{i}") for i in range(NT_L)] for s in range(NB)]
        DXC = [[Buf(f"dxc{s}_{i}") for i in range(NT_C)] for s in range(NB)]
        GS = [Buf(f"gs{k}") for k in range(KC)]

        P.dma("sp", ident_f[:], ident_in[:, :], w=[B_const])
        P.dma("sp", scT[:], cT_in[:, :, :], w=[B_scT])
        P.dma("sp", selT[:], sel_in[:, :], w=[B_const])
        P.dma("sp", bmodc[:], bmodc_in[:, :, :, :], w=[B_const])
        P.dma("sp", gprec[:], gprec_in[:, :, :, :], w=[B_const])
        P.op("dve", lambda e: e.tensor_copy(ident_b[:], ident_f[:]), r=[B_const], w=[B_const])
        P.op("act", lambda e: e.activation(out=scT[:], in_=scT[:], func=AF.Silu), r=[B_scT], w=[B_scT])

        st_state = {"i": 0}

        def load_w(sc_stage, SB, dst, src, nk, ncols, wbuf, kq=4):
            for k0 in range(0, nk, kq):
                k1 = min(nk, k0 + kq)
                i = st_state["i"] % 2
                st_state["i"] += 1
                stg = sc_stage[i]
                v = stg[:, 0:(k1 - k0) * ncols].rearrange("p (k n) -> p k n", n=ncols)
                P.dma("sp", v, src[:, k0:k1, :], w=[SB[i]])
                P.op("pool", lambda e, v=v, k0=k0, k1=k1: e.tensor_copy(dst[:, k0:k1, :], v), r=[SB[i]], w=[wbuf])

        def mod_phase(l):
            with P.scope() as sc:
                wm = [sc.sb("wm", [128, KC, 512], F32) for _ in range(2)]
                WM = [Buf("wm0"), Buf("wm1")]
                bg = sc.sb("bg", [R, D], F32)
                gp = sc.sb("gp", [R, D], F32)
                B_bg = Buf("bg")
                psA = sc.ps("psA", [128, 512], F32)
                psB = sc.ps("psB", [128, 512], F32)
                PSA, PSB = Buf("psA"), Buf("psB")
                P.dma("sp", bg[:], bmodg_in[:, l, :], w=[B_bg])
                P.dma("sp", gp[:], gpostr_in[:, l, :], w=[B_bg])
                wsrc = wmod_in[l].rearrange("(kc p) n -> p kc n", p=128)
                ncolp = (2 * D) // 512
                for piece in range((3 * D) // 512):
                    sl = piece % 2
                    n0 = piece * 512
                    P.dma("sp", wm[sl][:], wsrc[:, :, n0:n0 + 512], w=[WM[sl]])
                    if piece < ncolp:
                        for jj in range(4):
                            for kc in range(KC):
                                P.op("pe", lambda e, jj=jj, kc=kc, sl=sl: e.matmul(
                                    psA[:, jj * R:(jj + 1) * R], wm[sl][:, kc, jj * 128:(jj + 1) * 128], scT[:, kc, :],
                                    start=(jj == 0 and kc == 0), stop=(jj == 3 and kc == KC - 1)),
                                    r=[WM[sl], B_scT], w=[PSA])
                        j0 = piece * 4
                        P.op("dve", lambda e, j0=j0: e.tensor_tensor(
                            out=modc[:, j0:j0 + 4, :], in0=psA[:, 0:4 * R].rearrange("p (j r) -> p j r", r=R),
                            in1=bmodc[:, l, j0:j0 + 4, :], op=ALU.add), r=[PSA, B_const], w=[B_modc])
                    else:
                        nb = piece - ncolp
                        for kc in range(KC):
                            P.op("pe", lambda e, kc=kc, sl=sl: e.matmul(
                                psB[0:R, :], scT[:, kc, :], wm[sl][:, kc, :], start=(kc == 0), stop=(kc == KC - 1)),
                                r=[WM[sl], B_scT], w=[PSB])
                        P.op("dve", lambda e, nb=nb: e.tensor_tensor(
                            out=GTrow[0:R, nb * 512:(nb + 1) * 512], in0=psB[0:R, :],
                            in1=bg[0:R, nb * 512:(nb + 1) * 512], op=ALU.add), r=[PSB, B_bg], w=[B_GTrow])
                        P.op("dve", lambda e, nb=nb: e.tensor_tensor(
                            out=GTrow[0:R, nb * 512:(nb + 1) * 512], in0=GTrow[0:R, nb * 512:(nb + 1) * 512],
                            in1=gp[0:R, nb * 512:(nb + 1) * 512], op=ALU.mult), r=[B_GTrow, B_bg], w=[B_GTrow])
                P.op("dve", lambda e: e.scalar_tensor_tensor(
                    out=Acol[:], in0=modc[:, KC:2 * KC, :], scalar=1.0, in1=gprec[:, l, :, :],
                    op0=ALU.add, op1=ALU.mult), r=[B_modc, B_const], w=[B_Acol])

        def x_src(l, s, ti):
            if ti < NT_C:
                src = ctx_in if l == layers[0] else cscr
                return src[s, ti * 128:(ti + 1) * 128, :], DXC[s][ti]
            tl = ti - NT_C
            src = x_in if l == layers[0] else y_out
            return src[s, tl * 128:(tl + 1) * 128, :], DXL[s][tl]

        def phase_n(l, s, ctx_active):
            with P.scope() as sc:
                xt = [sc.sb("xt", [128, D], F32) for _ in range(2)]
                xh = [sc.sb("xh", [128, D], BF16) for _ in range(2)]
                junk = sc.sb("junk", [128, D], BF16)
                stt = [sc.sb("stt", [128, 4], F32) for _ in range(2)]
                pst = [sc.ps("pst", [128, KC * 128], BF16) for _ in range(2)]
                XT = [Buf("xt0"), Buf("xt1")]
                XH = [Buf("xh0"), Buf("xh1")]
                ST = [Buf("st0"), Buf("st1")]
                PST = [Buf("pst0"), Buf("pst1")]
                JK = Buf("junk")
                tiles = list(range(NT)) if ctx_active else list(range(NT_C, NT))
                for n, ti in enumerate(tiles):
                    p = n % 2
                    r = R - 1 if ti < NT_C else s
                    src, dbuf = x_src(l, s, ti)
                    P.dma("sp", xt[p][:], src, r=[dbuf], w=[XT[p]])
                    P.op("act", lambda e, p=p: e.activation(out=junk[:], in_=xt[p][:], func=AF.Square,
                                                            accum_out=stt[p][:, 0:1]), r=[XT[p]], w=[JK, ST[p]])
                    P.op("dve", lambda e, p=p: e.tensor_scalar(out=stt[p][:, 1:2], in0=stt[p][:, 0:1], scalar1=1.0 / D,
                                                               scalar2=EPS, op0=ALU.mult, op1=ALU.add), r=[ST[p]], w=[ST[p]])
                    P.op("act", lambda e, p=p: e.activation(out=stt[p][:, 2:3], in_=stt[p][:, 1:2], func=AF.Ln),
                         r=[ST[p]], w=[ST[p]])
                    P.op("act", lambda e, p=p: e.activation(out=stt[p][:, 3:4], in_=stt[p][:, 2:3], func=AF.Exp, scale=-0.5),
                         r=[ST[p]], w=[ST[p]])
                    P.op("dve", lambda e, p=p: e.tensor_scalar(out=xh[p][:], in0=xt[p][:], scalar1=stt[p][:, 3:4],
                                                               scalar2=None, op0=ALU.mult), r=[XT[p], ST[p]], w=[XH[p]])
                    for kc in range(KC):
                        P.op("pe", lambda e, p=p, kc=kc: e.transpose(
                            out=pst[p][:, kc * 128:(kc + 1) * 128], in_=xh[p][:, kc * 128:(kc + 1) * 128],
                            identity=ident_b[:]), r=[XH[p], B_const], w=[PST[p]])
                    for kc in range(KC):
                        ek = "act" if kc % 2 == 0 else "dve"
                        if ek == "act":
                            P.op("act", lambda e, p=p, kc=kc, ti=ti, r=r: e.activation(
                                out=hT[:, kc, ti * 128:(ti + 1) * 128], in_=pst[p][:, kc * 128:(kc + 1) * 128],
                                func=AF.Identity, scale=Acol[:, kc, r:r + 1], bias=modc[:, kc, r:r + 1]),
                                r=[PST[p], B_Acol, B_modc], w=[HT[ti]])
                        else:
                            P.op("dve", lambda e, p=p, kc=kc, ti=ti, r=r: e.tensor_scalar(
                                out=hT[:, kc, ti * 128:(ti + 1) * 128], in0=pst[p][:, kc * 128:(kc + 1) * 128],
                                scalar1=Acol[:, kc, r:r + 1], scalar2=modc[:, kc, r:r + 1], op0=ALU.mult, op1=ALU.add),
                                r=[PST[p], B_Acol, B_modc], w=[HT[ti]])

        def phase_o(l, s, wo_src, need_ctx):
            with P.scope() as sc:
                wO = sc.sb("wO", [128, KC, D], BF16)
                WO = Buf("wO")
                stg = [sc.sb("stg", [128, 2 * 512], F32) for _ in range(2)]
                SB = [Buf("stg0"), Buf("stg1")]
                GTb = [sc.sb("GTb", [128, D], F32) for _ in range(2)]
                B_GTb = Buf("GTb")
                xt = [sc.sb("xo", [128, D], F32) for _ in range(2)]
                tt = sc.sb("tt", [128, D], F32)
                junk = sc.sb("junko", [128, NBK, BW], BF16)
                stt = [sc.sb("stto", [128, 8], F32) for _ in range(2)]
                psy = [[sc.ps("psy", [128, 512], F32) for _ in range(NBK)] for _ in range(2 if NBK <= 4 else 1)]
                PSY = [[Buf("psy") for _ in range(NBK)] for _ in range(len(psy))]
                XT = [Buf("xo0"), Buf("xo1")]
                TT = Buf("tt")
                JKS = [Buf(f"jko{i}") for i in range(NBK)]
                ST = [Buf("sto0"), Buf("sto1")]
                tlo = 0 if need_ctx else NT_C * 128
                for kc in range(KC):
                    P.dma("sp", hT[:, kc, tlo:T], gscr[kc, :, tlo:T], r=[GS[kc]], w=HT)
                for nb in range(NBK):
                    load_w(stg, SB, wO[:, :, nb * BW:(nb + 1) * BW], wo_src[:, :, nb * BW:(nb + 1) * BW], KC, BW, WO, kq=2)
                rows = [s, R - 1] if need_ctx else [s]
                for gi, r in enumerate(rows):
                    for nb in range(NBK):
                        P.op("pe", lambda e, r=r, nb=nb: e.matmul(
                            psy[0][nb][:, 0:BW], selT[0:R, r * 128:(r + 1) * 128], GTrow[0:R, nb * BW:(nb + 1) * BW],
                            start=True, stop=True), r=[B_const, B_GTrow], w=[PSY[0][nb]])
                        P.op("act", lambda e, gi=gi, nb=nb: e.activation(
                            out=GTb[gi][:, nb * BW:(nb + 1) * BW], in_=psy[0][nb][:, 0:BW], func=AF.Copy),
                            r=[PSY[0][nb]], w=[B_GTb])
                tiles = list(range(NT)) if need_ctx else list(range(NT_C, NT))
                for n, ti in enumerate(tiles):
                    p = n % 2
                    pp = n % len(psy)
                    gi = 1 if ti < NT_C else 0
                    src, dbuf = x_src(l, s, ti)
                    P.dma("sp", xt[p][:], src, r=[dbuf], w=[XT[p]])
                    for nb in range(NBK):
                        for kc in range(KC):
                            P.op("pe", lambda e, pp=pp, nb=nb, kc=kc, ti=ti: e.matmul(
                                psy[pp][nb][:, 0:BW], hT[:, kc, ti * 128:(ti + 1) * 128],
                                wO[:, kc, nb * BW:(nb + 1) * BW], start=(kc == 0), stop=(kc == KC - 1)),
                                r=[HT[ti], WO], w=[PSY[pp][nb]])
                    for nb in range(NBK):
                        P.op("act", lambda e, p=p, pp=pp, nb=nb: e.activation(
                            out=junk[:, nb, :], in_=psy[pp][nb][:, 0:BW], func=AF.Square, accum_out=stt[p][:, nb:nb + 1]),
                            r=[PSY[pp][nb]], w=[JKS[nb], ST[p]])
                    P.op("dve", lambda e, p=p: e.reduce_sum(out=stt[p][:, 4:5], in_=stt[p][:, 0:NBK], axis=AX.X),
                         r=[ST[p]], w=[ST[p]])
                    P.op("dve", lambda e, p=p: e.tensor_scalar(out=stt[p][:, 5:6], in0=stt[p][:, 4:5], scalar1=1.0 / D,
                                                               scalar2=EPS, op0=ALU.mult, op1=ALU.add), r=[ST[p]], w=[ST[p]])
                    P.op("act", lambda e, p=p: e.activation(out=stt[p][:, 6:7], in_=stt[p][:, 5:6], func=AF.Ln),
                         r=[ST[p]], w=[ST[p]])
                    P.op("act", lambda e, p=p: e.activation(out=stt[p][:, 7:8], in_=stt[p][:, 6:7], func=AF.Exp, scale=-0.5),
                         r=[ST[p]], w=[ST[p]])
                    for nb in range(NBK):
                        P.op("dve", lambda e, p=p, pp=pp, nb=nb, gi=gi: e.scalar_tensor_tensor(
                            out=tt[:, nb * BW:(nb + 1) * BW], in0=psy[pp][nb][:, 0:BW], scalar=stt[p][:, 7:8],
                            in1=GTb[gi][:, nb * BW:(nb + 1) * BW], op0=ALU.mult, op1=ALU.mult),
                            r=[PSY[pp][nb], ST[p], B_GTb], w=[TT])
                    P.op("pool", lambda e, p=p: e.tensor_tensor(out=xt[p][:], in0=tt[:], in1=xt[p][:], op=ALU.add),
                         r=[TT, XT[p]], w=[XT[p]])
                    if ti < NT_C:
                        dst, dbuf2 = cscr[s, ti * 128:(ti + 1) * 128, :], DXC[s][ti]
                    else:
                        tl = ti - NT_C
                        dst, dbuf2 = y_out[s, tl * 128:(tl + 1) * 128, :], DXL[s][tl]
                    P.dma("sp", dst, xt[p][:], r=[XT[p]], w=[dbuf2])

        def odd_mixer(o, s, ctx_active):
            PG, PGC = cfg.PG, cfg.PGC
            segs = ([("c", 0, NT_C)] if ctx_active else []) + [("l", NT_C, NT_L)]
            with P.scope() as sc:
                stg = [sc.sb("stg", [128, 4 * 640], F32) for _ in range(2)]
                SB = [Buf("stg0"), Buf("stg1")]
                wU = sc.sb("wU", [128, KC, PG], BF16)
                wZ = sc.sb("wZ", [128, KC, PG], BF16)
                wP = sc.sb("wP", [128, PGC, PG], BF16)
                WU, WZ, WP = Buf("wU"), Buf("wZ"), Buf("wP")
                pcf = sc.sb("pcf", [128, n_pool_blocks, 128], F32)
                pcb = sc.sb("pcb", [128, n_pool_blocks, 128], BF16)
                B_pc = Buf("pc")
                lsc = sc.sb("lsc", [128, KC], F32)
                utm = sc.sb("utm", [128, NT, PG], BF16)
                UT = [Buf(f"ut{i}") for i in range(NT)]
                rT = sc.sb("rT", [128, PGC, T], BF16)
                RT = [Buf(f"rt{i}") for i in range(NT)]
                sz = [sc.sb("sz", [128, 512], BF16) for _ in range(2)]
                SZ = [Buf("sz0"), Buf("sz1")]
                gch = [sc.sb("gch", [128, T], BF16) for _ in range(2)]
                GCH = [Buf("gch0"), Buf("gch1")]
                psu = [sc.ps("psu", [128, 512], F32) for _ in range(2)]
                PSU = [Buf("psu0"), Buf("psu1")]
                psr = [sc.ps("psr", [128, 512], F32) for _ in range(2)]
                PSR = [Buf("psr0"), Buf("psr1")]
                psq = [sc.ps("psq", [128, 512], F32) for _ in range(2)]
                PSQ = [Buf("psq0"), Buf("psq1")]
                psz = [sc.ps("psz", [128, 512], F32) for _ in range(2)]
                PSZ = [Buf("psz0"), Buf("psz1")]
                P.dma("sp", pcf[:], poolc_in.rearrange("n p c -> p n c"), w=[B_pc])
                P.op("dve", lambda e: e.tensor_copy(pcb[:], pcf[:]), r=[B_pc], w=[B_pc])
                P.dma("sp", lsc[:], odsc_in[:, o, :], w=[B_pc])
                cu = cr = cq = 0
                gcount = 0
                for j in range(4):
                    load_w(stg, SB, wU[:], wOD_in[o, j][:, :, 0:PG], KC, PG, WU)
                    load_w(stg, SB, wZ[:], wOD_in[o, j][:, :, PG:2 * PG], KC, PG, WZ)
                    load_w(stg, SB, wP[:], wPL_in[o, j], PGC, PG, WP)
                    tiles = list(range(NT)) if ctx_active else list(range(NT_C, NT))
                    for ti in tiles:
                        for n0 in range(0, PG, 512):
                            nw = min(512, PG - n0)
                            p = cu % 2
                            cu += 1
                            for kc in range(KC):
                                P.op("pe", lambda e, p=p, kc=kc, ti=ti, n0=n0, nw=nw: e.matmul(
                                    psu[p][:, 0:nw], hT[:, kc, ti * 128:(ti + 1) * 128], wU[:, kc, n0:n0 + nw],
                                    start=(kc == 0), stop=(kc == KC - 1)), r=[HT[ti], WU], w=[PSU[p]])
                            P.op("act", lambda e, p=p, ti=ti, n0=n0, nw=nw: e.activation(
                                out=utm[:, ti, n0:n0 + nw], in_=psu[p][:, 0:nw], func=AF.Copy), r=[PSU[p]], w=[UT[ti]])
                    for (seg, t0, nts) in segs:
                        for fc in range(PGC):
                            for tb in range(0, nts, 4):
                                ntb = min(4, nts - tb)
                                p = cr % 2
                                cr += 1
                                for tq in range(ntb):
                                    tl = tb + tq
                                    dis = [di for di in (-1, 0, 1) if 0 <= tl + di < nts]
                                    for ii, di in enumerate(dis):
                                        bi = pool_maps[(seg, j, tl, di)]
                                        first = (tq == 0 and ii == 0)
                                        last = (tq == ntb - 1 and ii == len(dis) - 1)
                                        P.op("pe", lambda e, p=p, tq=tq, fc=fc, bi=bi, sti=t0 + tl + di, first=first, last=last:
                                             e.matmul(psr[p][:, tq * 128:(tq + 1) * 128],
                                                      utm[:, sti, fc * 128:(fc + 1) * 128], pcb[:, bi, :],
                                                      start=first, stop=last),
                                             r=[UT[t0 + tl + di], B_pc], w=[PSR[p]])
                                tg0 = (t0 + tb) * 128
                                P.op("dve", lambda e, p=p, fc=fc, tg0=tg0, ntb=ntb: e.tensor_copy(
                                    rT[:, fc, tg0:tg0 + ntb * 128], psr[p][:, 0:ntb * 128]),
                                    r=[PSR[p]], w=[RT[t0 + tb + q] for q in range(ntb)])
                    for fc in range(PGC):
                        gch_i = gcount % 2
                        gcount += 1
                        chunk = j * PGC + fc
                        for (seg, t0, nts) in segs:
                            for tb in range(0, nts, 4):
                                ntb = min(4, nts - tb)
                                nw = ntb * 128
                                tg0 = (t0 + tb) * 128
                                p = cq % 2
                                cq += 1
                                tbufs = [t0 + tb + q for q in range(ntb)]
                                for kc2 in range(PGC):
                                    P.op("pe", lambda e, p=p, kc2=kc2, fc=fc, tg0=tg0, nw=nw: e.matmul(
                                        psq[p][:, 0:nw], wP[:, kc2, fc * 128:(fc + 1) * 128], rT[:, kc2, tg0:tg0 + nw],
                                        start=(kc2 == 0), stop=(kc2 == PGC - 1)),
                                        r=[WP] + [RT[q] for q in tbufs], w=[PSQ[p]])
                                for kc in range(KC):
                                    P.op("pe", lambda e, p=p, kc=kc, fc=fc, tg0=tg0, nw=nw: e.matmul(
                                        psz[p][:, 0:nw], wZ[:, kc, fc * 128:(fc + 1) * 128], hT[:, kc, tg0:tg0 + nw],
                                        start=(kc == 0), stop=(kc == KC - 1)),
                                        r=[WZ] + [HT[q] for q in tbufs], w=[PSZ[p]])
                                P.op("act", lambda e, p=p, nw=nw: e.activation(out=sz[p][:, 0:nw], in_=psz[p][:, 0:nw],
                                                                                func=AF.Silu), r=[PSZ[p]], w=[SZ[p]])
                                P.op("dve", lambda e, p=p, nw=nw, tg0=tg0, gch_i=gch_i, chunk=chunk: e.scalar_tensor_tensor(
                                    out=gch[gch_i][:, tg0:tg0 + nw], in0=psq[p][:, 0:nw], scalar=lsc[:, chunk:chunk + 1],
                                    in1=sz[p][:, 0:nw], op0=ALU.mult, op1=ALU.mult),
                                    r=[PSQ[p], SZ[p], B_pc], w=[GCH[gch_i]])
                        tlo = 0 if ctx_active else NT_C * 128
                        P.dma("sp", gscr[chunk, :, tlo:T], gch[gch_i][:, tlo:T], r=[GCH[gch_i]], w=[GS[chunk]])

        even_mixer = make_even_mixer(cfg, nc, P, dict(
            hT=hT, HT=HT, ident_b=ident_b, B_const=B_const, gscr=gscr, GS=GS, load_w=load_w,
            wA_in=wA_in, wB_in=wB_in, lam_in=lam_in, subln_in=subln_in, hgn_in=hgn_in, lbl_in=lbl_in,
            cos_in=cos_in, sin_in=sin_in, hgM_in=hgM_in, hgmask_in=hgmask_in, hgw_in=hgw_in, hgcm_in=hgcm_in))

        for l in layers:
            even = (l % 2 == 0)
            need_ctx = l < cfg.DEPTH - 1
            ctx_active = even or need_ctx
            mod_phase(l)
            for s in range(NB):
                phase_n(l, s, ctx_active)
                if even:
                    even_mixer(l // 2, l, s, need_ctx)
                    phase_o(l, s, wEO_in[l // 2], need_ctx)
                else:
                    odd_mixer(l // 2, s, ctx_active and need_ctx)
                    phase_o(l, s, wOO_in[l // 2], need_ctx)
        P.final_wait("sp")
        build.stats = (P.nops, P.nwait, dict(P.cnt), dict(P.si))
    return nc


def make_even_mixer(cfg, nc, P, G):
    D, KC, T, NT, NT_C, NT_L, R, NB = cfg.D, cfg.KC, cfg.T, cfg.NT, cfg.NT_C, cfg.NT_L, cfg.R, cfg.NB
    hT, HT, ident_b, B_const, gscr, GS, load_w = (G[k] for k in ("hT", "HT", "ident_b", "B_const", "gscr", "GS", "load_w"))
    TC = cfg.T_CTX

    def rstd_ops(stt, ST, c_ss, c_out, n):
        P.op("dve", lambda e: e.tensor_scalar(out=stt[:, c_out:c_out + 1], in0=stt[:, c_ss:c_ss + 1], scalar1=1.0 / n,
                                              scalar2=EPS, op0=ALU.mult, op1=ALU.add), r=[ST], w=[ST])
        P.op("act", lambda e: e.activation(out=stt[:, c_out:c_out + 1], in_=stt[:, c_out:c_out + 1], func=AF.Ln),
             r=[ST], w=[ST])
        P.op("act", lambda e: e.activation(out=stt[:, c_out:c_out + 1], in_=stt[:, c_out:c_out + 1], func=AF.Exp, scale=-0.5),
             r=[ST], w=[ST])

    def attention_part(e_idx, l, s, need_ctx):
        lam_init = 0.8 - 0.6 * math.exp(-0.3 * l)
        with P.scope() as sc:
            stg = [sc.sb("stg", [128, 4 * 512], F32) for _ in range(2)]
            SB = [Buf("stg0"), Buf("stg1")]
            wbf = [sc.sb("wbfA", [128, KC, 512], BF16) for _ in range(2)]
            WB = [Buf("wbA0"), Buf("wbA1")]
            cosT = sc.sb("cosT", [128, NT_L, 128], F32)
            sinT = sc.sb("sinT", [128, NT_L, 128], F32)
            B_rope = Buf("rope")
            lamt = sc.sb("lamt", [128, 4, 64], F32)
            lamw = sc.sb("lamw", [128, 2, 64], F32)
            lams = sc.sb("lams", [128, 8], F32)
            B_lam = Buf("lam")
            Gt = sc.sb("Gt", [128, 128], F32)
            B_G = Buf("G")
            QKT = sc.sb("QKT", [128, 2, T], BF16)
            QK = [Buf(f"qk{i}") for i in range(NT)]
            Vaug = sc.sb("Vaug", [128, NT, 130], BF16)
            VA = [Buf(f"va{i}") for i in range(NT)]
            GGt = sc.sb("GGt", [128, NT, 128], F32)
            GGB = [Buf(f"gg{i}") for i in range(NT)]
            qktm = [sc.sb("qktm", [128, 256], BF16) for _ in range(2)]
            QKTM = [Buf("qktm0"), Buf("qktm1")]
            t1 = [sc.sb("t1", [128, 256], F32) for _ in range(2)]
            t2 = [sc.sb("t2", [128, 256], F32) for _ in range(2)]
            T1 = [Buf("t10"), Buf("t11")]
            T2 = [Buf("t20"), Buf("t21")]
            eg = [sc.sb("eg", [128, 128], F32) for _ in range(2)]
            EG = [Buf("eg0"), Buf("eg1")]
            Et = [sc.sb("Et", [128, 512], BF16) for _ in range(3)]
            ET = [Buf(f"et{i}") for i in range(3)]
            stt4 = [sc.sb("stt4", [128, 8], F32) for _ in range(4)]
            ST4 = [Buf(f"st4{i}") for i in range(4)]
            o14 = [sc.sb("o14", [128, 128], F32) for _ in range(4)]
            O14 = [Buf(f"o14{i}") for i in range(4)]
            o24 = [sc.sb("o24", [128, 128], F32) for _ in range(4)]
            O24 = [Buf(f"o24{i}") for i in range(4)]
            junk4 = sc.sb("junk4", [128, 4, 128], BF16)
            JK4 = [Buf(f"jk4{i}") for i in range(4)]
            atm4 = [sc.sb("atm4", [128, 128], BF16) for _ in range(4)]
            ATM4 = [Buf(f"atm4{i}") for i in range(4)]
            aT = [sc.sb("aT", [128, T], BF16) for _ in range(2)]
            AT = [Buf("aT0"), Buf("aT1")]
            psp = [sc.ps("psp", [128, 512], F32) for _ in range(2)]
            PSP = [Buf("psp0"), Buf("psp1")]
            pss = [sc.ps("pss", [128, 512], F32) for _ in range(2)]
            PSS = [Buf("pss0"), Buf("pss1")]
            pso = [sc.ps("pso", [128, 512], F32) for _ in range(3)]
            PSO = [Buf("pso0"), Buf("pso1"), Buf("pso2")]
            pst = sc.ps("pstA", [128, 1024], BF16)
            PSTq = Buf("pstq")
            PSTa = PSTq

            P.dma("sp", cosT[:], G["cos_in"][:, :, :], w=[B_rope])
            P.dma("sp", sinT[:], G["sin_in"][:, :, :], w=[B_rope])
            P.dma("sp", lamt[:], G["lam_in"][:, e_idx, :, :], w=[B_lam])
            P.dma("sp", Gt[:], G["subln_in"][:, e_idx, :], w=[B_G])
            P.op("dve", lambda e: e.tensor_scalar(out=Gt[:], in0=Gt[:], scalar1=(1.0 - lam_init), scalar2=None,
                                                  op0=ALU.mult), r=[B_G], w=[B_G])
            P.op("dve", lambda e: e.tensor_tensor(out=lamw[:, 0, :], in0=lamt[:, 0, :], in1=lamt[:, 1, :], op=ALU.mult),
                 r=[B_lam], w=[B_lam])
            P.op("dve", lambda e: e.tensor_tensor(out=lamw[:, 1, :], in0=lamt[:, 2, :], in1=lamt[:, 3, :], op=ALU.mult),
                 r=[B_lam], w=[B_lam])
            P.op("dve", lambda e: e.reduce_sum(out=lams[:, 0:2], in_=lamw[:, :, :], axis=AX.X), r=[B_lam], w=[B_lam])
            P.op("act", lambda e: e.activation(out=lams[:, 2:4], in_=lams[:, 0:2], func=AF.Exp), r=[B_lam], w=[B_lam])
            P.op("dve", lambda e: e.tensor_tensor(out=lams[:, 4:5], in0=lams[:, 2:3], in1=lams[:, 3:4], op=ALU.subtract),
                 r=[B_lam], w=[B_lam])
            P.op("dve", lambda e: e.tensor_scalar(out=lams[:, 5:6], in0=lams[:, 4:5], scalar1=lam_init, scalar2=-1.0,
                                                  op0=ALU.add, op1=ALU.mult), r=[B_lam], w=[B_lam])
            P.op("pool", lambda e: e.memset(Vaug[:, :, 128:130], 1.0), w=VA)

            cnt = {"p": 0, "s": 0, "e": 0, "ep": 0}
            for h in range(cfg.DA_HEADS):
                hp = h % 2
                load_w(stg, SB, wbf[hp][:], G["wA_in"][e_idx, h], KC, 512, WB[hp])
                for ti in range(NT):
                    p = cnt["p"] % 2
                    cnt["p"] += 1
                    for kc in range(KC):
                        P.op("pe", lambda e, p=p, kc=kc, ti=ti, hp=hp: e.matmul(
                            psp[p][:, :], hT[:, kc, ti * 128:(ti + 1) * 128], wbf[hp][:, kc, :],
                            start=(kc == 0), stop=(kc == KC - 1)), r=[HT[ti], WB[hp]], w=[PSP[p]])
                    if ti >= NT_C:
                        tl = ti - NT_C
                        for c0 in (0, 128):
                            X = psp[p][:, c0:c0 + 128]
                            Xv = X.rearrange("p (a h i) -> p a h i", h=2, i=16)
                            Sv = sinT[:, tl, :].rearrange("p (a h i) -> p a h i", h=2, i=16)
                            t2v = t2[p][:, c0:c0 + 128].rearrange("p (a h i) -> p a h i", h=2, i=16)
                            P.op("dve", lambda e, p=p, c0=c0, X=X, tl=tl: e.tensor_tensor(
                                out=t1[p][:, c0:c0 + 128], in0=X, in1=cosT[:, tl, :], op=ALU.mult),
                                r=[PSP[p], B_rope], w=[T1[p]])
                            P.op("dve", lambda e, Xv=Xv, Sv=Sv, t2v=t2v: e.tensor_tensor(
                                out=t2v[:, :, 0, :], in0=Xv[:, :, 1, :], in1=Sv[:, :, 0, :], op=ALU.mult),
                                r=[PSP[p], B_rope], w=[T2[p]])
                            P.op("dve", lambda e, Xv=Xv, Sv=Sv, t2v=t2v: e.tensor_tensor(
                                out=t2v[:, :, 1, :], in0=Xv[:, :, 0, :], in1=Sv[:, :, 1, :], op=ALU.mult),
                                r=[PSP[p], B_rope], w=[T2[p]])
                        P.op("pool", lambda e, p=p: e.tensor_tensor(out=qktm[p][:], in0=t1[p][:], in1=t2[p][:], op=ALU.add),
                             r=[T1[p], T2[p]], w=[QKTM[p]])
                    else:
                        P.op("act", lambda e, p=p: e.activation(out=qktm[p][:], in_=psp[p][:, 0:256], func=AF.Copy),
                             r=[PSP[p]], w=[QKTM[p]])
                    for c in range(2):
                        P.op("pe", lambda e, p=p, c=c: e.transpose(out=pst[:, c * 128:(c + 1) * 128],
                                                                    in_=qktm[p][:, c * 128:(c + 1) * 128], identity=ident_b[:]),
                             r=[QKTM[p], B_const], w=[PSTq])
                    P.op("act", lambda e, ti=ti: e.activation(
                        out=QKT[:, :, ti * 128:(ti + 1) * 128], in_=pst[:, 0:256].rearrange("p (c t) -> p c t", c=2),
                        func=AF.Copy), r=[PSTq], w=[QK[ti]])
                    P.op("act", lambda e, p=p, ti=ti: e.activation(out=Vaug[:, ti, 0:128], in_=psp[p][:, 256:384], func=AF.Copy),
                         r=[PSP[p]], w=[VA[ti]])
                    P.op("act", lambda e, p=p: e.activation(out=eg[p][:], in_=psp[p][:, 384:512], func=AF.Exp, scale=-1.0),
                         r=[PSP[p]], w=[EG[p]])
                    P.op("act", lambda e, p=p: e.activation(out=eg[p][:], in_=eg[p][:], func=AF.Ln, bias=1.0), r=[EG[p]], w=[EG[p]])
                    P.op("act", lambda e, p=p: e.activation(out=eg[p][:], in_=eg[p][:], func=AF.Exp, scale=-1.0), r=[EG[p]], w=[EG[p]])
                    P.op("pool", lambda e, p=p: e.tensor_tensor(out=eg[p][:], in0=eg[p][:], in1=Gt[:], op=ALU.mult),
                         r=[EG[p], B_G], w=[EG[p]])
                    P.op("dve", lambda e, p=p, ti=ti: e.tensor_tensor(out=GGt[:, ti, :], in0=psp[p][:, 384:512], in1=eg[p][:],
                                                                      op=ALU.mult), r=[PSP[p], EG[p]], w=[GGB[ti]])

                blocks = []
                if need_ctx:
                    blocks.append((0, TC, list(range(NT_C))))
                for qb in range(0, cfg.T_LAT, 512):
                    blocks.append((TC + qb, min(512, cfg.T_LAT - qb), list(range(NT))))
                for (q0, nq, kts) in blocks:
                    nqs = nq // 128
                    nacc = 2 * nqs
                    nbank = (nacc + 2) // 3
                    firsts = {b: True for b in range(nbank)}
                    qtiles = [q0 // 128 + i for i in range(nqs)]
                    items = [(ki, kt, m) for ki, kt in enumerate(kts) for m in range(2)]
                    pslot = {}

                    def emit_score(j):
                        ki, kt, m = items[j]
                        p = cnt["s"] % 2
                        cnt["s"] += 1
                        pslot[j] = p
                        P.op("pe", lambda e: e.matmul(
                            pss[p][:, 0:nq], QKT[m * 64:(m + 1) * 64, 1, kt * 128:(kt + 1) * 128],
                            QKT[m * 64:(m + 1) * 64, 0, q0:q0 + nq], start=True, stop=True),
                            r=[QK[kt]] + [QK[q] for q in qtiles], w=[PSS[p]])

                    emit_score(0)
                    for j, (ki, kt, m) in enumerate(items):
                        if j + 1 < len(items):
                            emit_score(j + 1)
                        p = pslot[j]
                        ei = cnt["e"] % 3
                        cnt["e"] += 1
                        P.op("act", lambda e, p=p, ei=ei: e.activation(
                            out=Et[ei][:, 0:nq], in_=pss[p][:, 0:nq], func=AF.Exp, scale=0.125),
                            r=[PSS[p]], w=[ET[ei]])
                        for qs in range(nqs):
                            idx = m * nqs + qs
                            b, slot = idx // 3, idx % 3
                            is_first = firsts[b]
                            firsts[b] = False
                            last_idx_in_bank = min(nacc - 1, b * 3 + 2)
                            is_last = (ki == len(kts) - 1) and (idx == last_idx_in_bank)
                            P.op("pe", lambda e, ei=ei, qs=qs, kt=kt, b=b, slot=slot, is_first=is_first, is_last=is_last:
                                 e.matmul(pso[b][:, slot * 129:slot * 129 + 129], Et[ei][:, qs * 128:(qs + 1) * 128],
                                          Vaug[:, kt, 0:129], start=is_first, stop=is_last),
                                 r=[ET[ei], VA[kt]], w=[PSO[b]])
                    QS_ = list(range(nqs))
                    tis = [q0 // 128 + qs for qs in QS_]
                    loc = []
                    for qs in QS_:
                        i0_, i1_ = qs, nqs + qs
                        loc.append((i0_ // 3, (i0_ % 3) * 129, i1_ // 3, (i1_ % 3) * 129))
                    for qs in QS_:
                        b0, s0, b1, s1 = loc[qs]
                        P.op("dve", lambda e, qs=qs, b0=b0, s0=s0: e.reciprocal(out=stt4[qs][:, 0:1], in_=pso[b0][:, s0 + 128:s0 + 129]),
                             r=[PSO[b0]], w=[ST4[qs]])
                        P.op("dve", lambda e, qs=qs, b1=b1, s1=s1: e.reciprocal(out=stt4[qs][:, 1:2], in_=pso[b1][:, s1 + 128:s1 + 129]),
                             r=[PSO[b1]], w=[ST4[qs]])
                    for qs in QS_:
                        P.op("dve", lambda e, qs=qs: e.tensor_tensor(out=stt4[qs][:, 2:3], in0=stt4[qs][:, 1:2], in1=lams[:, 5:6],
                                                                     op=ALU.mult), r=[ST4[qs], B_lam], w=[ST4[qs]])
                    for qs in QS_:
                        b0, s0, b1, s1 = loc[qs]
                        P.op("dve", lambda e, qs=qs, b0=b0, s0=s0: e.tensor_scalar(
                            out=o14[qs][:], in0=pso[b0][:, s0:s0 + 128], scalar1=stt4[qs][:, 0:1], scalar2=None, op0=ALU.mult),
                            r=[PSO[b0], ST4[qs]], w=[O14[qs]])
                    for qs in QS_:
                        b0, s0, b1, s1 = loc[qs]
                        P.op("dve", lambda e, qs=qs, b1=b1, s1=s1: e.scalar_tensor_tensor(
                            out=o24[qs][:], in0=pso[b1][:, s1:s1 + 128], scalar=stt4[qs][:, 2:3], in1=o14[qs][:],
                            op0=ALU.mult, op1=ALU.add), r=[PSO[b1], ST4[qs], O14[qs]], w=[O24[qs]])
                    for qs in QS_:
                        P.op("act", lambda e, qs=qs: e.activation(out=junk4[:, qs, :], in_=o24[qs][:], func=AF.Square,
                                                                  accum_out=stt4[qs][:, 3:4]), r=[O24[qs]], w=[JK4[qs], ST4[qs]])
                    for qs in QS_:
                        P.op("dve", lambda e, qs=qs: e.tensor_scalar(out=stt4[qs][:, 4:5], in0=stt4[qs][:, 3:4], scalar1=1.0 / 128,
                                                                     scalar2=EPS, op0=ALU.mult, op1=ALU.add), r=[ST4[qs]], w=[ST4[qs]])
                    for qs in QS_:
                        P.op("act", lambda e, qs=qs: e.activation(out=stt4[qs][:, 4:5], in_=stt4[qs][:, 4:5], func=AF.Ln),
                             r=[ST4[qs]], w=[ST4[qs]])
                    for qs in QS_:
                        P.op("act", lambda e, qs=qs: e.activation(out=stt4[qs][:, 4:5], in_=stt4[qs][:, 4:5], func=AF.Exp, scale=-0.5),
                             r=[ST4[qs]], w=[ST4[qs]])
                    for qs in QS_:
                        P.op("dve", lambda e, qs=qs: e.scalar_tensor_tensor(
                            out=atm4[qs][:], in0=o24[qs][:], scalar=stt4[qs][:, 4:5], in1=GGt[:, tis[qs], :],
                            op0=ALU.mult, op1=ALU.mult), r=[O24[qs], ST4[qs], GGB[tis[qs]]], w=[ATM4[qs]])
                    for qs in QS_:
                        P.op("pe", lambda e, qs=qs: e.transpose(out=pst[:, 256 + qs * 128:384 + qs * 128], in_=atm4[qs][:],
                                                                identity=ident_b[:]), r=[ATM4[qs], B_const], w=[PSTa])
                    P.op("act", lambda e: e.activation(out=aT[hp][:, q0:q0 + nq], in_=pst[:, 256:256 + nq], func=AF.Copy),
                         r=[PSTa], w=[AT[hp]])
                tlo = 0 if need_ctx else TC
                P.dma("sp", gscr[h, :, tlo:T], aT[hp][:, tlo:T], r=[AT[hp]], w=[GS[h]])

    def hgrn_part(e_idx, l, s, need_ctx):
        HW = cfg.HG_W
        with P.scope() as sc:
            stg = [sc.sb("stg", [128, 640], F32) for _ in range(2)]
            SB = [Buf("stg0"), Buf("stg1")]
            wbf = [sc.sb("wbfB", [128, KC, 640], BF16) for _ in range(2)]
            WB = [Buf("wbB0"), Buf("wbB1")]
            LBt = sc.sb("LBt", [128, 2, 128], F32)
            OML = sc.sb("OML", [128, 2, 128], F32)
            B_lb = Buf("lb")
            hgG = sc.sb("hgG", [128, 128], F32)
            M1 = sc.sb("M1", [128, 2, 128], F32)
            mask = sc.sb("mask", [128, 2, 128], F32)
            wcol = sc.sb("wcol", [128, 2, 12], F32)
            B_hc = Buf("hgc")
            qs_t = [sc.sb("qs", [128, 128], F32) for _ in range(2)]
            QS = [Buf("qs0"), Buf("qs1")]
            vv = sc.sb("vv", [128, NT, 128], BF16)
            VV = [Buf(f"vv{i}") for i in range(NT)]
            GGt = sc.sb("GGb", [128, NT, 128], BF16)
            GGB = [Buf(f"ggb{i}") for i in range(NT)]
            of = sc.sb("of", [128, NT, 128], F32)
            OF = [Buf(f"of{i}") for i in range(NT)]
            nsl = [2, NT]
            qkT = [sc.sb("qkTh", [128, nsl[d], 2, 128], BF16) for d in range(2)]
            ktm = [sc.sb("ktm", [128, nsl[d], 128], BF16) for d in range(2)]
            ecs = [sc.sb("ecs", [128, nsl[d], 12], F32) for d in range(2)]
            PREP = [[Buf(f"prep{d}_{i}") for i in range(nsl[d])] for d in range(2)]
            sg = [sc.sb("sg", [128, 128], F32) for _ in range(2)]
            SG = [Buf("sg0"), Buf("sg1")]
            ff = [sc.sb("ff", [128, 128], F32) for _ in range(2)]
            FF = [Buf("ff0"), Buf("ff1")]
            lf = [sc.sb("lf", [128, 128], F32) for _ in range(2)]
            LF = [Buf("lf0"), Buf("lf1")]
            kk = [sc.sb("kk", [128, 128], F32) for _ in range(2)]
            KK = [Buf("kk0"), Buf("kk1")]
            ed = [sc.sb("ed", [128, 2, 128], F32) for _ in range(2)]
            ED = [Buf("ed0"), Buf("ed1")]
            qtl = [sc.sb("qtl", [128, 128], BF16) for _ in range(2)]
            ee3 = [sc.sb("ee3", [128, 512], F32) for _ in range(2)]
            L3 = [sc.sb("L3", [128, 512], F32) for _ in range(2)]
            s3 = [sc.sb("s3", [128, 512], F32) for _ in range(2)]
            EE3 = [Buf("ee30"), Buf("ee31")]
            LL3 = [Buf("L30"), Buf("L31")]
            SS3 = [Buf("s30"), Buf("s31")]
            QTL = [Buf("qtl0"), Buf("qtl1")]
            eg = [sc.sb("egb", [128, 128], F32) for _ in range(2)]
            EG = [Buf("egb0"), Buf("egb1")]
            SBF = [[sc.sb("Sbf", [128, 128], BF16) for _ in range(2)] for _ in range(2)]
            SS = [Buf("S0"), Buf("S1")]
            SSB = [[Buf("Sb00"), Buf("Sb01")], [Buf("Sb10"), Buf("Sb11")]]
            qz = [sc.sb("qz", [128, 640], BF16) for _ in range(2)]
            QZ = [Buf("qz0"), Buf("qz1")]
            cm = sc.sb("cm", [128, 4], F32)
            cmb = sc.sb("cmb", [128, 4, 128], BF16)
            ones_t = sc.sb("ones_t", [128, 128], F32)
            vblk = [sc.sb("vblk", [128, 4, 128], BF16) for _ in range(2)]
            VB = [Buf("vb0"), Buf("vb1")]
            attm = [sc.sb("attm", [128, 128], BF16) for _ in range(2)]
            ATT = [Buf("att0"), Buf("att1")]
            tmpu4 = sc.sb("tmpu4", [128, 4, 128], F32)
            TU4 = Buf("tu4")
            S2 = [[sc.sb("S2", [128, 128], F32) for _ in range(2)] for _ in range(2)]
            SS2 = [[Buf("S200"), Buf("S201")], [Buf("S210"), Buf("S211")]]
            spar = [0, 0]
            osum = [sc.sb("osum", [128, 128], F32) for _ in range(2)]
            OS = [Buf("os0"), Buf("os1")]
            stt = [sc.sb("stb", [128, 8], F32) for _ in range(2)]
            ST = [Buf("stb0"), Buf("stb1")]
            junk = sc.sb("junkb", [128, 128], BF16)
            JK = Buf("junkb")
            btm = [sc.sb("btm", [128, 128], BF16) for _ in range(2)]
            BTM = [Buf("btm0"), Buf("btm1")]
            bT = [sc.sb("bT", [128, T], BF16)] * 2
            BT = [Buf("bT0")] * 2
            psp = [[sc.ps("pspb", [128, 512], F32) for _ in range(2)] for _ in range(2)]
            PSP = [[Buf("pspb00"), Buf("pspb01")], [Buf("pspb10"), Buf("pspb11")]]
            psd = sc.ps("psd", [128, 512], F32)
            PSD = Buf("psd")
            psa = sc.ps("psa", [128, 512], F32)
            PSA = Buf("psa")
            PSOt = PSA
            psu = sc.ps("psu", [128, 512], F32)
            PSU = Buf("psu")
            pst = sc.ps("pstB", [128, 1024], BF16)
            PSTq = Buf("pstqb")
            PSTb = PSTq

            P.op("pool", lambda e: e.memset(ones_t[:], 1.0), w=[B_hc])
            P.dma("sp", hgG[:], G["hgn_in"][:, e_idx, :], w=[B_hc])
            P.dma("sp", M1[:], G["hgM_in"].rearrange("d s t -> s d t"), w=[B_hc])
            P.dma("sp", mask[:], G["hgmask_in"].rearrange("d s t -> s d t"), w=[B_hc])
            P.dma("sp", wcol[:], G["hgw_in"].rearrange("d s c -> s d c"), w=[B_hc])
            P.dma("sp", cm[:], G["hgcm_in"][:, :], w=[B_hc])
            for c in range(4):
                P.op("dve", lambda e, c=c: e.tensor_scalar(out=cmb[:, c, :], in0=ones_t[:], scalar1=cm[:, c:c + 1], scalar2=None,
                                                           op0=ALU.mult), r=[B_hc], w=[B_hc])
            for i in range(2):
                P.op("pool", lambda e, i=i: e.memset(qz[i][:], 0.0), w=[QZ[i]])
            def load_lb(h):
                if e_idx == 0:
                    if h == 0:
                        P.op("pool", lambda e: e.memset(LBt[:], 0.0), w=[B_lb])
                        P.op("pool", lambda e: e.memset(OML[:], 1.0), w=[B_lb])
                    return
                P.dma("sp", LBt[:], G["lbl_in"][:, 0, :, h * 128:(h + 1) * 128], w=[B_lb])
                P.dma("sp", OML[:], G["lbl_in"][:, 1, :, h * 128:(h + 1) * 128], w=[B_lb])
                P.op("dve", lambda e: e.tensor_tensor(out=LBt[:], in0=LBt[:], in1=OML[:], op=ALU.subtract), r=[B_lb], w=[B_lb])
                P.op("act", lambda e: e.activation(out=LBt[:], in_=LBt[:], func=AF.Exp), r=[B_lb], w=[B_lb])
                P.op("dve", lambda e: e.tensor_scalar(out=LBt[:], in0=LBt[:], scalar1=1.0, scalar2=None, op0=ALU.add),
                     r=[B_lb], w=[B_lb])
                P.op("dve", lambda e: e.reciprocal(out=LBt[:], in_=LBt[:]), r=[B_lb], w=[B_lb])
                P.op("dve", lambda e: e.tensor_scalar(out=OML[:], in0=LBt[:], scalar1=-1.0, scalar2=1.0, op0=ALU.mult,
                                                      op1=ALU.add), r=[B_lb], w=[B_lb])

            cnt = {"p": 0, "t": 0, "st": 0, "fin": 0}

            sgn = -1.0 if e_idx == 0 else 1.0

            def tile_common(h, hp, ti, p):
                P.op("act", lambda e: e.activation(out=ee3[p][:, 0:384], in_=psp[p][0][:, 0:384], func=AF.Exp, scale=-1.0),
                     r=[PSP[p][0]], w=[EE3[p]])
                P.op("act", lambda e: e.activation(out=ee3[p][:, 384:512], in_=psp[p][1][:, 0:128], func=AF.Exp, scale=-1.0),
                     r=[PSP[p][1]], w=[EE3[p]])
                P.op("act", lambda e: e.activation(out=L3[p][:], in_=ee3[p][:], func=AF.Ln, bias=1.0), r=[EE3[p]], w=[LL3[p]])
                P.op("act", lambda e: e.activation(out=s3[p][:], in_=L3[p][:], func=AF.Exp, scale=-1.0), r=[LL3[p]], w=[SS3[p]])
                P.op("dve", lambda e: e.tensor_tensor(out=qs_t[p][:], in0=psp[p][0][:, 0:128], in1=s3[p][:, 0:128], op=ALU.mult),
                     r=[PSP[p][0], SS3[p]], w=[QS[p]])
                P.op("act", lambda e: e.activation(out=vv[:, ti, :], in_=psp[p][0][:, 384:512], func=AF.Copy),
                     r=[PSP[p][0]], w=[VV[ti]])
                P.op("pool", lambda e: e.tensor_tensor(out=eg[p][:], in0=s3[p][:, 384:512], in1=hgG[:], op=ALU.mult),
                     r=[SS3[p], B_hc], w=[EG[p]])
                P.op("dve", lambda e: e.tensor_tensor(out=GGt[:, ti, :], in0=psp[p][1][:, 0:128], in1=eg[p][:], op=ALU.mult),
                     r=[PSP[p][1], EG[p]], w=[GGB[ti]])

            def prep(h, hp, ti, p, d, slot):
                k = cnt["t"] % 2
                cnt["t"] += 1
                c0 = 128 + d * 128
                if e_idx == 0:
                    lfa = L3[p][:, c0:c0 + 128]
                    LFB = LL3[p]
                    P.op("pool", lambda e: e.tensor_tensor(out=kk[k][:], in0=ee3[p][:, c0:c0 + 128], in1=s3[p][:, c0:c0 + 128],
                                                           op=ALU.mult), r=[EE3[p], SS3[p]], w=[KK[k]])
                else:
                    P.op("pool", lambda e: e.tensor_tensor(out=sg[k][:], in0=s3[p][:, c0:c0 + 128], in1=OML[:, d, :], op=ALU.mult),
                         r=[SS3[p], B_lb], w=[SG[k]])
                    P.op("pool", lambda e: e.tensor_tensor(out=ff[k][:], in0=sg[k][:], in1=LBt[:, d, :], op=ALU.add),
                         r=[SG[k], B_lb], w=[FF[k]])
                    P.op("pool", lambda e: e.tensor_tensor(out=kk[k][:], in0=OML[:, d, :], in1=sg[k][:], op=ALU.subtract),
                         r=[SG[k], B_lb], w=[KK[k]])
                    P.op("act", lambda e: e.activation(out=lf[k][:], in_=ff[k][:], func=AF.Ln), r=[FF[k]], w=[LF[k]])
                    lfa = lf[k][:]
                    LFB = LF[k]
                P.op("pe", lambda e: e.matmul(psd[:, 0:128], M1[:, d, :], lfa, start=True, stop=True),
                     r=[B_hc, LFB], w=[PSD])
                P.op("pe", lambda e: e.matmul(psd[:, 128:140], lfa, wcol[:, d, :], start=True, stop=True),
                     r=[B_hc, LFB], w=[PSD])
                P.op("act", lambda e: e.activation(out=ed[k][:, 0, :], in_=psd[:, 0:128], func=AF.Exp, scale=sgn), r=[PSD], w=[ED[k]])
                P.op("act", lambda e: e.activation(out=ed[k][:, 1, :], in_=psd[:, 0:128], func=AF.Exp, scale=-sgn),
                     r=[PSD], w=[ED[k]])
                P.op("act", lambda e: e.activation(out=ecs[d][:, slot, :], in_=psd[:, 128:140], func=AF.Exp, scale=sgn),
                     r=[PSD], w=[PREP[d][slot]])
                P.op("pool", lambda e: e.tensor_tensor(out=qtl[k][:], in0=qs_t[p][:], in1=ed[k][:, 0, :], op=ALU.mult),
                     r=[QS[p], ED[k]], w=[QTL[k]])
                P.op("pool", lambda e: e.tensor_tensor(out=ktm[d][:, slot, :], in0=kk[k][:], in1=ed[k][:, 1, :], op=ALU.mult),
                     r=[KK[k], ED[k]], w=[PREP[d][slot]])
                P.op("pe", lambda e: e.transpose(out=pst[:, 0:128], in_=qtl[k][:], identity=ident_b[:]),
                     r=[QTL[k], B_const], w=[PSTq])
                P.op("pe", lambda e: e.transpose(out=pst[:, 128:256], in_=ktm[d][:, slot, :], identity=ident_b[:]),
                     r=[PREP[d][slot], B_const], w=[PSTq])
                P.op("act", lambda e: e.activation(out=qkT[d][:, slot, :, :],
                                                   in_=pst[:, 0:256].rearrange("p (c t) -> p c t", c=2), func=AF.Copy),
                     r=[PSTq], w=[PREP[d][slot]])

            def step(h, hp, ti, d, slot):
                a = cnt["st"] % 2
                cnt["st"] += 1
                P.op("pe", lambda e: e.matmul(psa[:, 0:128], qkT[d][:, slot, 1, :], qkT[d][:, slot, 0, :], start=True, stop=True),
                     r=[PREP[d][slot]], w=[PSA])
                P.op("dve", lambda e: e.tensor_tensor(out=attm[a][:], in0=psa[:, 0:128], in1=mask[:, d, :], op=ALU.mult),
                     r=[PSA, B_hc], w=[ATT[a]])
                P.op("dve", lambda e: e.tensor_tensor(out=vblk[a][:], in0=vv[:, ti, :].unsqueeze(1).broadcast_to([128, 4, 128]),
                                                      in1=cmb[:], op=ALU.mult), r=[VV[ti], B_hc], w=[VB[a]])
                P.op("pe", lambda e: e.matmul(psu[:, 0:512], ktm[d][:, slot, :], vblk[a][:, :, :].rearrange("p c v -> p (c v)"),
                                              start=True, stop=True), r=[PREP[d][slot], VB[a]], w=[PSU])
                P.op("pe", lambda e: e.matmul(psa[:, 128:256], attm[a][:], vv[:, ti, :], start=True, stop=False),
                     r=[ATT[a], VV[ti]], w=[PSOt])
                P.op("pool", lambda e: e.tensor_copy(
                    qz[a][:, 0:640].rearrange("p (c x) -> p c x", x=160)[:, :, 0:32],
                    qkT[d][:, slot, 0, :].rearrange("p (c i) -> p c i", i=32)), r=[PREP[d][slot]], w=[QZ[a]])
                order = range(4) if d == 0 else range(3, -1, -1)
                for c in range(4):
                    P.op("dve", lambda e, c=c: e.tensor_scalar(out=tmpu4[:, c, :], in0=psu[:, c * 128:(c + 1) * 128],
                                                               scalar1=ecs[d][:, slot, 3 * c + 1:3 * c + 2], scalar2=None,
                                                               op0=ALU.mult), r=[PSU, PREP[d][slot]], w=[TU4])
                for n, c in enumerate(order):
                    sb = n % 2
                    cur = spar[d]
                    nxt = 1 - cur
                    P.op("act", lambda e, c=c, sb=sb, cur=cur: e.activation(out=SBF[d][sb][:], in_=S2[d][cur][:], func=AF.Identity,
                                                                             scale=ecs[d][:, slot, 3 * c + 2:3 * c + 3]),
                         r=[SS2[d][cur], PREP[d][slot]], w=[SSB[d][sb]])
                    P.op("pe", lambda e, c=c, sb=sb, n=n: e.matmul(psa[:, 128:256], qz[a][:, c * 128:(c + 1) * 128],
                                                                    SBF[d][sb][:], start=False, stop=(n == 3)),
                         r=[QZ[a], SSB[d][sb]], w=[PSOt])
                    P.op("dve", lambda e, c=c, cur=cur, nxt=nxt: e.scalar_tensor_tensor(
                        out=S2[d][nxt][:], in0=S2[d][cur][:], scalar=ecs[d][:, slot, 3 * c:3 * c + 1], in1=tmpu4[:, c, :],
                        op0=ALU.mult, op1=ALU.add), r=[SS2[d][cur], TU4, PREP[d][slot]], w=[SS2[d][nxt]])
                    spar[d] = nxt
                if d == 0:
                    P.op("act", lambda e: e.activation(out=of[:, ti, :], in_=psa[:, 128:256], func=AF.Copy), r=[PSOt], w=[OF[ti]])
                else:
                    f = cnt["fin"] % 2
                    cnt["fin"] += 1
                    P.op("dve", lambda e: e.tensor_tensor(out=osum[f][:], in0=psa[:, 128:256], in1=of[:, ti, :], op=ALU.add),
                         r=[PSOt, OF[ti]], w=[OS[f]])
                    P.op("act", lambda e: e.activation(out=junk[:], in_=osum[f][:], func=AF.Square, accum_out=stt[f][:, 0:1]),
                         r=[OS[f]], w=[JK, ST[f]])
                    rstd_ops(stt[f], ST[f], 0, 1, 128)
                    P.op("dve", lambda e: e.scalar_tensor_tensor(out=btm[f][:], in0=osum[f][:], scalar=stt[f][:, 1:2],
                                                                 in1=GGt[:, ti, :], op0=ALU.mult, op1=ALU.mult),
                         r=[OS[f], ST[f], GGB[ti]], w=[BTM[f]])
                    P.op("pe", lambda e: e.transpose(out=pst[:, 256:384], in_=btm[f][:], identity=ident_b[:]),
                         r=[BTM[f], B_const], w=[PSTb])
                    P.op("act", lambda e: e.activation(out=bT[hp][:, ti * 128:(ti + 1) * 128], in_=pst[:, 256:384], func=AF.Copy),
                         r=[PSTb], w=[BT[hp]])

            for h in range(cfg.HG_HEADS):
                hp = h % 2
                load_w(stg, SB, wbf[hp][:], G["wB_in"][e_idx, h], KC, 640, WB[hp], kq=1)
                load_lb(h)
                P.op("pool", lambda e: e.memset(S2[0][spar[0]][:], 0.0), w=[SS2[0][spar[0]]])
                P.op("pool", lambda e: e.memset(S2[1][spar[1]][:], 0.0), w=[SS2[1][spar[1]]])
                pbase = cnt["p"]
                cnt["p"] += NT

                def inproj(ti):
                    p = (pbase + ti) % 2
                    for kc in range(KC):
                        P.op("pe", lambda e, kc=kc: e.matmul(
                            psp[p][0][:, :], hT[:, kc, ti * 128:(ti + 1) * 128], wbf[hp][:, kc, 0:512],
                            start=(kc == 0), stop=(kc == KC - 1)), r=[HT[ti], WB[hp]], w=[PSP[p][0]])
                    for kc in range(KC):
                        P.op("pe", lambda e, kc=kc: e.matmul(
                            psp[p][1][:, 0:128], hT[:, kc, ti * 128:(ti + 1) * 128], wbf[hp][:, kc, 512:640],
                            start=(kc == 0), stop=(kc == KC - 1)), r=[HT[ti], WB[hp]], w=[PSP[p][1]])

                inproj(0)
                for ti in range(NT):
                    p = (pbase + ti) % 2
                    if ti + 1 < NT:
                        inproj(ti + 1)
                    tile_common(h, hp, ti, p)
                    prep(h, hp, ti, p, 0, ti % 2)
                    prep(h, hp, ti, p, 1, ti)
                    step(h, hp, ti, 0, ti % 2)
                order_b = list(range(NT_C - 1, -1, -1)) + list(range(NT - 1, NT_C - 1, -1))
                for ti in order_b:
                    step(h, hp, ti, 1, ti)
                tlo = 0 if need_ctx else TC
                ch = cfg.DA_HEADS + h
                P.dma("sp", gscr[ch, :, tlo:T], bT[hp][:, tlo:T], r=[BT[hp]], w=[GS[ch]])

    def even_mixer(e_idx, l, s, need_ctx):
        attention_part(e_idx, l, s, need_ctx)
        hgrn_part(e_idx, l, s, need_ctx)

    return even_mixer


def prep_shared(cfg, inp):
    D, KC, R = cfg.D, cfg.KC, cfg.R
    f = lambda a: np.ascontiguousarray(a, dtype=np.float32)
    sh = {}
    sh["w_mod"] = f(inp["w_mod"])
    b_mod = np.asarray(inp["b_mod"], np.float32)
    bc = b_mod[:, :2 * D].reshape(cfg.DEPTH, 2 * KC, 128).transpose(2, 0, 1)
    sh["bmodc"] = f(np.repeat(bc[:, :, :, None], R, axis=3))
    gp = np.asarray(inp["g_pre"], np.float32).reshape(cfg.DEPTH, KC, 128).transpose(2, 0, 1)
    sh["gprec"] = f(np.repeat(gp[:, :, :, None], R, axis=3))
    sh["bmodg"] = f(np.repeat(b_mod[None, :, 2 * D:], R, axis=0))
    sh["gpostr"] = f(np.repeat(np.asarray(inp["g_post"], np.float32)[None], R, axis=0))
    sel = np.zeros((R, R * 128), np.float32)
    for r in range(R):
        sel[r, r * 128:(r + 1) * 128] = 1.0
    sh["sel"] = sel
    sh["ident"] = np.eye(128, dtype=np.float32)
    W = np.asarray(inp["ev_w_in"], np.float32)
    NE = cfg.N_EVEN
    A = W[:, :, :4 * cfg.DA_W].reshape(NE, KC, 128, 4, cfg.DA_HEADS, 128)
    sh["wA"] = f(A.transpose(0, 4, 2, 1, 3, 5).reshape(NE, cfg.DA_HEADS, 128, KC, 512))
    B = W[:, :, 4 * cfg.DA_W:].reshape(NE, KC, 128, 5, cfg.HG_HEADS, 128)
    sh["wB"] = f(B.transpose(0, 4, 2, 1, 3, 5).reshape(NE, cfg.HG_HEADS, 128, KC, 640))
    sh["wEO"] = f(np.asarray(inp["ev_w_out"], np.float32).reshape(NE, KC, 128, D).transpose(0, 2, 1, 3))
    bcast = lambda a: f(np.broadcast_to(np.asarray(a, np.float32)[None], (128,) + tuple(np.shape(a))))
    sh["lamb"] = bcast(inp["ev_lambda"])
    sh["sublnb"] = bcast(inp["ev_subln_g"])
    sh["hgnb"] = bcast(inp["ev_hg_norm_g"])
    sh["lblb"] = bcast(inp["ev_hg_lb_logits"])
    sh["ropec"], sh["ropes"] = rope_tables(cfg)
    sh["hgM"], sh["hgmask"], sh["hgw"] = hgrn_consts()
    cm = np.zeros((128, 4), np.float32)
    for c in range(4):
        cm[c * 32:(c + 1) * 32, c] = 1.0
    sh["hgcm"] = cm
    NO = cfg.N_ODD
    PG, PW = cfg.PG, cfg.D
    Wo = np.asarray(inp["od_w_in"], np.float32)
    U = Wo[:, :, :PW].reshape(NO, KC, 128, 4, PG)
    Z = Wo[:, :, PW:].reshape(NO, KC, 128, 4, PG)
    UZ = np.concatenate([U, Z], axis=4)
    sh["wOD"] = f(UZ.transpose(0, 3, 2, 1, 4))
    WP = np.asarray(inp["od_w_pool"], np.float32).reshape(NO, 4, cfg.PGC, 128, PG)
    sh["wPL"] = f(WP.transpose(0, 1, 3, 2, 4))
    sh["wOO"] = f(np.asarray(inp["od_w_out"], np.float32).reshape(NO, KC, 128, D).transpose(0, 2, 1, 3))
    sh["odsc"] = f(np.asarray(inp["od_scale"], np.float32).reshape(NO, KC, 128).transpose(2, 0, 1))
    pc, maps = pool_consts(cfg)
    sh["poolc"] = f(pc)
    return sh, maps, pc.shape[0]


def prep_core(cfg, inp, sh, b0):
    NB, KC, R = cfg.NB, cfg.KC, cfg.R
    m = dict(sh)
    m["x"] = np.ascontiguousarray(inp["x"][b0:b0 + NB], dtype=np.float32)
    m["ctx"] = np.ascontiguousarray(inp["ctx"][b0:b0 + NB], dtype=np.float32)
    rows = np.concatenate([np.asarray(inp["c"], np.float32)[b0:b0 + NB], np.asarray(inp["c_ctx"], np.float32)[None]], 0)
    m["cT"] = np.ascontiguousarray(rows.reshape(R, KC, 128).transpose(2, 1, 0))
    return m


def kernel(**inputs):
    cfg = FULL
    ncores = 8
    sh, maps, nblk = prep_shared(cfg, inputs)
    nc = build(cfg, pool_maps=maps, n_pool_blocks=nblk)
    in_maps = [prep_core(cfg, inputs, sh, i * cfg.NB) for i in range(ncores)]
    res = run_bass_kernel_spmd(nc, in_maps, core_ids=list(range(ncores)))
    out = np.concatenate([np.asarray(r["y"], dtype=np.float32) for r in res.results], axis=0)
    return out
```

```python
import math
from contextlib import ExitStack, contextmanager
import numpy as np
import concourse.bass as bass
import concourse.mybir as mybir
from concourse.bass_utils import run_bass_kernel_spmd

F32 = mybir.dt.float32
BF16 = mybir.dt.bfloat16
AF = mybir.ActivationFunctionType
ALU = mybir.AluOpType
AX = mybir.AxisListType
EPS = 1e-6
ROPE_BASE = 10000.0
POOL_WINDOWS = (2, 4, 8, 16)
HC = 32


class Cfg:
    def __init__(s, D=2048, T_LAT=2048, T_CTX=256, GRID_W=64, DA_HEADS=8, HG_HEADS=8, NB=2, DEPTH=4):
        s.D, s.T_LAT, s.T_CTX, s.GRID_W = D, T_LAT, T_CTX, GRID_W
        s.DA_HEADS, s.HG_HEADS, s.NB, s.DEPTH = DA_HEADS, HG_HEADS, NB, DEPTH
        s.KC = D // 128
        s.NT_C = T_CTX // 128
        s.NT_L = T_LAT // 128
        s.NT = s.NT_C + s.NT_L
        s.T = T_CTX + T_LAT
        s.DA_W = DA_HEADS * 128
        s.HG_W = HG_HEADS * 128
        s.EVEN_IN = 4 * s.DA_W + 5 * s.HG_W
        s.EVEN_OUT = s.DA_W + s.HG_W
        assert s.EVEN_OUT == D
        s.PG = D // 4
        s.PGC = s.PG // 128
        s.R = NB + 1
        s.NBK = max(1, D // 512)
        s.BW = min(512, D)
        s.N_EVEN = (DEPTH + 1) // 2
        s.N_ODD = DEPTH // 2


FULL = Cfg()


class Ev:
    __slots__ = ("sem", "val", "ek")

    def __init__(s, sem, val, ek):
        s.sem, s.val, s.ek = sem, val, ek


class Buf:
    __slots__ = ("name", "w", "rs")

    def __init__(s, name):
        s.name, s.w, s.rs = name, None, {}


ENG = ("pe", "act", "dve", "pool", "sp")
SEM_LIMIT = 15000


class Prog:
    def __init__(s, nc, stack, n_dma=32):
        s.nc = nc
        s.stack = stack
        s.e = {"pe": nc.tensor, "act": nc.scalar, "dve": nc.vector, "pool": nc.gpsimd, "sp": nc.sync}
        nsem = {"pe": 12, "act": 5, "dve": 6, "pool": 5, "sp": 1}
        s.sems = {k: [stack.enter_context(nc.semaphore(f"s_{k}_{i}")) for i in range(n)] for k, n in nsem.items()}
        s.si = {k: 0 for k in ENG}
        s.cnt = {k: 0 for k in ENG}
        s.seen = {k: {} for k in ENG}
        s.dsems = [stack.enter_context(nc.semaphore(f"s_dma_{i}")) for i in range(n_dma)]
        s.dval = [0] * n_dma
        s.di = 0
        s.uid = 0
        s.nwait = 0
        s.nops = 0

    def _wait(s, ek, ev):
        seen = s.seen[ek]
        if seen.get(ev.sem, 0) >= ev.val:
            return
        s.e[ek].wait_ge(ev.sem, ev.val)
        seen[ev.sem] = ev.val
        s.nwait += 1

    def _deps(s, ek, r, w):
        for b in r:
            if b.w is not None and not (b.w.ek == ek and ek == "pe"):
                s._wait(ek, b.w)
        for b in w:
            if b.w is not None and not (b.w.ek == ek and ek == "pe"):
                s._wait(ek, b.w)
            for ev in b.rs.values():
                if ev.ek != ek:
                    s._wait(ek, ev)

    def _mark(s, ev, r, w):
        for b in r:
            b.rs[ev.sem] = ev
        for b in w:
            b.w = ev
            b.rs = {}

    def op(s, ek, fn, r=(), w=()):
        s._deps(ek, r, w)
        ins = fn(s.e[ek])
        if s.cnt[ek] >= SEM_LIMIT:
            s.si[ek] += 1
            s.cnt[ek] = 0
        s.cnt[ek] += 1
        sem = s.sems[ek][s.si[ek]]
        ins.then_inc(sem, 1)
        ev = Ev(sem, s.cnt[ek], ek)
        s._mark(ev, r, w)
        s.nops += 1
        return ev

    def dma(s, qk, out, in_, r=(), w=()):
        s._deps(qk, r, w)
        i = s.di
        s.di = (s.di + 1) % len(s.dsems)
        sem = s.dsems[i]
        if s.dval[i] > 0:
            s._wait(qk, Ev(sem, s.dval[i], None))
        s.dval[i] += 16
        s.e[qk].dma_start(out=out, in_=in_).then_inc(sem, 16)
        ev = Ev(sem, s.dval[i], None)
        s._mark(ev, r, w)
        s.nops += 1
        return ev

    def barrier(s, engines=ENG):
        evs = [Ev(s.sems[k][s.si[k]], s.cnt[k], k) for k in ENG if s.cnt[k] > 0]
        evs += [Ev(s.dsems[i], s.dval[i], None) for i in range(len(s.dsems)) if s.dval[i] > 0]
        for ek in engines:
            for ev in evs:
                if ev.ek != ek:
                    s._wait(ek, ev)

    def final_wait(s, ek="sp"):
        s.barrier(engines=(ek,))

    @contextmanager
    def scope(s):
        st = ExitStack()
        sc = Scope(s, st)
        try:
            yield sc
        finally:
            s.barrier()
            st.close()


class Scope:
    def __init__(s, P, st):
        s.P, s.st = P, st

    def sb(s, name, shape, dt):
        s.P.uid += 1
        return s.st.enter_context(s.P.nc.sbuf_tensor(f"{name}_{s.P.uid}", list(shape), dt))

    def ps(s, name, shape, dt):
        s.P.uid += 1
        return s.st.enter_context(s.P.nc.psum_tensor(f"{name}_{s.P.uid}", list(shape), dt))


def rope_tables(cfg):
    T = cfg.T_LAT
    t = np.arange(T)
    row, col = t // cfg.GRID_W, t % cfg.GRID_W
    nf = 16
    inv = ROPE_BASE ** (-np.arange(nf, dtype=np.float32) / nf)
    cosT = np.zeros((T, 64), np.float32)
    sinT = np.zeros((T, 64), np.float32)
    for blk, pos in ((0, row), (1, col)):
        ang = pos.astype(np.float32)[:, None] * inv[None, :]
        c, sn = np.cos(ang), np.sin(ang)
        cosT[:, blk * 32:blk * 32 + 16] = c
        cosT[:, blk * 32 + 16:blk * 32 + 32] = c
        sinT[:, blk * 32:blk * 32 + 16] = -sn
        sinT[:, blk * 32 + 16:blk * 32 + 32] = sn
    cos2 = np.concatenate([cosT, cosT], 1)
    sin2 = np.concatenate([sinT, sinT], 1)
    cos2 = cos2.reshape(cfg.NT_L, 128, 128).transpose(1, 0, 2).copy()
    sin2 = sin2.reshape(cfg.NT_L, 128, 128).transpose(1, 0, 2).copy()
    return cos2, sin2


def hgrn_consts():
    n = 128
    s = np.arange(n)[:, None]
    t = np.arange(n)[None, :]
    same = (s // HC) == (t // HC)
    mid_f = (t // HC) * HC + HC // 2 - 1
    mid_b = (t // HC) * HC + HC // 2
    M1f = (same & (s <= t)).astype(np.float32) - (same & (s <= mid_f)).astype(np.float32)
    M1b = (same & (s >= t)).astype(np.float32) - (same & (s >= mid_b)).astype(np.float32)
    maskf = (same & (s <= t)).astype(np.float32)
    maskb = (same & (s >= t)).astype(np.float32)
    nchunk = n // HC
    wf = np.zeros((n, nchunk * 3), np.float32)
    wb = np.zeros((n, nchunk * 3), np.float32)
    for c in range(nchunk):
        lo, hi = c * HC, (c + 1) * HC
        mf = lo + HC // 2 - 1
        mb = lo + HC // 2
        idx = np.arange(n)
        inck = (idx >= lo) & (idx < hi)
        wf[:, c * 3 + 0] = inck
        wf[:, c * 3 + 1] = inck & (idx > mf)
        wf[:, c * 3 + 2] = inck & (idx <= mf)
        wb[:, c * 3 + 0] = inck
        wb[:, c * 3 + 1] = inck & (idx < mb)
        wb[:, c * 3 + 2] = inck & (idx >= mb)
    return np.stack([M1f, M1b]), np.stack([maskf, maskb]), np.stack([wf, wb])


def pool_blocks(T):
    out = {}
    nt = T // 128
    for wi, w in enumerate(POOL_WINDOWS):
        t = np.arange(T)
        lo = np.clip(t - w // 2, 0, T)
        hi = np.clip(t + (w - w // 2), 0, T)
        M = np.zeros((T, T), np.float32)
        for tt in range(T):
            M[tt, lo[tt]:hi[tt]] = 1.0 / (hi[tt] - lo[tt])
            M[tt, tt] -= 1.0
        MT = M.T
        for ti in range(nt):
            for di in (-1, 0, 1):
                si = ti + di
                if 0 <= si < nt:
                    out[(wi, ti, di)] = MT[si * 128:(si + 1) * 128, ti * 128:(ti + 1) * 128].copy()
    return out


def pool_consts(cfg):
    uniq = []
    keys = {}
    maps = {}
    for seg, T in (("c", cfg.T_CTX), ("l", cfg.T_LAT)):
        blocks = pool_blocks(T)
        for k, blk in blocks.items():
            kb = blk.tobytes()
            if kb not in keys:
                keys[kb] = len(uniq)
                uniq.append(blk)
            maps[(seg,) + k] = keys[kb]
    return np.stack(uniq), maps


def build(cfg, layers=None, pool_maps=None, n_pool_blocks=0):
    layers = list(range(cfg.DEPTH)) if layers is None else layers
    D, KC, T, NT, NT_C, NT_L, R, NB = cfg.D, cfg.KC, cfg.T, cfg.NT, cfg.NT_C, cfg.NT_L, cfg.R, cfg.NB
    BW, NBK = cfg.BW, cfg.NBK
    nc = bass.Bass("TRN2", target_bir_lowering=False)

    def din(name, shape, dt=F32):
        return nc.dram_tensor(name, list(shape), dt, kind="ExternalInput").ap()

    x_in = din("x", [NB, cfg.T_LAT, D])
    ctx_in = din("ctx", [NB, cfg.T_CTX, D])
    cT_in = din("cT", [128, KC, R])
    wmod_in = din("w_mod", [cfg.DEPTH, D, 3 * D])
    bmodc_in = din("bmodc", [128, cfg.DEPTH, 2 * KC, R])
    gprec_in = din("gprec", [128, cfg.DEPTH, KC, R])
    bmodg_in = din("bmodg", [R, cfg.DEPTH, D])
    gpostr_in = din("gpostr", [R, cfg.DEPTH, D])
    sel_in = din("sel", [R, R * 128])
    ident_in = din("ident", [128, 128])
    wA_in = din("wA", [cfg.N_EVEN, cfg.DA_HEADS, 128, KC, 512])
    wB_in = din("wB", [cfg.N_EVEN, cfg.HG_HEADS, 128, KC, 640])
    wEO_in = din("wEO", [cfg.N_EVEN, 128, KC, D])
    lam_in = din("lamb", [128, cfg.N_EVEN, 4, 64])
    subln_in = din("sublnb", [128, cfg.N_EVEN, 128])
    hgn_in = din("hgnb", [128, cfg.N_EVEN, 128])
    lbl_in = din("lblb", [128, cfg.N_EVEN, 2, cfg.HG_W])
    cos_in = din("ropec", [128, NT_L, 128])
    sin_in = din("ropes", [128, NT_L, 128])
    hgM_in = din("hgM", [2, 128, 128])
    hgmask_in = din("hgmask", [2, 128, 128])
    hgw_in = din("hgw", [2, 128, 12])
    hgcm_in = din("hgcm", [128, 4])
    wOD_in = din("wOD", [cfg.N_ODD, 4, 128, KC, 2 * cfg.PG])
    wPL_in = din("wPL", [cfg.N_ODD, 4, 128, cfg.PGC, cfg.PG])
    wOO_in = din("wOO", [cfg.N_ODD, 128, KC, D])
    odsc_in = din("odsc", [128, cfg.N_ODD, KC])
    poolc_in = din("poolc", [max(1, n_pool_blocks), 128, 128])

    y_out = nc.dram_tensor("y", [NB, cfg.T_LAT, D], F32, kind="ExternalOutput").ap()
    cscr = nc.dram_tensor("cscr", [NB, cfg.T_CTX, D], F32, kind="Internal").ap()
    gscr = nc.dram_tensor("gscr", [KC, 128, T], BF16, kind="Internal").ap()

    with ExitStack() as stack:
        P = Prog(nc, stack)
        top = Scope(P, stack)

        hT = top.sb("hT", [128, KC, T], BF16)
        HT = [Buf(f"HT{i}") for i in range(NT)]
        ident_f = top.sb("identf", [128, 128], F32)
        ident_b = top.sb("identb", [128, 128], BF16)
        scT = top.sb("scT", [128, KC, R], F32)
        selT = top.sb("selT", [R, R * 128], F32)
        modc = top.sb("modc", [128, 2 * KC, R], F32)
        Acol = top.sb("Acol", [128, KC, R], F32)
        GTrow = top.sb("GTrow", [R, D], F32)
        bmodc = top.sb("bmodcs", [128, cfg.DEPTH, 2 * KC, R], F32)
        gprec = top.sb("gprecs", [128, cfg.DEPTH, KC, R], F32)
        B_const = Buf("const")
        B_scT = Buf("scT")
        B_modc = Buf("modc")
        B_Acol = Buf("Acol")
        B_GTrow = Buf("GTrow")
        DXL = [[Buf(f"dxl{s}_{i}") for i in range(NT_L)] for s in range(NB)]
        DXC = [[Buf(f"dxc{s}_{i}") for i in range(NT_C)] for s in range(NB)]
        GS = [Buf(f"gs{k}") for k in range(KC)]

        P.dma("sp", ident_f[:], ident_in[:, :], w=[B_const])
        P.dma("sp", scT[:], cT_in[:, :, :], w=[B_scT])
        P.dma("sp", selT[:], sel_in[:, :], w=[B_const])
        P.dma("sp", bmodc[:], bmodc_in[:, :, :, :], w=[B_const])
        P.dma("sp", gprec[:], gprec_in[:, :, :, :], w=[B_const])
        P.op("dve", lambda e: e.tensor_copy(ident_b[:], ident_f[:]), r=[B_const], w=[B_const])
        P.op("act", lambda e: e.activation(out=scT[:], in_=scT[:], func=AF.Silu), r=[B_scT], w=[B_scT])

        st_state = {"i": 0}

        def load_w(sc_stage, SB, dst, src, nk, ncols, wbuf, kq=4):
            for k0 in range(0, nk, kq):
                k1 = min(nk, k0 + kq)
                i = st_state["i"] % 2
                st_state["i"] += 1
                stg = sc_stage[i]
                v = stg[:, 0:(k1 - k0) * ncols].rearrange("p (k n) -> p k n", n=ncols)
                P.dma("sp", v, src[:, k0:k1, :], w=[SB[i]])
                P.op("pool", lambda e, v=v, k0=k0, k1=k1: e.tensor_copy(dst[:, k0:k1, :], v), r=[SB[i]], w=[wbuf])

        def mod_phase(l):
            with P.scope() as sc:
                wm = [sc.sb("wm", [128, KC, 512], F32) for _ in range(2)]
                WM = [Buf("wm0"), Buf("wm1")]
                bg = sc.sb("bg", [R, D], F32)
                gp = sc.sb("gp", [R, D], F32)
                B_bg = Buf("bg")
                psA = sc.ps("psA", [128, 512], F32)
                psB = sc.ps("psB", [128, 512], F32)
                PSA, PSB = Buf("psA"), Buf("psB")
                P.dma("sp", bg[:], bmodg_in[:, l, :], w=[B_bg])
                P.dma("sp", gp[:], gpostr_in[:, l, :], w=[B_bg])
                wsrc = wmod_in[l].rearrange("(kc p) n -> p kc n", p=128)
                ncolp = (2 * D) // 512
                for piece in range((3 * D) // 512):
                    sl = piece % 2
                    n0 = piece * 512
                    P.dma("sp", wm[sl][:], wsrc[:, :, n0:n0 + 512], w=[WM[sl]])
                    if piece < ncolp:
                        for jj in range(4):
                            for kc in range(KC):
                                P.op("pe", lambda e, jj=jj, kc=kc, sl=sl: e.matmul(
                                    psA[:, jj * R:(jj + 1) * R], wm[sl][:, kc, jj * 128:(jj + 1) * 128], scT[:, kc, :],
                                    start=(jj == 0 and kc == 0), stop=(jj == 3 and kc == KC - 1)),
                                    r=[WM[sl], B_scT], w=[PSA])
                        j0 = piece * 4
                        P.op("dve", lambda e, j0=j0: e.tensor_tensor(
                            out=modc[:, j0:j0 + 4, :], in0=psA[:, 0:4 * R].rearrange("p (j r) -> p j r", r=R),
                            in1=bmodc[:, l, j0:j0 + 4, :], op=ALU.add), r=[PSA, B_const], w=[B_modc])
                    else:
                        nb = piece - ncolp
                        for kc in range(KC):
                            P.op("pe", lambda e, kc=kc, sl=sl: e.matmul(
                                psB[0:R, :], scT[:, kc, :], wm[sl][:, kc, :], start=(kc == 0), stop=(kc == KC - 1)),
                                r=[WM[sl], B_scT], w=[PSB])
                        P.op("dve", lambda e, nb=nb: e.tensor_tensor(
                            out=GTrow[0:R, nb * 512:(nb + 1) * 512], in0=psB[0:R, :],
                            in1=bg[0:R, nb * 512:(nb + 1) * 512], op=ALU.add), r=[PSB, B_bg], w=[B_GTrow])
                        P.op("dve", lambda e, nb=nb: e.tensor_tensor(
                            out=GTrow[0:R, nb * 512:(nb + 1) * 512], in0=GTrow[0:R, nb * 512:(nb + 1) * 512],
                            in1=gp[0:R, nb * 512:(nb + 1) * 512], op=ALU.mult), r=[B_GTrow, B_bg], w=[B_GTrow])
                P.op("dve", lambda e: e.scalar_tensor_tensor(
                    out=Acol[:], in0=modc[:, KC:2 * KC, :], scalar=1.0, in1=gprec[:, l, :, :],
                    op0=ALU.add, op1=ALU.mult), r=[B_modc, B_const], w=[B_Acol])

        def x_src(l, s, ti):
            if ti < NT_C:
                src = ctx_in if l == layers[0] else cscr
                return src[s, ti * 128:(ti + 1) * 128, :], DXC[s][ti]
            tl = ti - NT_C
            src = x_in if l == layers[0] else y_out
            return src[s, tl * 128:(tl + 1) * 128, :], DXL[s][tl]

        def phase_n(l, s, ctx_active):
            with P.scope() as sc:
                xt = [sc.sb("xt", [128, D], F32) for _ in range(2)]
                xh = [sc.sb("xh", [128, D], BF16) for _ in range(2)]
                junk = sc.sb("junk", [128, D], BF16)
                stt = [sc.sb("stt", [128, 4], F32) for _ in range(2)]
                pst = [sc.ps("pst", [128, KC * 128], BF16) for _ in range(2)]
                XT = [Buf("xt0"), Buf("xt1")]
                XH = [Buf("xh0"), Buf("xh1")]
                ST = [Buf("st0"), Buf("st1")]
                PST = [Buf("pst0"), Buf("pst1")]
                JK = Buf("junk")
                tiles = list(range(NT)) if ctx_active else list(range(NT_C, NT))
                for n, ti in enumerate(tiles):
                    p = n % 2
                    r = R - 1 if ti < NT_C else s
                    src, dbuf = x_src(l, s, ti)
                    P.dma("sp", xt[p][:], src, r=[dbuf], w=[XT[p]])
                    P.op("act", lambda e, p=p: e.activation(out=junk[:], in_=xt[p][:], func=AF.Square,
                                                            accum_out=stt[p][:, 0:1]), r=[XT[p]], w=[JK, ST[p]])
                    P.op("dve", lambda e, p=p: e.tensor_scalar(out=stt[p][:, 1:2], in0=stt[p][:, 0:1], scalar1=1.0 / D,
                                                               scalar2=EPS, op0=ALU.mult, op1=ALU.add), r=[ST[p]], w=[ST[p]])
                    P.op("act", lambda e, p=p: e.activation(out=stt[p][:, 2:3], in_=stt[p][:, 1:2], func=AF.Ln),
                         r=[ST[p]], w=[ST[p]])
                    P.op("act", lambda e, p=p: e.activation(out=stt[p][:, 3:4], in_=stt[p][:, 2:3], func=AF.Exp, scale=-0.5),
                         r=[ST[p]], w=[ST[p]])
                    P.op("dve", lambda e, p=p: e.tensor_scalar(out=xh[p][:], in0=xt[p][:], scalar1=stt[p][:, 3:4],
                                                               scalar2=None, op0=ALU.mult), r=[XT[p], ST[p]], w=[XH[p]])
                    for kc in range(KC):
                        P.op("pe", lambda e, p=p, kc=kc: e.transpose(
                            out=pst[p][:, kc * 128:(kc + 1) * 128], in_=xh[p][:, kc * 128:(kc + 1) * 128],
                            identity=ident_b[:]), r=[XH[p], B_const], w=[PST[p]])
                    for kc in range(KC):
                        ek = "act" if kc % 2 == 0 else "dve"
                        if ek == "act":
                            P.op("act", lambda e, p=p, kc=kc, ti=ti, r=r: e.activation(
                                out=hT[:, kc, ti * 128:(ti + 1) * 128], in_=pst[p][:, kc * 128:(kc + 1) * 128],
                                func=AF.Identity, scale=Acol[:, kc, r:r + 1], bias=modc[:, kc, r:r + 1]),
                                r=[PST[p], B_Acol, B_modc], w=[HT[ti]])
                        else:
                            P.op("dve", lambda e, p=p, kc=kc, ti=ti, r=r: e.tensor_scalar(
                                out=hT[:, kc, ti * 128:(ti + 1) * 128], in0=pst[p][:, kc * 128:(kc + 1) * 128],
                                scalar1=Acol[:, kc, r:r + 1], scalar2=modc[:, kc, r:r + 1], op0=ALU.mult, op1=ALU.add),
                                r=[PST[p], B_Acol, B_modc], w=[HT[ti]])

        def phase_o(l, s, wo_src, need_ctx):
            with P.scope() as sc:
                wO = sc.sb("wO", [128, KC, D], BF16)
                WO = Buf("wO")
                stg = [sc.sb("stg", [128, 2 * 512], F32) for _ in range(2)]
                SB = [Buf("stg0"), Buf("stg1")]
                GTb = [sc.sb("GTb", [128, D], F32) for _ in range(2)]
                B_GTb = Buf("GTb")
                xt = [sc.sb("xo", [128, D], F32) for _ in range(2)]
                tt = sc.sb("tt", [128, D], F32)
                junk = sc.sb("junko", [128, NBK, BW], BF16)
                stt = [sc.sb("stto", [128, 8], F32) for _ in range(2)]
                psy = [[sc.ps("psy", [128, 512], F32) for _ in range(NBK)] for _ in range(2 if NBK <= 4 else 1)]
                PSY = [[Buf("psy") for _ in range(NBK)] for _ in range(len(psy))]
                XT = [Buf("xo0"), Buf("xo1")]
                TT = Buf("tt")
                JKS = [Buf(f"jko{i}") for i in range(NBK)]
                ST = [Buf("sto0"), Buf("sto1")]
                tlo = 0 if need_ctx else NT_C * 128
                for kc in range(KC):
                    P.dma("sp", hT[:, kc, tlo:T], gscr[kc, :, tlo:T], r=[GS[kc]], w=HT)
                for nb in range(NBK):
                    load_w(stg, SB, wO[:, :, nb * BW:(nb + 1) * BW], wo_src[:, :, nb * BW:(nb + 1) * BW], KC, BW, WO, kq=2)
                rows = [s, R - 1] if need_ctx else [s]
                for gi, r in enumerate(rows):
                    for nb in range(NBK):
                        P.op("pe", lambda e, r=r, nb=nb: e.matmul(
                            psy[0][nb][:, 0:BW], selT[0:R, r * 128:(r + 1) * 128], GTrow[0:R, nb * BW:(nb + 1) * BW],
                            start=True, stop=True), r=[B_const, B_GTrow], w=[PSY[0][nb]])
                        P.op("act", lambda e, gi=gi, nb=nb: e.activation(
                            out=GTb[gi][:, nb * BW:(nb + 1) * BW], in_=psy[0][nb][:, 0:BW], func=AF.Copy),
                            r=[PSY[0][nb]], w=[B_GTb])
                tiles = list(range(NT)) if need_ctx else list(range(NT_C, NT))
                for n, ti in enumerate(tiles):
                    p = n % 2
                    pp = n % len(psy)
                    gi = 1 if ti < NT_C else 0
                    src, dbuf = x_src(l, s, ti)
                    P.dma("sp", xt[p][:], src, r=[dbuf], w=[XT[p]])
                    for nb in range(NBK):
                        for kc in range(KC):
                            P.op("pe", lambda e, pp=pp, nb=nb, kc=kc, ti=ti: e.matmul(
                                psy[pp][nb][:, 0:BW], hT[:, kc, ti * 128:(ti + 1) * 128],
                                wO[:, kc, nb * BW:(nb + 1) * BW], start=(kc == 0), stop=(kc == KC - 1)),
                                r=[HT[ti], WO], w=[PSY[pp][nb]])
                    for nb in range(NBK):
                        P.op("act", lambda e, p=p, pp=pp, nb=nb: e.activation(
                            out=junk[:, nb, :], in_=psy[pp][nb][:, 0:BW], func=AF.Square, accum_out=stt[p][:, nb:nb + 1]),
                            r=[PSY[pp][nb]], w=[JKS[nb], ST[p]])
                    P.op("dve", lambda e, p=p: e.reduce_sum(out=stt[p][:, 4:5], in_=stt[p][:, 0:NBK], axis=AX.X),
                         r=[ST[p]], w=[ST[p]])
                    P.op("dve", lambda e, p=p: e.tensor_scalar(out=stt[p][:, 5:6], in0=stt[p][:, 4:5], scalar1=1.0 / D,
                                                               scalar2=EPS, op0=ALU.mult, op1=ALU.add), r=[ST[p]], w=[ST[p]])
                    P.op("act", lambda e, p=p: e.activation(out=stt[p][:, 6:7], in_=stt[p][:, 5:6], func=AF.Ln),
                         r=[ST[p]], w=[ST[p]])
                    P.op("act", lambda e, p=p: e.activation(out=stt[p][:, 7:8], in_=stt[p][:, 6:7], func=AF.Exp, scale=-0.5),
                         r=[ST[p]], w=[ST[p]])
                    for nb in range(NBK):
                        P.op("dve", lambda e, p=p, pp=pp, nb=nb, gi=gi: e.scalar_tensor_tensor(
                            out=tt[:, nb * BW:(nb + 1) * BW], in0=psy[pp][nb][:, 0:BW], scalar=stt[p][:, 7:8],
                            in1=GTb[gi][:, nb * BW:(nb + 1) * BW], op0=ALU.mult, op1=ALU.mult),
                            r=[PSY[pp][nb], ST[p], B_GTb], w=[TT])
                    P.op("pool", lambda e, p=p: e.tensor_tensor(out=xt[p][:], in0=tt[:], in1=xt[p][:], op=ALU.add),
                         r=[TT, XT[p]], w=[XT[p]])
                    if ti < NT_C:
                        dst, dbuf2 = cscr[s, ti * 128:(ti + 1) * 128, :], DXC[s][ti]
                    else:
                        tl = ti - NT_C
                        dst, dbuf2 = y_out[s, tl * 128:(tl + 1) * 128, :], DXL[s][tl]
                    P.dma("sp", dst, xt[p][:], r=[XT[p]], w=[dbuf2])

        def odd_mixer(o, s, ctx_active):
            PG, PGC = cfg.PG, cfg.PGC
            segs = ([("c", 0, NT_C)] if ctx_active else []) + [("l", NT_C, NT_L)]
            with P.scope() as sc:
                stg = [sc.sb("stg", [128, 4 * 640], F32) for _ in range(2)]
                SB = [Buf("stg0"), Buf("stg1")]
                wU = sc.sb("wU", [128, KC, PG], BF16)
                wZ = sc.sb("wZ", [128, KC, PG], BF16)
                wP = sc.sb("wP", [128, PGC, PG], BF16)
                WU, WZ, WP = Buf("wU"), Buf("wZ"), Buf("wP")
                pcf = sc.sb("pcf", [128, n_pool_blocks, 128], F32)
                pcb = sc.sb("pcb", [128, n_pool_blocks, 128], BF16)
                B_pc = Buf("pc")
                lsc = sc.sb("lsc", [128, KC], F32)
                utm = sc.sb("utm", [128, NT, PG], BF16)
                UT = [Buf(f"ut{i}") for i in range(NT)]
                rT = sc.sb("rT", [128, PGC, T], BF16)
                RT = [Buf(f"rt{i}") for i in range(NT)]
                sz = [sc.sb("sz", [128, 512], BF16) for _ in range(2)]
                SZ = [Buf("sz0"), Buf("sz1")]
                gch = [sc.sb("gch", [128, T], BF16) for _ in range(2)]
                GCH = [Buf("gch0"), Buf("gch1")]
                psu = [sc.ps("psu", [128, 512], F32) for _ in range(2)]
                PSU = [Buf("psu0"), Buf("psu1")]
                psr = [sc.ps("psr", [128, 512], F32) for _ in range(2)]
                PSR = [Buf("psr0"), Buf("psr1")]
                psq = [sc.ps("psq", [128, 512], F32) for _ in range(2)]
                PSQ = [Buf("psq0"), Buf("psq1")]
                psz = [sc.ps("psz", [128, 512], F32) for _ in range(2)]
                PSZ = [Buf("psz0"), Buf("psz1")]
                P.dma("sp", pcf[:], poolc_in.rearrange("n p c -> p n c"), w=[B_pc])
                P.op("dve", lambda e: e.tensor_copy(pcb[:], pcf[:]), r=[B_pc], w=[B_pc])
                P.dma("sp", lsc[:], odsc_in[:, o, :], w=[B_pc])
                cu = cr = cq = 0
                gcount = 0
                for j in range(4):
                    load_w(stg, SB, wU[:], wOD_in[o, j][:, :, 0:PG], KC, PG, WU)
                    load_w(stg, SB, wZ[:], wOD_in[o, j][:, :, PG:2 * PG], KC, PG, WZ)
                    load_w(stg, SB, wP[:], wPL_in[o, j], PGC, PG, WP)
                    tiles = list(range(NT)) if ctx_active else list(range(NT_C, NT))
                    for ti in tiles:
                        for n0 in range(0, PG, 512):
                            nw = min(512, PG - n0)
                            p = cu % 2
                            cu += 1
                            for kc in range(KC):
                                P.op("pe", lambda e, p=p, kc=kc, ti=ti, n0=n0, nw=nw: e.matmul(
                                    psu[p][:, 0:nw], hT[:, kc, ti * 128:(ti + 1) * 128], wU[:, kc, n0:n0 + nw],
                                    start=(kc == 0), stop=(kc == KC - 1)), r=[HT[ti], WU], w=[PSU[p]])
                            P.op("act", lambda e, p=p, ti=ti, n0=n0, nw=nw: e.activation(
                                out=utm[:, ti, n0:n0 + nw], in_=psu[p][:, 0:nw], func=AF.Copy), r=[PSU[p]], w=[UT[ti]])
                    for (seg, t0, nts) in segs:
                        for fc in range(PGC):
                            for tb in range(0, nts, 4):
                                ntb = min(4, nts - tb)
                                p = cr % 2
                                cr += 1
                                for tq in range(ntb):
                                    tl = tb + tq
                                    dis = [di for di in (-1, 0, 1) if 0 <= tl + di < nts]
                                    for ii, di in enumerate(dis):
                                        bi = pool_maps[(seg, j, tl, di)]
                                        first = (tq == 0 and ii == 0)
                                        last = (tq == ntb - 1 and ii == len(dis) - 1)
                                        P.op("pe", lambda e, p=p, tq=tq, fc=fc, bi=bi, sti=t0 + tl + di, first=first, last=last:
                                             e.matmul(psr[p][:, tq * 128:(tq + 1) * 128],
                                                      utm[:, sti, fc * 128:(fc + 1) * 128], pcb[:, bi, :],
                                                      start=first, stop=last),
                                             r=[UT[t0 + tl + di], B_pc], w=[PSR[p]])
                                tg0 = (t0 + tb) * 128
                                P.op("dve", lambda e, p=p, fc=fc, tg0=tg0, ntb=ntb: e.tensor_copy(
                                    rT[:, fc, tg0:tg0 + ntb * 128], psr[p][:, 0:ntb * 128]),
                                    r=[PSR[p]], w=[RT[t0 + tb + q] for q in range(ntb)])
                    for fc in range(PGC):
                        gch_i = gcount % 2
                        gcount += 1
                        chunk = j * PGC + fc
                        for (seg, t0, nts) in segs:
                            for tb in range(0, nts, 4):
                                ntb = min(4, nts - tb)
                                nw = ntb * 128
                                tg0 = (t0 + tb) * 128
                                p = cq % 2
                                cq += 1
                                tbufs = [t0 + tb + q for q in range(ntb)]
                                for kc2 in range(PGC):
                                    P.op("pe", lambda e, p=p, kc2=kc2, fc=fc, tg0=tg0, nw=nw: e.matmul(
                                        psq[p][:, 0:nw], wP[:, kc2, fc * 128:(fc + 1) * 128], rT[:, kc2, tg0:tg0 + nw],
                                        start=(kc2 == 0), stop=(kc2 == PGC - 1)),
                                        r=[WP] + [RT[q] for q in tbufs], w=[PSQ[p]])
                                for kc in range(KC):
                                    P.op("pe", lambda e, p=p, kc=kc, fc=fc, tg0=tg0, nw=nw: e.matmul(
                                        psz[p][:, 0:nw], wZ[:, kc, fc * 128:(fc + 1) * 128], hT[:, kc, tg0:tg0 + nw],
                                        start=(kc == 0), stop=(kc == KC - 1)),
                                        r=[WZ] + [HT[q] for q in tbufs], w=[PSZ[p]])
                                P.op("act", lambda e, p=p, nw=nw: e.activation(out=sz[p][:, 0:nw], in_=psz[p][:, 0:nw],
                                                                                func=AF.Silu), r=[PSZ[p]], w=[SZ[p]])
                                P.op("dve", lambda e, p=p, nw=nw, tg0=tg0, gch_i=gch_i, chunk=chunk: e.scalar_tensor_tensor(
                                    out=gch[gch_i][:, tg0:tg0 + nw], in0=psq[p][:, 0:nw], scalar=lsc[:, chunk:chunk + 1],
                                    in1=sz[p][:, 0:nw], op0=ALU.mult, op1=ALU.mult),
                                    r=[PSQ[p], SZ[p], B_pc], w=[GCH[gch_i]])
                        tlo = 0 if ctx_active else NT_C * 128
                        P.dma("sp", gscr[chunk, :, tlo:T], gch[gch_i][:, tlo:T], r=[GCH[gch_i]], w=[GS[chunk]])

        even_mixer = make_even_mixer(cfg, nc, P, dict(
            hT=hT, HT=HT, ident_b=ident_b, B_const=B_const, gscr=gscr, GS=GS, load_w=load_w,
            wA_in=wA_in, wB_in=wB_in, lam_in=lam_in, subln_in=subln_in, hgn_in=hgn_in, lbl_in=lbl_in,
            cos_in=cos_in, sin_in=sin_in, hgM_in=hgM_in, hgmask_in=hgmask_in, hgw_in=hgw_in, hgcm_in=hgcm_in))

        for l in layers:
            even = (l % 2 == 0)
            need_ctx = l < cfg.DEPTH - 1
            ctx_active = even or need_ctx
            mod_phase(l)
            for s in range(NB):
                phase_n(l, s, ctx_active)
                if even:
                    even_mixer(l // 2, l, s, need_ctx)
                    phase_o(l, s, wEO_in[l // 2], need_ctx)
                else:
                    odd_mixer(l // 2, s, ctx_active and need_ctx)
                    phase_o(l, s, wOO_in[l // 2], need_ctx)
        P.final_wait("sp")
        build.stats = (P.nops, P.nwait, dict(P.cnt), dict(P.si))
    return nc


def make_even_mixer(cfg, nc, P, G):
    D, KC, T, NT, NT_C, NT_L, R, NB = cfg.D, cfg.KC, cfg.T, cfg.NT, cfg.NT_C, cfg.NT_L, cfg.R, cfg.NB
    hT, HT, ident_b, B_const, gscr, GS, load_w = (G[k] for k in ("hT", "HT", "ident_b", "B_const", "gscr", "GS", "load_w"))
    TC = cfg.T_CTX

    def rstd_ops(stt, ST, c_ss, c_out, n):
        P.op("dve", lambda e: e.tensor_scalar(out=stt[:, c_out:c_out + 1], in0=stt[:, c_ss:c_ss + 1], scalar1=1.0 / n,
                                              scalar2=EPS, op0=ALU.mult, op1=ALU.add), r=[ST], w=[ST])
        P.op("act", lambda e: e.activation(out=stt[:, c_out:c_out + 1], in_=stt[:, c_out:c_out + 1], func=AF.Ln),
             r=[ST], w=[ST])
        P.op("act", lambda e: e.activation(out=stt[:, c_out:c_out + 1], in_=stt[:, c_out:c_out + 1], func=AF.Exp, scale=-0.5),
             r=[ST], w=[ST])

    def attention_part(e_idx, l, s, need_ctx):
        lam_init = 0.8 - 0.6 * math.exp(-0.3 * l)
        with P.scope() as sc:
            stg = [sc.sb("stg", [128, 4 * 512], F32) for _ in range(2)]
            SB = [Buf("stg0"), Buf("stg1")]
            wbf = [sc.sb("wbfA", [128, KC, 512], BF16) for _ in range(2)]
            WB = [Buf("wbA0"), Buf("wbA1")]
            cosT = sc.sb("cosT", [128, NT_L, 128], F32)
            sinT = sc.sb("sinT", [128, NT_L, 128], F32)
            B_rope = Buf("rope")
            lamt = sc.sb("lamt", [128, 4, 64], F32)
            lamw = sc.sb("lamw", [128, 2, 64], F32)
            lams = sc.sb("lams", [128, 8], F32)
            B_lam = Buf("lam")
            Gt = sc.sb("Gt", [128, 128], F32)
            B_G = Buf("G")
            QKT = sc.sb("QKT", [128, 2, T], BF16)
            QK = [Buf(f"qk{i}") for i in range(NT)]
            Vaug = sc.sb("Vaug", [128, NT, 130], BF16)
            VA = [Buf(f"va{i}") for i in range(NT)]
            GGt = sc.sb("GGt", [128, NT, 128], F32)
            GGB = [Buf(f"gg{i}") for i in range(NT)]
            qktm = [sc.sb("qktm", [128, 256], BF16) for _ in range(2)]
            QKTM = [Buf("qktm0"), Buf("qktm1")]
            t1 = [sc.sb("t1", [128, 256], F32) for _ in range(2)]
            t2 = [sc.sb("t2", [128, 256], F32) for _ in range(2)]
            T1 = [Buf("t10"), Buf("t11")]
            T2 = [Buf("t20"), Buf("t21")]
            eg = [sc.sb("eg", [128, 128], F32) for _ in range(2)]
            EG = [Buf("eg0"), Buf("eg1")]
            Et = [sc.sb("Et", [128, 512], BF16) for _ in range(3)]
            ET = [Buf(f"et{i}") for i in range(3)]
            stt4 = [sc.sb("stt4", [128, 8], F32) for _ in range(4)]
            ST4 = [Buf(f"st4{i}") for i in range(4)]
            o14 = [sc.sb("o14", [128, 128], F32) for _ in range(4)]
            O14 = [Buf(f"o14{i}") for i in range(4)]
            o24 = [sc.sb("o24", [128, 128], F32) for _ in range(4)]
            O24 = [Buf(f"o24{i}") for i in range(4)]
            junk4 = sc.sb("junk4", [128, 4, 128], BF16)
            JK4 = [Buf(f"jk4{i}") for i in range(4)]
            atm4 = [sc.sb("atm4", [128, 128], BF16) for _ in range(4)]
            ATM4 = [Buf(f"atm4{i}") for i in range(4)]
            aT = [sc.sb("aT", [128, T], BF16) for _ in range(2)]
            AT = [Buf("aT0"), Buf("aT1")]
            psp = [sc.ps("psp", [128, 512], F32) for _ in range(2)]
            PSP = [Buf("psp0"), Buf("psp1")]
            pss = [sc.ps("pss", [128, 512], F32) for _ in range(2)]
            PSS = [Buf("pss0"), Buf("pss1")]
            pso = [sc.ps("pso", [128, 512], F32) for _ in range(3)]
            PSO = [Buf("pso0"), Buf("pso1"), Buf("pso2")]
            pst = sc.ps("pstA", [128, 1024], BF16)
            PSTq = Buf("pstq")
            PSTa = PSTq

            P.dma("sp", cosT[:], G["cos_in"][:, :, :], w=[B_rope])
            P.dma("sp", sinT[:], G["sin_in"][:, :, :], w=[B_rope])
            P.dma("sp", lamt[:], G["lam_in"][:, e_idx, :, :], w=[B_lam])
            P.dma("sp", Gt[:], G["subln_in"][:, e_idx, :], w=[B_G])
            P.op("dve", lambda e: e.tensor_scalar(out=Gt[:], in0=Gt[:], scalar1=(1.0 - lam_init), scalar2=None,
                                                  op0=ALU.mult), r=[B_G], w=[B_G])
            P.op("dve", lambda e: e.tensor_tensor(out=lamw[:, 0, :], in0=lamt[:, 0, :], in1=lamt[:, 1, :], op=ALU.mult),
                 r=[B_lam], w=[B_lam])
            P.op("dve", lambda e: e.tensor_tensor(out=lamw[:, 1, :], in0=lamt[:, 2, :], in1=lamt[:, 3, :], op=ALU.mult),
                 r=[B_lam], w=[B_lam])
            P.op("dve", lambda e: e.reduce_sum(out=lams[:, 0:2], in_=lamw[:, :, :], axis=AX.X), r=[B_lam], w=[B_lam])
            P.op("act", lambda e: e.activation(out=lams[:, 2:4], in_=lams[:, 0:2], func=AF.Exp), r=[B_lam], w=[B_lam])
            P.op("dve", lambda e: e.tensor_tensor(out=lams[:, 4:5], in0=lams[:, 2:3], in1=lams[:, 3:4], op=ALU.subtract),
                 r=[B_lam], w=[B_lam])
            P.op("dve", lambda e: e.tensor_scalar(out=lams[:, 5:6], in0=lams[:, 4:5], scalar1=lam_init, scalar2=-1.0,
                                                  op0=ALU.add, op1=ALU.mult), r=[B_lam], w=[B_lam])
            P.op("pool", lambda e: e.memset(Vaug[:, :, 128:130], 1.0), w=VA)

            cnt = {"p": 0, "s": 0, "e": 0, "ep": 0}
            for h in range(cfg.DA_HEADS):
                hp = h % 2
                load_w(stg, SB, wbf[hp][:], G["wA_in"][e_idx, h], KC, 512, WB[hp])
                for ti in range(NT):
                    p = cnt["p"] % 2
                    cnt["p"] += 1
                    for kc in range(KC):
                        P.op("pe", lambda e, p=p, kc=kc, ti=ti, hp=hp: e.matmul(
                            psp[p][:, :], hT[:, kc, ti * 128:(ti + 1) * 128], wbf[hp][:, kc, :],
                            start=(kc == 0), stop=(kc == KC - 1)), r=[HT[ti], WB[hp]], w=[PSP[p]])
                    if ti >= NT_C:
                        tl = ti - NT_C
                        for c0 in (0, 128):
                            X = psp[p][:, c0:c0 + 128]
                            Xv = X.rearrange("p (a h i) -> p a h i", h=2, i=16)
                            Sv = sinT[:, tl, :].rearrange("p (a h i) -> p a h i", h=2, i=16)
                            t2v = t2[p][:, c0:c0 + 128].rearrange("p (a h i) -> p a h i", h=2, i=16)
                            P.op("dve", lambda e, p=p, c0=c0, X=X, tl=tl: e.tensor_tensor(
                                out=t1[p][:, c0:c0 + 128], in0=X, in1=cosT[:, tl, :], op=ALU.mult),
                                r=[PSP[p], B_rope], w=[T1[p]])
                            P.op("dve", lambda e, Xv=Xv, Sv=Sv, t2v=t2v: e.tensor_tensor(
                                out=t2v[:, :, 0, :], in0=Xv[:, :, 1, :], in1=Sv[:, :, 0, :], op=ALU.mult),
                                r=[PSP[p], B_rope], w=[T2[p]])
                            P.op("dve", lambda e, Xv=Xv, Sv=Sv, t2v=t2v: e.tensor_tensor(
                                out=t2v[:, :, 1, :], in0=Xv[:, :, 0, :], in1=Sv[:, :, 1, :], op=ALU.mult),
                                r=[PSP[p], B_rope], w=[T2[p]])
                        P.op("pool", lambda e, p=p: e.tensor_tensor(out=qktm[p][:], in0=t1[p][:], in1=t2[p][:], op=ALU.add),
                             r=[T1[p], T2[p]], w=[QKTM[p]])
                    else:
                        P.op("act", lambda e, p=p: e.activation(out=qktm[p][:], in_=psp[p][:, 0:256], func=AF.Copy),
                             r=[PSP[p]], w=[QKTM[p]])
                    for c in range(2):
                        P.op("pe", lambda e, p=p, c=c: e.transpose(out=pst[:, c * 128:(c + 1) * 128],
                                                                    in_=qktm[p][:, c * 128:(c + 1) * 128], identity=ident_b[:]),
                             r=[QKTM[p], B_const], w=[PSTq])
                    P.op("act", lambda e, ti=ti: e.activation(
                        out=QKT[:, :, ti * 128:(ti + 1) * 128], in_=pst[:, 0:256].rearrange("p (c t) -> p c t", c=2),
                        func=AF.Copy), r=[PSTq], w=[QK[ti]])
                    P.op("act", lambda e, p=p, ti=ti: e.activation(out=Vaug[:, ti, 0:128], in_=psp[p][:, 256:384], func=AF.Copy),
                         r=[PSP[p]], w=[VA[ti]])
                    P.op("act", lambda e, p=p: e.activation(out=eg[p][:], in_=psp[p][:, 384:512], func=AF.Exp, scale=-1.0),
                         r=[PSP[p]], w=[EG[p]])
                    P.op("act", lambda e, p=p: e.activation(out=eg[p][:], in_=eg[p][:], func=AF.Ln, bias=1.0), r=[EG[p]], w=[EG[p]])
                    P.op("act", lambda e, p=p: e.activation(out=eg[p][:], in_=eg[p][:], func=AF.Exp, scale=-1.0), r=[EG[p]], w=[EG[p]])
                    P.op("pool", lambda e, p=p: e.tensor_tensor(out=eg[p][:], in0=eg[p][:], in1=Gt[:], op=ALU.mult),
                         r=[EG[p], B_G], w=[EG[p]])
                    P.op("dve", lambda e, p=p, ti=ti: e.tensor_tensor(out=GGt[:, ti, :], in0=psp[p][:, 384:512], in1=eg[p][:],
                                                                      op=ALU.mult), r=[PSP[p], EG[p]], w=[GGB[ti]])

                blocks = []
                if need_ctx:
                    blocks.append((0, TC, list(range(NT_C))))
                for qb in range(0, cfg.T_LAT, 512):
                    blocks.append((TC + qb, min(512, cfg.T_LAT - qb), list(range(NT))))
                for (q0, nq, kts) in blocks:
                    nqs = nq // 128
                    nacc = 2 * nqs
                    nbank = (nacc + 2) // 3
                    firsts = {b: True for b in range(nbank)}
                    qtiles = [q0 // 128 + i for i in range(nqs)]
                    items = [(ki, kt, m) for ki, kt in enumerate(kts) for m in range(2)]
                    pslot = {}

                    def emit_score(j):
                        ki, kt, m = items[j]
                        p = cnt["s"] % 2
                        cnt["s"] += 1
                        pslot[j] = p
                        P.op("pe", lambda e: e.matmul(
                            pss[p][:, 0:nq], QKT[m * 64:(m + 1) * 64, 1, kt * 128:(kt + 1) * 128],
                            QKT[m * 64:(m + 1) * 64, 0, q0:q0 + nq], start=True, stop=True),
                            r=[QK[kt]] + [QK[q] for q in qtiles], w=[PSS[p]])

                    emit_score(0)
                    for j, (ki, kt, m) in enumerate(items):
                        if j + 1 < len(items):
                            emit_score(j + 1)
                        p = pslot[j]
                        ei = cnt["e"] % 3
                        cnt["e"] += 1
                        P.op("act", lambda e, p=p, ei=ei: e.activation(
                            out=Et[ei][:, 0:nq], in_=pss[p][:, 0:nq], func=AF.Exp, scale=0.125),
                            r=[PSS[p]], w=[ET[ei]])
                        for qs in range(nqs):
                            idx = m * nqs + qs
                            b, slot = idx // 3, idx % 3
                            is_first = firsts[b]
                            firsts[b] = False
                            last_idx_in_bank = min(nacc - 1, b * 3 + 2)
                            is_last = (ki == len(kts) - 1) and (idx == last_idx_in_bank)
                            P.op("pe", lambda e, ei=ei, qs=qs, kt=kt, b=b, slot=slot, is_first=is_first, is_last=is_last:
                                 e.matmul(pso[b][:, slot * 129:slot * 129 + 129], Et[ei][:, qs * 128:(qs + 1) * 128],
                                          Vaug[:, kt, 0:129], start=is_first, stop=is_last),
                                 r=[ET[ei], VA[kt]], w=[PSO[b]])
                    QS_ = list(range(nqs))
                    tis = [q0 // 128 + qs for qs in QS_]
                    loc = []
                    for qs in QS_:
                        i0_, i1_ = qs, nqs + qs
                        loc.append((i0_ // 3, (i0_ % 3) * 129, i1_ // 3, (i1_ % 3) * 129))
                    for qs in QS_:
                        b0, s0, b1, s1 = loc[qs]
                        P.op("dve", lambda e, qs=qs, b0=b0, s0=s0: e.reciprocal(out=stt4[qs][:, 0:1], in_=pso[b0][:, s0 + 128:s0 + 129]),
                             r=[PSO[b0]], w=[ST4[qs]])
                        P.op("dve", lambda e, qs=qs, b1=b1, s1=s1: e.reciprocal(out=stt4[qs][:, 1:2], in_=pso[b1][:, s1 + 128:s1 + 129]),
                             r=[PSO[b1]], w=[ST4[qs]])
                    for qs in QS_:
                        P.op("dve", lambda e, qs=qs: e.tensor_tensor(out=stt4[qs][:, 2:3], in0=stt4[qs][:, 1:2], in1=lams[:, 5:6],
                                                                     op=ALU.mult), r=[ST4[qs], B_lam], w=[ST4[qs]])
                    for qs in QS_:
                        b0, s0, b1, s1 = loc[qs]
                        P.op("dve", lambda e, qs=qs, b0=b0, s0=s0: e.tensor_scalar(
                            out=o14[qs][:], in0=pso[b0][:, s0:s0 + 128], scalar1=stt4[qs][:, 0:1], scalar2=None, op0=ALU.mult),
                            r=[PSO[b0], ST4[qs]], w=[O14[qs]])
                    for qs in QS_:
                        b0, s0, b1, s1 = loc[qs]
                        P.op("dve", lambda e, qs=qs, b1=b1, s1=s1: e.scalar_tensor_tensor(
                            out=o24[qs][:], in0=pso[b1][:, s1:s1 + 128], scalar=stt4[qs][:, 2:3], in1=o14[qs][:],
                            op0=ALU.mult, op1=ALU.add), r=[PSO[b1], ST4[qs], O14[qs]], w=[O24[qs]])
                    for qs in QS_:
                        P.op("act", lambda e, qs=qs: e.activation(out=junk4[:, qs, :], in_=o24[qs][:], func=AF.Square,
                                                                  accum_out=stt4[qs][:, 3:4]), r=[O24[qs]], w=[JK4[qs], ST4[qs]])
                    for qs in QS_:
                        P.op("dve", lambda e, qs=qs: e.tensor_scalar(out=stt4[qs][:, 4:5], in0=stt4[qs][:, 3:4], scalar1=1.0 / 128,
                                                                     scalar2=EPS, op0=ALU.mult, op1=ALU.add), r=[ST4[qs]], w=[ST4[qs]])
                    for qs in QS_:
                        P.op("act", lambda e, qs=qs: e.activation(out=stt4[qs][:, 4:5], in_=stt4[qs][:, 4:5], func=AF.Ln),
                             r=[ST4[qs]], w=[ST4[qs]])
                    for qs in QS_:
                        P.op("act", lambda e, qs=qs: e.activation(out=stt4[qs][:, 4:5], in_=stt4[qs][:, 4:5], func=AF.Exp, scale=-0.5),
                             r=[ST4[qs]], w=[ST4[qs]])
                    for qs in QS_:
                        P.op("dve", lambda e, qs=qs: e.scalar_tensor_tensor(
                            out=atm4[qs][:], in0=o24[qs][:], scalar=stt4[qs][:, 4:5], in1=GGt[:, tis[qs], :],
                            op0=ALU.mult, op1=ALU.mult), r=[O24[qs], ST4[qs], GGB[tis[qs]]], w=[ATM4[qs]])
                    for qs in QS_:
                        P.op("pe", lambda e, qs=qs: e.transpose(out=pst[:, 256 + qs * 128:384 + qs * 128], in_=atm4[qs][:],
                                                                identity=ident_b[:]), r=[ATM4[qs], B_const], w=[PSTa])
                    P.op("act", lambda e: e.activation(out=aT[hp][:, q0:q0 + nq], in_=pst[:, 256:256 + nq], func=AF.Copy),
                         r=[PSTa], w=[AT[hp]])
                tlo = 0 if need_ctx else TC
                P.dma("sp", gscr[h, :, tlo:T], aT[hp][:, tlo:T], r=[AT[hp]], w=[GS[h]])

    def hgrn_part(e_idx, l, s, need_ctx):
        HW = cfg.HG_W
        with P.scope() as sc:
            stg = [sc.sb("stg", [128, 640], F32) for _ in range(2)]
            SB = [Buf("stg0"), Buf("stg1")]
            wbf = [sc.sb("wbfB", [128, KC, 640], BF16) for _ in range(2)]
            WB = [Buf("wbB0"), Buf("wbB1")]
            LBt = sc.sb("LBt", [128, 2, 128], F32)
            OML = sc.sb("OML", [128, 2, 128], F32)
            B_lb = Buf("lb")
            hgG = sc.sb("hgG", [128, 128], F32)
            M1 = sc.sb("M1", [128, 2, 128], F32)
            mask = sc.sb("mask", [128, 2, 128], F32)
            wcol = sc.sb("wcol", [128, 2, 12], F32)
            B_hc = Buf("hgc")
            qs_t = [sc.sb("qs", [128, 128], F32) for _ in range(2)]
            QS = [Buf("qs0"), Buf("qs1")]
            vv = sc.sb("vv", [128, NT, 128], BF16)
            VV = [Buf(f"vv{i}") for i in range(NT)]
            GGt = sc.sb("GGb", [128, NT, 128], BF16)
            GGB = [Buf(f"ggb{i}") for i in range(NT)]
            of = sc.sb("of", [128, NT, 128], F32)
            OF = [Buf(f"of{i}") for i in range(NT)]
            nsl = [2, NT]
            qkT = [sc.sb("qkTh", [128, nsl[d], 2, 128], BF16) for d in range(2)]
            ktm = [sc.sb("ktm", [128, nsl[d], 128], BF16) for d in range(2)]
            ecs = [sc.sb("ecs", [128, nsl[d], 12], F32) for d in range(2)]
            PREP = [[Buf(f"prep{d}_{i}") for i in range(nsl[d])] for d in range(2)]
            sg = [sc.sb("sg", [128, 128], F32) for _ in range(2)]
            SG = [Buf("sg0"), Buf("sg1")]
            ff = [sc.sb("ff", [128, 128], F32) for _ in range(2)]
            FF = [Buf("ff0"), Buf("ff1")]
            lf = [sc.sb("lf", [128, 128], F32) for _ in range(2)]
            LF = [Buf("lf0"), Buf("lf1")]
            kk = [sc.sb("kk", [128, 128], F32) for _ in range(2)]
            KK = [Buf("kk0"), Buf("kk1")]
            ed = [sc.sb("ed", [128, 2, 128], F32) for _ in range(2)]
            ED = [Buf("ed0"), Buf("ed1")]
            qtl = [sc.sb("qtl", [128, 128], BF16) for _ in range(2)]
            ee3 = [sc.sb("ee3", [128, 512], F32) for _ in range(2)]
            L3 = [sc.sb("L3", [128, 512], F32) for _ in range(2)]
            s3 = [sc.sb("s3", [128, 512], F32) for _ in range(2)]
            EE3 = [Buf("ee30"), Buf("ee31")]
            LL3 = [Buf("L30"), Buf("L31")]
            SS3 = [Buf("s30"), Buf("s31")]
            QTL = [Buf("qtl0"), Buf("qtl1")]
            eg = [sc.sb("egb", [128, 128], F32) for _ in range(2)]
            EG = [Buf("egb0"), Buf("egb1")]
            SBF = [[sc.sb("Sbf", [128, 128], BF16) for _ in range(2)] for _ in range(2)]
            SS = [Buf("S0"), Buf("S1")]
            SSB = [[Buf("Sb00"), Buf("Sb01")], [Buf("Sb10"), Buf("Sb11")]]
            qz = [sc.sb("qz", [128, 640], BF16) for _ in range(2)]
            QZ = [Buf("qz0"), Buf("qz1")]
            cm = sc.sb("cm", [128, 4], F32)
            cmb = sc.sb("cmb", [128, 4, 128], BF16)
            ones_t = sc.sb("ones_t", [128, 128], F32)
            vblk = [sc.sb("vblk", [128, 4, 128], BF16) for _ in range(2)]
            VB = [Buf("vb0"), Buf("vb1")]
            attm = [sc.sb("attm", [128, 128], BF16) for _ in range(2)]
            ATT = [Buf("att0"), Buf("att1")]
            tmpu4 = sc.sb("tmpu4", [128, 4, 128], F32)
            TU4 = Buf("tu4")
            S2 = [[sc.sb("S2", [128, 128], F32) for _ in range(2)] for _ in range(2)]
            SS2 = [[Buf("S200"), Buf("S201")], [Buf("S210"), Buf("S211")]]
            spar = [0, 0]
            osum = [sc.sb("osum", [128, 128], F32) for _ in range(2)]
            OS = [Buf("os0"), Buf("os1")]
            stt = [sc.sb("stb", [128, 8], F32) for _ in range(2)]
            ST = [Buf("stb0"), Buf("stb1")]
            junk = sc.sb("junkb", [128, 128], BF16)
            JK = Buf("junkb")
            btm = [sc.sb("btm", [128, 128], BF16) for _ in range(2)]
            BTM = [Buf("btm0"), Buf("btm1")]
            bT = [sc.sb("bT", [128, T], BF16)] * 2
            BT = [Buf("bT0")] * 2
            psp = [[sc.ps("pspb", [128, 512], F32) for _ in range(2)] for _ in range(2)]
            PSP = [[Buf("pspb00"), Buf("pspb01")], [Buf("pspb10"), Buf("pspb11")]]
            psd = sc.ps("psd", [128, 512], F32)
            PSD = Buf("psd")
            psa = sc.ps("psa", [128, 512], F32)
            PSA = Buf("psa")
            PSOt = PSA
            psu = sc.ps("psu", [128, 512], F32)
            PSU = Buf("psu")
            pst = sc.ps("pstB", [128, 1024], BF16)
            PSTq = Buf("pstqb")
            PSTb = PSTq

            P.op("pool", lambda e: e.memset(ones_t[:], 1.0), w=[B_hc])
            P.dma("sp", hgG[:], G["hgn_in"][:, e_idx, :], w=[B_hc])
            P.dma("sp", M1[:], G["hgM_in"].rearrange("d s t -> s d t"), w=[B_hc])
            P.dma("sp", mask[:], G["hgmask_in"].rearrange("d s t -> s d t"), w=[B_hc])
            P.dma("sp", wcol[:], G["hgw_in"].rearrange("d s c -> s d c"), w=[B_hc])
            P.dma("sp", cm[:], G["hgcm_in"][:, :], w=[B_hc])
            for c in range(4):
                P.op("dve", lambda e, c=c: e.tensor_scalar(out=cmb[:, c, :], in0=ones_t[:], scalar1=cm[:, c:c + 1], scalar2=None,
                                                           op0=ALU.mult), r=[B_hc], w=[B_hc])
            for i in range(2):
                P.op("pool", lambda e, i=i: e.memset(qz[i][:], 0.0), w=[QZ[i]])
            def load_lb(h):
                if e_idx == 0:
                    if h == 0:
                        P.op("pool", lambda e: e.memset(LBt[:], 0.0), w=[B_lb])
                        P.op("pool", lambda e: e.memset(OML[:], 1.0), w=[B_lb])
                    return
                P.dma("sp", LBt[:], G["lbl_in"][:, 0, :, h * 128:(h + 1) * 128], w=[B_lb])
                P.dma("sp", OML[:], G["lbl_in"][:, 1, :, h * 128:(h + 1) * 128], w=[B_lb])
                P.op("dve", lambda e: e.tensor_tensor(out=LBt[:], in0=LBt[:], in1=OML[:], op=ALU.subtract), r=[B_lb], w=[B_lb])
                P.op("act", lambda e: e.activation(out=LBt[:], in_=LBt[:], func=AF.Exp), r=[B_lb], w=[B_lb])
                P.op("dve", lambda e: e.tensor_scalar(out=LBt[:], in0=LBt[:], scalar1=1.0, scalar2=None, op0=ALU.add),
                     r=[B_lb], w=[B_lb])
                P.op("dve", lambda e: e.reciprocal(out=LBt[:], in_=LBt[:]), r=[B_lb], w=[B_lb])
                P.op("dve", lambda e: e.tensor_scalar(out=OML[:], in0=LBt[:], scalar1=-1.0, scalar2=1.0, op0=ALU.mult,
                                                      op1=ALU.add), r=[B_lb], w=[B_lb])

            cnt = {"p": 0, "t": 0, "st": 0, "fin": 0}

            sgn = -1.0 if e_idx == 0 else 1.0

            def tile_common(h, hp, ti, p):
                P.op("act", lambda e: e.activation(out=ee3[p][:, 0:384], in_=psp[p][0][:, 0:384], func=AF.Exp, scale=-1.0),
                     r=[PSP[p][0]], w=[EE3[p]])
                P.op("act", lambda e: e.activation(out=ee3[p][:, 384:512], in_=psp[p][1][:, 0:128], func=AF.Exp, scale=-1.0),
                     r=[PSP[p][1]], w=[EE3[p]])
                P.op("act", lambda e: e.activation(out=L3[p][:], in_=ee3[p][:], func=AF.Ln, bias=1.0), r=[EE3[p]], w=[LL3[p]])
                P.op("act", lambda e: e.activation(out=s3[p][:], in_=L3[p][:], func=AF.Exp, scale=-1.0), r=[LL3[p]], w=[SS3[p]])
                P.op("dve", lambda e: e.tensor_tensor(out=qs_t[p][:], in0=psp[p][0][:, 0:128], in1=s3[p][:, 0:128], op=ALU.mult),
                     r=[PSP[p][0], SS3[p]], w=[QS[p]])
                P.op("act", lambda e: e.activation(out=vv[:, ti, :], in_=psp[p][0][:, 384:512], func=AF.Copy),
                     r=[PSP[p][0]], w=[VV[ti]])
                P.op("pool", lambda e: e.tensor_tensor(out=eg[p][:], in0=s3[p][:, 384:512], in1=hgG[:], op=ALU.mult),
                     r=[SS3[p], B_hc], w=[EG[p]])
                P.op("dve", lambda e: e.tensor_tensor(out=GGt[:, ti, :], in0=psp[p][1][:, 0:128], in1=eg[p][:], op=ALU.mult),
                     r=[PSP[p][1], EG[p]], w=[GGB[ti]])

            def prep(h, hp, ti, p, d, slot):
                k = cnt["t"] % 2
                cnt["t"] += 1
                c0 = 128 + d * 128
                if e_idx == 0:
                    lfa = L3[p][:, c0:c0 + 128]
                    LFB = LL3[p]
                    P.op("pool", lambda e: e.tensor_tensor(out=kk[k][:], in0=ee3[p][:, c0:c0 + 128], in1=s3[p][:, c0:c0 + 128],
                                                           op=ALU.mult), r=[EE3[p], SS3[p]], w=[KK[k]])
                else:
                    P.op("pool", lambda e: e.tensor_tensor(out=sg[k][:], in0=s3[p][:, c0:c0 + 128], in1=OML[:, d, :], op=ALU.mult),
                         r=[SS3[p], B_lb], w=[SG[k]])
                    P.op("pool", lambda e: e.tensor_tensor(out=ff[k][:], in0=sg[k][:], in1=LBt[:, d, :], op=ALU.add),
                         r=[SG[k], B_lb], w=[FF[k]])
                    P.op("pool", lambda e: e.tensor_tensor(out=kk[k][:], in0=OML[:, d, :], in1=sg[k][:], op=ALU.subtract),
                         r=[SG[k], B_lb], w=[KK[k]])
                    P.op("act", lambda e: e.activation(out=lf[k][:], in_=ff[k][:], func=AF.Ln), r=[FF[k]], w=[LF[k]])
                    lfa = lf[k][:]
                    LFB = LF[k]
                P.op("pe", lambda e: e.matmul(psd[:, 0:128], M1[:, d, :], lfa, start=True, stop=True),
                     r=[B_hc, LFB], w=[PSD])
                P.op("pe", lambda e: e.matmul(psd[:, 128:140], lfa, wcol[:, d, :], start=True, stop=True),
                     r=[B_hc, LFB], w=[PSD])
                P.op("act", lambda e: e.activation(out=ed[k][:, 0, :], in_=psd[:, 0:128], func=AF.Exp, scale=sgn), r=[PSD], w=[ED[k]])
                P.op("act", lambda e: e.activation(out=ed[k][:, 1, :], in_=psd[:, 0:128], func=AF.Exp, scale=-sgn),
                     r=[PSD], w=[ED[k]])
                P.op("act", lambda e: e.activation(out=ecs[d][:, slot, :], in_=psd[:, 128:140], func=AF.Exp, scale=sgn),
                     r=[PSD], w=[PREP[d][slot]])
                P.op("pool", lambda e: e.tensor_tensor(out=qtl[k][:], in0=qs_t[p][:], in1=ed[k][:, 0, :], op=ALU.mult),
                     r=[QS[p], ED[k]], w=[QTL[k]])
                P.op("pool", lambda e: e.tensor_tensor(out=ktm[d][:, slot, :], in0=kk[k][:], in1=ed[k][:, 1, :], op=ALU.mult),
                     r=[KK[k], ED[k]], w=[PREP[d][slot]])
                P.op("pe", lambda e: e.transpose(out=pst[:, 0:128], in_=qtl[k][:], identity=ident_b[:]),
                     r=[QTL[k], B_const], w=[PSTq])
                P.op("pe", lambda e: e.transpose(out=pst[:, 128:256], in_=ktm[d][:, slot, :], identity=ident_b[:]),
                     r=[PREP[d][slot], B_const], w=[PSTq])
                P.op("act", lambda e: e.activation(out=qkT[d][:, slot, :, :],
                                                   in_=pst[:, 0:256].rearrange("p (c t) -> p c t", c=2), func=AF.Copy),
                     r=[PSTq], w=[PREP[d][slot]])

            def step(h, hp, ti, d, slot):
                a = cnt["st"] % 2
                cnt["st"] += 1
                P.op("pe", lambda e: e.matmul(psa[:, 0:128], qkT[d][:, slot, 1, :], qkT[d][:, slot, 0, :], start=True, stop=True),
                     r=[PREP[d][slot]], w=[PSA])
                P.op("dve", lambda e: e.tensor_tensor(out=attm[a][:], in0=psa[:, 0:128], in1=mask[:, d, :], op=ALU.mult),
                     r=[PSA, B_hc], w=[ATT[a]])
                P.op("dve", lambda e: e.tensor_tensor(out=vblk[a][:], in0=vv[:, ti, :].unsqueeze(1).broadcast_to([128, 4, 128]),
                                                      in1=cmb[:], op=ALU.mult), r=[VV[ti], B_hc], w=[VB[a]])
                P.op("pe", lambda e: e.matmul(psu[:, 0:512], ktm[d][:, slot, :], vblk[a][:, :, :].rearrange("p c v -> p (c v)"),
                                              start=True, stop=True), r=[PREP[d][slot], VB[a]], w=[PSU])
                P.op("pe", lambda e: e.matmul(psa[:, 128:256], attm[a][:], vv[:, ti, :], start=True, stop=False),
                     r=[ATT[a], VV[ti]], w=[PSOt])
                P.op("pool", lambda e: e.tensor_copy(
                    qz[a][:, 0:640].rearrange("p (c x) -> p c x", x=160)[:, :, 0:32],
                    qkT[d][:, slot, 0, :].rearrange("p (c i) -> p c i", i=32)), r=[PREP[d][slot]], w=[QZ[a]])
                order = range(4) if d == 0 else range(3, -1, -1)
                P.op("dve", lambda e: e.tensor_tensor(
                    out=tmpu4[:], in0=psu[:, 0:512].rearrange("p (c v) -> p c v", c=4),
                    in1=ecs[d][:, slot, :].rearrange("p (c j) -> p c j", j=3)[:, :, 1:2].broadcast_to([128, 4, 128]),
                    op=ALU.mult), r=[PSU, PREP[d][slot]], w=[TU4])
                for n, c in enumerate(order):
                    sb = n % 2
                    cur = spar[d]
                    nxt = 1 - cur
                    P.op("act", lambda e, c=c, sb=sb, cur=cur: e.activation(out=SBF[d][sb][:], in_=S2[d][cur][:], func=AF.Identity,
                                                                             scale=ecs[d][:, slot, 3 * c + 2:3 * c + 3]),
                         r=[SS2[d][cur], PREP[d][slot]], w=[SSB[d][sb]])
                    P.op("pe", lambda e, c=c, sb=sb, n=n: e.matmul(psa[:, 128:256], qz[a][:, c * 128:(c + 1) * 128],
                                                                    SBF[d][sb][:], start=False, stop=(n == 3)),
                         r=[QZ[a], SSB[d][sb]], w=[PSOt])
                    P.op("dve", lambda e, c=c, cur=cur, nxt=nxt: e.scalar_tensor_tensor(
                        out=S2[d][nxt][:], in0=S2[d][cur][:], scalar=ecs[d][:, slot, 3 * c:3 * c + 1], in1=tmpu4[:, c, :],
                        op0=ALU.mult, op1=ALU.add), r=[SS2[d][cur], TU4, PREP[d][slot]], w=[SS2[d][nxt]])
                    spar[d] = nxt
                if d == 0:
                    P.op("act", lambda e: e.activation(out=of[:, ti, :], in_=psa[:, 128:256], func=AF.Copy), r=[PSOt], w=[OF[ti]])
                else:
                    f = cnt["fin"] % 2
                    cnt["fin"] += 1
                    P.op("dve", lambda e: e.tensor_tensor(out=osum[f][:], in0=psa[:, 128:256], in1=of[:, ti, :], op=ALU.add),
                         r=[PSOt, OF[ti]], w=[OS[f]])
                    P.op("act", lambda e: e.activation(out=junk[:], in_=osum[f][:], func=AF.Square, accum_out=stt[f][:, 0:1]),
                         r=[OS[f]], w=[JK, ST[f]])
                    rstd_ops(stt[f], ST[f], 0, 1, 128)
                    P.op("dve", lambda e: e.scalar_tensor_tensor(out=btm[f][:], in0=osum[f][:], scalar=stt[f][:, 1:2],
                                                                 in1=GGt[:, ti, :], op0=ALU.mult, op1=ALU.mult),
                         r=[OS[f], ST[f], GGB[ti]], w=[BTM[f]])
                    P.op("pe", lambda e: e.transpose(out=pst[:, 256:384], in_=btm[f][:], identity=ident_b[:]),
                         r=[BTM[f], B_const], w=[PSTb])
                    P.op("act", lambda e: e.activation(out=bT[hp][:, ti * 128:(ti + 1) * 128], in_=pst[:, 256:384], func=AF.Copy),
                         r=[PSTb], w=[BT[hp]])

            for h in range(cfg.HG_HEADS):
                hp = h % 2
                load_w(stg, SB, wbf[hp][:], G["wB_in"][e_idx, h], KC, 640, WB[hp], kq=1)
                load_lb(h)
                P.op("pool", lambda e: e.memset(S2[0][spar[0]][:], 0.0), w=[SS2[0][spar[0]]])
                P.op("pool", lambda e: e.memset(S2[1][spar[1]][:], 0.0), w=[SS2[1][spar[1]]])
                pbase = cnt["p"]
                cnt["p"] += NT

                def inproj(ti):
                    p = (pbase + ti) % 2
                    for kc in range(KC):
                        P.op("pe", lambda e, kc=kc: e.matmul(
                            psp[p][0][:, :], hT[:, kc, ti * 128:(ti + 1) * 128], wbf[hp][:, kc, 0:512],
                            start=(kc == 0), stop=(kc == KC - 1)), r=[HT[ti], WB[hp]], w=[PSP[p][0]])
                    for kc in range(KC):
                        P.op("pe", lambda e, kc=kc: e.matmul(
                            psp[p][1][:, 0:128], hT[:, kc, ti * 128:(ti + 1) * 128], wbf[hp][:, kc, 512:640],
                            start=(kc == 0), stop=(kc == KC - 1)), r=[HT[ti], WB[hp]], w=[PSP[p][1]])

                inproj(0)
                for ti in range(NT):
                    p = (pbase + ti) % 2
                    if ti + 1 < NT:
                        inproj(ti + 1)
                    tile_common(h, hp, ti, p)
                    prep(h, hp, ti, p, 0, ti % 2)
                    prep(h, hp, ti, p, 1, ti)
                    step(h, hp, ti, 0, ti % 2)
                order_b = list(range(NT_C - 1, -1, -1)) + list(range(NT - 1, NT_C - 1, -1))
                for ti in order_b:
                    step(h, hp, ti, 1, ti)
                tlo = 0 if need_ctx else TC
                ch = cfg.DA_HEADS + h
                P.dma("sp", gscr[ch, :, tlo:T], bT[hp][:, tlo:T], r=[BT[hp]], w=[GS[ch]])

    def even_mixer(e_idx, l, s, need_ctx):
        attention_part(e_idx, l, s, need_ctx)
        hgrn_part(e_idx, l, s, need_ctx)

    return even_mixer


def prep_shared(cfg, inp):
    D, KC, R = cfg.D, cfg.KC, cfg.R
    f = lambda a: np.ascontiguousarray(a, dtype=np.float32)
    sh = {}
    sh["w_mod"] = f(inp["w_mod"])
    b_mod = np.asarray(inp["b_mod"], np.float32)
    bc = b_mod[:, :2 * D].reshape(cfg.DEPTH, 2 * KC, 128).transpose(2, 0, 1)
    sh["bmodc"] = f(np.repeat(bc[:, :, :, None], R, axis=3))
    gp = np.asarray(inp["g_pre"], np.float32).reshape(cfg.DEPTH, KC, 128).transpose(2, 0, 1)
    sh["gprec"] = f(np.repeat(gp[:, :, :, None], R, axis=3))
    sh["bmodg"] = f(np.repeat(b_mod[None, :, 2 * D:], R, axis=0))
    sh["gpostr"] = f(np.repeat(np.asarray(inp["g_post"], np.float32)[None], R, axis=0))
    sel = np.zeros((R, R * 128), np.float32)
    for r in range(R):
        sel[r, r * 128:(r + 1) * 128] = 1.0
    sh["sel"] = sel
    sh["ident"] = np.eye(128, dtype=np.float32)
    W = np.asarray(inp["ev_w_in"], np.float32)
    NE = cfg.N_EVEN
    A = W[:, :, :4 * cfg.DA_W].reshape(NE, KC, 128, 4, cfg.DA_HEADS, 128)
    sh["wA"] = f(A.transpose(0, 4, 2, 1, 3, 5).reshape(NE, cfg.DA_HEADS, 128, KC, 512))
    B = W[:, :, 4 * cfg.DA_W:].reshape(NE, KC, 128, 5, cfg.HG_HEADS, 128)
    sh["wB"] = f(B.transpose(0, 4, 2, 1, 3, 5).reshape(NE, cfg.HG_HEADS, 128, KC, 640))
    sh["wEO"] = f(np.asarray(inp["ev_w_out"], np.float32).reshape(NE, KC, 128, D).transpose(0, 2, 1, 3))
    bcast = lambda a: f(np.broadcast_to(np.asarray(a, np.float32)[None], (128,) + tuple(np.shape(a))))
    sh["lamb"] = bcast(inp["ev_lambda"])
    sh["sublnb"] = bcast(inp["ev_subln_g"])
    sh["hgnb"] = bcast(inp["ev_hg_norm_g"])
    sh["lblb"] = bcast(inp["ev_hg_lb_logits"])
    sh["ropec"], sh["ropes"] = rope_tables(cfg)
    sh["hgM"], sh["hgmask"], sh["hgw"] = hgrn_consts()
    cm = np.zeros((128, 4), np.float32)
    for c in range(4):
        cm[c * 32:(c + 1) * 32, c] = 1.0
    sh["hgcm"] = cm
    NO = cfg.N_ODD
    PG, PW = cfg.PG, cfg.D
    Wo = np.asarray(inp["od_w_in"], np.float32)
    U = Wo[:, :, :PW].reshape(NO, KC, 128, 4, PG)
    Z = Wo[:, :, PW:].reshape(NO, KC, 128, 4, PG)
    UZ = np.concatenate([U, Z], axis=4)
    sh["wOD"] = f(UZ.transpose(0, 3, 2, 1, 4))
    WP = np.asarray(inp["od_w_pool"], np.float32).reshape(NO, 4, cfg.PGC, 128, PG)
    sh["wPL"] = f(WP.transpose(0, 1, 3, 2, 4))
    sh["wOO"] = f(np.asarray(inp["od_w_out"], np.float32).reshape(NO, KC, 128, D).transpose(0, 2, 1, 3))
    sh["odsc"] = f(np.asarray(inp["od_scale"], np.float32).reshape(NO, KC, 128).transpose(2, 0, 1))
    pc, maps = pool_consts(cfg)
    sh["poolc"] = f(pc)
    return sh, maps, pc.shape[0]


def prep_core(cfg, inp, sh, b0):
    NB, KC, R = cfg.NB, cfg.KC, cfg.R
    m = dict(sh)
    m["x"] = np.ascontiguousarray(inp["x"][b0:b0 + NB], dtype=np.float32)
    m["ctx"] = np.ascontiguousarray(inp["ctx"][b0:b0 + NB], dtype=np.float32)
    rows = np.concatenate([np.asarray(inp["c"], np.float32)[b0:b0 + NB], np.asarray(inp["c_ctx"], np.float32)[None]], 0)
    m["cT"] = np.ascontiguousarray(rows.reshape(R, KC, 128).transpose(2, 1, 0))
    return m


def kernel(**inputs):
    cfg = FULL
    ncores = 8
    sh, maps, nblk = prep_shared(cfg, inputs)
    nc = build(cfg, pool_maps=maps, n_pool_blocks=nblk)
    in_maps = [prep_core(cfg, inputs, sh, i * cfg.NB) for i in range(ncores)]
    res = run_bass_kernel_spmd(nc, in_maps, core_ids=list(range(ncores)))
    out = np.concatenate([np.asarray(r["y"], dtype=np.float32) for r in res.results], axis=0)
    return out
```
